# Optimizing a Trainium2 kernel written in Bass

```python
import jax, jax.numpy as jnp
from jax import lax
import numpy as np

D_MODEL = 1024
BATCH = 32
SEQ = 2048
DEPTH = 2

MEM_LEN = 256
BRANCH_WIDTH = D_MODEL // 2
N_BRANCH = 4

RWKV_HEAD_DIM = 64
RWKV_HEADS = BRANCH_WIDTH // RWKV_HEAD_DIM
RWKV_LORA = 64
RWKV_GN_EPS = 64e-5

MLA_NOPE = 64
MLA_ROPE = 32
MLA_V = 64
MLA_HEADS = BRANCH_WIDTH // MLA_V
MLA_Q_LORA = 3 * D_MODEL // 8
MLA_KV_LORA = D_MODEL // 4
ROPE_THETA = 10000.0
Q_BLOCK = 128

CONV_KERNEL = 31

XATTN_HEADS = 4
XATTN_HEAD_DIM = BRANCH_WIDTH // XATTN_HEADS

DEEPNORM_ALPHA = (2.0 * DEPTH) ** 0.25
DEEPNORM_BETA = (8.0 * DEPTH) ** -0.25
LN_EPS = 1e-5
RMS_EPS = 1e-6
MASK_VALUE = -1e30

RWKV_SHIFT_COLS = 3 * BRANCH_WIDTH + 2 * RWKV_LORA
SPLIT_SIZES = (
    RWKV_SHIFT_COLS, BRANCH_WIDTH,
    MLA_Q_LORA, MLA_KV_LORA, MLA_ROPE, BRANCH_WIDTH,
    2 * BRANCH_WIDTH, BRANCH_WIDTH,
    BRANCH_WIDTH, BRANCH_WIDTH,
    N_BRANCH * D_MODEL,
)
SPLIT_POINTS = tuple(int(s) for s in np.cumsum(SPLIT_SIZES)[:-1])
IN_COLS = int(sum(SPLIT_SIZES))

kernel_name = "hybrid_rwkv7_mla_conformer_xattn_deepnorm"


def _layer_norm(x, g, b):
    xf = x.astype(jnp.float32)
    mu = jnp.mean(xf, -1, keepdims=True)
    var = jnp.mean(jnp.square(xf - mu), -1, keepdims=True)
    return ((xf - mu) * lax.rsqrt(var + LN_EPS)).astype(x.dtype) * g + b


def _rms_norm(x, g):
    xf = x.astype(jnp.float32)
    y = xf * lax.rsqrt(jnp.mean(jnp.square(xf), -1, keepdims=True) + RMS_EPS)
    return y.astype(x.dtype) * g


def _rope(x, cos, sin):
    half = x.shape[-1] // 2
    x1, x2 = x[..., :half], x[..., half:]
    return jnp.concatenate([x1 * cos - x2 * sin, x1 * sin + x2 * cos], axis=-1)


def _rwkv7_branch(p, gate, mu, w0, w2, a0, a2, k_k, k_a, r_k, lnx_g, lnx_b):
    B, S, _ = p.shape
    H, N, W = RWKV_HEADS, RWKV_HEAD_DIM, BRANCH_WIDTH
    p_prev = jnp.pad(p, ((0, 0), (1, 0), (0, 0)))[:, :-1]
    p = p + mu * (p_prev - p)
    r, k, v, wl, al = jnp.split(p, (W, 2 * W, 3 * W, 3 * W + RWKV_LORA), axis=-1)
    w_log = -jax.nn.softplus(-(w0 + jnp.tanh(wl) @ w2)) - 0.5
    decay = jnp.exp(-jnp.exp(w_log.astype(jnp.float32)))
    a = jax.nn.sigmoid(a0 + al @ a2)

    def heads(t):
        return t.reshape(B, S, H, N).astype(jnp.float32)

    kk = heads(k * k_k)
    kk = kk / jnp.maximum(jnp.linalg.norm(kk, axis=-1, keepdims=True), 1e-12)
    k = k * (1.0 + (a - 1.0) * k_a)
    r_h, k_h, v_h, a_h, w_h = heads(r), heads(k), heads(v), heads(a), heads(decay)
    xs = tuple(jnp.moveaxis(t, 1, 0) for t in (r_h, w_h, k_h, v_h, -kk, kk * a_h))

    def step(state, inp):
        r_t, w_t, k_t, v_t, a_t, b_t = inp
        sa = jnp.einsum('bhvk,bhk->bhv', state, a_t)
        state = (state * w_t[:, :, None, :] + sa[..., None] * b_t[:, :, None, :]
                 + v_t[..., None] * k_t[:, :, None, :])
        return state, jnp.einsum('bhvk,bhk->bhv', state, r_t)

    s0 = jnp.zeros((B, H, N, N), jnp.float32)
    _, y = lax.scan(step, s0, xs)
    y = jnp.moveaxis(y, 0, 1)
    y_mu = jnp.mean(y, -1, keepdims=True)
    y_var = jnp.mean(jnp.square(y - y_mu), -1, keepdims=True)
    y = ((y - y_mu) * lax.rsqrt(y_var + RWKV_GN_EPS)).reshape(B, S, W) * lnx_g + lnx_b
    bonus = jnp.sum(r_h * k_h * r_k, axis=-1, keepdims=True) * v_h
    y = y + bonus.reshape(B, S, W)
    return y.astype(gate.dtype) * jax.nn.silu(gate)


def _causal_block_attention(q, k, v, scale):
    B, S, H, Dq = q.shape
    nb = S // Q_BLOCK
    qb = jnp.moveaxis(q.reshape(B, nb, Q_BLOCK, H, Dq), 1, 0)
    key_idx = jnp.arange(S)

    def one_block(args):
        q_blk, i = args
        s = jnp.einsum('bqhd,bkhd->bhqk', q_blk, k).astype(jnp.float32) * scale
        q_idx = i * Q_BLOCK + jnp.arange(Q_BLOCK)
        s = jnp.where(key_idx[None, :] <= q_idx[:, None], s, MASK_VALUE)
        pr = jax.nn.softmax(s, axis=-1).astype(v.dtype)
        return jnp.einsum('bhqk,bkhd->bqhd', pr, v)

    o = lax.map(one_block, (qb, jnp.arange(nb)))
    return jnp.moveaxis(o, 0, 1).reshape(B, S, H, v.shape[-1])


def _mla_branch(q_lat, kv_lat, k_pe, gate, cos, sin, q_norm, w_uq, kv_norm, w_ukv):
    B, S, _ = q_lat.shape
    H = MLA_HEADS
    q = (_rms_norm(q_lat, q_norm) @ w_uq).reshape(B, S, H, MLA_NOPE + MLA_ROPE)
    kv = (_rms_norm(kv_lat, kv_norm) @ w_ukv).reshape(B, S, H, MLA_NOPE + MLA_V)
    q_nope, q_pe = q[..., :MLA_NOPE], q[..., MLA_NOPE:]
    k_nope, v = kv[..., :MLA_NOPE], kv[..., MLA_NOPE:]
    q_pe = _rope(q_pe, cos[:, :, None, :], sin[:, :, None, :])
    k_pe = _rope(k_pe, cos, sin)
    q_full = jnp.concatenate([q_nope, q_pe], axis=-1)
    k_full = jnp.concatenate(
        [k_nope, jnp.broadcast_to(k_pe[:, :, None, :], (B, S, H, MLA_ROPE))], axis=-1)
    o = _causal_block_attention(q_full, k_full, v, (MLA_NOPE + MLA_ROPE) ** -0.5)
    return o.reshape(B, S, BRANCH_WIDTH) * jax.nn.silu(gate)


def _conformer_conv_branch(u, gate, conv_w, conv_b, ln_g, ln_b):
    val, glu_gate = jnp.split(u, 2, axis=-1)
    h = val * jax.nn.sigmoid(glu_gate)
    h = lax.conv_general_dilated(
        h, conv_w[:, None, :], window_strides=(1,), padding=[(CONV_KERNEL - 1, 0)],
        dimension_numbers=('NWC', 'WIO', 'NWC'), feature_group_count=BRANCH_WIDTH) + conv_b
    h = jax.nn.silu(_layer_norm(h, ln_g, ln_b))
    return h * jax.nn.silu(gate)


def _memory_xattn_branch(q, gate, mem, w_mem_kv):
    B, S, _ = q.shape
    M = mem.shape[1]
    kv = mem @ w_mem_kv
    k = kv[..., :BRANCH_WIDTH].reshape(B, M, XATTN_HEADS, XATTN_HEAD_DIM)
    v = kv[..., BRANCH_WIDTH:].reshape(B, M, XATTN_HEADS, XATTN_HEAD_DIM)
    qh = q.reshape(B, S, XATTN_HEADS, XATTN_HEAD_DIM)
    s = jnp.einsum('bshd,bmhd->bhsm', qh, k).astype(jnp.float32) * XATTN_HEAD_DIM ** -0.5
    pr = jax.nn.softmax(s, axis=-1).astype(v.dtype)
    o = jnp.einsum('bhsm,bmhd->bshd', pr, v).reshape(B, S, BRANCH_WIDTH)
    return o * jax.nn.silu(gate)


def setup_inputs(seed: int = 0) -> dict:
    key = jax.random.key(seed)
    ks = jax.random.split(key, 32)
    f32 = jnp.float32
    L, D, W = DEPTH, D_MODEL, BRANCH_WIDTH

    def nrm(k, shape, scale):
        return jax.random.normal(k, shape, f32) * scale

    x = jax.random.normal(ks[0], (BATCH, SEQ, D), f32)
    mem = jax.random.normal(ks[1], (BATCH, MEM_LEN, D), f32)
    start = jax.random.randint(ks[2], (BATCH, 1), 0, 4096, dtype=jnp.int32)
    positions = start + jnp.arange(SEQ, dtype=jnp.int32)[None, :]
    return {
        "x": x,
        "mem": mem,
        "positions": positions,
        "w_in": nrm(ks[3], (L, D, IN_COLS), D ** -0.5),
        "b_gate": nrm(ks[4], (L, N_BRANCH, D), 0.1),
        "rwkv_mu": jax.random.uniform(ks[5], (L, RWKV_SHIFT_COLS), f32),
        "rwkv_w0": jax.random.uniform(ks[6], (L, W), f32, -6.0, 0.0),
        "rwkv_w2": nrm(ks[7], (L, RWKV_LORA, W), 0.5 * RWKV_LORA ** -0.5),
        "rwkv_a0": nrm(ks[8], (L, W), 0.1),
        "rwkv_a2": nrm(ks[9], (L, RWKV_LORA, W), 0.5 * RWKV_LORA ** -0.5),
        "rwkv_k_k": 0.85 + nrm(ks[10], (L, W), 0.05),
        "rwkv_k_a": 1.0 + nrm(ks[11], (L, W), 0.05),
        "rwkv_r_k": nrm(ks[12], (L, RWKV_HEADS, RWKV_HEAD_DIM), 0.3),
        "rwkv_lnx_g": 1.0 + nrm(ks[13], (L, W), 0.02),
        "rwkv_lnx_b": nrm(ks[14], (L, W), 0.02),
        "mla_q_norm": 1.0 + nrm(ks[15], (L, MLA_Q_LORA), 0.02),
        "mla_w_uq": nrm(ks[16], (L, MLA_Q_LORA, MLA_HEADS * (MLA_NOPE + MLA_ROPE)), MLA_Q_LORA ** -0.5),
        "mla_kv_norm": 1.0 + nrm(ks[17], (L, MLA_KV_LORA), 0.02),
        "mla_w_ukv": nrm(ks[18], (L, MLA_KV_LORA, MLA_HEADS * (MLA_NOPE + MLA_V)), MLA_KV_LORA ** -0.5),
        "conv_w": nrm(ks[19], (L, CONV_KERNEL, W), CONV_KERNEL ** -0.5),
        "conv_b": nrm(ks[20], (L, W), 0.02),
        "conv_ln_g": 1.0 + nrm(ks[21], (L, W), 0.02),
        "conv_ln_b": nrm(ks[22], (L, W), 0.02),
        "xattn_w_mem_kv": nrm(ks[23], (L, D, 2 * W), D ** -0.5),
        "w_o_branch": nrm(ks[24], (L, N_BRANCH, W, D), DEEPNORM_BETA * W ** -0.5),
        "w_out": nrm(ks[25], (L, D, D), DEEPNORM_BETA * D ** -0.5),
        "ln_g": 1.0 + nrm(ks[26], (L, D), 0.02),
        "ln_b": nrm(ks[27], (L, D), 0.02),
    }


def reference(x, mem, positions, w_in, b_gate, rwkv_mu, rwkv_w0, rwkv_w2, rwkv_a0, rwkv_a2,
              rwkv_k_k, rwkv_k_a, rwkv_r_k, rwkv_lnx_g, rwkv_lnx_b, mla_q_norm, mla_w_uq,
              mla_kv_norm, mla_w_ukv, conv_w, conv_b, conv_ln_g, conv_ln_b, xattn_w_mem_kv,
              w_o_branch, w_out, ln_g, ln_b):
    B, S, D = x.shape
    inv_freq = ROPE_THETA ** (-jnp.arange(0, MLA_ROPE, 2, dtype=jnp.float32) / MLA_ROPE)
    ang = positions.astype(jnp.float32)[..., None] * inv_freq
    cos = jnp.cos(ang).astype(x.dtype)
    sin = jnp.sin(ang).astype(x.dtype)
    for l in range(DEPTH):
        h = x @ w_in[l]
        (rw_p, rw_g, q_lat, kv_lat, k_pe, mla_g, conv_u, conv_g, xq, xg,
         merge) = jnp.split(h, SPLIT_POINTS, axis=-1)
        y_rwkv = _rwkv7_branch(rw_p, rw_g, rwkv_mu[l], rwkv_w0[l], rwkv_w2[l], rwkv_a0[l],
                               rwkv_a2[l], rwkv_k_k[l], rwkv_k_a[l], rwkv_r_k[l],
                               rwkv_lnx_g[l], rwkv_lnx_b[l])
        y_mla = _mla_branch(q_lat, kv_lat, k_pe, mla_g, cos, sin, mla_q_norm[l], mla_w_uq[l],
                            mla_kv_norm[l], mla_w_ukv[l])
        y_conv = _conformer_conv_branch(conv_u, conv_g, conv_w[l], conv_b[l], conv_ln_g[l],
                                        conv_ln_b[l])
        y_mem = _memory_xattn_branch(xq, xg, mem, xattn_w_mem_kv[l])
        branches = jnp.stack([y_rwkv, y_mla, y_conv, y_mem], axis=2)
        proj = jnp.einsum('bsnc,ncd->bsnd', branches, w_o_branch[l])
        gates = jax.nn.sigmoid(merge.reshape(B, S, N_BRANCH, D) + b_gate[l])
        merged = jnp.sum(gates * proj, axis=2)
        out = merged @ w_out[l]
        x = _layer_norm(DEEPNORM_ALPHA * x + out, ln_g[l], ln_b[l])
    return x
```

```python
import contextlib
import numpy as np
import concourse.bass as bass
import concourse.mybir as mybir
from concourse.bass_utils import run_bass_kernel_spmd

F32 = mybir.dt.float32
BF16 = mybir.dt.bfloat16
I32 = mybir.dt.int32
AF = mybir.ActivationFunctionType
ALU = mybir.AluOpType

NCORES = 8
SEQ_PER_CORE = 4
S = 2048
D = 1024
DEPTH = 2
IN_COLS = 10016
W = 512
ALPHA = (2.0 * DEPTH) ** 0.25
SEM_LIMIT = 30000
VCLOCK = False
NDQ = 12

OFF_RW = 0
OFF_QLAT = 2176
OFF_KVLAT = 2560
OFF_KPE = 2816
OFF_MGATE = 2848
OFF_CONV = 3360
OFF_XQ = 4896
OFF_MERGE = 5920

PC = {}
_o = 0
for _n, _c in [("mu", 13), ("w0", 4), ("a0", 4), ("k_k", 4), ("k_a", 4), ("r_k", 4), ("lnx_g", 4),
               ("lnx_b", 4), ("q_norm", 3), ("kv_norm", 2), ("conv_b", 4), ("conv_ln_g", 4),
               ("conv_ln_b", 4), ("b_gate", 32), ("conv_w", 124)]:
    PC[_n] = _o
    _o += _c
NPC_RAW = _o
PC["omm"] = NPC_RAW
PC["omka"] = NPC_RAW + 13
NPC = NPC_RAW + 17


class Buf:
    __slots__ = ("w", "r")

    def __init__(self):
        self.w = None
        self.r = {}


class Sched:
    def __init__(self, nc, es):
        self.nc = nc
        self.es = es
        self.eng = {"pe": nc.tensor, "act": nc.scalar, "dve": nc.vector, "pool": nc.gpsimd, "sp": nc.sync}
        self.sem = {}
        self.cnt = {}
        self.sid = {}
        self.nsem = 0
        self.waited = {k: {} for k in self.eng}
        self.latest = {}
        self.ninst = 0
        for k in self.eng:
            self._newsem(k)
        self.dq = {}
        self.dqi = {}
        for q in ("sp", "pool", "act"):
            lst = []
            for i in range(NDQ):
                s = es.enter_context(nc.semaphore(f"dq_{q}_{i}"))
                self.nsem += 1
                lst.append([s, 0, self.nsem])
            self.dq[q] = lst
            self.dqi[q] = 0
        self.psum = []
        self.psi = 0

    def _newsem(self, k):
        s = self.es.enter_context(self.nc.semaphore(f"s_{k}_{self.nsem}"))
        self.nsem += 1
        self.sem[k] = s
        self.cnt[k] = 0
        self.sid[k] = self.nsem

    def _wait(self, k, tok):
        sem, val, src, sid = tok[0], tok[1], tok[2], tok[3]
        if k == "pe" and src == "pe":
            return
        w = self.waited[k]
        if w.get(sid, 0) >= val:
            return
        self.eng[k].wait_ge(sem, val)
        self.ninst += 1
        w[sid] = val
        snap = tok[4] if (VCLOCK and len(tok) > 4) else None
        if snap:
            for a, b in snap.items():
                if w.get(a, 0) < b:
                    w[a] = b

    def _deps(self, reads, writes):
        toks = []
        for b in reads:
            if b.w is not None:
                toks.append(b.w)
        for b in writes:
            if b.w is not None:
                toks.append(b.w)
            toks.extend(b.r.values())
        return toks

    def _commit(self, tok, reads, writes):
        for b in reads:
            b.r[tok[3]] = tok
        for b in writes:
            b.w = tok
            b.r = {}
        self.latest[tok[3]] = tok

    def op(self, k, fn, reads=(), writes=()):
        for t in self._deps(reads, writes):
            self._wait(k, t)
        if self.cnt[k] >= SEM_LIMIT:
            self._newsem(k)
        inst = fn(self.eng[k])
        self.cnt[k] += 1
        self.ninst += 1
        inst.then_inc(self.sem[k], 1)
        snap = dict(self.waited[k])
        if k != "pe":
            snap[self.sid[k]] = self.cnt[k] - 1
        tok = (self.sem[k], self.cnt[k], k, self.sid[k], snap)
        self._commit(tok, reads, writes)
        return tok

    def dma(self, q, out, in_, reads=(), writes=()):
        for t in self._deps(reads, writes):
            self._wait(q, t)
        i = self.dqi[q]
        self.dqi[q] = (i + 1) % NDQ
        ent = self.dq[q][i]
        if ent[1] > 0:
            self._wait(q, (ent[0], 16 * ent[1], "dma", ent[2]))
        self.eng[q].dma_start(out=out, in_=in_).then_inc(ent[0], 16)
        self.ninst += 1
        ent[1] += 1
        tok = (ent[0], 16 * ent[1], "dma", ent[2], dict(self.waited[q]))
        self._commit(tok, reads, writes)
        return tok

    def barrier(self, engines=("pe", "act", "dve", "pool", "sp")):
        toks = list(self.latest.values())
        for k in engines:
            for t in toks:
                self._wait(k, t)

    def next_psum(self, n=8):
        self.psi = (self.psi + 1) % n
        return self.psum[self.psi]


def _consts_host():
    c = {}
    c["c_ident"] = np.eye(128, dtype=np.float32)
    bo = np.zeros((128, 128), np.float32)
    bo[:64, :64] = 1.0
    bo[64:, 64:] = 1.0
    c["c_bo"] = bo
    i = np.arange(64)
    strict = (i[:, None] < i[None, :]).astype(np.float32)
    incl = (i[:, None] <= i[None, :]).astype(np.float32)
    lower = (i[None, :] < i[:, None]).astype(np.float32)
    mA = np.zeros((128, 3, 2, 64), np.float32)
    for h in range(2):
        mA[h * 64:(h + 1) * 64, 0, h, :] = strict
        mA[h * 64:(h + 1) * 64, 1, h, :] = strict
        mA[h * 64:(h + 1) * 64, 2, h, :] = lower
    c["c_maskA"] = mA.reshape(128, 384)
    mB = np.zeros((128, 2, 64), np.float32)
    for h in range(2):
        mB[h * 64:(h + 1) * 64, :, :] = incl[:, None, :]
    c["c_maskB"] = mB.reshape(128, 128)
    mbd = np.zeros((128, 2), np.float32)
    mbd[:64, 0] = 1.0
    mbd[64:, 1] = 1.0
    c["c_mbd"] = mbd
    cm = np.ones((128, 128), np.float32)
    cm[:, 0] = 0.0
    cm[:, 64] = 0.0
    c["c_cmask"] = cm
    k = np.arange(128)[:, None]
    q = np.arange(512)[None, :]
    mm = np.stack([(q >= v * 128 + k) for v in range(4)], axis=1).astype(np.float32)
    c["c_cmla"] = mm.reshape(128, 2048)
    inv = (10000.0 ** (-np.arange(0, 32, 2, dtype=np.float32) / 32.0)).astype(np.float32)
    rp = np.zeros((128, 2), np.float32)
    rp[64:96, 0] = np.concatenate([inv, inv])
    rp[64:96, 1] = np.concatenate([-np.ones(16, np.float32), np.ones(16, np.float32)])
    c["c_rope"] = rp
    c["c_ones"] = np.ones((128, 128), np.float32)
    return c


CONST_SHAPES = {k: v.shape for k, v in _consts_host().items()}

def param_specs(SEQ_PER_CORE, DEPTH):
  return [
    ("x", [SEQ_PER_CORE, S, D], F32), ("mem", [SEQ_PER_CORE, 256, D], F32), ("positions", [SEQ_PER_CORE, S], I32),
    ("w_in", [DEPTH, D, IN_COLS], F32), ("b_gate", [DEPTH, 4, D], F32), ("rwkv_mu", [DEPTH, 1664], F32),
    ("rwkv_w0", [DEPTH, W], F32), ("rwkv_w2", [DEPTH, 64, W], F32), ("rwkv_a0", [DEPTH, W], F32),
    ("rwkv_a2", [DEPTH, 64, W], F32), ("rwkv_k_k", [DEPTH, W], F32), ("rwkv_k_a", [DEPTH, W], F32),
    ("rwkv_r_k", [DEPTH, 8, 64], F32), ("rwkv_lnx_g", [DEPTH, W], F32), ("rwkv_lnx_b", [DEPTH, W], F32),
    ("mla_q_norm", [DEPTH, 384], F32), ("mla_w_uq", [DEPTH, 384, 768], F32), ("mla_kv_norm", [DEPTH, 256], F32),
    ("mla_w_ukv", [DEPTH, 256, 1024], F32), ("conv_w", [DEPTH, 31, W], F32), ("conv_b", [DEPTH, W], F32),
    ("conv_ln_g", [DEPTH, W], F32), ("conv_ln_b", [DEPTH, W], F32), ("xattn_w_mem_kv", [DEPTH, D, 2 * W], F32),
    ("w_o_branch", [DEPTH, 4, W, D], F32), ("w_out", [DEPTH, D, D], F32), ("ln_g", [DEPTH, D], F32),
    ("ln_b", [DEPTH, D], F32),
  ]


PARAM_SPECS = param_specs(SEQ_PER_CORE, DEPTH)


class K:
    def __init__(self, cfg):
        self.cfg = cfg
        self.nc = bass.Bass("TRN2", target_bir_lowering=False)
        nc = self.nc
        self.din = {}
        nseq = cfg.get("nseq", SEQ_PER_CORE)
        nlay = cfg.get("nlay", DEPTH)
        self.single = cfg.get("single", False)
        for n, shp, dt in param_specs(nseq, nlay):
            self.din[n] = nc.dram_tensor(n, shp, dt, kind="ExternalInput").ap()
        for n, shp in CONST_SHAPES.items():
            self.din[n] = nc.dram_tensor(n, list(shp), F32, kind="ExternalInput").ap()
        self.out = nc.dram_tensor("out", [nseq, S, D], F32, kind="ExternalOutput").ap()
        dbg = cfg.get("debug", False)
        kind = "ExternalOutput" if dbg else "Internal"
        self.ybr = nc.dram_tensor("ybr", [4, W, S], BF16, kind=kind).ap()
        self.ybr_buf = [Buf() for _ in range(4)]
        self.x1 = nc.dram_tensor("x1s", [nseq, S, D], F32, kind=kind).ap()
        self.x1_buf = [Buf() for _ in range(SEQ_PER_CORE)]

    def sb(self, es, name, shape, dt):
        return es.enter_context(self.nc.sbuf_tensor(name, shape, dt))

    def build(self):
        nc = self.nc
        with contextlib.ExitStack() as es:
            self.S = Sched(nc, es)
            Sx = self.S
            for i in range(8):
                t = es.enter_context(nc.psum_tensor(f"ps{i}", [128, 512], F32))
                Sx.psum.append((t, Buf()))
            self.setup_consts(es)
            self.xT = self.sb(es, "xT", [128, 8, S], BF16)
            self.xT_b = Buf()
            self.ropeC = self.sb(es, "ropeC", [128, S], F32)
            self.ropeS = self.sb(es, "ropeS", [128, S], F32)
            self.rope_b = Buf()
            self.memT = self.sb(es, "memT", [128, 8, 256], BF16)
            self.memT_b = Buf()
            self.out_b = Buf()
            phases = self.cfg.get("phases", "RMCXE")
            seqs = self.cfg.get("seqs", list(range(SEQ_PER_CORE)))
            layers = self.cfg.get("layers", list(range(DEPTH)))
            for s in seqs:
                self.load_xT(s)
                if "M" in phases:
                    self.rope_tables(s)
                if "X" in phases:
                    self.load_memT(s)
                for l in layers:
                    if l == 0 or True:
                        self.load_params(l)
                    if "R" in phases:
                        self.phase_rwkv(s, l)
                    if "M" in phases:
                        self.phase_mla(s, l)
                    if "C" in phases:
                        self.phase_conv(s, l)
                    if "X" in phases:
                        self.phase_xattn(s, l)
                    if "E" in phases:
                        self.phase_epi(s, l)
            Sx.barrier(engines=("sp",))
        return nc

    def setup_consts(self, es):
        Sx = self.S
        self.cb = Buf()
        c = {}
        for n, shp in CONST_SHAPES.items():
            if n == "c_cmla":
                continue
            t = self.sb(es, "k_" + n, list(shp), F32)
            Sx.dma("sp", t[:], self.din[n], writes=[self.cb])
            c[n] = t
        self.c = c
        self.ident = c["c_ident"]
        self.bo = c["c_bo"]
        self.ident_bf = self.sb(es, "ident_bf", [128, 128], BF16)
        self.ones_bf = self.sb(es, "ones_bf", [128, 128], BF16)
        self.cmla_bf = self.sb(es, "cmla_bf", [128, 2048], BF16)
        Sx.op("pool", lambda e: e.tensor_copy(self.ident_bf[:], self.ident[:]), reads=[self.cb], writes=[self.cb])
        Sx.op("pool", lambda e: e.memset(self.ones_bf[:], 1.0), writes=[self.cb])
        with contextlib.ExitStack() as es2:
            tmpc = self.sb(es2, "k_cmla_tmp", [128, 2048], F32)
            Sx.dma("sp", tmpc[:], self.din["c_cmla"], writes=[self.cb])
            Sx.op("pool", lambda e: e.tensor_copy(self.cmla_bf[:], tmpc[:]), reads=[self.cb], writes=[self.cb])
            Sx.barrier()
        self.pcol = self.sb(es, "pcol", [128, NPC], F32)
        self.pcol_b = Buf()
        self.stageA = self.sb(es, "stageA", [128, 128], F32)
        self.stageB = self.sb(es, "stageB", [128, 128], F32)
        self.stage_b = Buf()
        self.lng_bc = self.sb(es, "lng_bc", [128, D], F32)
        self.lnb_bc = self.sb(es, "lnb_bc", [128, D], F32)
        self.ln_b_ = Buf()
        self.ones_row = self.sb(es, "ones_row", [1, 128], F32)
        Sx.op("pool", lambda e: e.memset(self.ones_row[:], 1.0), writes=[self.cb])

    def pc(self, name, j=0):
        i = PC[name] + j
        return self.pcol[:, i:i + 1]

    def load_params(self, l):
        Sx = self.S
        d = self.din
        rows = []

        def vec(name, key):
            ap = d[key][l]
            n = 1
            for s_ in ap.shape:
                n *= s_
            rows.append((PC[name], n // 128, ap))

        vec("mu", "rwkv_mu"); vec("w0", "rwkv_w0"); vec("a0", "rwkv_a0"); vec("k_k", "rwkv_k_k")
        vec("k_a", "rwkv_k_a"); vec("r_k", "rwkv_r_k"); vec("lnx_g", "rwkv_lnx_g"); vec("lnx_b", "rwkv_lnx_b")
        vec("q_norm", "mla_q_norm"); vec("kv_norm", "mla_kv_norm"); vec("conv_b", "conv_b")
        vec("conv_ln_g", "conv_ln_g"); vec("conv_ln_b", "conv_ln_b"); vec("b_gate", "b_gate"); vec("conv_w", "conv_w")
        for (c0, nr, ap) in rows:
            if len(ap.shape) == 2:
                if ap.shape[1] == 64:
                    flat = ap.rearrange("h n -> (h n)")
                    src = flat.rearrange("(r p) -> r p", p=128)
                elif ap.shape[0] == 31:
                    src = ap.rearrange("j (c p) -> (j c) p", p=128)
                else:
                    src = ap.rearrange("n (c p) -> (n c) p", p=128)
            else:
                src = ap.rearrange("(r p) -> r p", p=128)
            r = 0
            while r < nr:
                g = c0 + r
                if g < 128:
                    n = min(nr - r, 128 - g)
                    Sx.dma("sp", self.stageA[g:g + n, :], src[r:r + n, :], writes=[self.stage_b])
                else:
                    n = nr - r
                    Sx.dma("sp", self.stageB[g - 128:g - 128 + n, :], src[r:r + n, :], writes=[self.stage_b])
                r += n
        nb = NPC_RAW - 128
        ps, pb = Sx.next_psum()
        Sx.op("pe", lambda e: e.matmul(ps[:, 0:128], self.stageA[:, :], self.ident[:, :], start=True, stop=True),
              reads=[self.stage_b, self.cb], writes=[pb])
        Sx.op("pe", lambda e: e.matmul(ps[:, 128:128 + nb], self.stageB[0:nb, :], self.ident[0:nb, 0:nb], start=True, stop=True),
              reads=[self.stage_b, self.cb], writes=[pb])
        Sx.op("act", lambda e: e.activation(out=self.pcol[:, 0:NPC_RAW], in_=ps[:, 0:NPC_RAW], func=AF.Copy),
              reads=[pb], writes=[self.pcol_b])
        o = PC["omm"]
        Sx.op("dve", lambda e: e.tensor_scalar(out=self.pcol[:, o:o + 13], in0=self.pcol[:, 0:13], scalar1=-1.0, scalar2=1.0,
                                               op0=ALU.mult, op1=ALU.add), reads=[self.pcol_b], writes=[self.pcol_b])
        o2 = PC["omka"]
        ka = PC["k_a"]
        Sx.op("dve", lambda e: e.tensor_scalar(out=self.pcol[:, o2:o2 + 4], in0=self.pcol[:, ka:ka + 4], scalar1=-1.0, scalar2=1.0,
                                               op0=ALU.mult, op1=ALU.add), reads=[self.pcol_b], writes=[self.pcol_b])
        es3 = contextlib.ExitStack()
        self.lnrow = self.sb(es3, "lnrow", [1, 2 * D], F32)
        Sx.dma("sp", self.lnrow[0:1, 0:D], d["ln_g"][l:l + 1, :], writes=[self.ln_b_])
        Sx.dma("sp", self.lnrow[0:1, D:2 * D], d["ln_b"][l:l + 1, :], writes=[self.ln_b_])
        for j, dst in enumerate((self.lng_bc, self.lnb_bc)):
            for hh in range(2):
                ps, pb = Sx.next_psum()
                Sx.op("pe", lambda e, ps=ps, j=j, hh=hh: e.matmul(ps[:, :], self.ones_row[0:1, :],
                                                                  self.lnrow[0:1, j * D + hh * 512:j * D + hh * 512 + 512],
                                                                  start=True, stop=True),
                      reads=[self.ln_b_, self.cb], writes=[pb])
                Sx.op("act", lambda e, ps=ps, dst=dst, hh=hh: e.activation(out=dst[:, hh * 512:(hh + 1) * 512], in_=ps[:, :], func=AF.Copy),
                      reads=[pb], writes=[self.ln_b_])
        Sx.barrier()
        es3.close()

    def load_xT(self, s):
        Sx = self.S
        with contextlib.ExitStack() as es:
            xt = [self.sb(es, f"xtok{i}", [128, D], F32) for i in range(2)]
            xb = [Buf(), Buf()]
            for tt in range(S // 128):
                i = tt % 2
                Sx.dma("sp", xt[i][:], self.din["x"][s, tt * 128:(tt + 1) * 128, :], writes=[xb[i]])
                self.transpose_into_xT(xt[i], xb[i], tt)
            Sx.barrier()

    def transpose_into_xT(self, xtok, xbuf, tt):
        Sx = self.S
        for half in range(2):
            ps, pb = Sx.next_psum()
            for j in range(4):
                kc = half * 4 + j
                Sx.op("pe", lambda e, ps=ps, j=j, kc=kc: e.matmul(ps[:, j * 128:(j + 1) * 128], xtok[:, kc * 128:(kc + 1) * 128],
                                                                  self.ident[:, :], start=True, stop=True),
                      reads=[xbuf, self.cb], writes=[pb])
            out = self.xT[:, half * 4:(half + 1) * 4, tt * 128:(tt + 1) * 128]
            Sx.op("act" if half == 0 else "dve",
                  (lambda e, ps=ps, out=out: e.activation(out=out, in_=ps[:, :].rearrange("p (j t) -> p j t", j=4), func=AF.Copy)) if half == 0 else
                  (lambda e, ps=ps, out=out: e.tensor_copy(out, ps[:, :].rearrange("p (j t) -> p j t", j=4))),
                  reads=[pb], writes=[self.xT_b])

    def load_w_bf(self, dst, dst_buf, src, nkc):
        Sx = self.S
        v = src.rearrange("(kc p) c -> p kc c", p=128)
        for kc in range(nkc):
            Sx.dma("pool", dst[:, kc, :], v[:, kc, :], writes=[dst_buf])

    def phase_rwkv(self, s, l):
        Sx = self.S
        nc = self.nc
        d = self.din
        T = 128
        with contextlib.ExitStack() as es:
            sb = lambda n, shp, dt=F32: self.sb(es, "rw_" + n, shp, dt)
            wR = sb("wR", [128, 8, 2176], BF16); wRb = Buf()
            self.load_w_bf(wR, wRb, d["w_in"][l, :, OFF_RW:OFF_RW + 2176], 8)
            W2z = sb("W2z", [128, 512]); A2z = sb("A2z", [128, 512]); lb = Buf()
            Sx.op("pool", lambda e: e.memset(W2z[:], 0.0), writes=[lb])
            Sx.op("pool", lambda e: e.memset(A2z[:], 0.0), writes=[lb])
            Sx.dma("sp", W2z[0:64, :], d["rwkv_w2"][l], writes=[lb])
            Sx.dma("sp", A2z[64:128, :], d["rwkv_a2"][l], writes=[lb])
            p_raw = sb("p_raw", [128, 13, T + 1]); prb = Buf()
            Sx.op("pool", lambda e: e.memset(p_raw[:], 0.0), writes=[prb])
            pm = sb("pm", [128, 13, T]); pmb = Buf()
            sgate = sb("sgate", [128, 4, T]); sgb = Buf()
            T12 = sb("T12", [128, T]); t12b = Buf()
            names = ["lw", "logP", "asig", "eP", "eN", "ePm", "kk", "kkn", "kp", "rT", "t1", "t2", "t3"]
            tt_ = {n: sb(n, [128, 4, T]) for n in names}
            tb = {n: Buf() for n in names}
            Z = {n: sb("Z" + n, [128, 4, 2, 128]) for n in "abkv"}
            Zb_ = {n: Buf() for n in "abkv"}
            H = [sb(f"H{p}", [128, 128]) for p in range(4)]
            Hb = [Buf() for _ in range(4)]
            for p in range(4):
                Sx.op("pool", lambda e, p=p: e.memset(H[p][:], 0.0), writes=[Hb[p]])
            NSET = 2
            A_sb = [sb(f"A{i}", [128, 384]) for i in range(NSET)]; Ab = [Buf() for _ in range(NSET)]
            R_sb = [sb(f"R{i}", [128, 128]) for i in range(NSET)]; Rb = [Buf() for _ in range(NSET)]
            BKV = [sb(f"BKV{i}", [128, 384]) for i in range(NSET)]; BKVb = [Buf() for _ in range(NSET)]
            Wt = [[sb(f"W{i}_{j}", [128, 128]) for j in range(2)] for i in range(NSET)]
            Wtb = [[Buf() for j in range(2)] for i in range(NSET)]
            MP = [[sb(f"MP{i}_{j}", [128, 256]) for j in range(2)] for i in range(NSET)]
            MPb = [[Buf() for j in range(2)] for i in range(NSET)]
            X_sb = [sb(f"X{i}", [128, 128]) for i in range(NSET)]; Xb = [Buf() for _ in range(NSET)]
            U_sb = [sb(f"U{i}", [128, 128]) for i in range(NSET)]; Ub = [Buf() for _ in range(NSET)]
            HpC = [sb(f"HpC{i}", [128, 128]) for i in range(NSET)]; HpCb = [Buf() for _ in range(NSET)]
            Ycm = sb("Ycm", [128, 4, T]); Yb = Buf()
            ybf = sb("ybf", [128, 4, T], BF16); ybb = Buf()
            cb = self.cb
            ident = self.ident
            maskA = self.c["c_maskA"]; maskB = self.c["c_maskB"]; mbd = self.c["c_mbd"]; cmask = self.c["c_cmask"]
            pcb = self.pcol_b
            ydst = self.ybr[0].rearrange("(c p) t -> p c t", p=128)

            def flat(t):
                return t[:, :, :].rearrange("p c t -> p (c t)")

            for blk in range(self.cfg.get('nblk', S // T)):
                t0 = blk * T
                for cbk in range(17):
                    ps, pb = Sx.next_psum()
                    for kc in range(8):
                        Sx.op("pe", lambda e, ps=ps, kc=kc, cbk=cbk: e.matmul(
                            ps[:, 0:T], wR[:, kc, cbk * 128:(cbk + 1) * 128], self.xT[:, kc, t0:t0 + T],
                            start=(kc == 0), stop=(kc == 7)), reads=[wRb, self.xT_b], writes=[pb])
                    if cbk < 13:
                        Sx.op("act", lambda e, ps=ps, cbk=cbk: e.activation(out=p_raw[:, cbk, 1:T + 1], in_=ps[:, 0:T], func=AF.Copy),
                              reads=[pb], writes=[prb])
                    else:
                        Sx.op("act", lambda e, ps=ps, cbk=cbk: e.activation(out=sgate[:, cbk - 13, :], in_=ps[:, 0:T], func=AF.Silu),
                              reads=[pb], writes=[sgb])
                for cbk in range(13):
                    Sx.op("pool", lambda e, cbk=cbk: e.tensor_scalar(out=pm[:, cbk, :], in0=p_raw[:, cbk, 1:T + 1],
                                                                     scalar1=self.pc("omm", cbk), scalar2=None, op0=ALU.mult),
                          reads=[prb, pcb], writes=[pmb])
                    Sx.op("dve", lambda e, cbk=cbk: e.scalar_tensor_tensor(out=pm[:, cbk, :], in0=p_raw[:, cbk, 0:T],
                                                                           scalar=self.pc("mu", cbk), in1=pm[:, cbk, :],
                                                                           op0=ALU.mult, op1=ALU.add),
                          reads=[prb, pcb, pmb], writes=[pmb])
                Sx.op("pool", lambda e: e.tensor_copy(p_raw[:, :, 0:1], p_raw[:, :, T:T + 1]), reads=[prb], writes=[prb])
                r_ = pm[:, 0:4, :]; k_ = pm[:, 4:8, :]; v_ = pm[:, 8:12, :]
                Sx.op("act", lambda e: e.activation(out=T12[0:64, :], in_=pm[0:64, 12, :], func=AF.Tanh), reads=[pmb], writes=[t12b])
                Sx.op("dve", lambda e: e.tensor_copy(T12[64:128, :], pm[64:128, 12, :]), reads=[pmb], writes=[t12b])
                for c4 in range(4):
                    ps, pb = Sx.next_psum()
                    Sx.op("pe", lambda e, ps=ps, c4=c4: e.matmul(ps[:, 0:T], W2z[:, c4 * 128:(c4 + 1) * 128], T12[:, :], start=True, stop=True),
                          reads=[lb, t12b], writes=[pb])
                    Sx.op("pe", lambda e, ps=ps, c4=c4: e.matmul(ps[:, T:2 * T], A2z[:, c4 * 128:(c4 + 1) * 128], T12[:, :], start=True, stop=True),
                          reads=[lb, t12b], writes=[pb])
                    Sx.op("act", lambda e, ps=ps, c4=c4: e.activation(out=tt_["lw"][:, c4, :], in_=ps[:, 0:T], func=AF.Sigmoid,
                                                                      bias=self.pc("w0", c4)), reads=[pb, pcb], writes=[tb["lw"]])
                    Sx.op("act", lambda e, ps=ps, c4=c4: e.activation(out=tt_["asig"][:, c4, :], in_=ps[:, T:2 * T], func=AF.Sigmoid,
                                                                      bias=self.pc("a0", c4)), reads=[pb, pcb], writes=[tb["asig"]])
                Sx.op("pool", lambda e: e.tensor_scalar(out=flat(tt_["lw"]), in0=flat(tt_["lw"]), scalar1=-0.6065306597126334,
                                                        scalar2=None, op0=ALU.mult), reads=[tb["lw"]], writes=[tb["lw"]])
                for c4 in range(4):
                    Sx.op("dve", lambda e, c4=c4: e.tensor_tensor_scan(out=tt_["logP"][:, c4, :], data0=cmask[:, 0:T], data1=tt_["lw"][:, c4, :],
                                                                      initial=0.0, op0=ALU.mult, op1=ALU.add),
                          reads=[tb["lw"], cb], writes=[tb["logP"]])
                Sx.op("act", lambda e: e.activation(out=flat(tt_["eP"]), in_=flat(tt_["logP"]), func=AF.Exp), reads=[tb["logP"]], writes=[tb["eP"]])
                Sx.op("act", lambda e: e.activation(out=flat(tt_["eN"]), in_=flat(tt_["logP"]), func=AF.Exp, scale=-1.0),
                      reads=[tb["logP"]], writes=[tb["eN"]])
                Sx.op("pool", lambda e: e.tensor_tensor(out=flat(tt_["t1"]), in0=flat(tt_["logP"]), in1=flat(tt_["lw"]), op=ALU.subtract),
                      reads=[tb["logP"], tb["lw"]], writes=[tb["t1"]])
                Sx.op("act", lambda e: e.activation(out=flat(tt_["ePm"]), in_=flat(tt_["t1"]), func=AF.Exp), reads=[tb["t1"]], writes=[tb["ePm"]])
                for c4 in range(4):
                    Sx.op("pool", lambda e, c4=c4: e.tensor_scalar(out=tt_["kk"][:, c4, :], in0=k_[:, c4, :], scalar1=self.pc("k_k", c4),
                                                                   scalar2=None, op0=ALU.mult), reads=[pmb, pcb], writes=[tb["kk"]])
                Sx.op("act", lambda e: e.activation(out=flat(tt_["t2"]), in_=flat(tt_["kk"]), func=AF.Square), reads=[tb["kk"]], writes=[tb["t2"]])
                ps, pb = Sx.next_psum()
                Sx.op("pe", lambda e, ps=ps: e.matmul(ps[:, 0:4 * T], self.bo[:, :], flat(tt_["t2"]), start=True, stop=True),
                      reads=[tb["t2"], cb], writes=[pb])
                Sx.op("act", lambda e, ps=ps: e.activation(out=flat(tt_["t3"]), in_=ps[:, 0:4 * T], func=AF.Sqrt), reads=[pb], writes=[tb["t3"]])
                Sx.op("dve", lambda e: e.tensor_scalar(out=flat(tt_["t3"]), in0=flat(tt_["t3"]), scalar1=1e-12, scalar2=None, op0=ALU.max),
                      reads=[tb["t3"]], writes=[tb["t3"]])
                Sx.op("dve", lambda e: e.reciprocal(flat(tt_["t2"]), flat(tt_["t3"])), reads=[tb["t3"]], writes=[tb["t2"]])
                Sx.op("pool", lambda e: e.tensor_tensor(out=flat(tt_["kkn"]), in0=flat(tt_["kk"]), in1=flat(tt_["t2"]), op=ALU.mult),
                      reads=[tb["kk"], tb["t2"]], writes=[tb["kkn"]])
                for c4 in range(4):
                    Sx.op("act", lambda e, c4=c4: e.activation(out=tt_["t3"][:, c4, :], in_=tt_["asig"][:, c4, :], func=AF.Identity,
                                                               scale=self.pc("k_a", c4), bias=self.pc("omka", c4)),
                          reads=[tb["asig"], pcb], writes=[tb["t3"]])
                Sx.op("pool", lambda e: e.tensor_tensor(out=flat(tt_["kp"]), in0=k_.rearrange("p c t -> p (c t)"), in1=flat(tt_["t3"]), op=ALU.mult),
                      reads=[pmb, tb["t3"]], writes=[tb["kp"]])
                Sx.op("dve", lambda e: e.scalar_tensor_tensor(out=flat(tt_["t1"]), in0=flat(tt_["kkn"]), scalar=-1.0, in1=flat(tt_["ePm"]),
                                                              op0=ALU.mult, op1=ALU.mult), reads=[tb["kkn"], tb["ePm"]], writes=[tb["t1"]])
                Sx.op("pool", lambda e: e.tensor_tensor(out=flat(tt_["t2"]), in0=flat(tt_["kkn"]), in1=flat(tt_["asig"]), op=ALU.mult),
                      reads=[tb["kkn"], tb["asig"]], writes=[tb["t2"]])
                Sx.op("pool", lambda e: e.tensor_tensor(out=flat(tt_["t2"]), in0=flat(tt_["t2"]), in1=flat(tt_["eN"]), op=ALU.mult),
                      reads=[tb["t2"], tb["eN"]], writes=[tb["t2"]])
                Sx.op("dve", lambda e: e.tensor_tensor(out=flat(tt_["t3"]), in0=flat(tt_["kp"]), in1=flat(tt_["eN"]), op=ALU.mult),
                      reads=[tb["kp"], tb["eN"]], writes=[tb["t3"]])
                Sx.op("dve", lambda e: e.tensor_tensor(out=flat(tt_["rT"]), in0=r_.rearrange("p c t -> p (c t)"), in1=flat(tt_["eP"]), op=ALU.mult),
                      reads=[pmb, tb["eP"]], writes=[tb["rT"]])
                mb4 = mbd[:, 0:2].unsqueeze(1).unsqueeze(3).to_broadcast([128, 8, 2, 64])
                for zi, (zn, srcap, srcb) in enumerate([("a", tt_["t1"], tb["t1"]), ("b", tt_["t2"], tb["t2"]), ("k", tt_["t3"], tb["t3"]),
                                                        ("v", None, pmb)]):
                    if srcap is None:
                        sview = v_.rearrange("p c (h t) -> p (c h) t", h=2)
                    else:
                        sview = srcap[:, :, :].rearrange("p c (h t) -> p (c h) t", h=2)
                    in0 = sview.unsqueeze(2).to_broadcast([128, 8, 2, 64])
                    outv = Z[zn][:, :, :, :].rearrange("p c h (g t) -> p (c h) g t", g=2)
                    Sx.op("dve" if zi % 2 == 0 else "pool",
                          lambda e, outv=outv, in0=in0: e.tensor_tensor(out=outv, in0=in0, in1=mb4, op=ALU.mult),
                          reads=[srcb, cb], writes=[Zb_[zn]])
                for ch in range(2):
                    for pr in range(4):
                        si = pr % NSET
                        Za = Z["a"][:, pr, ch, :]; Zb = Z["b"][:, pr, ch, :]; Zk = Z["k"][:, pr, ch, :]; Zv = Z["v"][:, pr, ch, :]
                        rTu = tt_["rT"][:, pr, ch * 64:(ch + 1) * 64]
                        pC = tt_["eP"][:, pr, ch * 64 + 63:ch * 64 + 64]
                        psA, pbA = Sx.next_psum()
                        Sx.op("pe", lambda e: e.matmul(psA[:, 0:128], Zb, Za, start=True, stop=True), reads=[Zb_["b"], Zb_["a"]], writes=[pbA])
                        Sx.op("pe", lambda e: e.matmul(psA[:, 128:256], Zk, Za, start=True, stop=True), reads=[Zb_["k"], Zb_["a"]], writes=[pbA])
                        Sx.op("pe", lambda e: e.matmul(psA[:, 256:384], Za, Zb, start=True, stop=True), reads=[Zb_["b"], Zb_["a"]], writes=[pbA])
                        Sx.op("dve", lambda e: e.tensor_tensor(out=A_sb[si][:, :], in0=psA[:, 0:384], in1=maskA[:, :], op=ALU.mult),
                              reads=[pbA, cb], writes=[Ab[si]])
                        psB, pbB = Sx.next_psum()
                        Sx.op("pe", lambda e: e.matmul(psB[:, 0:64], Zb, rTu, start=True, stop=True), reads=[Zb_["b"], tb["rT"]], writes=[pbB])
                        Sx.op("pe", lambda e: e.matmul(psB[:, 64:128], Zk, rTu, start=True, stop=True), reads=[Zb_["k"], tb["rT"]], writes=[pbB])
                        Sx.op("dve", lambda e: e.tensor_tensor(out=R_sb[si][:, :], in0=psB[:, 0:128], in1=maskB[:, :], op=ALU.mult),
                              reads=[pbB, cb], writes=[Rb[si]])
                        psC, pbC = Sx.next_psum()
                        Sx.op("pe", lambda e: e.matmul(psC[:, 0:128], Zb, ident[:, :], start=True, stop=True), reads=[Zb_["b"], cb], writes=[pbC])
                        Sx.op("pe", lambda e: e.matmul(psC[:, 128:256], Zk, ident[:, :], start=True, stop=True), reads=[Zb_["k"], cb], writes=[pbC])
                        Sx.op("pe", lambda e: e.matmul(psC[:, 256:384], Zv, ident[:, :], start=True, stop=True), reads=[Zb_["v"], cb], writes=[pbC])
                        Sx.op("act", lambda e: e.activation(out=BKV[si][:, :], in_=psC[:, 0:384], func=AF.Copy), reads=[pbC], writes=[BKVb[si]])
                        Sx.op("pool", lambda e: e.tensor_tensor(out=Wt[si][0][:, :], in0=A_sb[si][:, 0:128], in1=ident[:, :], op=ALU.add),
                              reads=[Ab[si], cb], writes=[Wtb[si][0]])
                        Mprev = A_sb[si][:, 0:128]; Pprev = A_sb[si][:, 256:384]; mpb_prev = Ab[si]
                        for j in range(1, 6):
                            cur = j % 2
                            psN, pbN = Sx.next_psum()
                            Sx.op("pe", lambda e, psN=psN, Mprev=Mprev, Pprev=Pprev: e.matmul(psN[:, 0:128], Pprev, Mprev, start=True, stop=True),
                                  reads=[mpb_prev], writes=[pbN])
                            Sx.op("pe", lambda e, psN=psN, Mprev=Mprev, Pprev=Pprev: e.matmul(psN[:, 128:256], Mprev, Pprev, start=True, stop=True),
                                  reads=[mpb_prev], writes=[pbN])
                            Sx.op("act", lambda e, psN=psN, cur=cur: e.activation(out=MP[si][cur][:, :], in_=psN[:, 0:256], func=AF.Copy),
                                  reads=[pbN], writes=[MPb[si][cur]])
                            Mprev = MP[si][cur][:, 0:128]; Pprev = MP[si][cur][:, 128:256]; mpb_prev = MPb[si][cur]
                            psW, pbW = Sx.next_psum()
                            Sx.op("pe", lambda e, psW=psW, Pprev=Pprev, j=j: e.matmul(psW[:, 0:128], Pprev, Wt[si][(j - 1) % 2][:, :], start=True, stop=True),
                                  reads=[mpb_prev, Wtb[si][(j - 1) % 2]], writes=[pbW])
                            Sx.op("dve", lambda e, psW=psW, j=j: e.tensor_tensor(out=Wt[si][j % 2][:, :], in0=psW[:, 0:128], in1=Wt[si][(j - 1) % 2][:, :], op=ALU.add),
                                  reads=[pbW, Wtb[si][(j - 1) % 2]], writes=[Wtb[si][j % 2]])
                        TT = Wt[si][1]; TTb = Wtb[si][1]
                        psX, pbX = Sx.next_psum()
                        Sx.op("pe", lambda e: e.matmul(psX[:, 0:128], Za, H[pr][:, :], start=True, stop=False), reads=[Zb_["a"], Hb[pr]], writes=[pbX])
                        Sx.op("pe", lambda e: e.matmul(psX[:, 0:128], A_sb[si][:, 128:256], BKV[si][:, 256:384], start=False, stop=True),
                              reads=[Ab[si], BKVb[si]], writes=[pbX])
                        Sx.op("act", lambda e: e.activation(out=X_sb[si][:, :], in_=psX[:, 0:128], func=AF.Copy), reads=[pbX], writes=[Xb[si]])
                        psU, pbU = Sx.next_psum()
                        Sx.op("pe", lambda e: e.matmul(psU[:, 0:128], TT[:, :], X_sb[si][:, :], start=True, stop=True), reads=[TTb, Xb[si]], writes=[pbU])
                        Sx.op("dve", lambda e: e.tensor_copy(U_sb[si][:, :], psU[:, 0:128]), reads=[pbU], writes=[Ub[si]])
                        psY, pbY = Sx.next_psum()
                        Sx.op("pe", lambda e: e.matmul(psY[:, 0:64], H[pr][:, :], rTu, start=True, stop=False), reads=[Hb[pr], tb["rT"]], writes=[pbY])
                        Sx.op("pe", lambda e: e.matmul(psY[:, 0:64], U_sb[si][:, :], R_sb[si][:, 0:64], start=False, stop=False),
                              reads=[Ub[si], Rb[si]], writes=[pbY])
                        Sx.op("pe", lambda e: e.matmul(psY[:, 0:64], BKV[si][:, 256:384], R_sb[si][:, 64:128], start=False, stop=True),
                              reads=[BKVb[si], Rb[si]], writes=[pbY])
                        Sx.op("act", lambda e: e.activation(out=Ycm[:, pr, ch * 64:(ch + 1) * 64], in_=psY[:, 0:64], func=AF.Copy), reads=[pbY], writes=[Yb])
                        psG, pbG = Sx.next_psum()
                        Sx.op("pe", lambda e: e.matmul(psG[:, 0:128], BKV[si][:, 0:128], U_sb[si][:, :], start=True, stop=False),
                              reads=[BKVb[si], Ub[si]], writes=[pbG])
                        Sx.op("pe", lambda e: e.matmul(psG[:, 0:128], BKV[si][:, 128:256], BKV[si][:, 256:384], start=False, stop=True),
                              reads=[BKVb[si]], writes=[pbG])
                        Sx.op("pool", lambda e: e.tensor_scalar(out=HpC[si][:, :], in0=H[pr][:, :], scalar1=pC, scalar2=None, op0=ALU.mult),
                              reads=[Hb[pr], tb["eP"]], writes=[HpCb[si]])
                        Sx.op("dve", lambda e: e.scalar_tensor_tensor(out=H[pr][:, :], in0=psG[:, 0:128], scalar=pC, in1=HpC[si][:, :],
                                                                      op0=ALU.mult, op1=ALU.add),
                              reads=[pbG, HpCb[si], tb["eP"]], writes=[Hb[pr]])
                NT_ = 4 * T
                psM, pbM = Sx.next_psum()
                Sx.op("pe", lambda e: e.matmul(psM[:, 0:NT_], self.bo[:, :], flat(Ycm), start=True, stop=True), reads=[Yb, cb], writes=[pbM])
                Sx.op("act", lambda e: e.activation(out=flat(tt_["t1"]), in_=flat(Ycm), func=AF.Square), reads=[Yb], writes=[tb["t1"]])
                psQ, pbQ = Sx.next_psum()
                Sx.op("pe", lambda e: e.matmul(psQ[:, 0:NT_], self.bo[:, :], flat(tt_["t1"]), start=True, stop=True), reads=[tb["t1"], cb], writes=[pbQ])
                Sx.op("act", lambda e: e.activation(out=flat(tt_["t2"]), in_=psM[:, 0:NT_], func=AF.Copy, scale=1.0 / 64.0), reads=[pbM], writes=[tb["t2"]])
                Sx.op("pool", lambda e: e.tensor_tensor(out=flat(tt_["t3"]), in0=flat(tt_["t2"]), in1=flat(tt_["t2"]), op=ALU.mult),
                      reads=[tb["t2"]], writes=[tb["t3"]])
                Sx.op("dve", lambda e: e.scalar_tensor_tensor(out=flat(tt_["t3"]), in0=psQ[:, 0:NT_], scalar=1.0 / 64.0, in1=flat(tt_["t3"]),
                                                              op0=ALU.mult, op1=ALU.subtract), reads=[pbQ, tb["t3"]], writes=[tb["t3"]])
                Sx.op("dve", lambda e: e.tensor_scalar(out=flat(tt_["t3"]), in0=flat(tt_["t3"]), scalar1=64e-5, scalar2=None, op0=ALU.add),
                      reads=[tb["t3"]], writes=[tb["t3"]])
                Sx.op("act", lambda e: e.activation(out=flat(tt_["t3"]), in_=flat(tt_["t3"]), func=AF.Sqrt), reads=[tb["t3"]], writes=[tb["t3"]])
                Sx.op("dve", lambda e: e.reciprocal(flat(tt_["t1"]), flat(tt_["t3"])), reads=[tb["t3"]], writes=[tb["t1"]])
                Sx.op("pool", lambda e: e.tensor_tensor(out=flat(tt_["t2"]), in0=flat(Ycm), in1=flat(tt_["t2"]), op=ALU.subtract),
                      reads=[Yb, tb["t2"]], writes=[tb["t2"]])
                Sx.op("pool", lambda e: e.tensor_tensor(out=flat(tt_["t2"]), in0=flat(tt_["t2"]), in1=flat(tt_["t1"]), op=ALU.mult),
                      reads=[tb["t2"], tb["t1"]], writes=[tb["t2"]])
                for c4 in range(4):
                    Sx.op("act", lambda e, c4=c4: e.activation(out=tt_["t2"][:, c4, :], in_=tt_["t2"][:, c4, :], func=AF.Identity,
                                                               scale=self.pc("lnx_g", c4), bias=self.pc("lnx_b", c4)),
                          reads=[tb["t2"], pcb], writes=[tb["t2"]])
                    Sx.op("dve", lambda e, c4=c4: e.scalar_tensor_tensor(out=tt_["t1"][:, c4, :], in0=r_[:, c4, :], scalar=self.pc("r_k", c4),
                                                                         in1=tt_["kp"][:, c4, :], op0=ALU.mult, op1=ALU.mult),
                          reads=[pmb, tb["kp"], pcb, tb["t1"]], writes=[tb["t1"]])
                psR, pbR = Sx.next_psum()
                Sx.op("pe", lambda e: e.matmul(psR[:, 0:NT_], self.bo[:, :], flat(tt_["t1"]), start=True, stop=True), reads=[tb["t1"], cb], writes=[pbR])
                Sx.op("dve", lambda e: e.tensor_tensor(out=flat(tt_["t3"]), in0=psR[:, 0:NT_], in1=v_.rearrange("p c t -> p (c t)"), op=ALU.mult),
                      reads=[pbR, pmb], writes=[tb["t3"]])
                Sx.op("pool", lambda e: e.tensor_tensor(out=flat(tt_["t3"]), in0=flat(tt_["t3"]), in1=flat(tt_["t2"]), op=ALU.add),
                      reads=[tb["t3"], tb["t2"]], writes=[tb["t3"]])
                Sx.op("pool", lambda e: e.tensor_tensor(out=flat(ybf), in0=flat(tt_["t3"]), in1=flat(sgate), op=ALU.mult),
                      reads=[tb["t3"], sgb], writes=[ybb])
                Sx.dma("sp", ydst[:, :, t0:t0 + T], ybf[:, :, :], reads=[ybb], writes=[self.ybr_buf[0]])
            Sx.barrier()

    def rope_tables(self, s):
        Sx = self.S
        rp = self.c["c_rope"]
        cb = self.cb
        with contextlib.ExitStack() as es:
            posi = self.sb(es, "posi", [128, S], I32)
            ang = self.sb(es, "ang", [128, S], F32)
            t1 = self.sb(es, "rp_t1", [128, S], F32)
            t2 = self.sb(es, "rp_t2", [128, S], F32)
            ki = self.sb(es, "rp_ki", [128, S], I32)
            b = Buf()
            R = slice(64, 96)
            Sx.dma("sp", posi[R, :], self.din["positions"][s:s + 1, :].broadcast_to([32, S]), writes=[b])
            Sx.op("dve", lambda e: e.tensor_copy(ang[R, :], posi[R, :]), reads=[b], writes=[b])
            Sx.op("dve", lambda e: e.tensor_scalar(out=ang[R, :], in0=ang[R, :], scalar1=rp[R, 0:1], scalar2=None, op0=ALU.mult),
                  reads=[b, cb], writes=[b])
            TWO_PI = 6.283185307179586
            for which, dst in ((0, self.ropeS), (1, self.ropeC)):
                shift = 0.0 if which == 0 else 1.5707963267948966
                Sx.op("dve", lambda e: e.tensor_scalar(out=t1[R, :], in0=ang[R, :], scalar1=shift, scalar2=None, op0=ALU.add), reads=[b], writes=[b])
                Sx.op("dve", lambda e: e.tensor_scalar(out=t2[R, :], in0=t1[R, :], scalar1=1.0 / TWO_PI, scalar2=0.5, op0=ALU.mult, op1=ALU.add),
                      reads=[b], writes=[b])
                Sx.op("dve", lambda e: e.tensor_copy(ki[R, :], t2[R, :]), reads=[b], writes=[b])
                Sx.op("dve", lambda e: e.tensor_copy(t2[R, :], ki[R, :]), reads=[b], writes=[b])
                Sx.op("dve", lambda e: e.scalar_tensor_tensor(out=t1[R, :], in0=t2[R, :], scalar=-TWO_PI, in1=t1[R, :], op0=ALU.mult, op1=ALU.add),
                      reads=[b], writes=[b])
                Sx.op("dve", lambda e: e.tensor_scalar(out=t2[R, :], in0=t1[R, :], scalar1=-3.141592653589793, scalar2=TWO_PI, op0=ALU.is_lt, op1=ALU.mult),
                      reads=[b], writes=[b])
                Sx.op("dve", lambda e: e.tensor_tensor(out=t1[R, :], in0=t1[R, :], in1=t2[R, :], op=ALU.add), reads=[b], writes=[b])
                Sx.op("dve", lambda e: e.tensor_scalar(out=t2[R, :], in0=t1[R, :], scalar1=3.141592653589793, scalar2=-TWO_PI, op0=ALU.is_gt, op1=ALU.mult),
                      reads=[b], writes=[b])
                Sx.op("dve", lambda e: e.tensor_tensor(out=t1[R, :], in0=t1[R, :], in1=t2[R, :], op=ALU.add), reads=[b], writes=[b])
                Sx.op("dve", lambda e: e.tensor_scalar(out=t1[R, :], in0=t1[R, :], scalar1=3.1415925, scalar2=-3.1415925, op0=ALU.min, op1=ALU.max),
                      reads=[b], writes=[b])
                Sx.op("act", lambda e, dst=dst: e.activation(out=dst[R, :], in_=t1[R, :], func=AF.Sin), reads=[b], writes=[self.rope_b])
            Sx.op("dve", lambda e: e.tensor_scalar(out=self.ropeS[R, :], in0=self.ropeS[R, :], scalar1=rp[R, 1:2], scalar2=None, op0=ALU.mult),
                  reads=[self.rope_b, cb], writes=[self.rope_b])
            Sx.barrier()

    def load_memT(self, s):
        Sx = self.S
        with contextlib.ExitStack() as es:
            mt = [self.sb(es, f"mtok{i}", [128, D], F32) for i in range(2)]
            mb = [Buf(), Buf()]
            for tt in range(2):
                Sx.dma("sp", mt[tt][:], self.din["mem"][s, tt * 128:(tt + 1) * 128, :], writes=[mb[tt]])
                for half in range(2):
                    ps, pb = Sx.next_psum()
                    for j in range(4):
                        kc = half * 4 + j
                        Sx.op("pe", lambda e: e.matmul(ps[:, j * 128:(j + 1) * 128], mt[tt][:, kc * 128:(kc + 1) * 128], self.ident[:, :], start=True, stop=True),
                              reads=[mb[tt], self.cb], writes=[pb])
                    Sx.op("act", lambda e: e.activation(out=self.memT[:, half * 4:(half + 1) * 4, tt * 128:(tt + 1) * 128],
                                                        in_=ps[:, :].rearrange("p (j t) -> p j t", j=4), func=AF.Copy),
                          reads=[pb], writes=[self.memT_b])
            Sx.barrier()

    def inproj(self, ps, pb, wt, wb, c0, ncol, t0, ntok):
        Sx = self.S
        for kc in range(8):
            Sx.op("pe", lambda e, kc=kc: e.matmul(ps[0:ncol, 0:ntok], wt[:, kc, c0:c0 + ncol], self.xT[:, kc, t0:t0 + ntok],
                                                  start=(kc == 0), stop=(kc == 7)), reads=[wb, self.xT_b], writes=[pb])

    def rms_latent(self, es, tag, lat_f, sq, nch, dim, gname, outn, bufs):
        Sx = self.S
        lb, sqb, ob, rb, rstd = bufs
        ps, pb = Sx.next_psum(4)
        for i in range(nch):
            Sx.op("pe", lambda e, i=i: e.matmul(ps[:, :], self.c["c_ones"][:, :], sq[:, i, :], start=(i == 0), stop=(i == nch - 1)),
                  reads=[sqb, self.cb], writes=[pb])
        Sx.op("dve", lambda e: e.tensor_scalar(out=rstd[:, :], in0=ps[:, :], scalar1=1.0 / dim, scalar2=1e-6, op0=ALU.mult, op1=ALU.add),
              reads=[pb], writes=[rb])
        Sx.op("act", lambda e: e.activation(out=rstd[:, :], in_=rstd[:, :], func=AF.Sqrt), reads=[rb], writes=[rb])
        Sx.op("dve", lambda e: e.reciprocal(rstd[:, :], rstd[:, :]), reads=[rb], writes=[rb])
        for i in range(nch):
            Sx.op("dve", lambda e, i=i: e.scalar_tensor_tensor(out=outn[:, i, :], in0=lat_f[:, i, :], scalar=self.pc(gname, i), in1=rstd[:, :],
                                                               op0=ALU.mult, op1=ALU.mult), reads=[lb, rb, self.pcol_b], writes=[ob])

    def phase_mla(self, s, l):
        Sx = self.S
        d = self.din
        cb = self.cb
        scale = 96.0 ** -0.5
        with contextlib.ExitStack() as es:
            sb = lambda n, shp, dt=F32: self.sb(es, "ml_" + n, shp, dt)
            wM = sb("wM", [128, 8, 1184], BF16); wMb = Buf()
            self.load_w_bf(wM, wMb, d["w_in"][l, :, OFF_QLAT:OFF_QLAT + 1184], 8)
            wks = sb("wks", [128, 8, 96], BF16); wksb = Buf()
            Sx.op("pool", lambda e: e.memset(wks[:], 0.0), writes=[wksb])
            vk = d["w_in"][l, :, OFF_KPE:OFF_KPE + 32].rearrange("(kc p) c -> p kc c", p=128)
            Sx.dma("pool", wks[:, :, 64:80], vk[:, :, 16:32], writes=[wksb])
            Sx.dma("pool", wks[:, :, 80:96], vk[:, :, 0:16], writes=[wksb])
            Wuq = sb("Wuq", [128, 3, 768], BF16); Wuqb = Buf()
            self.load_w_bf(Wuq, Wuqb, d["mla_w_uq"][l], 3)
            Wus = sb("Wus", [128, 3, 768], BF16); Wusb = Buf()
            Wq4 = Wuq[:, :, :].rearrange("p k (h c) -> p k h c", h=8)
            Ws4 = Wus[:, :, :].rearrange("p k (h c) -> p k h c", h=8)
            Sx.op("pool", lambda e: e.tensor_copy(Ws4[:, :, :, 0:64], Wq4[:, :, :, 0:64]), reads=[Wuqb], writes=[Wusb])
            Sx.op("pool", lambda e: e.tensor_copy(Ws4[:, :, :, 64:80], Wq4[:, :, :, 80:96]), reads=[Wuqb], writes=[Wusb])
            Sx.op("pool", lambda e: e.tensor_copy(Ws4[:, :, :, 80:96], Wq4[:, :, :, 64:80]), reads=[Wuqb], writes=[Wusb])
            Wukv = sb("Wukv", [128, 2, 1024], BF16); Wukvb = Buf()
            self.load_w_bf(Wukv, Wukvb, d["mla_w_ukv"][l], 2)
            QT = sb("QT", [128, 8, 512], BF16); QTb = Buf()
            KT = sb("KT", [128, 8, S], BF16); KTb = Buf()
            V = sb("V", [128, 16, 512], BF16); Vb = Buf()
            qlf = sb("qlf", [128, 3, 512]); qlb = Buf()
            qsq = sb("qsq", [128, 3, 512]); qsb = Buf()
            qn = sb("qn", [128, 3, 512], BF16); qnb = Buf()
            klf = sb("klf", [128, 2, 512]); klb = Buf()
            ksq = sb("ksq", [128, 2, 512]); ksb = Buf()
            kvn = sb("kvn", [128, 2, 512], BF16); knb = Buf()
            rq = sb("rq", [128, 512]); rqb = Buf()
            rk = sb("rk", [128, 512]); rkb = Buf()
            ta = sb("ta", [128, 512]); tab = Buf()
            tbb_ = sb("tb", [128, 512]); tbb = Buf()
            PT = [sb(f"PT{i}", [128, 512], BF16) for i in range(3)]; PTb = [Buf() for _ in range(3)]
            sgt = sb("sgt", [128, 512]); sgb = Buf()
            rl = sb("rl", [128, 512]); rlb = Buf()
            yb = [sb(f"yb{i}", [128, 512], BF16) for i in range(2)]; ybb = [Buf(), Buf()]
            ydst = self.ybr[1]
            R = slice(64, 96)
            pti = 0
            for g in range(4):
                t0 = g * 512
                for i in range(3):
                    ps, pb = Sx.next_psum(4)
                    self.inproj(ps, pb, wM, wMb, i * 128, 128, t0, 512)
                    Sx.op("act", lambda e: e.activation(out=qlf[:, i, :], in_=ps[:, :], func=AF.Copy), reads=[pb], writes=[qlb])
                    Sx.op("act", lambda e: e.activation(out=qsq[:, i, :], in_=ps[:, :], func=AF.Square), reads=[pb], writes=[qsb])
                self.rms_latent(es, "q", qlf, qsq, 3, 384.0, "q_norm", qn, (qlb, qsb, qnb, rqb, rq))
                for i in range(2):
                    ps, pb = Sx.next_psum(4)
                    self.inproj(ps, pb, wM, wMb, 384 + i * 128, 128, t0, 512)
                    Sx.op("act", lambda e: e.activation(out=klf[:, i, :], in_=ps[:, :], func=AF.Copy), reads=[pb], writes=[klb])
                    Sx.op("act", lambda e: e.activation(out=ksq[:, i, :], in_=ps[:, :], func=AF.Square), reads=[pb], writes=[ksb])
                self.rms_latent(es, "k", klf, ksq, 2, 256.0, "kv_norm", kvn, (klb, ksb, knb, rkb, rk))
                ps1, pb1 = Sx.next_psum(4)
                self.inproj(ps1, pb1, wM, wMb, 576, 96, t0, 512)
                ps2, pb2 = Sx.next_psum(4)
                self.inproj(ps2, pb2, wks, wksb, 0, 96, t0, 512)
                Sx.op("dve", lambda e: e.tensor_tensor(out=ta[R, :], in0=ps1[R, :], in1=self.ropeC[R, t0:t0 + 512], op=ALU.mult),
                      reads=[pb1, self.rope_b], writes=[tab])
                Sx.op("dve", lambda e: e.tensor_tensor(out=tbb_[R, :], in0=ps2[R, :], in1=self.ropeS[R, t0:t0 + 512], op=ALU.mult),
                      reads=[pb2, self.rope_b], writes=[tbb])
                Sx.op("pool", lambda e: e.tensor_tensor(out=ta[R, :], in0=ta[R, :], in1=tbb_[R, :], op=ALU.add), reads=[tab, tbb], writes=[tab])
                for h in range(8):
                    Sx.op("pool" if h % 2 else "act",
                          (lambda e: e.tensor_copy(KT[R, h, t0:t0 + 512], ta[R, :])) if h % 2 else
                          (lambda e: e.activation(out=KT[R, h, t0:t0 + 512], in_=ta[R, :], func=AF.Copy)),
                          reads=[tab], writes=[KTb])
                for h in range(8):
                    ps1, pb1 = Sx.next_psum(4)
                    ps2, pb2 = Sx.next_psum(4)
                    for kc in range(3):
                        Sx.op("pe", lambda e: e.matmul(ps1[0:96, :], Wuq[:, kc, h * 96:(h + 1) * 96], qn[:, kc, :], start=(kc == 0), stop=(kc == 2)),
                              reads=[Wuqb, qnb], writes=[pb1])
                    for kc in range(3):
                        Sx.op("pe", lambda e: e.matmul(ps2[0:96, :], Wus[:, kc, h * 96:(h + 1) * 96], qn[:, kc, :], start=(kc == 0), stop=(kc == 2)),
                              reads=[Wusb, qnb], writes=[pb2])
                    Sx.op("act", lambda e: e.activation(out=QT[0:64, h, :], in_=ps1[0:64, :], func=AF.Copy), reads=[pb1], writes=[QTb])
                    Sx.op("dve", lambda e: e.tensor_tensor(out=ta[R, :], in0=ps1[R, :], in1=self.ropeC[R, t0:t0 + 512], op=ALU.mult),
                          reads=[pb1, self.rope_b], writes=[tab])
                    Sx.op("dve", lambda e: e.tensor_tensor(out=tbb_[R, :], in0=ps2[R, :], in1=self.ropeS[R, t0:t0 + 512], op=ALU.mult),
                          reads=[pb2, self.rope_b], writes=[tbb])
                    Sx.op("pool", lambda e: e.tensor_tensor(out=QT[R, h, :], in0=ta[R, :], in1=tbb_[R, :], op=ALU.add), reads=[tab, tbb], writes=[QTb])
                    ps3, pb3 = Sx.next_psum(4)
                    for kc in range(2):
                        Sx.op("pe", lambda e: e.matmul(ps3[0:64, :], Wukv[:, kc, h * 128:h * 128 + 64], kvn[:, kc, :], start=(kc == 0), stop=(kc == 1)),
                              reads=[Wukvb, knb], writes=[pb3])
                    Sx.op("act", lambda e: e.activation(out=KT[0:64, h, t0:t0 + 512], in_=ps3[0:64, :], func=AF.Copy), reads=[pb3], writes=[KTb])
                Wv = Wukv[:, :, :].rearrange("p k (h c) -> p k h c", h=8)
                for tt in range(4):
                    ps, pb = Sx.next_psum(4)
                    for kc in range(2):
                        Sx.op("pe", lambda e: e.matmul(ps[:, :].rearrange("p (h c) -> p h c", h=8), kvn[:, kc, tt * 128:(tt + 1) * 128], Wv[:, kc, :, 64:128],
                                                       start=(kc == 0), stop=(kc == 1)), reads=[Wukvb, knb], writes=[pb])
                    Sx.op("dve", lambda e: e.tensor_copy(V[:, g * 4 + tt, :], ps[:, :]), reads=[pb], writes=[Vb])
                for pr in range(4):
                    accs = []
                    for hh in range(2):
                        h = pr * 2 + hh
                        o_ps, o_pb = Sx.psum[4 + hh * 2]
                        l_ps, l_pb = Sx.psum[5 + hh * 2]
                        accs.append((o_ps, o_pb, l_ps, l_pb))
                        nj = 4 * g + 4
                        for j in range(nj):
                            sps, spb = Sx.next_psum(4)
                            Sx.op("pe", lambda e: e.matmul(sps[:, :], KT[0:96, h, j * 128:(j + 1) * 128], QT[0:96, h, :], start=True, stop=True),
                                  reads=[KTb, QTb], writes=[spb])
                            pt = PT[pti]; ptb = PTb[pti]; pti = (pti + 1) % 3
                            Sx.op("act", lambda e: e.activation(out=pt[:, :], in_=sps[:, :], func=AF.Exp, scale=scale), reads=[spb], writes=[ptb])
                            if j >= 4 * g:
                                v = j - 4 * g
                                Sx.op("pool", lambda e: e.tensor_tensor(out=pt[:, :], in0=pt[:, :], in1=self.cmla_bf[:, v * 512:(v + 1) * 512], op=ALU.mult),
                                      reads=[ptb, cb], writes=[ptb])
                            Sx.op("pe", lambda e: e.matmul(o_ps[:, :], V[:, j, pr * 128:(pr + 1) * 128], pt[:, :], start=(j == 0), stop=(j == nj - 1)),
                                  reads=[Vb, ptb], writes=[o_pb])
                            Sx.op("pe", lambda e: e.matmul(l_ps[:, :], self.ones_bf[:, :], pt[:, :], start=(j == 0), stop=(j == nj - 1)),
                                  reads=[cb, ptb], writes=[l_pb])
                    gps, gpb = Sx.next_psum(4)
                    self.inproj(gps, gpb, wM, wMb, 672 + pr * 128, 128, t0, 512)
                    Sx.op("act", lambda e: e.activation(out=sgt[:, :], in_=gps[:, :], func=AF.Silu), reads=[gpb], writes=[sgb])
                    yt = yb[pr % 2]; ytb = ybb[pr % 2]
                    for hh in range(2):
                        o_ps, o_pb, l_ps, l_pb = accs[hh]
                        HR = slice(hh * 64, hh * 64 + 64)
                        Sx.op("dve", lambda e: e.reciprocal(rl[HR, :], l_ps[HR, :]), reads=[l_pb], writes=[rlb])
                        Sx.op("dve", lambda e: e.tensor_tensor(out=rl[HR, :], in0=o_ps[HR, :], in1=rl[HR, :], op=ALU.mult), reads=[o_pb, rlb], writes=[rlb])
                        Sx.op("pool", lambda e: e.tensor_tensor(out=yt[HR, :], in0=rl[HR, :], in1=sgt[HR, :], op=ALU.mult), reads=[rlb, sgb], writes=[ytb])
                    Sx.dma("sp", ydst[pr * 128:(pr + 1) * 128, t0:t0 + 512], yt[:, :], reads=[ytb], writes=[self.ybr_buf[1]])
            Sx.barrier()

    def phase_xattn(self, s, l):
        Sx = self.S
        d = self.din
        cb = self.cb
        scale = 128.0 ** -0.5
        with contextlib.ExitStack() as es:
            sb = lambda n, shp, dt=F32: self.sb(es, "xa_" + n, shp, dt)
            wkv = sb("wkv", [128, 8, 1024], BF16); wkvb = Buf()
            self.load_w_bf(wkv, wkvb, d["xattn_w_mem_kv"][l], 8)
            wX = sb("wX", [128, 8, 1024], BF16); wXb = Buf()
            self.load_w_bf(wX, wXb, d["w_in"][l, :, OFF_XQ:OFF_XQ + 1024], 8)
            KxT = sb("KxT", [128, 4, 256], BF16); Kb = Buf()
            Vx = sb("Vx", [128, 2, 512], BF16); Vb = Buf()
            qx = sb("qx", [128, 512], BF16); qb = Buf()
            PT = [sb(f"PT{i}", [128, 512], BF16) for i in range(2)]; PTb = [Buf(), Buf()]
            sgt = sb("sgt", [128, 512]); sgb = Buf()
            rl = sb("rl", [128, 512]); rlb = Buf()
            yb = [sb(f"yb{i}", [128, 512], BF16) for i in range(2)]; ybb = [Buf(), Buf()]
            for h in range(4):
                ps, pb = Sx.next_psum(4)
                for kc in range(8):
                    Sx.op("pe", lambda e: e.matmul(ps[:, 0:256], wkv[:, kc, h * 128:(h + 1) * 128], self.memT[:, kc, :], start=(kc == 0), stop=(kc == 7)),
                          reads=[wkvb, self.memT_b], writes=[pb])
                Sx.op("act", lambda e: e.activation(out=KxT[:, h, :], in_=ps[:, 0:256], func=AF.Copy), reads=[pb], writes=[Kb])
            for mt in range(2):
                ps, pb = Sx.next_psum(4)
                for kc in range(8):
                    Sx.op("pe", lambda e: e.matmul(ps[:, :], self.memT[:, kc, mt * 128:(mt + 1) * 128], wkv[:, kc, 512:1024], start=(kc == 0), stop=(kc == 7)),
                          reads=[wkvb, self.memT_b], writes=[pb])
                Sx.op("act", lambda e: e.activation(out=Vx[:, mt, :], in_=ps[:, :], func=AF.Copy), reads=[pb], writes=[Vb])
            ydst = self.ybr[3]
            k = 0
            for g in range(4):
                t0 = g * 512
                for h in range(4):
                    ps, pb = Sx.next_psum(4)
                    self.inproj(ps, pb, wX, wXb, h * 128, 128, t0, 512)
                    Sx.op("act", lambda e: e.activation(out=qx[:, :], in_=ps[:, :], func=AF.Copy), reads=[pb], writes=[qb])
                    o_ps, o_pb = Sx.psum[4]
                    l_ps, l_pb = Sx.psum[5]
                    for mt in range(2):
                        sps, spb = Sx.next_psum(4)
                        Sx.op("pe", lambda e: e.matmul(sps[:, :], KxT[:, h, mt * 128:(mt + 1) * 128], qx[:, :], start=True, stop=True), reads=[Kb, qb], writes=[spb])
                        pt = PT[mt]; ptb = PTb[mt]
                        Sx.op("act", lambda e: e.activation(out=pt[:, :], in_=sps[:, :], func=AF.Exp, scale=scale), reads=[spb], writes=[ptb])
                        Sx.op("pe", lambda e: e.matmul(o_ps[:, :], Vx[:, mt, h * 128:(h + 1) * 128], pt[:, :], start=(mt == 0), stop=(mt == 1)),
                              reads=[Vb, ptb], writes=[o_pb])
                        Sx.op("pe", lambda e: e.matmul(l_ps[:, :], self.ones_bf[:, :], pt[:, :], start=(mt == 0), stop=(mt == 1)),
                              reads=[cb, ptb], writes=[l_pb])
                    gps, gpb = Sx.next_psum(4)
                    self.inproj(gps, gpb, wX, wXb, 512 + h * 128, 128, t0, 512)
                    Sx.op("act", lambda e: e.activation(out=sgt[:, :], in_=gps[:, :], func=AF.Silu), reads=[gpb], writes=[sgb])
                    yt = yb[k % 2]; ytb = ybb[k % 2]; k += 1
                    Sx.op("dve", lambda e: e.reciprocal(rl[:, :], l_ps[:, :]), reads=[l_pb], writes=[rlb])
                    Sx.op("dve", lambda e: e.tensor_tensor(out=rl[:, :], in0=o_ps[:, :], in1=rl[:, :], op=ALU.mult), reads=[o_pb, rlb], writes=[rlb])
                    Sx.op("pool", lambda e: e.tensor_tensor(out=yt[:, :], in0=rl[:, :], in1=sgt[:, :], op=ALU.mult), reads=[rlb, sgb], writes=[ytb])
                    Sx.dma("sp", ydst[h * 128:(h + 1) * 128, t0:t0 + 512], yt[:, :], reads=[ytb], writes=[self.ybr_buf[3]])
            Sx.barrier()

    def phase_conv(self, s, l):
        Sx = self.S
        d = self.din
        cb = self.cb
        pcb = self.pcol_b
        ones = self.c["c_ones"]
        with contextlib.ExitStack() as es:
            sb = lambda n, shp, dt=F32: self.sb(es, "cv_" + n, shp, dt)
            wC = sb("wC", [128, 8, 1536], BF16); wCb = Buf()
            self.load_w_bf(wC, wCb, d["w_in"][l, :, OFF_CONV:OFF_CONV + 1536], 8)
            Dg = sb("Dg", [128, 4, 31, 128], BF16); Dgb = Buf()
            for c in range(4):
                for j in range(31):
                    Sx.op("pool" if (j % 2) else "dve",
                          lambda e: e.tensor_scalar(out=Dg[:, c, j, :], in0=self.ident[:, :], scalar1=self.pc("conv_w", j * 4 + c), scalar2=None, op0=ALU.mult),
                          reads=[cb, pcb], writes=[Dgb])
            hb = sb("hb", [128, 4, 30 + S], BF16); hbb = Buf()
            Sx.op("pool", lambda e: e.memset(hb[:, :, 0:30], 0.0), writes=[hbb])
            sig = sb("sig", [128, 512]); sigb = Buf()
            cv = sb("cvv", [128, 4, 512]); cvb = Buf()
            sq = sb("sq", [128, 4, 512]); sqb = Buf()
            mean = sb("mean", [128, 512]); mb = Buf()
            rstd = sb("rstd", [128, 512]); rb = Buf()
            t1 = sb("t1", [128, 512]); t1b = Buf()
            sgt = sb("sgt", [128, 512]); sgb = Buf()
            yb = [sb(f"yb{i}", [128, 512], BF16) for i in range(2)]; ybb = [Buf(), Buf()]
            ydst = self.ybr[2]
            for g in range(4):
                t0 = g * 512
                for c in range(4):
                    ps1, pb1 = Sx.next_psum()
                    self.inproj(ps1, pb1, wC, wCb, c * 128, 128, t0, 512)
                    ps2, pb2 = Sx.next_psum()
                    self.inproj(ps2, pb2, wC, wCb, 512 + c * 128, 128, t0, 512)
                    Sx.op("act", lambda e: e.activation(out=sig[:, :], in_=ps2[:, :], func=AF.Sigmoid), reads=[pb2], writes=[sigb])
                    Sx.op("dve", lambda e: e.tensor_tensor(out=hb[:, c, 30 + t0:30 + t0 + 512], in0=ps1[:, :], in1=sig[:, :], op=ALU.mult),
                          reads=[pb1, sigb], writes=[hbb])
                for c in range(4):
                    ps, pb = Sx.next_psum()
                    for j in range(31):
                        Sx.op("pe", lambda e: e.matmul(ps[:, :], Dg[:, c, j, :], hb[:, c, t0 + j:t0 + j + 512], start=(j == 0), stop=(j == 30)),
                              reads=[Dgb, hbb], writes=[pb])
                    Sx.op("act", lambda e: e.activation(out=cv[:, c, :], in_=ps[:, :], func=AF.Identity, bias=self.pc("conv_b", c)), reads=[pb, pcb], writes=[cvb])
                Sx.op("act", lambda e: e.activation(out=sq[:, :, :].rearrange("p c t -> p (c t)"), in_=cv[:, :, :].rearrange("p c t -> p (c t)"), func=AF.Square),
                      reads=[cvb], writes=[sqb])
                psM, pbM = Sx.next_psum()
                psQ, pbQ = Sx.next_psum()
                for c in range(4):
                    Sx.op("pe", lambda e: e.matmul(psM[:, :], ones[:, :], cv[:, c, :], start=(c == 0), stop=(c == 3)), reads=[cvb, cb], writes=[pbM])
                for c in range(4):
                    Sx.op("pe", lambda e: e.matmul(psQ[:, :], ones[:, :], sq[:, c, :], start=(c == 0), stop=(c == 3)), reads=[sqb, cb], writes=[pbQ])
                Sx.op("act", lambda e: e.activation(out=mean[:, :], in_=psM[:, :], func=AF.Copy, scale=1.0 / 512.0), reads=[pbM], writes=[mb])
                Sx.op("pool", lambda e: e.tensor_tensor(out=t1[:, :], in0=mean[:, :], in1=mean[:, :], op=ALU.mult), reads=[mb], writes=[t1b])
                Sx.op("dve", lambda e: e.scalar_tensor_tensor(out=rstd[:, :], in0=psQ[:, :], scalar=1.0 / 512.0, in1=t1[:, :], op0=ALU.mult, op1=ALU.subtract),
                      reads=[pbQ, t1b], writes=[rb])
                Sx.op("dve", lambda e: e.tensor_scalar(out=rstd[:, :], in0=rstd[:, :], scalar1=1e-5, scalar2=None, op0=ALU.add), reads=[rb], writes=[rb])
                Sx.op("act", lambda e: e.activation(out=rstd[:, :], in_=rstd[:, :], func=AF.Sqrt), reads=[rb], writes=[rb])
                Sx.op("dve", lambda e: e.reciprocal(rstd[:, :], rstd[:, :]), reads=[rb], writes=[rb])
                for c in range(4):
                    Sx.op("pool", lambda e: e.tensor_tensor(out=t1[:, :], in0=cv[:, c, :], in1=mean[:, :], op=ALU.subtract), reads=[cvb, mb, t1b], writes=[t1b])
                    Sx.op("dve", lambda e: e.tensor_tensor(out=t1[:, :], in0=t1[:, :], in1=rstd[:, :], op=ALU.mult), reads=[t1b, rb], writes=[t1b])
                    Sx.op("act", lambda e: e.activation(out=t1[:, :], in_=t1[:, :], func=AF.Silu, scale=self.pc("conv_ln_g", c), bias=self.pc("conv_ln_b", c)),
                          reads=[t1b, pcb], writes=[t1b])
                    gps, gpb = Sx.next_psum()
                    self.inproj(gps, gpb, wC, wCb, 1024 + c * 128, 128, t0, 512)
                    Sx.op("act", lambda e: e.activation(out=sgt[:, :], in_=gps[:, :], func=AF.Silu), reads=[gpb], writes=[sgb])
                    yt = yb[c % 2]; ytb = ybb[c % 2]
                    Sx.op("pool", lambda e: e.tensor_tensor(out=yt[:, :], in0=t1[:, :], in1=sgt[:, :], op=ALU.mult), reads=[t1b, sgb], writes=[ytb])
                    Sx.dma("sp", ydst[c * 128:(c + 1) * 128, t0:t0 + 512], yt[:, :], reads=[ytb], writes=[self.ybr_buf[2]])
            Sx.barrier()

    def phase_epi(self, s, l):
        Sx = self.S
        d = self.din
        cb = self.cb
        pcb = self.pcol_b
        with contextlib.ExitStack() as es0:
            mT = self.sb(es0, "ep_mT", [128, 8, S], BF16); mTb = Buf()
            with contextlib.ExitStack() as es:
                sb = lambda n, shp, dt=F32: self.sb(es, "e1_" + n, shp, dt)
                yin = sb("yin", [128, 4, 4, S], BF16); yinb = Buf()
                for n in range(4):
                    src = self.ybr[n].rearrange("(c p) t -> p c t", p=128)
                    for c in range(4):
                        Sx.dma("sp", yin[:, n, c, :], src[:, c, :], reads=[self.ybr_buf[n]], writes=[yinb])
                wG = [sb(f"wG{i}", [128, 8, 4, 128], BF16) for i in range(2)]; wGb = [Buf(), Buf()]
                Wo = [sb(f"Wo{i}", [128, 4, 4, 128], BF16) for i in range(2)]; Wob = [Buf(), Buf()]
                gt = sb("gt", [128, 512]); gtb = Buf()
                acc = sb("acc", [128, 512]); accb = Buf()
                tmp = sb("tmp", [128, 512]); tmpb = Buf()
                for db in range(8):
                    i = db % 2
                    for n in range(4):
                        c0 = OFF_MERGE + n * 1024 + db * 128
                        vg = d["w_in"][l, :, c0:c0 + 128].rearrange("(kc p) c -> p kc c", p=128)
                        Sx.dma("pool", wG[i][:, :, n, :], vg, writes=[wGb[i]])
                        vo = d["w_o_branch"][l, n, :, db * 128:(db + 1) * 128].rearrange("(cc p) c -> p cc c", p=128)
                        Sx.dma("pool", Wo[i][:, n, :, :], vo, writes=[Wob[i]])
                    for g in range(4):
                        t0 = g * 512
                        for n in range(4):
                            psP, pbP = Sx.next_psum()
                            for cc in range(4):
                                Sx.op("pe", lambda e: e.matmul(psP[:, :], Wo[i][:, n, cc, :], yin[:, n, cc, t0:t0 + 512], start=(cc == 0), stop=(cc == 3)),
                                      reads=[Wob[i], yinb], writes=[pbP])
                            psG, pbG = Sx.next_psum()
                            for kc in range(8):
                                Sx.op("pe", lambda e: e.matmul(psG[:, :], wG[i][:, kc, n, :], self.xT[:, kc, t0:t0 + 512], start=(kc == 0), stop=(kc == 7)),
                                      reads=[wGb[i], self.xT_b], writes=[pbG])
                            Sx.op("act", lambda e: e.activation(out=gt[:, :], in_=psG[:, :], func=AF.Sigmoid, bias=self.pc("b_gate", n * 8 + db)),
                                  reads=[pbG, pcb], writes=[gtb])
                            if n == 0:
                                Sx.op("dve", lambda e: e.tensor_tensor(out=acc[:, :], in0=psP[:, :], in1=gt[:, :], op=ALU.mult), reads=[pbP, gtb], writes=[accb])
                            else:
                                Sx.op("dve", lambda e: e.tensor_tensor(out=tmp[:, :], in0=psP[:, :], in1=gt[:, :], op=ALU.mult), reads=[pbP, gtb], writes=[tmpb])
                                if n < 3:
                                    Sx.op("pool", lambda e: e.tensor_tensor(out=acc[:, :], in0=acc[:, :], in1=tmp[:, :], op=ALU.add), reads=[accb, tmpb], writes=[accb])
                                else:
                                    Sx.op("pool", lambda e: e.tensor_tensor(out=mT[:, db, t0:t0 + 512], in0=acc[:, :], in1=tmp[:, :], op=ALU.add),
                                          reads=[accb, tmpb], writes=[mTb])
                Sx.barrier()
            with contextlib.ExitStack() as es:
                sb = lambda n, shp, dt=F32: self.sb(es, "e2_" + n, shp, dt)
                Wout = sb("Wout", [128, 8, D], BF16); Woutb = Buf()
                self.load_w_bf(Wout, Woutb, d["w_out"][l], 8)
                xt = [sb(f"xt{i}", [128, D]) for i in range(2)]; xtb = [Buf(), Buf()]
                z = [sb(f"z{i}", [128, D]) for i in range(2)]; zb = [Buf(), Buf()]
                st = sb("st", [128, 12]); stb = Buf()
                mv = sb("mv", [128, 2]); mvb = Buf()
                rs = sb("rs", [128, 1]); rsb = Buf()
                for tt in range(S // 128):
                    i = tt % 2
                    if l == 0:
                        Sx.dma("sp", xt[i][:], d["x"][s, tt * 128:(tt + 1) * 128, :], writes=[xtb[i]])
                    else:
                        Sx.dma("sp", xt[i][:], self.x1[s, tt * 128:(tt + 1) * 128, :], reads=[self.x1_buf[s]], writes=[xtb[i]])
                    for half in range(2):
                        ps, pb = Sx.next_psum()
                        for kc in range(8):
                            Sx.op("pe", lambda e: e.matmul(ps[:, :], mT[:, kc, tt * 128:(tt + 1) * 128], Wout[:, kc, half * 512:(half + 1) * 512],
                                                           start=(kc == 0), stop=(kc == 7)), reads=[mTb, Woutb], writes=[pb])
                        Sx.op("dve", lambda e: e.scalar_tensor_tensor(out=z[i][:, half * 512:(half + 1) * 512], in0=xt[i][:, half * 512:(half + 1) * 512],
                                                                      scalar=float(ALPHA), in1=ps[:, :], op0=ALU.mult, op1=ALU.add),
                              reads=[xtb[i], pb], writes=[zb[i]])
                        Sx.op("dve", lambda e: e.bn_stats(st[:, half * 6:(half + 1) * 6], z[i][:, half * 512:(half + 1) * 512]), reads=[zb[i]], writes=[stb])
                    Sx.op("dve", lambda e: e.bn_aggr(mv[:, :], st[:, :]), reads=[stb], writes=[mvb])
                    Sx.op("dve", lambda e: e.tensor_scalar(out=rs[:, :], in0=mv[:, 1:2], scalar1=1e-5, scalar2=None, op0=ALU.add), reads=[mvb], writes=[rsb])
                    Sx.op("act", lambda e: e.activation(out=rs[:, :], in_=rs[:, :], func=AF.Sqrt), reads=[rsb], writes=[rsb])
                    Sx.op("dve", lambda e: e.reciprocal(rs[:, :], rs[:, :]), reads=[rsb], writes=[rsb])
                    Sx.op("dve", lambda e: e.tensor_scalar(out=z[i][:, :], in0=z[i][:, :], scalar1=mv[:, 0:1], scalar2=rs[:, 0:1], op0=ALU.subtract, op1=ALU.mult),
                          reads=[zb[i], mvb, rsb], writes=[zb[i]])
                    Sx.op("pool", lambda e: e.tensor_tensor(out=z[i][:, :], in0=z[i][:, :], in1=self.lng_bc[:, :], op=ALU.mult), reads=[zb[i], self.ln_b_], writes=[zb[i]])
                    Sx.op("pool", lambda e: e.tensor_tensor(out=z[i][:, :], in0=z[i][:, :], in1=self.lnb_bc[:, :], op=ALU.add), reads=[zb[i], self.ln_b_], writes=[zb[i]])
                    if l == DEPTH - 1 or self.single:
                        Sx.dma("sp", self.out[s, tt * 128:(tt + 1) * 128, :], z[i][:, :], reads=[zb[i]], writes=[self.out_b])
                    else:
                        Sx.dma("sp", self.x1[s, tt * 128:(tt + 1) * 128, :], z[i][:, :], reads=[zb[i]], writes=[self.x1_buf[s]])
                        self.transpose_into_xT(z[i], zb[i], tt)
                Sx.barrier()


def _shard_inputs(inputs):
    consts = _consts_host()
    maps = []
    for c in range(NCORES):
        m = {}
        sl = slice(c * SEQ_PER_CORE, (c + 1) * SEQ_PER_CORE)
        for n, shp, dt in PARAM_SPECS:
            a = np.asarray(inputs[n])
            if n in ("x", "mem", "positions"):
                a = a[sl]
            m[n] = np.ascontiguousarray(a)
        m.update(consts)
        maps.append(m)
    return maps


_PROG = {}


def kernel(**inputs):
    if "p" not in _PROG:
        kb = K({"nseq": 1, "nlay": 1, "single": True, "seqs": [0], "layers": [0]})
        _PROG["p"] = kb.build()
    nc = _PROG["p"]
    consts = _consts_host()
    xs = np.asarray(inputs["x"], dtype=np.float32)
    names = [n for n, _, _ in PARAM_SPECS if n not in ("x", "mem", "positions")]
    out = np.empty_like(xs)
    for slot in range(SEQ_PER_CORE):
        cur = [np.ascontiguousarray(xs[c * SEQ_PER_CORE + slot][None]) for c in range(NCORES)]
        for l in range(DEPTH):
            wl = {n: np.ascontiguousarray(np.asarray(inputs[n])[l:l + 1]) for n in names}
            maps = []
            for c in range(NCORES):
                b = c * SEQ_PER_CORE + slot
                m = dict(wl)
                m["x"] = cur[c]
                m["mem"] = np.ascontiguousarray(np.asarray(inputs["mem"])[b:b + 1])
                m["positions"] = np.ascontiguousarray(np.asarray(inputs["positions"])[b:b + 1])
                m.update(consts)
                maps.append(m)
            res = run_bass_kernel_spmd(nc, maps, core_ids=list(range(NCORES)))
            cur = [np.ascontiguousarray(np.asarray(r["out"], dtype=np.float32)) for r in res.results]
        for c in range(NCORES):
            out[c * SEQ_PER_CORE + slot] = cur[c][0]
    return out
```

```python
import contextlib
import numpy as np
import concourse.bass as bass
import concourse.mybir as mybir
from concourse.bass_utils import run_bass_kernel_spmd

F32 = mybir.dt.float32
BF16 = mybir.dt.bfloat16
I32 = mybir.dt.int32
AF = mybir.ActivationFunctionType
ALU = mybir.AluOpType

NCORES = 8
SEQ_PER_CORE = 4
S = 2048
D = 1024
DEPTH = 2
IN_COLS = 10016
W = 512
ALPHA = (2.0 * DEPTH) ** 0.25
SEM_LIMIT = 30000
VCLOCK = False
NDQ = 12

OFF_RW = 0
OFF_QLAT = 2176
OFF_KVLAT = 2560
OFF_KPE = 2816
OFF_MGATE = 2848
OFF_CONV = 3360
OFF_XQ = 4896
OFF_MERGE = 5920

PC = {}
_o = 0
for _n, _c in [("mu", 13), ("w0", 4), ("a0", 4), ("k_k", 4), ("k_a", 4), ("r_k", 4), ("lnx_g", 4),
               ("lnx_b", 4), ("q_norm", 3), ("kv_norm", 2), ("conv_b", 4), ("conv_ln_g", 4),
               ("conv_ln_b", 4), ("b_gate", 32), ("conv_w", 124)]:
    PC[_n] = _o
    _o += _c
NPC_RAW = _o
PC["omm"] = NPC_RAW
PC["omka"] = NPC_RAW + 13
NPC = NPC_RAW + 17


class Buf:
    __slots__ = ("w", "r")

    def __init__(self):
        self.w = None
        self.r = {}


class Sched:
    def __init__(self, nc, es):
        self.nc = nc
        self.es = es
        self.eng = {"pe": nc.tensor, "act": nc.scalar, "dve": nc.vector, "pool": nc.gpsimd, "sp": nc.sync}
        self.sem = {}
        self.cnt = {}
        self.sid = {}
        self.nsem = 0
        self.waited = {k: {} for k in self.eng}
        self.latest = {}
        self.ninst = 0
        for k in self.eng:
            self._newsem(k)
        self.dq = {}
        self.dqi = {}
        for q in ("sp", "pool", "act"):
            lst = []
            for i in range(NDQ):
                s = es.enter_context(nc.semaphore(f"dq_{q}_{i}"))
                self.nsem += 1
                lst.append([s, 0, self.nsem])
            self.dq[q] = lst
            self.dqi[q] = 0
        self.psum = []
        self.psi = 0

    def _newsem(self, k):
        s = self.es.enter_context(self.nc.semaphore(f"s_{k}_{self.nsem}"))
        self.nsem += 1
        self.sem[k] = s
        self.cnt[k] = 0
        self.sid[k] = self.nsem

    def _wait(self, k, tok):
        sem, val, src, sid = tok[0], tok[1], tok[2], tok[3]
        if k == "pe" and src == "pe":
            return
        w = self.waited[k]
        if w.get(sid, 0) >= val:
            return
        self.eng[k].wait_ge(sem, val)
        self.ninst += 1
        w[sid] = val
        snap = tok[4] if (VCLOCK and len(tok) > 4) else None
        if snap:
            for a, b in snap.items():
                if w.get(a, 0) < b:
                    w[a] = b

    def _deps(self, reads, writes):
        toks = []
        for b in reads:
            if b.w is not None:
                toks.append(b.w)
        for b in writes:
            if b.w is not None:
                toks.append(b.w)
            toks.extend(b.r.values())
        return toks

    def _commit(self, tok, reads, writes):
        for b in reads:
            b.r[tok[3]] = tok
        for b in writes:
            b.w = tok
            b.r = {}
        self.latest[tok[3]] = tok

    def op(self, k, fn, reads=(), writes=()):
        for t in self._deps(reads, writes):
            self._wait(k, t)
        if self.cnt[k] >= SEM_LIMIT:
            self._newsem(k)
        inst = fn(self.eng[k])
        self.cnt[k] += 1
        self.ninst += 1
        inst.then_inc(self.sem[k], 1)
        snap = dict(self.waited[k])
        if k != "pe":
            snap[self.sid[k]] = self.cnt[k] - 1
        tok = (self.sem[k], self.cnt[k], k, self.sid[k], snap)
        self._commit(tok, reads, writes)
        return tok

    def dma(self, q, out, in_, reads=(), writes=()):
        for t in self._deps(reads, writes):
            self._wait(q, t)
        i = self.dqi[q]
        self.dqi[q] = (i + 1) % NDQ
        ent = self.dq[q][i]
        if ent[1] > 0:
            self._wait(q, (ent[0], 16 * ent[1], "dma", ent[2]))
        self.eng[q].dma_start(out=out, in_=in_).then_inc(ent[0], 16)
        self.ninst += 1
        ent[1] += 1
        tok = (ent[0], 16 * ent[1], "dma", ent[2], dict(self.waited[q]))
        self._commit(tok, reads, writes)
        return tok

    def barrier(self, engines=("pe", "act", "dve", "pool", "sp")):
        toks = list(self.latest.values())
        for k in engines:
            for t in toks:
                self._wait(k, t)

    def next_psum(self, n=8):
        self.psi = (self.psi + 1) % n
        return self.psum[self.psi]


def _consts_host():
    c = {}
    c["c_ident"] = np.eye(128, dtype=np.float32)
    bo = np.zeros((128, 128), np.float32)
    bo[:64, :64] = 1.0
    bo[64:, 64:] = 1.0
    c["c_bo"] = bo
    i = np.arange(64)
    strict = (i[:, None] < i[None, :]).astype(np.float32)
    incl = (i[:, None] <= i[None, :]).astype(np.float32)
    lower = (i[None, :] < i[:, None]).astype(np.float32)
    mA = np.zeros((128, 3, 2, 64), np.float32)
    for h in range(2):
        mA[h * 64:(h + 1) * 64, 0, h, :] = strict
        mA[h * 64:(h + 1) * 64, 1, h, :] = strict
        mA[h * 64:(h + 1) * 64, 2, h, :] = lower
    c["c_maskA"] = mA.reshape(128, 384)
    mB = np.zeros((128, 2, 64), np.float32)
    for h in range(2):
        mB[h * 64:(h + 1) * 64, :, :] = incl[:, None, :]
    c["c_maskB"] = mB.reshape(128, 128)
    mbd = np.zeros((128, 2), np.float32)
    mbd[:64, 0] = 1.0
    mbd[64:, 1] = 1.0
    c["c_mbd"] = mbd
    cm = np.ones((128, 128), np.float32)
    cm[:, 0] = 0.0
    cm[:, 64] = 0.0
    c["c_cmask"] = cm
    k = np.arange(128)[:, None]
    q = np.arange(512)[None, :]
    mm = np.stack([(q >= v * 128 + k) for v in range(4)], axis=1).astype(np.float32)
    c["c_cmla"] = mm.reshape(128, 2048)
    inv = (10000.0 ** (-np.arange(0, 32, 2, dtype=np.float32) / 32.0)).astype(np.float32)
    rp = np.zeros((128, 2), np.float32)
    rp[64:96, 0] = np.concatenate([inv, inv])
    rp[64:96, 1] = np.concatenate([-np.ones(16, np.float32), np.ones(16, np.float32)])
    c["c_rope"] = rp
    c["c_ones"] = np.ones((128, 128), np.float32)
    return c


CONST_SHAPES = {k: v.shape for k, v in _consts_host().items()}

def param_specs(SEQ_PER_CORE, DEPTH):
  return [
    ("x", [SEQ_PER_CORE, S, D], F32), ("mem", [SEQ_PER_CORE, 256, D], F32), ("positions", [SEQ_PER_CORE, S], I32),
    ("w_in", [DEPTH, D, IN_COLS], F32), ("b_gate", [DEPTH, 4, D], F32), ("rwkv_mu", [DEPTH, 1664], F32),
    ("rwkv_w0", [DEPTH, W], F32), ("rwkv_w2", [DEPTH, 64, W], F32), ("rwkv_a0", [DEPTH, W], F32),
    ("rwkv_a2", [DEPTH, 64, W], F32), ("rwkv_k_k", [DEPTH, W], F32), ("rwkv_k_a", [DEPTH, W], F32),
    ("rwkv_r_k", [DEPTH, 8, 64], F32), ("rwkv_lnx_g", [DEPTH, W], F32), ("rwkv_lnx_b", [DEPTH, W], F32),
    ("mla_q_norm", [DEPTH, 384], F32), ("mla_w_uq", [DEPTH, 384, 768], F32), ("mla_kv_norm", [DEPTH, 256], F32),
    ("mla_w_ukv", [DEPTH, 256, 1024], F32), ("conv_w", [DEPTH, 31, W], F32), ("conv_b", [DEPTH, W], F32),
    ("conv_ln_g", [DEPTH, W], F32), ("conv_ln_b", [DEPTH, W], F32), ("xattn_w_mem_kv", [DEPTH, D, 2 * W], F32),
    ("w_o_branch", [DEPTH, 4, W, D], F32), ("w_out", [DEPTH, D, D], F32), ("ln_g", [DEPTH, D], F32),
    ("ln_b", [DEPTH, D], F32),
  ]


PARAM_SPECS = param_specs(SEQ_PER_CORE, DEPTH)


class K:
    def __init__(self, cfg):
        self.cfg = cfg
        self.nc = bass.Bass("TRN2", target_bir_lowering=False)
        nc = self.nc
        self.din = {}
        nseq = cfg.get("nseq", SEQ_PER_CORE)
        nlay = cfg.get("nlay", DEPTH)
        self.single = cfg.get("single", False)
        for n, shp, dt in param_specs(nseq, nlay):
            self.din[n] = nc.dram_tensor(n, shp, dt, kind="ExternalInput").ap()
        for n, shp in CONST_SHAPES.items():
            self.din[n] = nc.dram_tensor(n, list(shp), F32, kind="ExternalInput").ap()
        self.out = nc.dram_tensor("out", [nseq, S, D], F32, kind="ExternalOutput").ap()
        dbg = cfg.get("debug", False)
        kind = "ExternalOutput" if dbg else "Internal"
        self.ybr = nc.dram_tensor("ybr", [4, W, S], BF16, kind=kind).ap()
        self.ybr_buf = [Buf() for _ in range(4)]
        self.x1 = nc.dram_tensor("x1s", [nseq, S, D], F32, kind=kind).ap()
        self.x1_buf = [Buf() for _ in range(SEQ_PER_CORE)]

    def sb(self, es, name, shape, dt):
        self._uid = getattr(self, "_uid", 0) + 1
        return es.enter_context(self.nc.sbuf_tensor(f"{name}_{self._uid}", shape, dt))

    def build(self):
        nc = self.nc
        with contextlib.ExitStack() as es:
            self.S = Sched(nc, es)
            Sx = self.S
            for i in range(8):
                t = es.enter_context(nc.psum_tensor(f"ps{i}", [128, 512], F32))
                Sx.psum.append((t, Buf()))
            self.setup_consts(es)
            self.xT = self.sb(es, "xT", [128, 8, S], BF16)
            self.xT_b = Buf()
            self.ropeC = self.sb(es, "ropeC", [128, S], F32)
            self.ropeS = self.sb(es, "ropeS", [128, S], F32)
            self.rope_b = Buf()
            self.memT = self.sb(es, "memT", [128, 8, 256], BF16)
            self.memT_b = Buf()
            self.out_b = Buf()
            phases = self.cfg.get("phases", "RMCXE")
            seqs = self.cfg.get("seqs", list(range(SEQ_PER_CORE)))
            layers = self.cfg.get("layers", list(range(DEPTH)))
            for s in seqs:
                self.load_xT(s)
                if "M" in phases:
                    self.rope_tables(s)
                if "X" in phases:
                    self.load_memT(s)
                for l in layers:
                    if l == 0 or True:
                        self.load_params(l)
                    if "R" in phases:
                        self.phase_rwkv(s, l)
                    if "M" in phases:
                        self.phase_mla(s, l)
                    if "C" in phases:
                        self.phase_conv(s, l)
                    if "X" in phases:
                        self.phase_xattn(s, l)
                    if "E" in phases:
                        self.phase_epi(s, l)
            Sx.barrier(engines=("sp",))
        return nc

    def setup_consts(self, es):
        Sx = self.S
        self.cb = Buf()
        c = {}
        for n, shp in CONST_SHAPES.items():
            if n == "c_cmla":
                continue
            t = self.sb(es, "k_" + n, list(shp), F32)
            Sx.dma("sp", t[:], self.din[n], writes=[self.cb])
            c[n] = t
        self.c = c
        self.ident = c["c_ident"]
        self.bo = c["c_bo"]
        self.ident_bf = self.sb(es, "ident_bf", [128, 128], BF16)
        self.ones_bf = self.sb(es, "ones_bf", [128, 128], BF16)
        self.cmla_bf = self.sb(es, "cmla_bf", [128, 2048], BF16)
        Sx.op("pool", lambda e: e.tensor_copy(self.ident_bf[:], self.ident[:]), reads=[self.cb], writes=[self.cb])
        Sx.op("pool", lambda e: e.memset(self.ones_bf[:], 1.0), writes=[self.cb])
        with contextlib.ExitStack() as es2:
            tmpc = self.sb(es2, "k_cmla_tmp", [128, 2048], F32)
            Sx.dma("sp", tmpc[:], self.din["c_cmla"], writes=[self.cb])
            Sx.op("pool", lambda e: e.tensor_copy(self.cmla_bf[:], tmpc[:]), reads=[self.cb], writes=[self.cb])
            Sx.barrier()
        self.pcol = self.sb(es, "pcol", [128, NPC], F32)
        self.pcol_b = Buf()
        self.stageA = self.sb(es, "stageA", [128, 128], F32)
        self.stageB = self.sb(es, "stageB", [128, 128], F32)
        self.stage_b = Buf()
        self.lng_bc = self.sb(es, "lng_bc", [128, D], F32)
        self.lnb_bc = self.sb(es, "lnb_bc", [128, D], F32)
        self.ln_b_ = Buf()
        self.ones_row = self.sb(es, "ones_row", [1, 128], F32)
        Sx.op("pool", lambda e: e.memset(self.ones_row[:], 1.0), writes=[self.cb])

    def pc(self, name, j=0):
        i = PC[name] + j
        return self.pcol[:, i:i + 1]

    def load_params(self, l):
        Sx = self.S
        d = self.din
        rows = []

        def vec(name, key):
            ap = d[key][l]
            n = 1
            for s_ in ap.shape:
                n *= s_
            rows.append((PC[name], n // 128, ap))

        vec("mu", "rwkv_mu"); vec("w0", "rwkv_w0"); vec("a0", "rwkv_a0"); vec("k_k", "rwkv_k_k")
        vec("k_a", "rwkv_k_a"); vec("r_k", "rwkv_r_k"); vec("lnx_g", "rwkv_lnx_g"); vec("lnx_b", "rwkv_lnx_b")
        vec("q_norm", "mla_q_norm"); vec("kv_norm", "mla_kv_norm"); vec("conv_b", "conv_b")
        vec("conv_ln_g", "conv_ln_g"); vec("conv_ln_b", "conv_ln_b"); vec("b_gate", "b_gate"); vec("conv_w", "conv_w")
        for (c0, nr, ap) in rows:
            if len(ap.shape) == 2:
                if ap.shape[1] == 64:
                    flat = ap.rearrange("h n -> (h n)")
                    src = flat.rearrange("(r p) -> r p", p=128)
                elif ap.shape[0] == 31:
                    src = ap.rearrange("j (c p) -> (j c) p", p=128)
                else:
                    src = ap.rearrange("n (c p) -> (n c) p", p=128)
            else:
                src = ap.rearrange("(r p) -> r p", p=128)
            r = 0
            while r < nr:
                g = c0 + r
                if g < 128:
                    n = min(nr - r, 128 - g)
                    Sx.dma("sp", self.stageA[g:g + n, :], src[r:r + n, :], writes=[self.stage_b])
                else:
                    n = nr - r
                    Sx.dma("sp", self.stageB[g - 128:g - 128 + n, :], src[r:r + n, :], writes=[self.stage_b])
                r += n
        nb = NPC_RAW - 128
        ps, pb = Sx.next_psum()
        Sx.op("pe", lambda e: e.matmul(ps[:, 0:128], self.stageA[:, :], self.ident[:, :], start=True, stop=True),
              reads=[self.stage_b, self.cb], writes=[pb])
        Sx.op("pe", lambda e: e.matmul(ps[:, 128:128 + nb], self.stageB[0:nb, :], self.ident[0:nb, 0:nb], start=True, stop=True),
              reads=[self.stage_b, self.cb], writes=[pb])
        Sx.op("act", lambda e: e.activation(out=self.pcol[:, 0:NPC_RAW], in_=ps[:, 0:NPC_RAW], func=AF.Copy),
              reads=[pb], writes=[self.pcol_b])
        o = PC["omm"]
        Sx.op("dve", lambda e: e.tensor_scalar(out=self.pcol[:, o:o + 13], in0=self.pcol[:, 0:13], scalar1=-1.0, scalar2=1.0,
                                               op0=ALU.mult, op1=ALU.add), reads=[self.pcol_b], writes=[self.pcol_b])
        o2 = PC["omka"]
        ka = PC["k_a"]
        Sx.op("dve", lambda e: e.tensor_scalar(out=self.pcol[:, o2:o2 + 4], in0=self.pcol[:, ka:ka + 4], scalar1=-1.0, scalar2=1.0,
                                               op0=ALU.mult, op1=ALU.add), reads=[self.pcol_b], writes=[self.pcol_b])
        es3 = contextlib.ExitStack()
        self.lnrow = self.sb(es3, "lnrow", [1, 2 * D], F32)
        Sx.dma("sp", self.lnrow[0:1, 0:D], d["ln_g"][l:l + 1, :], writes=[self.ln_b_])
        Sx.dma("sp", self.lnrow[0:1, D:2 * D], d["ln_b"][l:l + 1, :], writes=[self.ln_b_])
        for j, dst in enumerate((self.lng_bc, self.lnb_bc)):
            for hh in range(2):
                ps, pb = Sx.next_psum()
                Sx.op("pe", lambda e, ps=ps, j=j, hh=hh: e.matmul(ps[:, :], self.ones_row[0:1, :],
                                                                  self.lnrow[0:1, j * D + hh * 512:j * D + hh * 512 + 512],
                                                                  start=True, stop=True),
                      reads=[self.ln_b_, self.cb], writes=[pb])
                Sx.op("act", lambda e, ps=ps, dst=dst, hh=hh: e.activation(out=dst[:, hh * 512:(hh + 1) * 512], in_=ps[:, :], func=AF.Copy),
                      reads=[pb], writes=[self.ln_b_])
        Sx.barrier()
        es3.close()

    def load_xT(self, s):
        Sx = self.S
        with contextlib.ExitStack() as es:
            xt = [self.sb(es, f"xtok{i}", [128, D], F32) for i in range(2)]
            xb = [Buf(), Buf()]
            for tt in range(S // 128):
                i = tt % 2
                Sx.dma("sp", xt[i][:], self.din["x"][s, tt * 128:(tt + 1) * 128, :], writes=[xb[i]])
                self.transpose_into_xT(xt[i], xb[i], tt)
            Sx.barrier()

    def transpose_into_xT(self, xtok, xbuf, tt):
        Sx = self.S
        for half in range(2):
            ps, pb = Sx.next_psum()
            for j in range(4):
                kc = half * 4 + j
                Sx.op("pe", lambda e, ps=ps, j=j, kc=kc: e.matmul(ps[:, j * 128:(j + 1) * 128], xtok[:, kc * 128:(kc + 1) * 128],
                                                                  self.ident[:, :], start=True, stop=True),
                      reads=[xbuf, self.cb], writes=[pb])
            out = self.xT[:, half * 4:(half + 1) * 4, tt * 128:(tt + 1) * 128]
            Sx.op("act" if half == 0 else "dve",
                  (lambda e, ps=ps, out=out: e.activation(out=out, in_=ps[:, :].rearrange("p (j t) -> p j t", j=4), func=AF.Copy)) if half == 0 else
                  (lambda e, ps=ps, out=out: e.tensor_copy(out, ps[:, :].rearrange("p (j t) -> p j t", j=4))),
                  reads=[pb], writes=[self.xT_b])

    def load_w_bf(self, dst, dst_buf, src, nkc):
        Sx = self.S
        v = src.rearrange("(kc p) c -> p kc c", p=128)
        for kc in range(nkc):
            Sx.dma("pool", dst[:, kc, :], v[:, kc, :], writes=[dst_buf])

    def phase_rwkv(self, s, l):
        Sx = self.S
        nc = self.nc
        d = self.din
        T = 128
        with contextlib.ExitStack() as es:
            sb = lambda n, shp, dt=F32: self.sb(es, "rw_" + n, shp, dt)
            wR = sb("wR", [128, 8, 2176], BF16); wRb = Buf()
            self.load_w_bf(wR, wRb, d["w_in"][l, :, OFF_RW:OFF_RW + 2176], 8)
            W2z = sb("W2z", [128, 512]); A2z = sb("A2z", [128, 512]); lb = Buf()
            Sx.op("pool", lambda e: e.memset(W2z[:], 0.0), writes=[lb])
            Sx.op("pool", lambda e: e.memset(A2z[:], 0.0), writes=[lb])
            Sx.dma("sp", W2z[0:64, :], d["rwkv_w2"][l], writes=[lb])
            Sx.dma("sp", A2z[64:128, :], d["rwkv_a2"][l], writes=[lb])
            p_raw = sb("p_raw", [128, 13, T + 1]); prb = Buf()
            Sx.op("pool", lambda e: e.memset(p_raw[:], 0.0), writes=[prb])
            pm = sb("pm", [128, 13, T]); pmb = Buf()
            sgate = sb("sgate", [128, 4, T]); sgb = Buf()
            T12 = sb("T12", [128, T]); t12b = Buf()
            names = ["lw", "logP", "asig", "eP", "eN", "ePm", "kk", "kkn", "kp", "rT", "t1", "t2", "t3"]
            tt_ = {n: sb(n, [128, 4, T]) for n in names}
            tb = {n: Buf() for n in names}
            Z = {n: sb("Z" + n, [128, 4, 2, 128]) for n in "abkv"}
            Zb_ = {n: Buf() for n in "abkv"}
            H = [sb(f"H{p}", [128, 128]) for p in range(4)]
            Hb = [Buf() for _ in range(4)]
            for p in range(4):
                Sx.op("pool", lambda e, p=p: e.memset(H[p][:], 0.0), writes=[Hb[p]])
            NSET = 2
            A_sb = [sb(f"A{i}", [128, 384]) for i in range(NSET)]; Ab = [Buf() for _ in range(NSET)]
            R_sb = [sb(f"R{i}", [128, 128]) for i in range(NSET)]; Rb = [Buf() for _ in range(NSET)]
            BKV = [sb(f"BKV{i}", [128, 384]) for i in range(NSET)]; BKVb = [Buf() for _ in range(NSET)]
            Wt = [[sb(f"W{i}_{j}", [128, 128]) for j in range(2)] for i in range(NSET)]
            Wtb = [[Buf() for j in range(2)] for i in range(NSET)]
            MP = [[sb(f"MP{i}_{j}", [128, 256]) for j in range(2)] for i in range(NSET)]
            MPb = [[Buf() for j in range(2)] for i in range(NSET)]
            X_sb = [sb(f"X{i}", [128, 128]) for i in range(NSET)]; Xb = [Buf() for _ in range(NSET)]
            U_sb = [sb(f"U{i}", [128, 128]) for i in range(NSET)]; Ub = [Buf() for _ in range(NSET)]
            HpC = [sb(f"HpC{i}", [128, 128]) for i in range(NSET)]; HpCb = [Buf() for _ in range(NSET)]
            Ycm = sb("Ycm", [128, 4, T]); Yb = Buf()
            ybf = sb("ybf", [128, 4, T], BF16); ybb = Buf()
            cb = self.cb
            ident = self.ident
            maskA = self.c["c_maskA"]; maskB = self.c["c_maskB"]; mbd = self.c["c_mbd"]; cmask = self.c["c_cmask"]
            pcb = self.pcol_b
            ydst = self.ybr[0].rearrange("(c p) t -> p c t", p=128)

            def flat(t):
                return t[:, :, :].rearrange("p c t -> p (c t)")

            for blk in range(self.cfg.get('nblk', S // T)):
                t0 = blk * T
                for cbk in range(17):
                    ps, pb = Sx.next_psum()
                    for kc in range(8):
                        Sx.op("pe", lambda e, ps=ps, kc=kc, cbk=cbk: e.matmul(
                            ps[:, 0:T], wR[:, kc, cbk * 128:(cbk + 1) * 128], self.xT[:, kc, t0:t0 + T],
                            start=(kc == 0), stop=(kc == 7)), reads=[wRb, self.xT_b], writes=[pb])
                    if cbk < 13:
                        Sx.op("act", lambda e, ps=ps, cbk=cbk: e.activation(out=p_raw[:, cbk, 1:T + 1], in_=ps[:, 0:T], func=AF.Copy),
                              reads=[pb], writes=[prb])
                    else:
                        Sx.op("act", lambda e, ps=ps, cbk=cbk: e.activation(out=sgate[:, cbk - 13, :], in_=ps[:, 0:T], func=AF.Silu),
                              reads=[pb], writes=[sgb])
                for cbk in range(13):
                    Sx.op("pool", lambda e, cbk=cbk: e.tensor_scalar(out=pm[:, cbk, :], in0=p_raw[:, cbk, 1:T + 1],
                                                                     scalar1=self.pc("omm", cbk), scalar2=None, op0=ALU.mult),
                          reads=[prb, pcb], writes=[pmb])
                    Sx.op("dve", lambda e, cbk=cbk: e.scalar_tensor_tensor(out=pm[:, cbk, :], in0=p_raw[:, cbk, 0:T],
                                                                           scalar=self.pc("mu", cbk), in1=pm[:, cbk, :],
                                                                           op0=ALU.mult, op1=ALU.add),
                          reads=[prb, pcb, pmb], writes=[pmb])
                Sx.op("pool", lambda e: e.tensor_copy(p_raw[:, :, 0:1], p_raw[:, :, T:T + 1]), reads=[prb], writes=[prb])
                r_ = pm[:, 0:4, :]; k_ = pm[:, 4:8, :]; v_ = pm[:, 8:12, :]
                Sx.op("act", lambda e: e.activation(out=T12[0:64, :], in_=pm[0:64, 12, :], func=AF.Tanh), reads=[pmb], writes=[t12b])
                Sx.op("dve", lambda e: e.tensor_copy(T12[64:128, :], pm[64:128, 12, :]), reads=[pmb], writes=[t12b])
                for c4 in range(4):
                    ps, pb = Sx.next_psum()
                    Sx.op("pe", lambda e, ps=ps, c4=c4: e.matmul(ps[:, 0:T], W2z[:, c4 * 128:(c4 + 1) * 128], T12[:, :], start=True, stop=True),
                          reads=[lb, t12b], writes=[pb])
                    Sx.op("pe", lambda e, ps=ps, c4=c4: e.matmul(ps[:, T:2 * T], A2z[:, c4 * 128:(c4 + 1) * 128], T12[:, :], start=True, stop=True),
                          reads=[lb, t12b], writes=[pb])
                    Sx.op("act", lambda e, ps=ps, c4=c4: e.activation(out=tt_["lw"][:, c4, :], in_=ps[:, 0:T], func=AF.Sigmoid,
                                                                      bias=self.pc("w0", c4)), reads=[pb, pcb], writes=[tb["lw"]])
                    Sx.op("act", lambda e, ps=ps, c4=c4: e.activation(out=tt_["asig"][:, c4, :], in_=ps[:, T:2 * T], func=AF.Sigmoid,
                                                                      bias=self.pc("a0", c4)), reads=[pb, pcb], writes=[tb["asig"]])
                Sx.op("pool", lambda e: e.tensor_scalar(out=flat(tt_["lw"]), in0=flat(tt_["lw"]), scalar1=-0.6065306597126334,
                                                        scalar2=None, op0=ALU.mult), reads=[tb["lw"]], writes=[tb["lw"]])
                for c4 in range(4):
                    Sx.op("dve", lambda e, c4=c4: e.tensor_tensor_scan(out=tt_["logP"][:, c4, :], data0=cmask[:, 0:T], data1=tt_["lw"][:, c4, :],
                                                                      initial=0.0, op0=ALU.mult, op1=ALU.add),
                          reads=[tb["lw"], cb], writes=[tb["logP"]])
                Sx.op("act", lambda e: e.activation(out=flat(tt_["eP"]), in_=flat(tt_["logP"]), func=AF.Exp), reads=[tb["logP"]], writes=[tb["eP"]])
                Sx.op("act", lambda e: e.activation(out=flat(tt_["eN"]), in_=flat(tt_["logP"]), func=AF.Exp, scale=-1.0),
                      reads=[tb["logP"]], writes=[tb["eN"]])
                Sx.op("pool", lambda e: e.tensor_tensor(out=flat(tt_["t1"]), in0=flat(tt_["logP"]), in1=flat(tt_["lw"]), op=ALU.subtract),
                      reads=[tb["logP"], tb["lw"]], writes=[tb["t1"]])
                Sx.op("act", lambda e: e.activation(out=flat(tt_["ePm"]), in_=flat(tt_["t1"]), func=AF.Exp), reads=[tb["t1"]], writes=[tb["ePm"]])
                for c4 in range(4):
                    Sx.op("pool", lambda e, c4=c4: e.tensor_scalar(out=tt_["kk"][:, c4, :], in0=k_[:, c4, :], scalar1=self.pc("k_k", c4),
                                                                   scalar2=None, op0=ALU.mult), reads=[pmb, pcb], writes=[tb["kk"]])
                Sx.op("act", lambda e: e.activation(out=flat(tt_["t2"]), in_=flat(tt_["kk"]), func=AF.Square), reads=[tb["kk"]], writes=[tb["t2"]])
                ps, pb = Sx.next_psum()
                Sx.op("pe", lambda e, ps=ps: e.matmul(ps[:, 0:4 * T], self.bo[:, :], flat(tt_["t2"]), start=True, stop=True),
                      reads=[tb["t2"], cb], writes=[pb])
                Sx.op("act", lambda e, ps=ps: e.activation(out=flat(tt_["t3"]), in_=ps[:, 0:4 * T], func=AF.Sqrt), reads=[pb], writes=[tb["t3"]])
                Sx.op("dve", lambda e: e.tensor_scalar(out=flat(tt_["t3"]), in0=flat(tt_["t3"]), scalar1=1e-12, scalar2=None, op0=ALU.max),
                      reads=[tb["t3"]], writes=[tb["t3"]])
                Sx.op("dve", lambda e: e.reciprocal(flat(tt_["t2"]), flat(tt_["t3"])), reads=[tb["t3"]], writes=[tb["t2"]])
                Sx.op("pool", lambda e: e.tensor_tensor(out=flat(tt_["kkn"]), in0=flat(tt_["kk"]), in1=flat(tt_["t2"]), op=ALU.mult),
                      reads=[tb["kk"], tb["t2"]], writes=[tb["kkn"]])
                for c4 in range(4):
                    Sx.op("act", lambda e, c4=c4: e.activation(out=tt_["t3"][:, c4, :], in_=tt_["asig"][:, c4, :], func=AF.Identity,
                                                               scale=self.pc("k_a", c4), bias=self.pc("omka", c4)),
                          reads=[tb["asig"], pcb], writes=[tb["t3"]])
                Sx.op("pool", lambda e: e.tensor_tensor(out=flat(tt_["kp"]), in0=k_.rearrange("p c t -> p (c t)"), in1=flat(tt_["t3"]), op=ALU.mult),
                      reads=[pmb, tb["t3"]], writes=[tb["kp"]])
                Sx.op("dve", lambda e: e.scalar_tensor_tensor(out=flat(tt_["t1"]), in0=flat(tt_["kkn"]), scalar=-1.0, in1=flat(tt_["ePm"]),
                                                              op0=ALU.mult, op1=ALU.mult), reads=[tb["kkn"], tb["ePm"]], writes=[tb["t1"]])
                Sx.op("pool", lambda e: e.tensor_tensor(out=flat(tt_["t2"]), in0=flat(tt_["kkn"]), in1=flat(tt_["asig"]), op=ALU.mult),
                      reads=[tb["kkn"], tb["asig"]], writes=[tb["t2"]])
                Sx.op("pool", lambda e: e.tensor_tensor(out=flat(tt_["t2"]), in0=flat(tt_["t2"]), in1=flat(tt_["eN"]), op=ALU.mult),
                      reads=[tb["t2"], tb["eN"]], writes=[tb["t2"]])
                Sx.op("dve", lambda e: e.tensor_tensor(out=flat(tt_["t3"]), in0=flat(tt_["kp"]), in1=flat(tt_["eN"]), op=ALU.mult),
                      reads=[tb["kp"], tb["eN"]], writes=[tb["t3"]])
                Sx.op("dve", lambda e: e.tensor_tensor(out=flat(tt_["rT"]), in0=r_.rearrange("p c t -> p (c t)"), in1=flat(tt_["eP"]), op=ALU.mult),
                      reads=[pmb, tb["eP"]], writes=[tb["rT"]])
                mb4 = mbd[:, 0:2].unsqueeze(1).unsqueeze(3).to_broadcast([128, 8, 2, 64])
                for zi, (zn, srcap, srcb) in enumerate([("a", tt_["t1"], tb["t1"]), ("b", tt_["t2"], tb["t2"]), ("k", tt_["t3"], tb["t3"]),
                                                        ("v", None, pmb)]):
                    if srcap is None:
                        sview = v_.rearrange("p c (h t) -> p (c h) t", h=2)
                    else:
                        sview = srcap[:, :, :].rearrange("p c (h t) -> p (c h) t", h=2)
                    in0 = sview.unsqueeze(2).to_broadcast([128, 8, 2, 64])
                    outv = Z[zn][:, :, :, :].rearrange("p c h (g t) -> p (c h) g t", g=2)
                    Sx.op("dve" if zi % 2 == 0 else "pool",
                          lambda e, outv=outv, in0=in0: e.tensor_tensor(out=outv, in0=in0, in1=mb4, op=ALU.mult),
                          reads=[srcb, cb], writes=[Zb_[zn]])
                for ch in range(2):
                    for pr in range(4):
                        si = pr % NSET
                        Za = Z["a"][:, pr, ch, :]; Zb = Z["b"][:, pr, ch, :]; Zk = Z["k"][:, pr, ch, :]; Zv = Z["v"][:, pr, ch, :]
                        rTu = tt_["rT"][:, pr, ch * 64:(ch + 1) * 64]
                        pC = tt_["eP"][:, pr, ch * 64 + 63:ch * 64 + 64]
                        psA, pbA = Sx.next_psum()
                        Sx.op("pe", lambda e: e.matmul(psA[:, 0:128], Zb, Za, start=True, stop=True), reads=[Zb_["b"], Zb_["a"]], writes=[pbA])
                        Sx.op("pe", lambda e: e.matmul(psA[:, 128:256], Zk, Za, start=True, stop=True), reads=[Zb_["k"], Zb_["a"]], writes=[pbA])
                        Sx.op("pe", lambda e: e.matmul(psA[:, 256:384], Za, Zb, start=True, stop=True), reads=[Zb_["b"], Zb_["a"]], writes=[pbA])
                        Sx.op("dve", lambda e: e.tensor_tensor(out=A_sb[si][:, :], in0=psA[:, 0:384], in1=maskA[:, :], op=ALU.mult),
                              reads=[pbA, cb], writes=[Ab[si]])
                        psB, pbB = Sx.next_psum()
                        Sx.op("pe", lambda e: e.matmul(psB[:, 0:64], Zb, rTu, start=True, stop=True), reads=[Zb_["b"], tb["rT"]], writes=[pbB])
                        Sx.op("pe", lambda e: e.matmul(psB[:, 64:128], Zk, rTu, start=True, stop=True), reads=[Zb_["k"], tb["rT"]], writes=[pbB])
                        Sx.op("dve", lambda e: e.tensor_tensor(out=R_sb[si][:, :], in0=psB[:, 0:128], in1=maskB[:, :], op=ALU.mult),
                              reads=[pbB, cb], writes=[Rb[si]])
                        psC, pbC = Sx.next_psum()
                        Sx.op("pe", lambda e: e.matmul(psC[:, 0:128], Zb, ident[:, :], start=True, stop=True), reads=[Zb_["b"], cb], writes=[pbC])
                        Sx.op("pe", lambda e: e.matmul(psC[:, 128:256], Zk, ident[:, :], start=True, stop=True), reads=[Zb_["k"], cb], writes=[pbC])
                        Sx.op("pe", lambda e: e.matmul(psC[:, 256:384], Zv, ident[:, :], start=True, stop=True), reads=[Zb_["v"], cb], writes=[pbC])
                        Sx.op("act", lambda e: e.activation(out=BKV[si][:, :], in_=psC[:, 0:384], func=AF.Copy), reads=[pbC], writes=[BKVb[si]])
                        Sx.op("pool", lambda e: e.tensor_tensor(out=Wt[si][0][:, :], in0=A_sb[si][:, 0:128], in1=ident[:, :], op=ALU.add),
                              reads=[Ab[si], cb], writes=[Wtb[si][0]])
                        Mprev = A_sb[si][:, 0:128]; Pprev = A_sb[si][:, 256:384]; mpb_prev = Ab[si]
                        for j in range(1, 6):
                            cur = j % 2
                            psN, pbN = Sx.next_psum()
                            Sx.op("pe", lambda e, psN=psN, Mprev=Mprev, Pprev=Pprev: e.matmul(psN[:, 0:128], Pprev, Mprev, start=True, stop=True),
                                  reads=[mpb_prev], writes=[pbN])
                            Sx.op("pe", lambda e, psN=psN, Mprev=Mprev, Pprev=Pprev: e.matmul(psN[:, 128:256], Mprev, Pprev, start=True, stop=True),
                                  reads=[mpb_prev], writes=[pbN])
                            Sx.op("act", lambda e, psN=psN, cur=cur: e.activation(out=MP[si][cur][:, :], in_=psN[:, 0:256], func=AF.Copy),
                                  reads=[pbN], writes=[MPb[si][cur]])
                            Mprev = MP[si][cur][:, 0:128]; Pprev = MP[si][cur][:, 128:256]; mpb_prev = MPb[si][cur]
                            psW, pbW = Sx.next_psum()
                            Sx.op("pe", lambda e, psW=psW, Pprev=Pprev, j=j: e.matmul(psW[:, 0:128], Pprev, Wt[si][(j - 1) % 2][:, :], start=True, stop=True),
                                  reads=[mpb_prev, Wtb[si][(j - 1) % 2]], writes=[pbW])
                            Sx.op("dve", lambda e, psW=psW, j=j: e.tensor_tensor(out=Wt[si][j % 2][:, :], in0=psW[:, 0:128], in1=Wt[si][(j - 1) % 2][:, :], op=ALU.add),
                                  reads=[pbW, Wtb[si][(j - 1) % 2]], writes=[Wtb[si][j % 2]])
                        TT = Wt[si][1]; TTb = Wtb[si][1]
                        psX, pbX = Sx.next_psum()
                        Sx.op("pe", lambda e: e.matmul(psX[:, 0:128], Za, H[pr][:, :], start=True, stop=False), reads=[Zb_["a"], Hb[pr]], writes=[pbX])
                        Sx.op("pe", lambda e: e.matmul(psX[:, 0:128], A_sb[si][:, 128:256], BKV[si][:, 256:384], start=False, stop=True),
                              reads=[Ab[si], BKVb[si]], writes=[pbX])
                        Sx.op("act", lambda e: e.activation(out=X_sb[si][:, :], in_=psX[:, 0:128], func=AF.Copy), reads=[pbX], writes=[Xb[si]])
                        psU, pbU = Sx.next_psum()
                        Sx.op("pe", lambda e: e.matmul(psU[:, 0:128], TT[:, :], X_sb[si][:, :], start=True, stop=True), reads=[TTb, Xb[si]], writes=[pbU])
                        Sx.op("dve", lambda e: e.tensor_copy(U_sb[si][:, :], psU[:, 0:128]), reads=[pbU], writes=[Ub[si]])
                        psY, pbY = Sx.next_psum()
                        Sx.op("pe", lambda e: e.matmul(psY[:, 0:64], H[pr][:, :], rTu, start=True, stop=False), reads=[Hb[pr], tb["rT"]], writes=[pbY])
                        Sx.op("pe", lambda e: e.matmul(psY[:, 0:64], U_sb[si][:, :], R_sb[si][:, 0:64], start=False, stop=False),
                              reads=[Ub[si], Rb[si]], writes=[pbY])
                        Sx.op("pe", lambda e: e.matmul(psY[:, 0:64], BKV[si][:, 256:384], R_sb[si][:, 64:128], start=False, stop=True),
                              reads=[BKVb[si], Rb[si]], writes=[pbY])
                        Sx.op("act", lambda e: e.activation(out=Ycm[:, pr, ch * 64:(ch + 1) * 64], in_=psY[:, 0:64], func=AF.Copy), reads=[pbY], writes=[Yb])
                        psG, pbG = Sx.next_psum()
                        Sx.op("pe", lambda e: e.matmul(psG[:, 0:128], BKV[si][:, 0:128], U_sb[si][:, :], start=True, stop=False),
                              reads=[BKVb[si], Ub[si]], writes=[pbG])
                        Sx.op("pe", lambda e: e.matmul(psG[:, 0:128], BKV[si][:, 128:256], BKV[si][:, 256:384], start=False, stop=True),
                              reads=[BKVb[si]], writes=[pbG])
                        Sx.op("pool", lambda e: e.tensor_scalar(out=HpC[si][:, :], in0=H[pr][:, :], scalar1=pC, scalar2=None, op0=ALU.mult),
                              reads=[Hb[pr], tb["eP"]], writes=[HpCb[si]])
                        Sx.op("dve", lambda e: e.scalar_tensor_tensor(out=H[pr][:, :], in0=psG[:, 0:128], scalar=pC, in1=HpC[si][:, :],
                                                                      op0=ALU.mult, op1=ALU.add),
                              reads=[pbG, HpCb[si], tb["eP"]], writes=[Hb[pr]])
                NT_ = 4 * T
                psM, pbM = Sx.next_psum()
                Sx.op("pe", lambda e: e.matmul(psM[:, 0:NT_], self.bo[:, :], flat(Ycm), start=True, stop=True), reads=[Yb, cb], writes=[pbM])
                Sx.op("act", lambda e: e.activation(out=flat(tt_["t1"]), in_=flat(Ycm), func=AF.Square), reads=[Yb], writes=[tb["t1"]])
                psQ, pbQ = Sx.next_psum()
                Sx.op("pe", lambda e: e.matmul(psQ[:, 0:NT_], self.bo[:, :], flat(tt_["t1"]), start=True, stop=True), reads=[tb["t1"], cb], writes=[pbQ])
                Sx.op("act", lambda e: e.activation(out=flat(tt_["t2"]), in_=psM[:, 0:NT_], func=AF.Copy, scale=1.0 / 64.0), reads=[pbM], writes=[tb["t2"]])
                Sx.op("pool", lambda e: e.tensor_tensor(out=flat(tt_["t3"]), in0=flat(tt_["t2"]), in1=flat(tt_["t2"]), op=ALU.mult),
                      reads=[tb["t2"]], writes=[tb["t3"]])
                Sx.op("dve", lambda e: e.scalar_tensor_tensor(out=flat(tt_["t3"]), in0=psQ[:, 0:NT_], scalar=1.0 / 64.0, in1=flat(tt_["t3"]),
                                                              op0=ALU.mult, op1=ALU.subtract), reads=[pbQ, tb["t3"]], writes=[tb["t3"]])
                Sx.op("dve", lambda e: e.tensor_scalar(out=flat(tt_["t3"]), in0=flat(tt_["t3"]), scalar1=64e-5, scalar2=None, op0=ALU.add),
                      reads=[tb["t3"]], writes=[tb["t3"]])
                Sx.op("act", lambda e: e.activation(out=flat(tt_["t3"]), in_=flat(tt_["t3"]), func=AF.Sqrt), reads=[tb["t3"]], writes=[tb["t3"]])
                Sx.op("dve", lambda e: e.reciprocal(flat(tt_["t1"]), flat(tt_["t3"])), reads=[tb["t3"]], writes=[tb["t1"]])
                Sx.op("pool", lambda e: e.tensor_tensor(out=flat(tt_["t2"]), in0=flat(Ycm), in1=flat(tt_["t2"]), op=ALU.subtract),
                      reads=[Yb, tb["t2"]], writes=[tb["t2"]])
                Sx.op("pool", lambda e: e.tensor_tensor(out=flat(tt_["t2"]), in0=flat(tt_["t2"]), in1=flat(tt_["t1"]), op=ALU.mult),
                      reads=[tb["t2"], tb["t1"]], writes=[tb["t2"]])
                for c4 in range(4):
                    Sx.op("act", lambda e, c4=c4: e.activation(out=tt_["t2"][:, c4, :], in_=tt_["t2"][:, c4, :], func=AF.Identity,
                                                               scale=self.pc("lnx_g", c4), bias=self.pc("lnx_b", c4)),
                          reads=[tb["t2"], pcb], writes=[tb["t2"]])
                    Sx.op("dve", lambda e, c4=c4: e.scalar_tensor_tensor(out=tt_["t1"][:, c4, :], in0=r_[:, c4, :], scalar=self.pc("r_k", c4),
                                                                         in1=tt_["kp"][:, c4, :], op0=ALU.mult, op1=ALU.mult),
                          reads=[pmb, tb["kp"], pcb, tb["t1"]], writes=[tb["t1"]])
                psR, pbR = Sx.next_psum()
                Sx.op("pe", lambda e: e.matmul(psR[:, 0:NT_], self.bo[:, :], flat(tt_["t1"]), start=True, stop=True), reads=[tb["t1"], cb], writes=[pbR])
                Sx.op("dve", lambda e: e.tensor_tensor(out=flat(tt_["t3"]), in0=psR[:, 0:NT_], in1=v_.rearrange("p c t -> p (c t)"), op=ALU.mult),
                      reads=[pbR, pmb], writes=[tb["t3"]])
                Sx.op("pool", lambda e: e.tensor_tensor(out=flat(tt_["t3"]), in0=flat(tt_["t3"]), in1=flat(tt_["t2"]), op=ALU.add),
                      reads=[tb["t3"], tb["t2"]], writes=[tb["t3"]])
                Sx.op("pool", lambda e: e.tensor_tensor(out=flat(ybf), in0=flat(tt_["t3"]), in1=flat(sgate), op=ALU.mult),
                      reads=[tb["t3"], sgb], writes=[ybb])
                Sx.dma("sp", ydst[:, :, t0:t0 + T], ybf[:, :, :], reads=[ybb], writes=[self.ybr_buf[0]])
            Sx.barrier()

    def rope_tables(self, s):
        Sx = self.S
        rp = self.c["c_rope"]
        cb = self.cb
        with contextlib.ExitStack() as es:
            posi = self.sb(es, "posi", [128, S], I32)
            ang = self.sb(es, "ang", [128, S], F32)
            t1 = self.sb(es, "rp_t1", [128, S], F32)
            t2 = self.sb(es, "rp_t2", [128, S], F32)
            ki = self.sb(es, "rp_ki", [128, S], I32)
            b = Buf()
            R = slice(64, 96)
            Sx.dma("sp", posi[R, :], self.din["positions"][s:s + 1, :].broadcast_to([32, S]), writes=[b])
            Sx.op("dve", lambda e: e.tensor_copy(ang[R, :], posi[R, :]), reads=[b], writes=[b])
            Sx.op("dve", lambda e: e.tensor_scalar(out=ang[R, :], in0=ang[R, :], scalar1=rp[R, 0:1], scalar2=None, op0=ALU.mult),
                  reads=[b, cb], writes=[b])
            TWO_PI = 6.283185307179586
            for which, dst in ((0, self.ropeS), (1, self.ropeC)):
                shift = 0.0 if which == 0 else 1.5707963267948966
                Sx.op("dve", lambda e: e.tensor_scalar(out=t1[R, :], in0=ang[R, :], scalar1=shift, scalar2=None, op0=ALU.add), reads=[b], writes=[b])
                Sx.op("dve", lambda e: e.tensor_scalar(out=t2[R, :], in0=t1[R, :], scalar1=1.0 / TWO_PI, scalar2=0.5, op0=ALU.mult, op1=ALU.add),
                      reads=[b], writes=[b])
                Sx.op("dve", lambda e: e.tensor_copy(ki[R, :], t2[R, :]), reads=[b], writes=[b])
                Sx.op("dve", lambda e: e.tensor_copy(t2[R, :], ki[R, :]), reads=[b], writes=[b])
                Sx.op("dve", lambda e: e.scalar_tensor_tensor(out=t1[R, :], in0=t2[R, :], scalar=-TWO_PI, in1=t1[R, :], op0=ALU.mult, op1=ALU.add),
                      reads=[b], writes=[b])
                Sx.op("dve", lambda e: e.tensor_scalar(out=t2[R, :], in0=t1[R, :], scalar1=-3.141592653589793, scalar2=TWO_PI, op0=ALU.is_lt, op1=ALU.mult),
                      reads=[b], writes=[b])
                Sx.op("dve", lambda e: e.tensor_tensor(out=t1[R, :], in0=t1[R, :], in1=t2[R, :], op=ALU.add), reads=[b], writes=[b])
                Sx.op("dve", lambda e: e.tensor_scalar(out=t2[R, :], in0=t1[R, :], scalar1=3.141592653589793, scalar2=-TWO_PI, op0=ALU.is_gt, op1=ALU.mult),
                      reads=[b], writes=[b])
                Sx.op("dve", lambda e: e.tensor_tensor(out=t1[R, :], in0=t1[R, :], in1=t2[R, :], op=ALU.add), reads=[b], writes=[b])
                Sx.op("dve", lambda e: e.tensor_scalar(out=t1[R, :], in0=t1[R, :], scalar1=3.1415925, scalar2=-3.1415925, op0=ALU.min, op1=ALU.max),
                      reads=[b], writes=[b])
                Sx.op("act", lambda e, dst=dst: e.activation(out=dst[R, :], in_=t1[R, :], func=AF.Sin), reads=[b], writes=[self.rope_b])
            Sx.op("dve", lambda e: e.tensor_scalar(out=self.ropeS[R, :], in0=self.ropeS[R, :], scalar1=rp[R, 1:2], scalar2=None, op0=ALU.mult),
                  reads=[self.rope_b, cb], writes=[self.rope_b])
            Sx.barrier()

    def load_memT(self, s):
        Sx = self.S
        with contextlib.ExitStack() as es:
            mt = [self.sb(es, f"mtok{i}", [128, D], F32) for i in range(2)]
            mb = [Buf(), Buf()]
            for tt in range(2):
                Sx.dma("sp", mt[tt][:], self.din["mem"][s, tt * 128:(tt + 1) * 128, :], writes=[mb[tt]])
                for half in range(2):
                    ps, pb = Sx.next_psum()
                    for j in range(4):
                        kc = half * 4 + j
                        Sx.op("pe", lambda e: e.matmul(ps[:, j * 128:(j + 1) * 128], mt[tt][:, kc * 128:(kc + 1) * 128], self.ident[:, :], start=True, stop=True),
                              reads=[mb[tt], self.cb], writes=[pb])
                    Sx.op("act", lambda e: e.activation(out=self.memT[:, half * 4:(half + 1) * 4, tt * 128:(tt + 1) * 128],
                                                        in_=ps[:, :].rearrange("p (j t) -> p j t", j=4), func=AF.Copy),
                          reads=[pb], writes=[self.memT_b])
            Sx.barrier()

    def inproj(self, ps, pb, wt, wb, c0, ncol, t0, ntok):
        Sx = self.S
        for kc in range(8):
            Sx.op("pe", lambda e, kc=kc: e.matmul(ps[0:ncol, 0:ntok], wt[:, kc, c0:c0 + ncol], self.xT[:, kc, t0:t0 + ntok],
                                                  start=(kc == 0), stop=(kc == 7)), reads=[wb, self.xT_b], writes=[pb])

    def rms_latent(self, es, tag, lat_f, sq, nch, dim, gname, outn, bufs):
        Sx = self.S
        lb, sqb, ob, rb, rstd = bufs
        ps, pb = Sx.next_psum(4)
        for i in range(nch):
            Sx.op("pe", lambda e, i=i: e.matmul(ps[:, :], self.c["c_ones"][:, :], sq[:, i, :], start=(i == 0), stop=(i == nch - 1)),
                  reads=[sqb, self.cb], writes=[pb])
        Sx.op("dve", lambda e: e.tensor_scalar(out=rstd[:, :], in0=ps[:, :], scalar1=1.0 / dim, scalar2=1e-6, op0=ALU.mult, op1=ALU.add),
              reads=[pb], writes=[rb])
        Sx.op("act", lambda e: e.activation(out=rstd[:, :], in_=rstd[:, :], func=AF.Sqrt), reads=[rb], writes=[rb])
        Sx.op("dve", lambda e: e.reciprocal(rstd[:, :], rstd[:, :]), reads=[rb], writes=[rb])
        for i in range(nch):
            Sx.op("dve", lambda e, i=i: e.scalar_tensor_tensor(out=outn[:, i, :], in0=lat_f[:, i, :], scalar=self.pc(gname, i), in1=rstd[:, :],
                                                               op0=ALU.mult, op1=ALU.mult), reads=[lb, rb, self.pcol_b], writes=[ob])

    def phase_mla(self, s, l):
        Sx = self.S
        d = self.din
        cb = self.cb
        scale = 96.0 ** -0.5
        with contextlib.ExitStack() as es:
            sb = lambda n, shp, dt=F32: self.sb(es, "ml_" + n, shp, dt)
            wM = sb("wM", [128, 8, 1184], BF16); wMb = Buf()
            self.load_w_bf(wM, wMb, d["w_in"][l, :, OFF_QLAT:OFF_QLAT + 1184], 8)
            wks = sb("wks", [128, 8, 96], BF16); wksb = Buf()
            Sx.op("pool", lambda e: e.memset(wks[:], 0.0), writes=[wksb])
            vk = d["w_in"][l, :, OFF_KPE:OFF_KPE + 32].rearrange("(kc p) c -> p kc c", p=128)
            Sx.dma("pool", wks[:, :, 64:80], vk[:, :, 16:32], writes=[wksb])
            Sx.dma("pool", wks[:, :, 80:96], vk[:, :, 0:16], writes=[wksb])
            Wuq = sb("Wuq", [128, 3, 768], BF16); Wuqb = Buf()
            self.load_w_bf(Wuq, Wuqb, d["mla_w_uq"][l], 3)
            Wus = sb("Wus", [128, 3, 768], BF16); Wusb = Buf()
            Wq4 = Wuq[:, :, :].rearrange("p k (h c) -> p k h c", h=8)
            Ws4 = Wus[:, :, :].rearrange("p k (h c) -> p k h c", h=8)
            Sx.op("pool", lambda e: e.tensor_copy(Ws4[:, :, :, 0:64], Wq4[:, :, :, 0:64]), reads=[Wuqb], writes=[Wusb])
            Sx.op("pool", lambda e: e.tensor_copy(Ws4[:, :, :, 64:80], Wq4[:, :, :, 80:96]), reads=[Wuqb], writes=[Wusb])
            Sx.op("pool", lambda e: e.tensor_copy(Ws4[:, :, :, 80:96], Wq4[:, :, :, 64:80]), reads=[Wuqb], writes=[Wusb])
            Wukv = sb("Wukv", [128, 2, 1024], BF16); Wukvb = Buf()
            self.load_w_bf(Wukv, Wukvb, d["mla_w_ukv"][l], 2)
            QT = sb("QT", [128, 8, 512], BF16); QTb = Buf()
            KT = sb("KT", [128, 8, S], BF16); KTb = Buf()
            V = sb("V", [128, 16, 512], BF16); Vb = Buf()
            qlf = sb("qlf", [128, 3, 512]); qlb = Buf()
            qsq = sb("qsq", [128, 3, 512]); qsb = Buf()
            qn = sb("qn", [128, 3, 512], BF16); qnb = Buf()
            klf = sb("klf", [128, 2, 512]); klb = Buf()
            ksq = sb("ksq", [128, 2, 512]); ksb = Buf()
            kvn = sb("kvn", [128, 2, 512], BF16); knb = Buf()
            rq = sb("rq", [128, 512]); rqb = Buf()
            rk = sb("rk", [128, 512]); rkb = Buf()
            ta = sb("ta", [128, 512]); tab = Buf()
            tbb_ = sb("tb", [128, 512]); tbb = Buf()
            PT = [sb(f"PT{i}", [128, 512], BF16) for i in range(3)]; PTb = [Buf() for _ in range(3)]
            sgt = sb("sgt", [128, 512]); sgb = Buf()
            rl = sb("rl", [128, 512]); rlb = Buf()
            yb = [sb(f"yb{i}", [128, 512], BF16) for i in range(2)]; ybb = [Buf(), Buf()]
            ydst = self.ybr[1]
            R = slice(64, 96)
            pti = 0
            for g in range(4):
                t0 = g * 512
                for i in range(3):
                    ps, pb = Sx.next_psum(4)
                    self.inproj(ps, pb, wM, wMb, i * 128, 128, t0, 512)
                    Sx.op("act", lambda e: e.activation(out=qlf[:, i, :], in_=ps[:, :], func=AF.Copy), reads=[pb], writes=[qlb])
                    Sx.op("act", lambda e: e.activation(out=qsq[:, i, :], in_=ps[:, :], func=AF.Square), reads=[pb], writes=[qsb])
                self.rms_latent(es, "q", qlf, qsq, 3, 384.0, "q_norm", qn, (qlb, qsb, qnb, rqb, rq))
                for i in range(2):
                    ps, pb = Sx.next_psum(4)
                    self.inproj(ps, pb, wM, wMb, 384 + i * 128, 128, t0, 512)
                    Sx.op("act", lambda e: e.activation(out=klf[:, i, :], in_=ps[:, :], func=AF.Copy), reads=[pb], writes=[klb])
                    Sx.op("act", lambda e: e.activation(out=ksq[:, i, :], in_=ps[:, :], func=AF.Square), reads=[pb], writes=[ksb])
                self.rms_latent(es, "k", klf, ksq, 2, 256.0, "kv_norm", kvn, (klb, ksb, knb, rkb, rk))
                ps1, pb1 = Sx.next_psum(4)
                self.inproj(ps1, pb1, wM, wMb, 576, 96, t0, 512)
                ps2, pb2 = Sx.next_psum(4)
                self.inproj(ps2, pb2, wks, wksb, 0, 96, t0, 512)
                Sx.op("dve", lambda e: e.tensor_tensor(out=ta[R, :], in0=ps1[R, :], in1=self.ropeC[R, t0:t0 + 512], op=ALU.mult),
                      reads=[pb1, self.rope_b], writes=[tab])
                Sx.op("dve", lambda e: e.tensor_tensor(out=tbb_[R, :], in0=ps2[R, :], in1=self.ropeS[R, t0:t0 + 512], op=ALU.mult),
                      reads=[pb2, self.rope_b], writes=[tbb])
                Sx.op("pool", lambda e: e.tensor_tensor(out=ta[R, :], in0=ta[R, :], in1=tbb_[R, :], op=ALU.add), reads=[tab, tbb], writes=[tab])
                for h in range(8):
                    Sx.op("pool" if h % 2 else "act",
                          (lambda e: e.tensor_copy(KT[R, h, t0:t0 + 512], ta[R, :])) if h % 2 else
                          (lambda e: e.activation(out=KT[R, h, t0:t0 + 512], in_=ta[R, :], func=AF.Copy)),
                          reads=[tab], writes=[KTb])
                for h in range(8):
                    ps1, pb1 = Sx.next_psum(4)
                    ps2, pb2 = Sx.next_psum(4)
                    for kc in range(3):
                        Sx.op("pe", lambda e: e.matmul(ps1[0:96, :], Wuq[:, kc, h * 96:(h + 1) * 96], qn[:, kc, :], start=(kc == 0), stop=(kc == 2)),
                              reads=[Wuqb, qnb], writes=[pb1])
                    for kc in range(3):
                        Sx.op("pe", lambda e: e.matmul(ps2[0:96, :], Wus[:, kc, h * 96:(h + 1) * 96], qn[:, kc, :], start=(kc == 0), stop=(kc == 2)),
                              reads=[Wusb, qnb], writes=[pb2])
                    Sx.op("act", lambda e: e.activation(out=QT[0:64, h, :], in_=ps1[0:64, :], func=AF.Copy), reads=[pb1], writes=[QTb])
                    Sx.op("dve", lambda e: e.tensor_tensor(out=ta[R, :], in0=ps1[R, :], in1=self.ropeC[R, t0:t0 + 512], op=ALU.mult),
                          reads=[pb1, self.rope_b], writes=[tab])
                    Sx.op("dve", lambda e: e.tensor_tensor(out=tbb_[R, :], in0=ps2[R, :], in1=self.ropeS[R, t0:t0 + 512], op=ALU.mult),
                          reads=[pb2, self.rope_b], writes=[tbb])
                    Sx.op("pool", lambda e: e.tensor_tensor(out=QT[R, h, :], in0=ta[R, :], in1=tbb_[R, :], op=ALU.add), reads=[tab, tbb], writes=[QTb])
                    ps3, pb3 = Sx.next_psum(4)
                    for kc in range(2):
                        Sx.op("pe", lambda e: e.matmul(ps3[0:64, :], Wukv[:, kc, h * 128:h * 128 + 64], kvn[:, kc, :], start=(kc == 0), stop=(kc == 1)),
                              reads=[Wukvb, knb], writes=[pb3])
                    Sx.op("act", lambda e: e.activation(out=KT[0:64, h, t0:t0 + 512], in_=ps3[0:64, :], func=AF.Copy), reads=[pb3], writes=[KTb])
                Wv = Wukv[:, :, :].rearrange("p k (h c) -> p k h c", h=8)
                for tt in range(4):
                    ps, pb = Sx.next_psum(4)
                    for kc in range(2):
                        Sx.op("pe", lambda e: e.matmul(ps[:, :].rearrange("p (h c) -> p h c", h=8), kvn[:, kc, tt * 128:(tt + 1) * 128], Wv[:, kc, :, 64:128],
                                                       start=(kc == 0), stop=(kc == 1)), reads=[Wukvb, knb], writes=[pb])
                    Sx.op("dve", lambda e: e.tensor_copy(V[:, g * 4 + tt, :], ps[:, :]), reads=[pb], writes=[Vb])
                for pr in range(4):
                    accs = []
                    for hh in range(2):
                        h = pr * 2 + hh
                        o_ps, o_pb = Sx.psum[4 + hh * 2]
                        l_ps, l_pb = Sx.psum[5 + hh * 2]
                        accs.append((o_ps, o_pb, l_ps, l_pb))
                        nj = 4 * g + 4
                        for j in range(nj):
                            sps, spb = Sx.next_psum(4)
                            Sx.op("pe", lambda e: e.matmul(sps[:, :], KT[0:96, h, j * 128:(j + 1) * 128], QT[0:96, h, :], start=True, stop=True),
                                  reads=[KTb, QTb], writes=[spb])
                            pt = PT[pti]; ptb = PTb[pti]; pti = (pti + 1) % 3
                            Sx.op("act", lambda e: e.activation(out=pt[:, :], in_=sps[:, :], func=AF.Exp, scale=scale), reads=[spb], writes=[ptb])
                            if j >= 4 * g:
                                v = j - 4 * g
                                Sx.op("pool", lambda e: e.tensor_tensor(out=pt[:, :], in0=pt[:, :], in1=self.cmla_bf[:, v * 512:(v + 1) * 512], op=ALU.mult),
                                      reads=[ptb, cb], writes=[ptb])
                            Sx.op("pe", lambda e: e.matmul(o_ps[:, :], V[:, j, pr * 128:(pr + 1) * 128], pt[:, :], start=(j == 0), stop=(j == nj - 1)),
                                  reads=[Vb, ptb], writes=[o_pb])
                            Sx.op("pe", lambda e: e.matmul(l_ps[:, :], self.ones_bf[:, :], pt[:, :], start=(j == 0), stop=(j == nj - 1)),
                                  reads=[cb, ptb], writes=[l_pb])
                    gps, gpb = Sx.next_psum(4)
                    self.inproj(gps, gpb, wM, wMb, 672 + pr * 128, 128, t0, 512)
                    Sx.op("act", lambda e: e.activation(out=sgt[:, :], in_=gps[:, :], func=AF.Silu), reads=[gpb], writes=[sgb])
                    yt = yb[pr % 2]; ytb = ybb[pr % 2]
                    for hh in range(2):
                        o_ps, o_pb, l_ps, l_pb = accs[hh]
                        HR = slice(hh * 64, hh * 64 + 64)
                        Sx.op("dve", lambda e: e.reciprocal(rl[HR, :], l_ps[HR, :]), reads=[l_pb], writes=[rlb])
                        Sx.op("dve", lambda e: e.tensor_tensor(out=rl[HR, :], in0=o_ps[HR, :], in1=rl[HR, :], op=ALU.mult), reads=[o_pb, rlb], writes=[rlb])
                        Sx.op("pool", lambda e: e.tensor_tensor(out=yt[HR, :], in0=rl[HR, :], in1=sgt[HR, :], op=ALU.mult), reads=[rlb, sgb], writes=[ytb])
                    Sx.dma("sp", ydst[pr * 128:(pr + 1) * 128, t0:t0 + 512], yt[:, :], reads=[ytb], writes=[self.ybr_buf[1]])
            Sx.barrier()

    def phase_xattn(self, s, l):
        Sx = self.S
        d = self.din
        cb = self.cb
        scale = 128.0 ** -0.5
        with contextlib.ExitStack() as es:
            sb = lambda n, shp, dt=F32: self.sb(es, "xa_" + n, shp, dt)
            wkv = sb("wkv", [128, 8, 1024], BF16); wkvb = Buf()
            self.load_w_bf(wkv, wkvb, d["xattn_w_mem_kv"][l], 8)
            wX = sb("wX", [128, 8, 1024], BF16); wXb = Buf()
            self.load_w_bf(wX, wXb, d["w_in"][l, :, OFF_XQ:OFF_XQ + 1024], 8)
            KxT = sb("KxT", [128, 4, 256], BF16); Kb = Buf()
            Vx = sb("Vx", [128, 2, 512], BF16); Vb = Buf()
            qx = sb("qx", [128, 512], BF16); qb = Buf()
            PT = [sb(f"PT{i}", [128, 512], BF16) for i in range(2)]; PTb = [Buf(), Buf()]
            sgt = sb("sgt", [128, 512]); sgb = Buf()
            rl = sb("rl", [128, 512]); rlb = Buf()
            yb = [sb(f"yb{i}", [128, 512], BF16) for i in range(2)]; ybb = [Buf(), Buf()]
            for h in range(4):
                ps, pb = Sx.next_psum(4)
                for kc in range(8):
                    Sx.op("pe", lambda e: e.matmul(ps[:, 0:256], wkv[:, kc, h * 128:(h + 1) * 128], self.memT[:, kc, :], start=(kc == 0), stop=(kc == 7)),
                          reads=[wkvb, self.memT_b], writes=[pb])
                Sx.op("act", lambda e: e.activation(out=KxT[:, h, :], in_=ps[:, 0:256], func=AF.Copy), reads=[pb], writes=[Kb])
            for mt in range(2):
                ps, pb = Sx.next_psum(4)
                for kc in range(8):
                    Sx.op("pe", lambda e: e.matmul(ps[:, :], self.memT[:, kc, mt * 128:(mt + 1) * 128], wkv[:, kc, 512:1024], start=(kc == 0), stop=(kc == 7)),
                          reads=[wkvb, self.memT_b], writes=[pb])
                Sx.op("act", lambda e: e.activation(out=Vx[:, mt, :], in_=ps[:, :], func=AF.Copy), reads=[pb], writes=[Vb])
            ydst = self.ybr[3]
            k = 0
            for g in range(4):
                t0 = g * 512
                for h in range(4):
                    ps, pb = Sx.next_psum(4)
                    self.inproj(ps, pb, wX, wXb, h * 128, 128, t0, 512)
                    Sx.op("act", lambda e: e.activation(out=qx[:, :], in_=ps[:, :], func=AF.Copy), reads=[pb], writes=[qb])
                    o_ps, o_pb = Sx.psum[4]
                    l_ps, l_pb = Sx.psum[5]
                    for mt in range(2):
                        sps, spb = Sx.next_psum(4)
                        Sx.op("pe", lambda e: e.matmul(sps[:, :], KxT[:, h, mt * 128:(mt + 1) * 128], qx[:, :], start=True, stop=True), reads=[Kb, qb], writes=[spb])
                        pt = PT[mt]; ptb = PTb[mt]
                        Sx.op("act", lambda e: e.activation(out=pt[:, :], in_=sps[:, :], func=AF.Exp, scale=scale), reads=[spb], writes=[ptb])
                        Sx.op("pe", lambda e: e.matmul(o_ps[:, :], Vx[:, mt, h * 128:(h + 1) * 128], pt[:, :], start=(mt == 0), stop=(mt == 1)),
                              reads=[Vb, ptb], writes=[o_pb])
                        Sx.op("pe", lambda e: e.matmul(l_ps[:, :], self.ones_bf[:, :], pt[:, :], start=(mt == 0), stop=(mt == 1)),
                              reads=[cb, ptb], writes=[l_pb])
                    gps, gpb = Sx.next_psum(4)
                    self.inproj(gps, gpb, wX, wXb, 512 + h * 128, 128, t0, 512)
                    Sx.op("act", lambda e: e.activation(out=sgt[:, :], in_=gps[:, :], func=AF.Silu), reads=[gpb], writes=[sgb])
                    yt = yb[k % 2]; ytb = ybb[k % 2]; k += 1
                    Sx.op("dve", lambda e: e.reciprocal(rl[:, :], l_ps[:, :]), reads=[l_pb], writes=[rlb])
                    Sx.op("dve", lambda e: e.tensor_tensor(out=rl[:, :], in0=o_ps[:, :], in1=rl[:, :], op=ALU.mult), reads=[o_pb, rlb], writes=[rlb])
                    Sx.op("pool", lambda e: e.tensor_tensor(out=yt[:, :], in0=rl[:, :], in1=sgt[:, :], op=ALU.mult), reads=[rlb, sgb], writes=[ytb])
                    Sx.dma("sp", ydst[h * 128:(h + 1) * 128, t0:t0 + 512], yt[:, :], reads=[ytb], writes=[self.ybr_buf[3]])
            Sx.barrier()

    def phase_conv(self, s, l):
        Sx = self.S
        d = self.din
        cb = self.cb
        pcb = self.pcol_b
        ones = self.c["c_ones"]
        with contextlib.ExitStack() as es:
            sb = lambda n, shp, dt=F32: self.sb(es, "cv_" + n, shp, dt)
            wC = sb("wC", [128, 8, 1536], BF16); wCb = Buf()
            self.load_w_bf(wC, wCb, d["w_in"][l, :, OFF_CONV:OFF_CONV + 1536], 8)
            Dg = sb("Dg", [128, 4, 31, 128], BF16); Dgb = Buf()
            for c in range(4):
                for j in range(31):
                    Sx.op("pool" if (j % 2) else "dve",
                          lambda e: e.tensor_scalar(out=Dg[:, c, j, :], in0=self.ident[:, :], scalar1=self.pc("conv_w", j * 4 + c), scalar2=None, op0=ALU.mult),
                          reads=[cb, pcb], writes=[Dgb])
            hb = sb("hb", [128, 4, 30 + S], BF16); hbb = Buf()
            Sx.op("pool", lambda e: e.memset(hb[:, :, 0:30], 0.0), writes=[hbb])
            sig = sb("sig", [128, 512]); sigb = Buf()
            cv = sb("cvv", [128, 4, 512]); cvb = Buf()
            sq = sb("sq", [128, 4, 512]); sqb = Buf()
            mean = sb("mean", [128, 512]); mb = Buf()
            rstd = sb("rstd", [128, 512]); rb = Buf()
            t1 = sb("t1", [128, 512]); t1b = Buf()
            sgt = sb("sgt", [128, 512]); sgb = Buf()
            yb = [sb(f"yb{i}", [128, 512], BF16) for i in range(2)]; ybb = [Buf(), Buf()]
            ydst = self.ybr[2]
            for g in range(4):
                t0 = g * 512
                for c in range(4):
                    ps1, pb1 = Sx.next_psum()
                    self.inproj(ps1, pb1, wC, wCb, c * 128, 128, t0, 512)
                    ps2, pb2 = Sx.next_psum()
                    self.inproj(ps2, pb2, wC, wCb, 512 + c * 128, 128, t0, 512)
                    Sx.op("act", lambda e: e.activation(out=sig[:, :], in_=ps2[:, :], func=AF.Sigmoid), reads=[pb2], writes=[sigb])
                    Sx.op("dve", lambda e: e.tensor_tensor(out=hb[:, c, 30 + t0:30 + t0 + 512], in0=ps1[:, :], in1=sig[:, :], op=ALU.mult),
                          reads=[pb1, sigb], writes=[hbb])
                for c in range(4):
                    ps, pb = Sx.next_psum()
                    for j in range(31):
                        Sx.op("pe", lambda e: e.matmul(ps[:, :], Dg[:, c, j, :], hb[:, c, t0 + j:t0 + j + 512], start=(j == 0), stop=(j == 30)),
                              reads=[Dgb, hbb], writes=[pb])
                    Sx.op("act", lambda e: e.activation(out=cv[:, c, :], in_=ps[:, :], func=AF.Identity, bias=self.pc("conv_b", c)), reads=[pb, pcb], writes=[cvb])
                Sx.op("act", lambda e: e.activation(out=sq[:, :, :].rearrange("p c t -> p (c t)"), in_=cv[:, :, :].rearrange("p c t -> p (c t)"), func=AF.Square),
                      reads=[cvb], writes=[sqb])
                psM, pbM = Sx.next_psum()
                psQ, pbQ = Sx.next_psum()
                for c in range(4):
                    Sx.op("pe", lambda e: e.matmul(psM[:, :], ones[:, :], cv[:, c, :], start=(c == 0), stop=(c == 3)), reads=[cvb, cb], writes=[pbM])
                for c in range(4):
                    Sx.op("pe", lambda e: e.matmul(psQ[:, :], ones[:, :], sq[:, c, :], start=(c == 0), stop=(c == 3)), reads=[sqb, cb], writes=[pbQ])
                Sx.op("act", lambda e: e.activation(out=mean[:, :], in_=psM[:, :], func=AF.Copy, scale=1.0 / 512.0), reads=[pbM], writes=[mb])
                Sx.op("pool", lambda e: e.tensor_tensor(out=t1[:, :], in0=mean[:, :], in1=mean[:, :], op=ALU.mult), reads=[mb], writes=[t1b])
                Sx.op("dve", lambda e: e.scalar_tensor_tensor(out=rstd[:, :], in0=psQ[:, :], scalar=1.0 / 512.0, in1=t1[:, :], op0=ALU.mult, op1=ALU.subtract),
                      reads=[pbQ, t1b], writes=[rb])
                Sx.op("dve", lambda e: e.tensor_scalar(out=rstd[:, :], in0=rstd[:, :], scalar1=1e-5, scalar2=None, op0=ALU.add), reads=[rb], writes=[rb])
                Sx.op("act", lambda e: e.activation(out=rstd[:, :], in_=rstd[:, :], func=AF.Sqrt), reads=[rb], writes=[rb])
                Sx.op("dve", lambda e: e.reciprocal(rstd[:, :], rstd[:, :]), reads=[rb], writes=[rb])
                for c in range(4):
                    Sx.op("pool", lambda e: e.tensor_tensor(out=t1[:, :], in0=cv[:, c, :], in1=mean[:, :], op=ALU.subtract), reads=[cvb, mb, t1b], writes=[t1b])
                    Sx.op("dve", lambda e: e.tensor_tensor(out=t1[:, :], in0=t1[:, :], in1=rstd[:, :], op=ALU.mult), reads=[t1b, rb], writes=[t1b])
                    Sx.op("act", lambda e: e.activation(out=t1[:, :], in_=t1[:, :], func=AF.Silu, scale=self.pc("conv_ln_g", c), bias=self.pc("conv_ln_b", c)),
                          reads=[t1b, pcb], writes=[t1b])
                    gps, gpb = Sx.next_psum()
                    self.inproj(gps, gpb, wC, wCb, 1024 + c * 128, 128, t0, 512)
                    Sx.op("act", lambda e: e.activation(out=sgt[:, :], in_=gps[:, :], func=AF.Silu), reads=[gpb], writes=[sgb])
                    yt = yb[c % 2]; ytb = ybb[c % 2]
                    Sx.op("pool", lambda e: e.tensor_tensor(out=yt[:, :], in0=t1[:, :], in1=sgt[:, :], op=ALU.mult), reads=[t1b, sgb], writes=[ytb])
                    Sx.dma("sp", ydst[c * 128:(c + 1) * 128, t0:t0 + 512], yt[:, :], reads=[ytb], writes=[self.ybr_buf[2]])
            Sx.barrier()

    def phase_epi(self, s, l):
        Sx = self.S
        d = self.din
        cb = self.cb
        pcb = self.pcol_b
        with contextlib.ExitStack() as es0:
            mT = self.sb(es0, "ep_mT", [128, 8, S], BF16); mTb = Buf()
            with contextlib.ExitStack() as es:
                sb = lambda n, shp, dt=F32: self.sb(es, "e1_" + n, shp, dt)
                yin = sb("yin", [128, 4, 4, S], BF16); yinb = Buf()
                for n in range(4):
                    src = self.ybr[n].rearrange("(c p) t -> p c t", p=128)
                    for c in range(4):
                        Sx.dma("sp", yin[:, n, c, :], src[:, c, :], reads=[self.ybr_buf[n]], writes=[yinb])
                wG = [sb(f"wG{i}", [128, 8, 4, 128], BF16) for i in range(2)]; wGb = [Buf(), Buf()]
                Wo = [sb(f"Wo{i}", [128, 4, 4, 128], BF16) for i in range(2)]; Wob = [Buf(), Buf()]
                gt = sb("gt", [128, 512]); gtb = Buf()
                acc = sb("acc", [128, 512]); accb = Buf()
                tmp = sb("tmp", [128, 512]); tmpb = Buf()
                for db in range(8):
                    i = db % 2
                    for n in range(4):
                        c0 = OFF_MERGE + n * 1024 + db * 128
                        vg = d["w_in"][l, :, c0:c0 + 128].rearrange("(kc p) c -> p kc c", p=128)
                        Sx.dma("pool", wG[i][:, :, n, :], vg, writes=[wGb[i]])
                        vo = d["w_o_branch"][l, n, :, db * 128:(db + 1) * 128].rearrange("(cc p) c -> p cc c", p=128)
                        Sx.dma("pool", Wo[i][:, n, :, :], vo, writes=[Wob[i]])
                    for g in range(4):
                        t0 = g * 512
                        for n in range(4):
                            psP, pbP = Sx.next_psum()
                            for cc in range(4):
                                Sx.op("pe", lambda e: e.matmul(psP[:, :], Wo[i][:, n, cc, :], yin[:, n, cc, t0:t0 + 512], start=(cc == 0), stop=(cc == 3)),
                                      reads=[Wob[i], yinb], writes=[pbP])
                            psG, pbG = Sx.next_psum()
                            for kc in range(8):
                                Sx.op("pe", lambda e: e.matmul(psG[:, :], wG[i][:, kc, n, :], self.xT[:, kc, t0:t0 + 512], start=(kc == 0), stop=(kc == 7)),
                                      reads=[wGb[i], self.xT_b], writes=[pbG])
                            Sx.op("act", lambda e: e.activation(out=gt[:, :], in_=psG[:, :], func=AF.Sigmoid, bias=self.pc("b_gate", n * 8 + db)),
                                  reads=[pbG, pcb], writes=[gtb])
                            if n == 0:
                                Sx.op("dve", lambda e: e.tensor_tensor(out=acc[:, :], in0=psP[:, :], in1=gt[:, :], op=ALU.mult), reads=[pbP, gtb], writes=[accb])
                            else:
                                Sx.op("dve", lambda e: e.tensor_tensor(out=tmp[:, :], in0=psP[:, :], in1=gt[:, :], op=ALU.mult), reads=[pbP, gtb], writes=[tmpb])
                                if n < 3:
                                    Sx.op("pool", lambda e: e.tensor_tensor(out=acc[:, :], in0=acc[:, :], in1=tmp[:, :], op=ALU.add), reads=[accb, tmpb], writes=[accb])
                                else:
                                    Sx.op("pool", lambda e: e.tensor_tensor(out=mT[:, db, t0:t0 + 512], in0=acc[:, :], in1=tmp[:, :], op=ALU.add),
                                          reads=[accb, tmpb], writes=[mTb])
                Sx.barrier()
            with contextlib.ExitStack() as es:
                sb = lambda n, shp, dt=F32: self.sb(es, "e2_" + n, shp, dt)
                Wout = sb("Wout", [128, 8, D], BF16); Woutb = Buf()
                self.load_w_bf(Wout, Woutb, d["w_out"][l], 8)
                xt = [sb(f"xt{i}", [128, D]) for i in range(2)]; xtb = [Buf(), Buf()]
                z = [sb(f"z{i}", [128, D]) for i in range(2)]; zb = [Buf(), Buf()]
                st = sb("st", [128, 12]); stb = Buf()
                mv = sb("mv", [128, 2]); mvb = Buf()
                rs = sb("rs", [128, 1]); rsb = Buf()
                for tt in range(S // 128):
                    i = tt % 2
                    if l == 0:
                        Sx.dma("sp", xt[i][:], d["x"][s, tt * 128:(tt + 1) * 128, :], writes=[xtb[i]])
                    else:
                        Sx.dma("sp", xt[i][:], self.x1[s, tt * 128:(tt + 1) * 128, :], reads=[self.x1_buf[s]], writes=[xtb[i]])
                    for half in range(2):
                        ps, pb = Sx.next_psum()
                        for kc in range(8):
                            Sx.op("pe", lambda e: e.matmul(ps[:, :], mT[:, kc, tt * 128:(tt + 1) * 128], Wout[:, kc, half * 512:(half + 1) * 512],
                                                           start=(kc == 0), stop=(kc == 7)), reads=[mTb, Woutb], writes=[pb])
                        Sx.op("dve", lambda e: e.scalar_tensor_tensor(out=z[i][:, half * 512:(half + 1) * 512], in0=xt[i][:, half * 512:(half + 1) * 512],
                                                                      scalar=float(ALPHA), in1=ps[:, :], op0=ALU.mult, op1=ALU.add),
                              reads=[xtb[i], pb], writes=[zb[i]])
                        Sx.op("dve", lambda e: e.bn_stats(st[:, half * 6:(half + 1) * 6], z[i][:, half * 512:(half + 1) * 512]), reads=[zb[i]], writes=[stb])
                    Sx.op("dve", lambda e: e.bn_aggr(mv[:, :], st[:, :]), reads=[stb], writes=[mvb])
                    Sx.op("dve", lambda e: e.tensor_scalar(out=rs[:, :], in0=mv[:, 1:2], scalar1=1e-5, scalar2=None, op0=ALU.add), reads=[mvb], writes=[rsb])
                    Sx.op("act", lambda e: e.activation(out=rs[:, :], in_=rs[:, :], func=AF.Sqrt), reads=[rsb], writes=[rsb])
                    Sx.op("dve", lambda e: e.reciprocal(rs[:, :], rs[:, :]), reads=[rsb], writes=[rsb])
                    Sx.op("dve", lambda e: e.tensor_scalar(out=z[i][:, :], in0=z[i][:, :], scalar1=mv[:, 0:1], scalar2=rs[:, 0:1], op0=ALU.subtract, op1=ALU.mult),
                          reads=[zb[i], mvb, rsb], writes=[zb[i]])
                    Sx.op("pool", lambda e: e.tensor_tensor(out=z[i][:, :], in0=z[i][:, :], in1=self.lng_bc[:, :], op=ALU.mult), reads=[zb[i], self.ln_b_], writes=[zb[i]])
                    Sx.op("pool", lambda e: e.tensor_tensor(out=z[i][:, :], in0=z[i][:, :], in1=self.lnb_bc[:, :], op=ALU.add), reads=[zb[i], self.ln_b_], writes=[zb[i]])
                    if l == DEPTH - 1 or self.single:
                        Sx.dma("sp", self.out[s, tt * 128:(tt + 1) * 128, :], z[i][:, :], reads=[zb[i]], writes=[self.out_b])
                    else:
                        Sx.dma("sp", self.x1[s, tt * 128:(tt + 1) * 128, :], z[i][:, :], reads=[zb[i]], writes=[self.x1_buf[s]])
                        self.transpose_into_xT(z[i], zb[i], tt)
                Sx.barrier()


def _shard_inputs(inputs):
    consts = _consts_host()
    maps = []
    for c in range(NCORES):
        m = {}
        sl = slice(c * SEQ_PER_CORE, (c + 1) * SEQ_PER_CORE)
        for n, shp, dt in PARAM_SPECS:
            a = np.asarray(inputs[n])
            if n in ("x", "mem", "positions"):
                a = a[sl]
            m[n] = np.ascontiguousarray(a)
        m.update(consts)
        maps.append(m)
    return maps


_PROG = {}


FUSED = True


def kernel(**inputs):
    if FUSED:
        if "f" not in _PROG:
            _PROG["f"] = K({}).build()
        res = run_bass_kernel_spmd(_PROG["f"], _shard_inputs(inputs), core_ids=list(range(NCORES)))
        return np.concatenate([np.asarray(r["out"], dtype=np.float32) for r in res.results], axis=0)
    return kernel_unfused(**inputs)


def kernel_unfused(**inputs):
    if "p" not in _PROG:
        kb = K({"nseq": 1, "nlay": 1, "single": True, "seqs": [0], "layers": [0]})
        _PROG["p"] = kb.build()
    nc = _PROG["p"]
    consts = _consts_host()
    xs = np.asarray(inputs["x"], dtype=np.float32)
    names = [n for n, _, _ in PARAM_SPECS if n not in ("x", "mem", "positions")]
    out = np.empty_like(xs)
    for slot in range(SEQ_PER_CORE):
        cur = [np.ascontiguousarray(xs[c * SEQ_PER_CORE + slot][None]) for c in range(NCORES)]
        for l in range(DEPTH):
            wl = {n: np.ascontiguousarray(np.asarray(inputs[n])[l:l + 1]) for n in names}
            maps = []
            for c in range(NCORES):
                b = c * SEQ_PER_CORE + slot
                m = dict(wl)
                m["x"] = cur[c]
                m["mem"] = np.ascontiguousarray(np.asarray(inputs["mem"])[b:b + 1])
                m["positions"] = np.ascontiguousarray(np.asarray(inputs["positions"])[b:b + 1])
                m.update(consts)
                maps.append(m)
            res = run_bass_kernel_spmd(nc, maps, core_ids=list(range(NCORES)))
            cur = [np.ascontiguousarray(np.asarray(r["out"], dtype=np.float32)) for r in res.results]
        for c in range(NCORES):
            out[c * SEQ_PER_CORE + slot] = cur[c][0]
    return out
```

```python
import contextlib
import numpy as np
import concourse.bass as bass
import concourse.mybir as mybir
from concourse.bass_utils import run_bass_kernel_spmd

F32 = mybir.dt.float32
BF16 = mybir.dt.bfloat16
I32 = mybir.dt.int32
AF = mybir.ActivationFunctionType
ALU = mybir.AluOpType

NCORES = 8
SEQ_PER_CORE = 4
S = 2048
D = 1024
DEPTH = 2
IN_COLS = 10016
W = 512
ALPHA = (2.0 * DEPTH) ** 0.25
SEM_LIMIT = 30000
VCLOCK = False
NDQ = 12

OFF_RW = 0
OFF_QLAT = 2176
OFF_KVLAT = 2560
OFF_KPE = 2816
OFF_MGATE = 2848
OFF_CONV = 3360
OFF_XQ = 4896
OFF_MERGE = 5920

PC = {}
_o = 0
for _n, _c in [("mu", 13), ("w0", 4), ("a0", 4), ("k_k", 4), ("k_a", 4), ("r_k", 4), ("lnx_g", 4),
               ("lnx_b", 4), ("q_norm", 3), ("kv_norm", 2), ("conv_b", 4), ("conv_ln_g", 4),
               ("conv_ln_b", 4), ("b_gate", 32), ("conv_w", 124)]:
    PC[_n] = _o
    _o += _c
NPC_RAW = _o
PC["omm"] = NPC_RAW
PC["omka"] = NPC_RAW + 13
NPC = NPC_RAW + 17


class Buf:
    __slots__ = ("w", "r")

    def __init__(self):
        self.w = []
        self.r = {}


class Sched:
    def __init__(self, nc, es):
        self.nc = nc
        self.es = es
        self.eng = {"pe": nc.tensor, "act": nc.scalar, "dve": nc.vector, "pool": nc.gpsimd, "sp": nc.sync}
        self.sem = {}
        self.cnt = {}
        self.sid = {}
        self.nsem = 0
        self.waited = {k: {} for k in self.eng}
        self.latest = {}
        self.ninst = 0
        for k in self.eng:
            self._newsem(k)
        self.dq = {}
        self.dqi = {}
        for q in ("sp", "pool", "act"):
            lst = []
            for i in range(NDQ):
                s = es.enter_context(nc.semaphore(f"dq_{q}_{i}"))
                self.nsem += 1
                lst.append([s, 0, self.nsem])
            self.dq[q] = lst
            self.dqi[q] = 0
        self.psum = []
        self.psi = 0

    def _newsem(self, k):
        s = self.es.enter_context(self.nc.semaphore(f"s_{k}_{self.nsem}"))
        self.nsem += 1
        self.sem[k] = s
        self.cnt[k] = 0
        self.sid[k] = self.nsem

    def _wait(self, k, tok):
        sem, val, src, sid = tok[0], tok[1], tok[2], tok[3]
        if k == "pe" and src == "pe":
            return
        w = self.waited[k]
        if w.get(sid, 0) >= val:
            return
        self.eng[k].wait_ge(sem, val)
        self.ninst += 1
        w[sid] = val
        snap = tok[4] if (VCLOCK and len(tok) > 4) else None
        if snap:
            for a, b in snap.items():
                if w.get(a, 0) < b:
                    w[a] = b

    def _deps(self, reads, writes):
        toks = []
        for b in reads:
            toks.extend(b.w)
        for b in writes:
            toks.extend(b.w)
            toks.extend(b.r.values())
        return toks

    def _commit(self, tok, reads, writes):
        for b in reads:
            b.r[tok[3]] = tok
        for b in writes:
            b.w = [tok]
            b.r = {}
        self.latest[tok[3]] = tok

    def dma_group(self, q, pairs, reads=(), writes=()):
        deps = self._deps(reads, writes)
        toks = []
        lim = 1 if q == "pool" else 4
        for (out, in_) in pairs:
            for t in deps:
                self._wait(q, t)
            if len(toks) >= lim:
                self._wait(q, toks[len(toks) - lim])
            i = self.dqi[q]
            self.dqi[q] = (i + 1) % NDQ
            ent = self.dq[q][i]
            if ent[1] > 0:
                self._wait(q, (ent[0], 16 * ent[1], "dma", ent[2]))
            self.eng[q].dma_start(out=out, in_=in_).then_inc(ent[0], 16)
            self.ninst += 1
            ent[1] += 1
            tok = (ent[0], 16 * ent[1], "dma", ent[2], dict(self.waited[q]))
            toks.append(tok)
            self.latest[tok[3]] = tok
        for b in reads:
            for tok in toks:
                b.r[tok[3]] = tok
        for b in writes:
            b.w = list(toks)
            b.r = {}

    def op(self, k, fn, reads=(), writes=()):
        for t in self._deps(reads, writes):
            self._wait(k, t)
        if self.cnt[k] >= SEM_LIMIT:
            self._newsem(k)
        inst = fn(self.eng[k])
        self.cnt[k] += 1
        self.ninst += 1
        inst.then_inc(self.sem[k], 1)
        snap = dict(self.waited[k])
        if k != "pe":
            snap[self.sid[k]] = self.cnt[k] - 1
        tok = (self.sem[k], self.cnt[k], k, self.sid[k], snap)
        self._commit(tok, reads, writes)
        return tok

    def dma(self, q, out, in_, reads=(), writes=()):
        for t in self._deps(reads, writes):
            self._wait(q, t)
        i = self.dqi[q]
        self.dqi[q] = (i + 1) % NDQ
        ent = self.dq[q][i]
        if ent[1] > 0:
            self._wait(q, (ent[0], 16 * ent[1], "dma", ent[2]))
        self.eng[q].dma_start(out=out, in_=in_).then_inc(ent[0], 16)
        self.ninst += 1
        ent[1] += 1
        tok = (ent[0], 16 * ent[1], "dma", ent[2], dict(self.waited[q]))
        self._commit(tok, reads, writes)
        return tok

    def barrier(self, engines=("pe", "act", "dve", "pool", "sp")):
        toks = list(self.latest.values())
        for k in engines:
            for t in toks:
                self._wait(k, t)

    def next_psum(self, n=8):
        self.psi = (self.psi + 1) % n
        return self.psum[self.psi]


def _consts_host():
    c = {}
    c["c_ident"] = np.eye(128, dtype=np.float32)
    bo = np.zeros((128, 128), np.float32)
    bo[:64, :64] = 1.0
    bo[64:, 64:] = 1.0
    c["c_bo"] = bo
    i = np.arange(64)
    strict = (i[:, None] < i[None, :]).astype(np.float32)
    incl = (i[:, None] <= i[None, :]).astype(np.float32)
    lower = (i[None, :] < i[:, None]).astype(np.float32)
    mA = np.zeros((128, 3, 2, 64), np.float32)
    for h in range(2):
        mA[h * 64:(h + 1) * 64, 0, h, :] = strict
        mA[h * 64:(h + 1) * 64, 1, h, :] = strict
        mA[h * 64:(h + 1) * 64, 2, h, :] = lower
    c["c_maskA"] = mA.reshape(128, 384)
    mB = np.zeros((128, 2, 64), np.float32)
    for h in range(2):
        mB[h * 64:(h + 1) * 64, :, :] = incl[:, None, :]
    c["c_maskB"] = mB.reshape(128, 128)
    mbd = np.zeros((128, 2), np.float32)
    mbd[:64, 0] = 1.0
    mbd[64:, 1] = 1.0
    c["c_mbd"] = mbd
    cm = np.ones((128, 128), np.float32)
    cm[:, 0] = 0.0
    cm[:, 64] = 0.0
    c["c_cmask"] = cm
    k = np.arange(128)[:, None]
    q = np.arange(512)[None, :]
    mm = np.stack([(q >= v * 128 + k) for v in range(4)], axis=1).astype(np.float32)
    c["c_cmla"] = mm.reshape(128, 2048)
    inv = (10000.0 ** (-np.arange(0, 32, 2, dtype=np.float32) / 32.0)).astype(np.float32)
    rp = np.zeros((128, 2), np.float32)
    rp[64:96, 0] = np.concatenate([inv, inv])
    rp[64:96, 1] = np.concatenate([-np.ones(16, np.float32), np.ones(16, np.float32)])
    c["c_rope"] = rp
    c["c_ones"] = np.ones((128, 128), np.float32)
    return c


CONST_SHAPES = {k: v.shape for k, v in _consts_host().items()}

def param_specs(SEQ_PER_CORE, DEPTH):
  return [
    ("x", [SEQ_PER_CORE, S, D], F32), ("mem", [SEQ_PER_CORE, 256, D], F32), ("positions", [SEQ_PER_CORE, S], I32),
    ("w_in", [DEPTH, D, IN_COLS], F32), ("b_gate", [DEPTH, 4, D], F32), ("rwkv_mu", [DEPTH, 1664], F32),
    ("rwkv_w0", [DEPTH, W], F32), ("rwkv_w2", [DEPTH, 64, W], F32), ("rwkv_a0", [DEPTH, W], F32),
    ("rwkv_a2", [DEPTH, 64, W], F32), ("rwkv_k_k", [DEPTH, W], F32), ("rwkv_k_a", [DEPTH, W], F32),
    ("rwkv_r_k", [DEPTH, 8, 64], F32), ("rwkv_lnx_g", [DEPTH, W], F32), ("rwkv_lnx_b", [DEPTH, W], F32),
    ("mla_q_norm", [DEPTH, 384], F32), ("mla_w_uq", [DEPTH, 384, 768], F32), ("mla_kv_norm", [DEPTH, 256], F32),
    ("mla_w_ukv", [DEPTH, 256, 1024], F32), ("conv_w", [DEPTH, 31, W], F32), ("conv_b", [DEPTH, W], F32),
    ("conv_ln_g", [DEPTH, W], F32), ("conv_ln_b", [DEPTH, W], F32), ("xattn_w_mem_kv", [DEPTH, D, 2 * W], F32),
    ("w_o_branch", [DEPTH, 4, W, D], F32), ("w_out", [DEPTH, D, D], F32), ("ln_g", [DEPTH, D], F32),
    ("ln_b", [DEPTH, D], F32),
  ]


PARAM_SPECS = param_specs(SEQ_PER_CORE, DEPTH)


class K:
    def __init__(self, cfg):
        self.cfg = cfg
        self.nc = bass.Bass("TRN2", target_bir_lowering=False)
        nc = self.nc
        self.din = {}
        nseq = cfg.get("nseq", SEQ_PER_CORE)
        nlay = cfg.get("nlay", DEPTH)
        self.single = cfg.get("single", False)
        for n, shp, dt in param_specs(nseq, nlay):
            self.din[n] = nc.dram_tensor(n, shp, dt, kind="ExternalInput").ap()
        for n, shp in CONST_SHAPES.items():
            self.din[n] = nc.dram_tensor(n, list(shp), F32, kind="ExternalInput").ap()
        self.out = nc.dram_tensor("out", [nseq, S, D], F32, kind="ExternalOutput").ap()
        dbg = cfg.get("debug", False)
        kind = "ExternalOutput" if dbg else "Internal"
        self.ybr = nc.dram_tensor("ybr", [4, W, S], BF16, kind=kind).ap()
        self.ybr_buf = [Buf() for _ in range(4)]
        self.x1 = nc.dram_tensor("x1s", [nseq, S, D], F32, kind=kind).ap()
        self.x1_buf = [Buf() for _ in range(SEQ_PER_CORE)]

    def sb(self, es, name, shape, dt):
        self._uid = getattr(self, "_uid", 0) + 1
        return es.enter_context(self.nc.sbuf_tensor(f"{name}_{self._uid}", shape, dt))

    def build(self):
        nc = self.nc
        with contextlib.ExitStack() as es:
            self.S = Sched(nc, es)
            Sx = self.S
            for i in range(8):
                t = es.enter_context(nc.psum_tensor(f"ps{i}", [128, 512], F32))
                Sx.psum.append((t, Buf()))
            self.setup_consts(es)
            self.xT = self.sb(es, "xT", [128, 8, S], BF16)
            self.xT_b = Buf()
            self.ropeC = self.sb(es, "ropeC", [128, S], F32)
            self.ropeS = self.sb(es, "ropeS", [128, S], F32)
            self.rope_b = Buf()
            self.memT = self.sb(es, "memT", [128, 8, 256], BF16)
            self.memT_b = Buf()
            self.out_b = Buf()
            phases = self.cfg.get("phases", "RMCXE")
            seqs = self.cfg.get("seqs", list(range(SEQ_PER_CORE)))
            layers = self.cfg.get("layers", list(range(DEPTH)))
            for s in seqs:
                self.load_xT(s)
                if "M" in phases:
                    self.rope_tables(s)
                if "X" in phases:
                    self.load_memT(s)
                for l in layers:
                    if l == 0 or True:
                        self.load_params(l)
                    if "R" in phases:
                        self.phase_rwkv(s, l)
                    if "M" in phases:
                        self.phase_mla(s, l)
                    if "C" in phases:
                        self.phase_conv(s, l)
                    if "X" in phases:
                        self.phase_xattn(s, l)
                    if "E" in phases:
                        self.phase_epi(s, l)
            Sx.barrier(engines=("sp",))
        return nc

    def setup_consts(self, es):
        Sx = self.S
        self.cb = Buf()
        c = {}
        for n, shp in CONST_SHAPES.items():
            if n == "c_cmla":
                continue
            t = self.sb(es, "k_" + n, list(shp), F32)
            Sx.dma("sp", t[:], self.din[n], writes=[self.cb])
            c[n] = t
        self.c = c
        self.ident = c["c_ident"]
        self.bo = c["c_bo"]
        self.ident_bf = self.sb(es, "ident_bf", [128, 128], BF16)
        self.ones_bf = self.sb(es, "ones_bf", [128, 128], BF16)
        self.cmla_bf = self.sb(es, "cmla_bf", [128, 2048], BF16)
        Sx.op("pool", lambda e: e.tensor_copy(self.ident_bf[:], self.ident[:]), reads=[self.cb], writes=[self.cb])
        Sx.op("pool", lambda e: e.memset(self.ones_bf[:], 1.0), writes=[self.cb])
        with contextlib.ExitStack() as es2:
            tmpc = self.sb(es2, "k_cmla_tmp", [128, 2048], F32)
            Sx.dma("sp", tmpc[:], self.din["c_cmla"], writes=[self.cb])
            Sx.op("pool", lambda e: e.tensor_copy(self.cmla_bf[:], tmpc[:]), reads=[self.cb], writes=[self.cb])
            Sx.barrier()
        self.pcol = self.sb(es, "pcol", [128, NPC], F32)
        self.pcol_b = Buf()
        self.stageA = self.sb(es, "stageA", [128, 128], F32)
        self.stageB = self.sb(es, "stageB", [128, 128], F32)
        self.stage_b = Buf()
        self.lng_bc = self.sb(es, "lng_bc", [128, D], F32)
        self.lnb_bc = self.sb(es, "lnb_bc", [128, D], F32)
        self.ln_b_ = Buf()
        self.ones_row = self.sb(es, "ones_row", [1, 128], F32)
        Sx.op("pool", lambda e: e.memset(self.ones_row[:], 1.0), writes=[self.cb])

    def pc(self, name, j=0):
        i = PC[name] + j
        return self.pcol[:, i:i + 1]

    def load_params(self, l):
        Sx = self.S
        d = self.din
        rows = []

        def vec(name, key):
            ap = d[key][l]
            n = 1
            for s_ in ap.shape:
                n *= s_
            rows.append((PC[name], n // 128, ap))

        vec("mu", "rwkv_mu"); vec("w0", "rwkv_w0"); vec("a0", "rwkv_a0"); vec("k_k", "rwkv_k_k")
        vec("k_a", "rwkv_k_a"); vec("r_k", "rwkv_r_k"); vec("lnx_g", "rwkv_lnx_g"); vec("lnx_b", "rwkv_lnx_b")
        vec("q_norm", "mla_q_norm"); vec("kv_norm", "mla_kv_norm"); vec("conv_b", "conv_b")
        vec("conv_ln_g", "conv_ln_g"); vec("conv_ln_b", "conv_ln_b"); vec("b_gate", "b_gate"); vec("conv_w", "conv_w")
        for (c0, nr, ap) in rows:
            if len(ap.shape) == 2:
                if ap.shape[1] == 64:
                    flat = ap.rearrange("h n -> (h n)")
                    src = flat.rearrange("(r p) -> r p", p=128)
                elif ap.shape[0] == 31:
                    src = ap.rearrange("j (c p) -> (j c) p", p=128)
                else:
                    src = ap.rearrange("n (c p) -> (n c) p", p=128)
            else:
                src = ap.rearrange("(r p) -> r p", p=128)
            r = 0
            while r < nr:
                g = c0 + r
                if g < 128:
                    n = min(nr - r, 128 - g)
                    Sx.dma("sp", self.stageA[g:g + n, :], src[r:r + n, :], writes=[self.stage_b])
                else:
                    n = nr - r
                    Sx.dma("sp", self.stageB[g - 128:g - 128 + n, :], src[r:r + n, :], writes=[self.stage_b])
                r += n
        nb = NPC_RAW - 128
        ps, pb = Sx.next_psum()
        Sx.op("pe", lambda e: e.matmul(ps[:, 0:128], self.stageA[:, :], self.ident[:, :], start=True, stop=True),
              reads=[self.stage_b, self.cb], writes=[pb])
        Sx.op("pe", lambda e: e.matmul(ps[:, 128:128 + nb], self.stageB[0:nb, :], self.ident[0:nb, 0:nb], start=True, stop=True),
              reads=[self.stage_b, self.cb], writes=[pb])
        Sx.op("act", lambda e: e.activation(out=self.pcol[:, 0:NPC_RAW], in_=ps[:, 0:NPC_RAW], func=AF.Copy),
              reads=[pb], writes=[self.pcol_b])
        o = PC["omm"]
        Sx.op("dve", lambda e: e.tensor_scalar(out=self.pcol[:, o:o + 13], in0=self.pcol[:, 0:13], scalar1=-1.0, scalar2=1.0,
                                               op0=ALU.mult, op1=ALU.add), reads=[self.pcol_b], writes=[self.pcol_b])
        o2 = PC["omka"]
        ka = PC["k_a"]
        Sx.op("dve", lambda e: e.tensor_scalar(out=self.pcol[:, o2:o2 + 4], in0=self.pcol[:, ka:ka + 4], scalar1=-1.0, scalar2=1.0,
                                               op0=ALU.mult, op1=ALU.add), reads=[self.pcol_b], writes=[self.pcol_b])
        es3 = contextlib.ExitStack()
        self.lnrow = self.sb(es3, "lnrow", [1, 2 * D], F32)
        Sx.dma("sp", self.lnrow[0:1, 0:D], d["ln_g"][l:l + 1, :], writes=[self.ln_b_])
        Sx.dma("sp", self.lnrow[0:1, D:2 * D], d["ln_b"][l:l + 1, :], writes=[self.ln_b_])
        for j, dst in enumerate((self.lng_bc, self.lnb_bc)):
            for hh in range(2):
                ps, pb = Sx.next_psum()
                Sx.op("pe", lambda e, ps=ps, j=j, hh=hh: e.matmul(ps[:, :], self.ones_row[0:1, :],
                                                                  self.lnrow[0:1, j * D + hh * 512:j * D + hh * 512 + 512],
                                                                  start=True, stop=True),
                      reads=[self.ln_b_, self.cb], writes=[pb])
                Sx.op("act", lambda e, ps=ps, dst=dst, hh=hh: e.activation(out=dst[:, hh * 512:(hh + 1) * 512], in_=ps[:, :], func=AF.Copy),
                      reads=[pb], writes=[self.ln_b_])
        Sx.barrier()
        es3.close()

    def load_xT(self, s):
        Sx = self.S
        with contextlib.ExitStack() as es:
            xt = [self.sb(es, f"xtok{i}", [128, D], F32) for i in range(2)]
            xb = [Buf(), Buf()]
            for tt in range(S // 128):
                i = tt % 2
                Sx.dma("sp", xt[i][:], self.din["x"][s, tt * 128:(tt + 1) * 128, :], writes=[xb[i]])
                self.transpose_into_xT(xt[i], xb[i], tt)
            Sx.barrier()

    def transpose_into_xT(self, xtok, xbuf, tt):
        Sx = self.S
        for half in range(2):
            ps, pb = Sx.next_psum()
            for j in range(4):
                kc = half * 4 + j
                Sx.op("pe", lambda e, ps=ps, j=j, kc=kc: e.matmul(ps[:, j * 128:(j + 1) * 128], xtok[:, kc * 128:(kc + 1) * 128],
                                                                  self.ident[:, :], start=True, stop=True),
                      reads=[xbuf, self.cb], writes=[pb])
            out = self.xT[:, half * 4:(half + 1) * 4, tt * 128:(tt + 1) * 128]
            Sx.op("act" if half == 0 else "dve",
                  (lambda e, ps=ps, out=out: e.activation(out=out, in_=ps[:, :].rearrange("p (j t) -> p j t", j=4), func=AF.Copy)) if half == 0 else
                  (lambda e, ps=ps, out=out: e.tensor_copy(out, ps[:, :].rearrange("p (j t) -> p j t", j=4))),
                  reads=[pb], writes=[self.xT_b])

    def load_w_bf(self, dst, dst_buf, src, nkc):
        Sx = self.S
        v = src.rearrange("(kc p) c -> p kc c", p=128)
        Sx.dma_group("pool", [(dst[:, kc, :], v[:, kc, :]) for kc in range(nkc)], writes=[dst_buf])

    def phase_rwkv(self, s, l):
        Sx = self.S
        nc = self.nc
        d = self.din
        T = 128
        with contextlib.ExitStack() as es:
            sb = lambda n, shp, dt=F32: self.sb(es, "rw_" + n, shp, dt)
            wR = sb("wR", [128, 8, 2176], BF16); wRb = Buf()
            self.load_w_bf(wR, wRb, d["w_in"][l, :, OFF_RW:OFF_RW + 2176], 8)
            W2z = sb("W2z", [128, 512]); A2z = sb("A2z", [128, 512]); lb = Buf()
            Sx.op("pool", lambda e: e.memset(W2z[:], 0.0), writes=[lb])
            Sx.op("pool", lambda e: e.memset(A2z[:], 0.0), writes=[lb])
            Sx.dma("sp", W2z[0:64, :], d["rwkv_w2"][l], writes=[lb])
            Sx.dma("sp", A2z[64:128, :], d["rwkv_a2"][l], writes=[lb])
            p_raw = sb("p_raw", [128, 13, T + 1]); prb = Buf()
            Sx.op("pool", lambda e: e.memset(p_raw[:], 0.0), writes=[prb])
            pm = sb("pm", [128, 13, T]); pmc = [Buf() for _ in range(13)]
            pm_r = pmc[0:4]; pm_k = pmc[4:8]; pm_v = pmc[8:12]
            sgate = sb("sgate", [128, 4, T]); sgb = Buf()
            T12 = sb("T12", [128, T]); t12b = Buf()
            names = ["lw", "logP", "asig", "eP", "eN", "ePm", "kk", "kkn", "kp", "rT", "t1", "t2", "t3"]
            tt_ = {n: sb(n, [128, 4, T]) for n in names}
            tb = {n: Buf() for n in names}
            Z = {n: sb("Z" + n, [128, 4, 2, 128]) for n in "abkv"}
            Zb_ = {n: Buf() for n in "abkv"}
            H = [sb(f"H{p}", [128, 128]) for p in range(4)]
            Hb = [Buf() for _ in range(4)]
            for p in range(4):
                Sx.op("pool", lambda e, p=p: e.memset(H[p][:], 0.0), writes=[Hb[p]])
            NSET = self.cfg.get("nset", 4)
            A_sb = [sb(f"A{i}", [128, 384]) for i in range(NSET)]; Ab = [Buf() for _ in range(NSET)]
            R_sb = [sb(f"R{i}", [128, 128]) for i in range(NSET)]; Rb = [Buf() for _ in range(NSET)]
            BKV = [sb(f"BKV{i}", [128, 384]) for i in range(NSET)]; BKVb = [Buf() for _ in range(NSET)]
            Wt = [[sb(f"W{i}_{j}", [128, 128]) for j in range(2)] for i in range(NSET)]
            Wtb = [[Buf() for j in range(2)] for i in range(NSET)]
            MP = [[sb(f"MP{i}_{j}", [128, 256]) for j in range(2)] for i in range(NSET)]
            MPb = [[Buf() for j in range(2)] for i in range(NSET)]
            X_sb = [sb(f"X{i}", [128, 128]) for i in range(NSET)]; Xb = [Buf() for _ in range(NSET)]
            U_sb = [sb(f"U{i}", [128, 128]) for i in range(NSET)]; Ub = [Buf() for _ in range(NSET)]
            HpC = [sb(f"HpC{i}", [128, 128]) for i in range(NSET)]; HpCb = [Buf() for _ in range(NSET)]
            Ycm = sb("Ycm", [128, 4, T]); Ybp = [Buf() for _ in range(4)]
            ybf = sb("ybf", [128, 4, T], BF16); ybb = Buf()
            cb = self.cb
            ident = self.ident
            maskA = self.c["c_maskA"]; maskB = self.c["c_maskB"]; mbd = self.c["c_mbd"]; cmask = self.c["c_cmask"]
            pcb = self.pcol_b
            ydst = self.ybr[0].rearrange("(c p) t -> p c t", p=128)

            def flat(t):
                return t[:, :, :].rearrange("p c t -> p (c t)")

            for blk in range(self.cfg.get('nblk', S // T)):
                t0 = blk * T
                for cbk in range(17):
                    ps, pb = Sx.next_psum()
                    for kc in range(8):
                        Sx.op("pe", lambda e, ps=ps, kc=kc, cbk=cbk: e.matmul(
                            ps[:, 0:T], wR[:, kc, cbk * 128:(cbk + 1) * 128], self.xT[:, kc, t0:t0 + T],
                            start=(kc == 0), stop=(kc == 7)), reads=[wRb, self.xT_b], writes=[pb])
                    if cbk < 13:
                        Sx.op("act", lambda e, ps=ps, cbk=cbk: e.activation(out=p_raw[:, cbk, 1:T + 1], in_=ps[:, 0:T], func=AF.Copy),
                              reads=[pb], writes=[prb])
                    else:
                        Sx.op("act", lambda e, ps=ps, cbk=cbk: e.activation(out=sgate[:, cbk - 13, :], in_=ps[:, 0:T], func=AF.Silu),
                              reads=[pb], writes=[sgb])
                for cbk in range(13):
                    Sx.op("pool", lambda e, cbk=cbk: e.tensor_scalar(out=pm[:, cbk, :], in0=p_raw[:, cbk, 1:T + 1],
                                                                     scalar1=self.pc("omm", cbk), scalar2=None, op0=ALU.mult),
                          reads=[prb, pcb], writes=[pmc[cbk]])
                    Sx.op("dve", lambda e, cbk=cbk: e.scalar_tensor_tensor(out=pm[:, cbk, :], in0=p_raw[:, cbk, 0:T],
                                                                           scalar=self.pc("mu", cbk), in1=pm[:, cbk, :],
                                                                           op0=ALU.mult, op1=ALU.add),
                          reads=[prb, pcb], writes=[pmc[cbk]])
                Sx.op("pool", lambda e: e.tensor_copy(p_raw[:, :, 0:1], p_raw[:, :, T:T + 1]), reads=[prb], writes=[prb])
                r_ = pm[:, 0:4, :]; k_ = pm[:, 4:8, :]; v_ = pm[:, 8:12, :]
                if self.cfg.get("stop_after", 9) < 1:
                    continue
                Sx.op("act", lambda e: e.activation(out=T12[0:64, :], in_=pm[0:64, 12, :], func=AF.Tanh), reads=[pmc[12]], writes=[t12b])
                Sx.op("dve", lambda e: e.tensor_copy(T12[64:128, :], pm[64:128, 12, :]), reads=[pmc[12]], writes=[t12b])
                for c4 in range(4):
                    ps, pb = Sx.next_psum()
                    Sx.op("pe", lambda e, ps=ps, c4=c4: e.matmul(ps[:, 0:T], W2z[:, c4 * 128:(c4 + 1) * 128], T12[:, :], start=True, stop=True),
                          reads=[lb, t12b], writes=[pb])
                    Sx.op("pe", lambda e, ps=ps, c4=c4: e.matmul(ps[:, T:2 * T], A2z[:, c4 * 128:(c4 + 1) * 128], T12[:, :], start=True, stop=True),
                          reads=[lb, t12b], writes=[pb])
                    Sx.op("act", lambda e, ps=ps, c4=c4: e.activation(out=tt_["lw"][:, c4, :], in_=ps[:, 0:T], func=AF.Sigmoid,
                                                                      bias=self.pc("w0", c4)), reads=[pb, pcb], writes=[tb["lw"]])
                    Sx.op("act", lambda e, ps=ps, c4=c4: e.activation(out=tt_["asig"][:, c4, :], in_=ps[:, T:2 * T], func=AF.Sigmoid,
                                                                      bias=self.pc("a0", c4)), reads=[pb, pcb], writes=[tb["asig"]])
                Sx.op("pool", lambda e: e.tensor_scalar(out=flat(tt_["lw"]), in0=flat(tt_["lw"]), scalar1=-0.6065306597126334,
                                                        scalar2=None, op0=ALU.mult), reads=[tb["lw"]], writes=[tb["lw"]])
                for c4 in range(4):
                    Sx.op("dve", lambda e, c4=c4: e.tensor_tensor_scan(out=tt_["logP"][:, c4, :], data0=cmask[:, 0:T], data1=tt_["lw"][:, c4, :],
                                                                      initial=0.0, op0=ALU.mult, op1=ALU.add),
                          reads=[tb["lw"], cb], writes=[tb["logP"]])
                Sx.op("act", lambda e: e.activation(out=flat(tt_["eP"]), in_=flat(tt_["logP"]), func=AF.Exp), reads=[tb["logP"]], writes=[tb["eP"]])
                Sx.op("act", lambda e: e.activation(out=flat(tt_["eN"]), in_=flat(tt_["logP"]), func=AF.Exp, scale=-1.0),
                      reads=[tb["logP"]], writes=[tb["eN"]])
                Sx.op("pool", lambda e: e.tensor_tensor(out=flat(tt_["t1"]), in0=flat(tt_["logP"]), in1=flat(tt_["lw"]), op=ALU.subtract),
                      reads=[tb["logP"], tb["lw"]], writes=[tb["t1"]])
                Sx.op("act", lambda e: e.activation(out=flat(tt_["ePm"]), in_=flat(tt_["t1"]), func=AF.Exp), reads=[tb["t1"]], writes=[tb["ePm"]])
                for c4 in range(4):
                    Sx.op("pool", lambda e, c4=c4: e.tensor_scalar(out=tt_["kk"][:, c4, :], in0=k_[:, c4, :], scalar1=self.pc("k_k", c4),
                                                                   scalar2=None, op0=ALU.mult), reads=[pm_k[c4], pcb], writes=[tb["kk"]])
                Sx.op("act", lambda e: e.activation(out=flat(tt_["t2"]), in_=flat(tt_["kk"]), func=AF.Square), reads=[tb["kk"]], writes=[tb["t2"]])
                ps, pb = Sx.next_psum()
                Sx.op("pe", lambda e, ps=ps: e.matmul(ps[:, 0:4 * T], self.bo[:, :], flat(tt_["t2"]), start=True, stop=True),
                      reads=[tb["t2"], cb], writes=[pb])
                Sx.op("act", lambda e, ps=ps: e.activation(out=flat(tt_["t3"]), in_=ps[:, 0:4 * T], func=AF.Sqrt), reads=[pb], writes=[tb["t3"]])
                Sx.op("dve", lambda e: e.tensor_scalar(out=flat(tt_["t3"]), in0=flat(tt_["t3"]), scalar1=1e-12, scalar2=None, op0=ALU.max),
                      reads=[tb["t3"]], writes=[tb["t3"]])
                Sx.op("dve", lambda e: e.reciprocal(flat(tt_["t2"]), flat(tt_["t3"])), reads=[tb["t3"]], writes=[tb["t2"]])
                Sx.op("pool", lambda e: e.tensor_tensor(out=flat(tt_["kkn"]), in0=flat(tt_["kk"]), in1=flat(tt_["t2"]), op=ALU.mult),
                      reads=[tb["kk"], tb["t2"]], writes=[tb["kkn"]])
                for c4 in range(4):
                    Sx.op("act", lambda e, c4=c4: e.activation(out=tt_["t3"][:, c4, :], in_=tt_["asig"][:, c4, :], func=AF.Identity,
                                                               scale=self.pc("k_a", c4), bias=self.pc("omka", c4)),
                          reads=[tb["asig"], pcb], writes=[tb["t3"]])
                Sx.op("pool", lambda e: e.tensor_tensor(out=flat(tt_["kp"]), in0=k_.rearrange("p c t -> p (c t)"), in1=flat(tt_["t3"]), op=ALU.mult),
                      reads=pm_k + [tb["t3"]], writes=[tb["kp"]])
                Sx.op("dve", lambda e: e.scalar_tensor_tensor(out=flat(tt_["t1"]), in0=flat(tt_["kkn"]), scalar=-1.0, in1=flat(tt_["ePm"]),
                                                              op0=ALU.mult, op1=ALU.mult), reads=[tb["kkn"], tb["ePm"]], writes=[tb["t1"]])
                Sx.op("pool", lambda e: e.tensor_tensor(out=flat(tt_["t2"]), in0=flat(tt_["kkn"]), in1=flat(tt_["asig"]), op=ALU.mult),
                      reads=[tb["kkn"], tb["asig"]], writes=[tb["t2"]])
                Sx.op("pool", lambda e: e.tensor_tensor(out=flat(tt_["t2"]), in0=flat(tt_["t2"]), in1=flat(tt_["eN"]), op=ALU.mult),
                      reads=[tb["t2"], tb["eN"]], writes=[tb["t2"]])
                Sx.op("dve", lambda e: e.tensor_tensor(out=flat(tt_["t3"]), in0=flat(tt_["kp"]), in1=flat(tt_["eN"]), op=ALU.mult),
                      reads=[tb["kp"], tb["eN"]], writes=[tb["t3"]])
                Sx.op("dve", lambda e: e.tensor_tensor(out=flat(tt_["rT"]), in0=r_.rearrange("p c t -> p (c t)"), in1=flat(tt_["eP"]), op=ALU.mult),
                      reads=pm_r + [tb["eP"]], writes=[tb["rT"]])
                mb4 = mbd[:, 0:2].unsqueeze(1).unsqueeze(3).to_broadcast([128, 8, 2, 64])
                for zi, (zn, srcap, srcb) in enumerate([("a", tt_["t1"], tb["t1"]), ("b", tt_["t2"], tb["t2"]), ("k", tt_["t3"], tb["t3"]),
                                                        ("v", None, None)]):
                    if srcap is None:
                        sview = v_.rearrange("p c (h t) -> p (c h) t", h=2)
                    else:
                        sview = srcap[:, :, :].rearrange("p c (h t) -> p (c h) t", h=2)
                    in0 = sview.unsqueeze(2).to_broadcast([128, 8, 2, 64])
                    outv = Z[zn][:, :, :, :].rearrange("p c h (g t) -> p (c h) g t", g=2)
                    Sx.op("dve" if zi % 2 == 0 else "pool",
                          lambda e, outv=outv, in0=in0: e.tensor_tensor(out=outv, in0=in0, in1=mb4, op=ALU.mult),
                          reads=(pm_v if srcb is None else [srcb]) + [cb], writes=[Zb_[zn]])
                if self.cfg.get("stop_after", 9) < 2:
                    continue
                for ch in range(2):
                    U_ = []
                    for pr in range(4):
                        U_.append(dict(Za=Z["a"][:, pr, ch, :], Zb=Z["b"][:, pr, ch, :], Zk=Z["k"][:, pr, ch, :], Zv=Z["v"][:, pr, ch, :],
                                       rTu=tt_["rT"][:, pr, ch * 64:(ch + 1) * 64], pC=tt_["eP"][:, pr, ch * 64 + 63:ch * 64 + 64]))
                    for PRS in self.cfg.get('pr_groups', [[0, 1, 2, 3]]):
                        for pr in PRS:
                            u = U_[pr]; si = pr % NSET
                            Za, Zb, Zk, Zv, rTu = u["Za"], u["Zb"], u["Zk"], u["Zv"], u["rTu"]
                            psA, pbA = Sx.next_psum()
                            Sx.op("pe", lambda e: e.matmul(psA[:, 0:128], Zb, Za, start=True, stop=True), reads=[Zb_["b"], Zb_["a"]], writes=[pbA])
                            Sx.op("pe", lambda e: e.matmul(psA[:, 128:256], Zk, Za, start=True, stop=True), reads=[Zb_["k"], Zb_["a"]], writes=[pbA])
                            Sx.op("pe", lambda e: e.matmul(psA[:, 256:384], Za, Zb, start=True, stop=True), reads=[Zb_["b"], Zb_["a"]], writes=[pbA])
                            Sx.op("dve", lambda e: e.tensor_tensor(out=A_sb[si][:, :], in0=psA[:, 0:384], in1=maskA[:, :], op=ALU.mult),
                                  reads=[pbA, cb], writes=[Ab[si]])
                            psB, pbB = Sx.next_psum()
                            Sx.op("pe", lambda e: e.matmul(psB[:, 0:64], Zb, rTu, start=True, stop=True), reads=[Zb_["b"], tb["rT"]], writes=[pbB])
                            Sx.op("pe", lambda e: e.matmul(psB[:, 64:128], Zk, rTu, start=True, stop=True), reads=[Zb_["k"], tb["rT"]], writes=[pbB])
                            Sx.op("dve", lambda e: e.tensor_tensor(out=R_sb[si][:, :], in0=psB[:, 0:128], in1=maskB[:, :], op=ALU.mult),
                                  reads=[pbB, cb], writes=[Rb[si]])
                        for pr in PRS:
                            u = U_[pr]; si = pr % NSET
                            Za, Zb, Zk, Zv, rTu = u["Za"], u["Zb"], u["Zk"], u["Zv"], u["rTu"]
                            psC, pbC = Sx.next_psum()
                            Sx.op("pe", lambda e: e.matmul(psC[:, 0:128], Zb, ident[:, :], start=True, stop=True), reads=[Zb_["b"], cb], writes=[pbC])
                            Sx.op("pe", lambda e: e.matmul(psC[:, 128:256], Zk, ident[:, :], start=True, stop=True), reads=[Zb_["k"], cb], writes=[pbC])
                            Sx.op("pe", lambda e: e.matmul(psC[:, 256:384], Zv, ident[:, :], start=True, stop=True), reads=[Zb_["v"], cb], writes=[pbC])
                            Sx.op("act", lambda e: e.activation(out=BKV[si][:, :], in_=psC[:, 0:384], func=AF.Copy), reads=[pbC], writes=[BKVb[si]])
                            Sx.op("pool", lambda e: e.tensor_tensor(out=Wt[si][0][:, :], in0=A_sb[si][:, 0:128], in1=ident[:, :], op=ALU.add),
                                  reads=[Ab[si], cb], writes=[Wtb[si][0]])
                        Mp = {pr: A_sb[pr % NSET][:, 0:128] for pr in PRS}
                        Pp = {pr: A_sb[pr % NSET][:, 256:384] for pr in PRS}
                        mpb = {pr: Ab[pr % NSET] for pr in PRS}
                        for j in range(1, 6):
                            cur = j % 2
                            for pr in PRS:
                                si = pr % NSET
                                psN, pbN = Sx.next_psum()
                                Sx.op("pe", lambda e: e.matmul(psN[:, 0:128], Pp[pr], Mp[pr], start=True, stop=True), reads=[mpb[pr]], writes=[pbN])
                                Sx.op("pe", lambda e: e.matmul(psN[:, 128:256], Mp[pr], Pp[pr], start=True, stop=True), reads=[mpb[pr]], writes=[pbN])
                                Sx.op("act", lambda e: e.activation(out=MP[si][cur][:, :], in_=psN[:, 0:256], func=AF.Copy), reads=[pbN], writes=[MPb[si][cur]])
                                Mp[pr] = MP[si][cur][:, 0:128]; Pp[pr] = MP[si][cur][:, 128:256]; mpb[pr] = MPb[si][cur]
                            for pr in PRS:
                                si = pr % NSET
                                psW, pbW = Sx.next_psum()
                                Sx.op("pe", lambda e: e.matmul(psW[:, 0:128], Pp[pr], Wt[si][(j - 1) % 2][:, :], start=True, stop=True),
                                      reads=[mpb[pr], Wtb[si][(j - 1) % 2]], writes=[pbW])
                                Sx.op("dve", lambda e: e.tensor_tensor(out=Wt[si][j % 2][:, :], in0=psW[:, 0:128], in1=Wt[si][(j - 1) % 2][:, :], op=ALU.add),
                                      reads=[pbW, Wtb[si][(j - 1) % 2]], writes=[Wtb[si][j % 2]])
                        for pr in PRS:
                            u = U_[pr]; si = pr % NSET
                            psX, pbX = Sx.next_psum()
                            Sx.op("pe", lambda e: e.matmul(psX[:, 0:128], u["Za"], H[pr][:, :], start=True, stop=False), reads=[Zb_["a"], Hb[pr]], writes=[pbX])
                            Sx.op("pe", lambda e: e.matmul(psX[:, 0:128], A_sb[si][:, 128:256], BKV[si][:, 256:384], start=False, stop=True),
                                  reads=[Ab[si], BKVb[si]], writes=[pbX])
                            Sx.op("act", lambda e: e.activation(out=X_sb[si][:, :], in_=psX[:, 0:128], func=AF.Copy), reads=[pbX], writes=[Xb[si]])
                        for pr in PRS:
                            si = pr % NSET
                            psU, pbU = Sx.next_psum()
                            Sx.op("pe", lambda e: e.matmul(psU[:, 0:128], Wt[si][1][:, :], X_sb[si][:, :], start=True, stop=True), reads=[Wtb[si][1], Xb[si]], writes=[pbU])
                            Sx.op("dve", lambda e: e.tensor_copy(U_sb[si][:, :], psU[:, 0:128]), reads=[pbU], writes=[Ub[si]])
                        for pr in PRS:
                            u = U_[pr]; si = pr % NSET
                            psY, pbY = Sx.next_psum()
                            Sx.op("pe", lambda e: e.matmul(psY[:, 0:64], H[pr][:, :], u["rTu"], start=True, stop=False), reads=[Hb[pr], tb["rT"]], writes=[pbY])
                            Sx.op("pe", lambda e: e.matmul(psY[:, 0:64], U_sb[si][:, :], R_sb[si][:, 0:64], start=False, stop=False),
                                  reads=[Ub[si], Rb[si]], writes=[pbY])
                            Sx.op("pe", lambda e: e.matmul(psY[:, 0:64], BKV[si][:, 256:384], R_sb[si][:, 64:128], start=False, stop=True),
                                  reads=[BKVb[si], Rb[si]], writes=[pbY])
                            Sx.op("act", lambda e: e.activation(out=Ycm[:, pr, ch * 64:(ch + 1) * 64], in_=psY[:, 0:64], func=AF.Copy), reads=[pbY], writes=[Ybp[pr]])
                            psG, pbG = Sx.next_psum()
                            Sx.op("pe", lambda e: e.matmul(psG[:, 0:128], BKV[si][:, 0:128], U_sb[si][:, :], start=True, stop=False),
                                  reads=[BKVb[si], Ub[si]], writes=[pbG])
                            Sx.op("pe", lambda e: e.matmul(psG[:, 0:128], BKV[si][:, 128:256], BKV[si][:, 256:384], start=False, stop=True),
                                  reads=[BKVb[si]], writes=[pbG])
                            Sx.op("pool", lambda e: e.tensor_scalar(out=HpC[si][:, :], in0=H[pr][:, :], scalar1=u["pC"], scalar2=None, op0=ALU.mult),
                                  reads=[Hb[pr], tb["eP"]], writes=[HpCb[si]])
                            Sx.op("dve", lambda e: e.scalar_tensor_tensor(out=H[pr][:, :], in0=psG[:, 0:128], scalar=u["pC"], in1=HpC[si][:, :],
                                                                          op0=ALU.mult, op1=ALU.add),
                                  reads=[pbG, HpCb[si], tb["eP"]], writes=[Hb[pr]])
                if self.cfg.get("stop_after", 9) < 3:
                    continue
                NT_ = 4 * T
                psM, pbM = Sx.next_psum()
                Sx.op("pe", lambda e: e.matmul(psM[:, 0:NT_], self.bo[:, :], flat(Ycm), start=True, stop=True), reads=Ybp + [cb], writes=[pbM])
                Sx.op("act", lambda e: e.activation(out=flat(tt_["t1"]), in_=flat(Ycm), func=AF.Square), reads=Ybp, writes=[tb["t1"]])
                psQ, pbQ = Sx.next_psum()
                Sx.op("pe", lambda e: e.matmul(psQ[:, 0:NT_], self.bo[:, :], flat(tt_["t1"]), start=True, stop=True), reads=[tb["t1"], cb], writes=[pbQ])
                Sx.op("act", lambda e: e.activation(out=flat(tt_["t2"]), in_=psM[:, 0:NT_], func=AF.Copy, scale=1.0 / 64.0), reads=[pbM], writes=[tb["t2"]])
                Sx.op("pool", lambda e: e.tensor_tensor(out=flat(tt_["t3"]), in0=flat(tt_["t2"]), in1=flat(tt_["t2"]), op=ALU.mult),
                      reads=[tb["t2"]], writes=[tb["t3"]])
                Sx.op("dve", lambda e: e.scalar_tensor_tensor(out=flat(tt_["t3"]), in0=psQ[:, 0:NT_], scalar=1.0 / 64.0, in1=flat(tt_["t3"]),
                                                              op0=ALU.mult, op1=ALU.subtract), reads=[pbQ, tb["t3"]], writes=[tb["t3"]])
                Sx.op("dve", lambda e: e.tensor_scalar(out=flat(tt_["t3"]), in0=flat(tt_["t3"]), scalar1=64e-5, scalar2=None, op0=ALU.add),
                      reads=[tb["t3"]], writes=[tb["t3"]])
                Sx.op("act", lambda e: e.activation(out=flat(tt_["t3"]), in_=flat(tt_["t3"]), func=AF.Sqrt), reads=[tb["t3"]], writes=[tb["t3"]])
                Sx.op("dve", lambda e: e.reciprocal(flat(tt_["t1"]), flat(tt_["t3"])), reads=[tb["t3"]], writes=[tb["t1"]])
                Sx.op("pool", lambda e: e.tensor_tensor(out=flat(tt_["t2"]), in0=flat(Ycm), in1=flat(tt_["t2"]), op=ALU.subtract),
                      reads=Ybp + [tb["t2"]], writes=[tb["t2"]])
                Sx.op("pool", lambda e: e.tensor_tensor(out=flat(tt_["t2"]), in0=flat(tt_["t2"]), in1=flat(tt_["t1"]), op=ALU.mult),
                      reads=[tb["t2"], tb["t1"]], writes=[tb["t2"]])
                for c4 in range(4):
                    Sx.op("act", lambda e, c4=c4: e.activation(out=tt_["t2"][:, c4, :], in_=tt_["t2"][:, c4, :], func=AF.Identity,
                                                               scale=self.pc("lnx_g", c4), bias=self.pc("lnx_b", c4)),
                          reads=[tb["t2"], pcb], writes=[tb["t2"]])
                    Sx.op("dve", lambda e, c4=c4: e.scalar_tensor_tensor(out=tt_["t1"][:, c4, :], in0=r_[:, c4, :], scalar=self.pc("r_k", c4),
                                                                         in1=tt_["kp"][:, c4, :], op0=ALU.mult, op1=ALU.mult),
                          reads=[pm_r[c4], tb["kp"], pcb, tb["t1"]], writes=[tb["t1"]])
                psR, pbR = Sx.next_psum()
                Sx.op("pe", lambda e: e.matmul(psR[:, 0:NT_], self.bo[:, :], flat(tt_["t1"]), start=True, stop=True), reads=[tb["t1"], cb], writes=[pbR])
                Sx.op("dve", lambda e: e.tensor_tensor(out=flat(tt_["t3"]), in0=psR[:, 0:NT_], in1=v_.rearrange("p c t -> p (c t)"), op=ALU.mult),
                      reads=[pbR] + pm_v, writes=[tb["t3"]])
                Sx.op("pool", lambda e: e.tensor_tensor(out=flat(tt_["t3"]), in0=flat(tt_["t3"]), in1=flat(tt_["t2"]), op=ALU.add),
                      reads=[tb["t3"], tb["t2"]], writes=[tb["t3"]])
                Sx.op("pool", lambda e: e.tensor_tensor(out=flat(ybf), in0=flat(tt_["t3"]), in1=flat(sgate), op=ALU.mult),
                      reads=[tb["t3"], sgb], writes=[ybb])
                Sx.dma("sp", ydst[:, :, t0:t0 + T], ybf[:, :, :], reads=[ybb], writes=[self.ybr_buf[0]])
            Sx.barrier()

    def rope_tables(self, s):
        Sx = self.S
        rp = self.c["c_rope"]
        cb = self.cb
        with contextlib.ExitStack() as es:
            posi = self.sb(es, "posi", [128, S], I32)
            ang = self.sb(es, "ang", [128, S], F32)
            t1 = self.sb(es, "rp_t1", [128, S], F32)
            t2 = self.sb(es, "rp_t2", [128, S], F32)
            ki = self.sb(es, "rp_ki", [128, S], I32)
            b = Buf()
            R = slice(64, 96)
            Sx.dma("sp", posi[R, :], self.din["positions"][s:s + 1, :].broadcast_to([32, S]), writes=[b])
            Sx.op("dve", lambda e: e.tensor_copy(ang[R, :], posi[R, :]), reads=[b], writes=[b])
            Sx.op("dve", lambda e: e.tensor_scalar(out=ang[R, :], in0=ang[R, :], scalar1=rp[R, 0:1], scalar2=None, op0=ALU.mult),
                  reads=[b, cb], writes=[b])
            TWO_PI = 6.283185307179586
            for which, dst in ((0, self.ropeS), (1, self.ropeC)):
                shift = 0.0 if which == 0 else 1.5707963267948966
                Sx.op("dve", lambda e: e.tensor_scalar(out=t1[R, :], in0=ang[R, :], scalar1=shift, scalar2=None, op0=ALU.add), reads=[b], writes=[b])
                Sx.op("dve", lambda e: e.tensor_scalar(out=t2[R, :], in0=t1[R, :], scalar1=1.0 / TWO_PI, scalar2=0.5, op0=ALU.mult, op1=ALU.add),
                      reads=[b], writes=[b])
                Sx.op("dve", lambda e: e.tensor_copy(ki[R, :], t2[R, :]), reads=[b], writes=[b])
                Sx.op("dve", lambda e: e.tensor_copy(t2[R, :], ki[R, :]), reads=[b], writes=[b])
                Sx.op("dve", lambda e: e.scalar_tensor_tensor(out=t1[R, :], in0=t2[R, :], scalar=-TWO_PI, in1=t1[R, :], op0=ALU.mult, op1=ALU.add),
                      reads=[b], writes=[b])
                Sx.op("dve", lambda e: e.tensor_scalar(out=t2[R, :], in0=t1[R, :], scalar1=-3.141592653589793, scalar2=TWO_PI, op0=ALU.is_lt, op1=ALU.mult),
                      reads=[b], writes=[b])
                Sx.op("dve", lambda e: e.tensor_tensor(out=t1[R, :], in0=t1[R, :], in1=t2[R, :], op=ALU.add), reads=[b], writes=[b])
                Sx.op("dve", lambda e: e.tensor_scalar(out=t2[R, :], in0=t1[R, :], scalar1=3.141592653589793, scalar2=-TWO_PI, op0=ALU.is_gt, op1=ALU.mult),
                      reads=[b], writes=[b])
                Sx.op("dve", lambda e: e.tensor_tensor(out=t1[R, :], in0=t1[R, :], in1=t2[R, :], op=ALU.add), reads=[b], writes=[b])
                Sx.op("dve", lambda e: e.tensor_scalar(out=t1[R, :], in0=t1[R, :], scalar1=3.1415925, scalar2=-3.1415925, op0=ALU.min, op1=ALU.max),
                      reads=[b], writes=[b])
                Sx.op("act", lambda e, dst=dst: e.activation(out=dst[R, :], in_=t1[R, :], func=AF.Sin), reads=[b], writes=[self.rope_b])
            Sx.op("dve", lambda e: e.tensor_scalar(out=self.ropeS[R, :], in0=self.ropeS[R, :], scalar1=rp[R, 1:2], scalar2=None, op0=ALU.mult),
                  reads=[self.rope_b, cb], writes=[self.rope_b])
            Sx.barrier()

    def load_memT(self, s):
        Sx = self.S
        with contextlib.ExitStack() as es:
            mt = [self.sb(es, f"mtok{i}", [128, D], F32) for i in range(2)]
            mb = [Buf(), Buf()]
            for tt in range(2):
                Sx.dma("sp", mt[tt][:], self.din["mem"][s, tt * 128:(tt + 1) * 128, :], writes=[mb[tt]])
                for half in range(2):
                    ps, pb = Sx.next_psum()
                    for j in range(4):
                        kc = half * 4 + j
                        Sx.op("pe", lambda e: e.matmul(ps[:, j * 128:(j + 1) * 128], mt[tt][:, kc * 128:(kc + 1) * 128], self.ident[:, :], start=True, stop=True),
                              reads=[mb[tt], self.cb], writes=[pb])
                    Sx.op("act", lambda e: e.activation(out=self.memT[:, half * 4:(half + 1) * 4, tt * 128:(tt + 1) * 128],
                                                        in_=ps[:, :].rearrange("p (j t) -> p j t", j=4), func=AF.Copy),
                          reads=[pb], writes=[self.memT_b])
            Sx.barrier()

    def inproj(self, ps, pb, wt, wb, c0, ncol, t0, ntok):
        Sx = self.S
        for kc in range(8):
            Sx.op("pe", lambda e, kc=kc: e.matmul(ps[0:ncol, 0:ntok], wt[:, kc, c0:c0 + ncol], self.xT[:, kc, t0:t0 + ntok],
                                                  start=(kc == 0), stop=(kc == 7)), reads=[wb, self.xT_b], writes=[pb])

    def rms_latent(self, es, tag, lat_f, sq, nch, dim, gname, outn, bufs):
        Sx = self.S
        lb, sqb, ob, rb, rstd = bufs
        ps, pb = Sx.next_psum(4)
        for i in range(nch):
            Sx.op("pe", lambda e, i=i: e.matmul(ps[:, :], self.c["c_ones"][:, :], sq[:, i, :], start=(i == 0), stop=(i == nch - 1)),
                  reads=[sqb, self.cb], writes=[pb])
        Sx.op("dve", lambda e: e.tensor_scalar(out=rstd[:, :], in0=ps[:, :], scalar1=1.0 / dim, scalar2=1e-6, op0=ALU.mult, op1=ALU.add),
              reads=[pb], writes=[rb])
        Sx.op("act", lambda e: e.activation(out=rstd[:, :], in_=rstd[:, :], func=AF.Sqrt), reads=[rb], writes=[rb])
        Sx.op("dve", lambda e: e.reciprocal(rstd[:, :], rstd[:, :]), reads=[rb], writes=[rb])
        for i in range(nch):
            Sx.op("dve", lambda e, i=i: e.scalar_tensor_tensor(out=outn[:, i, :], in0=lat_f[:, i, :], scalar=self.pc(gname, i), in1=rstd[:, :],
                                                               op0=ALU.mult, op1=ALU.mult), reads=[lb, rb, self.pcol_b], writes=[ob])

    def phase_mla(self, s, l):
        Sx = self.S
        d = self.din
        cb = self.cb
        scale = 96.0 ** -0.5
        with contextlib.ExitStack() as es:
            sb = lambda n, shp, dt=F32: self.sb(es, "ml_" + n, shp, dt)
            wM = sb("wM", [128, 8, 1184], BF16); wMb = Buf()
            self.load_w_bf(wM, wMb, d["w_in"][l, :, OFF_QLAT:OFF_QLAT + 1184], 8)
            wks = sb("wks", [128, 8, 96], BF16); wksb = Buf()
            Sx.op("pool", lambda e: e.memset(wks[:], 0.0), writes=[wksb])
            vk = d["w_in"][l, :, OFF_KPE:OFF_KPE + 32].rearrange("(kc p) c -> p kc c", p=128)
            Sx.dma("pool", wks[:, :, 64:80], vk[:, :, 16:32], writes=[wksb])
            Sx.dma("pool", wks[:, :, 80:96], vk[:, :, 0:16], writes=[wksb])
            Wuq = sb("Wuq", [128, 3, 768], BF16); Wuqb = Buf()
            self.load_w_bf(Wuq, Wuqb, d["mla_w_uq"][l], 3)
            Wus = sb("Wus", [128, 3, 768], BF16); Wusb = Buf()
            Wq4 = Wuq[:, :, :].rearrange("p k (h c) -> p k h c", h=8)
            Ws4 = Wus[:, :, :].rearrange("p k (h c) -> p k h c", h=8)
            Sx.op("pool", lambda e: e.tensor_copy(Ws4[:, :, :, 0:64], Wq4[:, :, :, 0:64]), reads=[Wuqb], writes=[Wusb])
            Sx.op("pool", lambda e: e.tensor_copy(Ws4[:, :, :, 64:80], Wq4[:, :, :, 80:96]), reads=[Wuqb], writes=[Wusb])
            Sx.op("pool", lambda e: e.tensor_copy(Ws4[:, :, :, 80:96], Wq4[:, :, :, 64:80]), reads=[Wuqb], writes=[Wusb])
            Wukv = sb("Wukv", [128, 2, 1024], BF16); Wukvb = Buf()
            self.load_w_bf(Wukv, Wukvb, d["mla_w_ukv"][l], 2)
            QT = sb("QT", [128, 8, 512], BF16); QTh = [Buf() for _ in range(8)]
            KT = sb("KT", [128, 8, S], BF16); KTh = [Buf() for _ in range(8)]
            V = sb("V", [128, 16, 512], BF16); Vb = Buf()
            qlf = sb("qlf", [128, 3, 512]); qlb = Buf()
            qsq = sb("qsq", [128, 3, 512]); qsb = Buf()
            qn = sb("qn", [128, 3, 512], BF16); qnb = Buf()
            klf = sb("klf", [128, 2, 512]); klb = Buf()
            ksq = sb("ksq", [128, 2, 512]); ksb = Buf()
            kvn = sb("kvn", [128, 2, 512], BF16); knb = Buf()
            rq = sb("rq", [128, 512]); rqb = Buf()
            rk = sb("rk", [128, 512]); rkb = Buf()
            tas = [sb(f"ta{i}", [128, 512]) for i in range(2)]; tabs = [Buf(), Buf()]
            tbs = [sb(f"tb{i}", [128, 512]) for i in range(2)]; tbbs = [Buf(), Buf()]
            ta = tas[0]; tab = tabs[0]; tbb_ = tbs[0]; tbb = tbbs[0]
            PT = [sb(f"PT{i}", [128, 512], BF16) for i in range(3)]; PTb = [Buf() for _ in range(3)]
            sgt = sb("sgt", [128, 512]); sgb = Buf()
            rl = sb("rl", [128, 512]); rlb = Buf()
            yb = [sb(f"yb{i}", [128, 512], BF16) for i in range(2)]; ybb = [Buf(), Buf()]
            ydst = self.ybr[1]
            R = slice(64, 96)
            pti = 0
            for g in range(4):
                t0 = g * 512
                for i in range(3):
                    ps, pb = Sx.next_psum(4)
                    self.inproj(ps, pb, wM, wMb, i * 128, 128, t0, 512)
                    Sx.op("act", lambda e: e.activation(out=qlf[:, i, :], in_=ps[:, :], func=AF.Copy), reads=[pb], writes=[qlb])
                    Sx.op("act", lambda e: e.activation(out=qsq[:, i, :], in_=ps[:, :], func=AF.Square), reads=[pb], writes=[qsb])
                self.rms_latent(es, "q", qlf, qsq, 3, 384.0, "q_norm", qn, (qlb, qsb, qnb, rqb, rq))
                for i in range(2):
                    ps, pb = Sx.next_psum(4)
                    self.inproj(ps, pb, wM, wMb, 384 + i * 128, 128, t0, 512)
                    Sx.op("act", lambda e: e.activation(out=klf[:, i, :], in_=ps[:, :], func=AF.Copy), reads=[pb], writes=[klb])
                    Sx.op("act", lambda e: e.activation(out=ksq[:, i, :], in_=ps[:, :], func=AF.Square), reads=[pb], writes=[ksb])
                self.rms_latent(es, "k", klf, ksq, 2, 256.0, "kv_norm", kvn, (klb, ksb, knb, rkb, rk))
                ps1, pb1 = Sx.next_psum(4)
                self.inproj(ps1, pb1, wM, wMb, 576, 96, t0, 512)
                ps2, pb2 = Sx.next_psum(4)
                self.inproj(ps2, pb2, wks, wksb, 0, 96, t0, 512)
                Sx.op("dve", lambda e: e.tensor_tensor(out=ta[R, :], in0=ps1[R, :], in1=self.ropeC[R, t0:t0 + 512], op=ALU.mult),
                      reads=[pb1, self.rope_b], writes=[tab])
                Sx.op("dve", lambda e: e.tensor_tensor(out=tbb_[R, :], in0=ps2[R, :], in1=self.ropeS[R, t0:t0 + 512], op=ALU.mult),
                      reads=[pb2, self.rope_b], writes=[tbb])
                Sx.op("pool", lambda e: e.tensor_tensor(out=ta[R, :], in0=ta[R, :], in1=tbb_[R, :], op=ALU.add), reads=[tab, tbb], writes=[tab])
                for h in range(8):
                    Sx.op("pool" if h % 2 else "act",
                          (lambda e: e.tensor_copy(KT[R, h, t0:t0 + 512], ta[R, :])) if h % 2 else
                          (lambda e: e.activation(out=KT[R, h, t0:t0 + 512], in_=ta[R, :], func=AF.Copy)),
                          reads=[tab], writes=[KTh[h]])
                for h in range(8):
                    ta = tas[h % 2]; tab = tabs[h % 2]; tbb_ = tbs[h % 2]; tbb = tbbs[h % 2]
                    ps1, pb1 = Sx.next_psum(4)
                    ps2, pb2 = Sx.next_psum(4)
                    for kc in range(3):
                        Sx.op("pe", lambda e: e.matmul(ps1[0:96, :], Wuq[:, kc, h * 96:(h + 1) * 96], qn[:, kc, :], start=(kc == 0), stop=(kc == 2)),
                              reads=[Wuqb, qnb], writes=[pb1])
                    for kc in range(3):
                        Sx.op("pe", lambda e: e.matmul(ps2[0:96, :], Wus[:, kc, h * 96:(h + 1) * 96], qn[:, kc, :], start=(kc == 0), stop=(kc == 2)),
                              reads=[Wusb, qnb], writes=[pb2])
                    Sx.op("act", lambda e: e.activation(out=QT[0:64, h, :], in_=ps1[0:64, :], func=AF.Copy), reads=[pb1], writes=[QTh[h]])
                    Sx.op("dve", lambda e: e.tensor_tensor(out=ta[R, :], in0=ps1[R, :], in1=self.ropeC[R, t0:t0 + 512], op=ALU.mult),
                          reads=[pb1, self.rope_b], writes=[tab])
                    Sx.op("dve", lambda e: e.tensor_tensor(out=tbb_[R, :], in0=ps2[R, :], in1=self.ropeS[R, t0:t0 + 512], op=ALU.mult),
                          reads=[pb2, self.rope_b], writes=[tbb])
                    Sx.op("pool", lambda e: e.tensor_tensor(out=QT[R, h, :], in0=ta[R, :], in1=tbb_[R, :], op=ALU.add), reads=[tab, tbb], writes=[QTh[h]])
                    ps3, pb3 = Sx.next_psum(4)
                    for kc in range(2):
                        Sx.op("pe", lambda e: e.matmul(ps3[0:64, :], Wukv[:, kc, h * 128:h * 128 + 64], kvn[:, kc, :], start=(kc == 0), stop=(kc == 1)),
                              reads=[Wukvb, knb], writes=[pb3])
                    Sx.op("act", lambda e: e.activation(out=KT[0:64, h, t0:t0 + 512], in_=ps3[0:64, :], func=AF.Copy), reads=[pb3], writes=[KTh[h]])
                Wv = Wukv[:, :, :].rearrange("p k (h c) -> p k h c", h=8)
                for tt in range(4):
                    ps, pb = Sx.next_psum(4)
                    for kc in range(2):
                        Sx.op("pe", lambda e: e.matmul(ps[:, :].rearrange("p (h c) -> p h c", h=8), kvn[:, kc, tt * 128:(tt + 1) * 128], Wv[:, kc, :, 64:128],
                                                       start=(kc == 0), stop=(kc == 1)), reads=[Wukvb, knb], writes=[pb])
                    Sx.op("dve", lambda e: e.tensor_copy(V[:, g * 4 + tt, :], ps[:, :]), reads=[pb], writes=[Vb])
                for pr in range(4):
                    accs = []
                    for hh in range(2):
                        h = pr * 2 + hh
                        o_ps, o_pb = Sx.psum[4 + hh * 2]
                        l_ps, l_pb = Sx.psum[5 + hh * 2]
                        accs.append((o_ps, o_pb, l_ps, l_pb))
                        nj = 4 * g + 4
                        for j in range(nj):
                            sps, spb = Sx.next_psum(4)
                            Sx.op("pe", lambda e: e.matmul(sps[:, :], KT[0:96, h, j * 128:(j + 1) * 128], QT[0:96, h, :], start=True, stop=True),
                                  reads=[KTh[h], QTh[h]], writes=[spb])
                            pt = PT[pti]; ptb = PTb[pti]; pti = (pti + 1) % 3
                            Sx.op("act", lambda e: e.activation(out=pt[:, :], in_=sps[:, :], func=AF.Exp, scale=scale), reads=[spb], writes=[ptb])
                            if j >= 4 * g:
                                v = j - 4 * g
                                Sx.op("pool", lambda e: e.tensor_tensor(out=pt[:, :], in0=pt[:, :], in1=self.cmla_bf[:, v * 512:(v + 1) * 512], op=ALU.mult),
                                      reads=[ptb, cb], writes=[ptb])
                            Sx.op("pe", lambda e: e.matmul(o_ps[:, :], V[:, j, pr * 128:(pr + 1) * 128], pt[:, :], start=(j == 0), stop=(j == nj - 1)),
                                  reads=[Vb, ptb], writes=[o_pb])
                            Sx.op("pe", lambda e: e.matmul(l_ps[:, :], self.ones_bf[:, :], pt[:, :], start=(j == 0), stop=(j == nj - 1)),
                                  reads=[cb, ptb], writes=[l_pb])
                    gps, gpb = Sx.next_psum(4)
                    self.inproj(gps, gpb, wM, wMb, 672 + pr * 128, 128, t0, 512)
                    Sx.op("act", lambda e: e.activation(out=sgt[:, :], in_=gps[:, :], func=AF.Silu), reads=[gpb], writes=[sgb])
                    yt = yb[pr % 2]; ytb = ybb[pr % 2]
                    for hh in range(2):
                        o_ps, o_pb, l_ps, l_pb = accs[hh]
                        HR = slice(hh * 64, hh * 64 + 64)
                        Sx.op("dve", lambda e: e.reciprocal(rl[HR, :], l_ps[HR, :]), reads=[l_pb], writes=[rlb])
                        Sx.op("dve", lambda e: e.tensor_tensor(out=rl[HR, :], in0=o_ps[HR, :], in1=rl[HR, :], op=ALU.mult), reads=[o_pb, rlb], writes=[rlb])
                        Sx.op("pool", lambda e: e.tensor_tensor(out=yt[HR, :], in0=rl[HR, :], in1=sgt[HR, :], op=ALU.mult), reads=[rlb, sgb], writes=[ytb])
                    Sx.dma("sp", ydst[pr * 128:(pr + 1) * 128, t0:t0 + 512], yt[:, :], reads=[ytb], writes=[self.ybr_buf[1]])
            Sx.barrier()

    def phase_xattn(self, s, l):
        Sx = self.S
        d = self.din
        cb = self.cb
        scale = 128.0 ** -0.5
        with contextlib.ExitStack() as es:
            sb = lambda n, shp, dt=F32: self.sb(es, "xa_" + n, shp, dt)
            wkv = sb("wkv", [128, 8, 1024], BF16); wkvb = Buf()
            self.load_w_bf(wkv, wkvb, d["xattn_w_mem_kv"][l], 8)
            wX = sb("wX", [128, 8, 1024], BF16); wXb = Buf()
            self.load_w_bf(wX, wXb, d["w_in"][l, :, OFF_XQ:OFF_XQ + 1024], 8)
            KxT = sb("KxT", [128, 4, 256], BF16); Kb = Buf()
            Vx = sb("Vx", [128, 2, 512], BF16); Vb = Buf()
            qx = sb("qx", [128, 512], BF16); qb = Buf()
            PT = [sb(f"PT{i}", [128, 512], BF16) for i in range(2)]; PTb = [Buf(), Buf()]
            sgt = sb("sgt", [128, 512]); sgb = Buf()
            rl = sb("rl", [128, 512]); rlb = Buf()
            yb = [sb(f"yb{i}", [128, 512], BF16) for i in range(2)]; ybb = [Buf(), Buf()]
            for h in range(4):
                ps, pb = Sx.next_psum(4)
                for kc in range(8):
                    Sx.op("pe", lambda e: e.matmul(ps[:, 0:256], wkv[:, kc, h * 128:(h + 1) * 128], self.memT[:, kc, :], start=(kc == 0), stop=(kc == 7)),
                          reads=[wkvb, self.memT_b], writes=[pb])
                Sx.op("act", lambda e: e.activation(out=KxT[:, h, :], in_=ps[:, 0:256], func=AF.Copy), reads=[pb], writes=[Kb])
            for mt in range(2):
                ps, pb = Sx.next_psum(4)
                for kc in range(8):
                    Sx.op("pe", lambda e: e.matmul(ps[:, :], self.memT[:, kc, mt * 128:(mt + 1) * 128], wkv[:, kc, 512:1024], start=(kc == 0), stop=(kc == 7)),
                          reads=[wkvb, self.memT_b], writes=[pb])
                Sx.op("act", lambda e: e.activation(out=Vx[:, mt, :], in_=ps[:, :], func=AF.Copy), reads=[pb], writes=[Vb])
            ydst = self.ybr[3]
            k = 0
            for g in range(4):
                t0 = g * 512
                for h in range(4):
                    ps, pb = Sx.next_psum(4)
                    self.inproj(ps, pb, wX, wXb, h * 128, 128, t0, 512)
                    Sx.op("act", lambda e: e.activation(out=qx[:, :], in_=ps[:, :], func=AF.Copy), reads=[pb], writes=[qb])
                    o_ps, o_pb = Sx.psum[4]
                    l_ps, l_pb = Sx.psum[5]
                    for mt in range(2):
                        sps, spb = Sx.next_psum(4)
                        Sx.op("pe", lambda e: e.matmul(sps[:, :], KxT[:, h, mt * 128:(mt + 1) * 128], qx[:, :], start=True, stop=True), reads=[Kb, qb], writes=[spb])
                        pt = PT[mt]; ptb = PTb[mt]
                        Sx.op("act", lambda e: e.activation(out=pt[:, :], in_=sps[:, :], func=AF.Exp, scale=scale), reads=[spb], writes=[ptb])
                        Sx.op("pe", lambda e: e.matmul(o_ps[:, :], Vx[:, mt, h * 128:(h + 1) * 128], pt[:, :], start=(mt == 0), stop=(mt == 1)),
                              reads=[Vb, ptb], writes=[o_pb])
                        Sx.op("pe", lambda e: e.matmul(l_ps[:, :], self.ones_bf[:, :], pt[:, :], start=(mt == 0), stop=(mt == 1)),
                              reads=[cb, ptb], writes=[l_pb])
                    gps, gpb = Sx.next_psum(4)
                    self.inproj(gps, gpb, wX, wXb, 512 + h * 128, 128, t0, 512)
                    Sx.op("act", lambda e: e.activation(out=sgt[:, :], in_=gps[:, :], func=AF.Silu), reads=[gpb], writes=[sgb])
                    yt = yb[k % 2]; ytb = ybb[k % 2]; k += 1
                    Sx.op("dve", lambda e: e.reciprocal(rl[:, :], l_ps[:, :]), reads=[l_pb], writes=[rlb])
                    Sx.op("dve", lambda e: e.tensor_tensor(out=rl[:, :], in0=o_ps[:, :], in1=rl[:, :], op=ALU.mult), reads=[o_pb, rlb], writes=[rlb])
                    Sx.op("pool", lambda e: e.tensor_tensor(out=yt[:, :], in0=rl[:, :], in1=sgt[:, :], op=ALU.mult), reads=[rlb, sgb], writes=[ytb])
                    Sx.dma("sp", ydst[h * 128:(h + 1) * 128, t0:t0 + 512], yt[:, :], reads=[ytb], writes=[self.ybr_buf[3]])
            Sx.barrier()

    def phase_conv(self, s, l):
        Sx = self.S
        d = self.din
        cb = self.cb
        pcb = self.pcol_b
        ones = self.c["c_ones"]
        with contextlib.ExitStack() as es:
            sb = lambda n, shp, dt=F32: self.sb(es, "cv_" + n, shp, dt)
            wC = sb("wC", [128, 8, 1536], BF16); wCb = Buf()
            self.load_w_bf(wC, wCb, d["w_in"][l, :, OFF_CONV:OFF_CONV + 1536], 8)
            Dg = sb("Dg", [128, 4, 31, 128], BF16); Dgb = Buf()
            for c in range(4):
                for j in range(31):
                    Sx.op("pool" if (j % 2) else "dve",
                          lambda e: e.tensor_scalar(out=Dg[:, c, j, :], in0=self.ident[:, :], scalar1=self.pc("conv_w", j * 4 + c), scalar2=None, op0=ALU.mult),
                          reads=[cb, pcb], writes=[Dgb])
            hb = sb("hb", [128, 4, 30 + S], BF16); hbb = Buf()
            Sx.op("pool", lambda e: e.memset(hb[:, :, 0:30], 0.0), writes=[hbb])
            sig = sb("sig", [128, 512]); sigb = Buf()
            cv = sb("cvv", [128, 4, 512]); cvb = Buf()
            sq = sb("sq", [128, 4, 512]); sqb = Buf()
            mean = sb("mean", [128, 512]); mb = Buf()
            rstd = sb("rstd", [128, 512]); rb = Buf()
            t1 = sb("t1", [128, 512]); t1b = Buf()
            sgt = sb("sgt", [128, 512]); sgb = Buf()
            yb = [sb(f"yb{i}", [128, 512], BF16) for i in range(2)]; ybb = [Buf(), Buf()]
            ydst = self.ybr[2]
            for g in range(4):
                t0 = g * 512
                for c in range(4):
                    ps1, pb1 = Sx.next_psum()
                    self.inproj(ps1, pb1, wC, wCb, c * 128, 128, t0, 512)
                    ps2, pb2 = Sx.next_psum()
                    self.inproj(ps2, pb2, wC, wCb, 512 + c * 128, 128, t0, 512)
                    Sx.op("act", lambda e: e.activation(out=sig[:, :], in_=ps2[:, :], func=AF.Sigmoid), reads=[pb2], writes=[sigb])
                    Sx.op("dve", lambda e: e.tensor_tensor(out=hb[:, c, 30 + t0:30 + t0 + 512], in0=ps1[:, :], in1=sig[:, :], op=ALU.mult),
                          reads=[pb1, sigb], writes=[hbb])
                for c in range(4):
                    ps, pb = Sx.next_psum()
                    for j in range(31):
                        Sx.op("pe", lambda e: e.matmul(ps[:, :], Dg[:, c, j, :], hb[:, c, t0 + j:t0 + j + 512], start=(j == 0), stop=(j == 30)),
                              reads=[Dgb, hbb], writes=[pb])
                    Sx.op("act", lambda e: e.activation(out=cv[:, c, :], in_=ps[:, :], func=AF.Identity, bias=self.pc("conv_b", c)), reads=[pb, pcb], writes=[cvb])
                Sx.op("act", lambda e: e.activation(out=sq[:, :, :].rearrange("p c t -> p (c t)"), in_=cv[:, :, :].rearrange("p c t -> p (c t)"), func=AF.Square),
                      reads=[cvb], writes=[sqb])
                psM, pbM = Sx.next_psum()
                psQ, pbQ = Sx.next_psum()
                for c in range(4):
                    Sx.op("pe", lambda e: e.matmul(psM[:, :], ones[:, :], cv[:, c, :], start=(c == 0), stop=(c == 3)), reads=[cvb, cb], writes=[pbM])
                for c in range(4):
                    Sx.op("pe", lambda e: e.matmul(psQ[:, :], ones[:, :], sq[:, c, :], start=(c == 0), stop=(c == 3)), reads=[sqb, cb], writes=[pbQ])
                Sx.op("act", lambda e: e.activation(out=mean[:, :], in_=psM[:, :], func=AF.Copy, scale=1.0 / 512.0), reads=[pbM], writes=[mb])
                Sx.op("pool", lambda e: e.tensor_tensor(out=t1[:, :], in0=mean[:, :], in1=mean[:, :], op=ALU.mult), reads=[mb], writes=[t1b])
                Sx.op("dve", lambda e: e.scalar_tensor_tensor(out=rstd[:, :], in0=psQ[:, :], scalar=1.0 / 512.0, in1=t1[:, :], op0=ALU.mult, op1=ALU.subtract),
                      reads=[pbQ, t1b], writes=[rb])
                Sx.op("dve", lambda e: e.tensor_scalar(out=rstd[:, :], in0=rstd[:, :], scalar1=1e-5, scalar2=None, op0=ALU.add), reads=[rb], writes=[rb])
                Sx.op("act", lambda e: e.activation(out=rstd[:, :], in_=rstd[:, :], func=AF.Sqrt), reads=[rb], writes=[rb])
                Sx.op("dve", lambda e: e.reciprocal(rstd[:, :], rstd[:, :]), reads=[rb], writes=[rb])
                for c in range(4):
                    Sx.op("pool", lambda e: e.tensor_tensor(out=t1[:, :], in0=cv[:, c, :], in1=mean[:, :], op=ALU.subtract), reads=[cvb, mb, t1b], writes=[t1b])
                    Sx.op("dve", lambda e: e.tensor_tensor(out=t1[:, :], in0=t1[:, :], in1=rstd[:, :], op=ALU.mult), reads=[t1b, rb], writes=[t1b])
                    Sx.op("act", lambda e: e.activation(out=t1[:, :], in_=t1[:, :], func=AF.Silu, scale=self.pc("conv_ln_g", c), bias=self.pc("conv_ln_b", c)),
                          reads=[t1b, pcb], writes=[t1b])
                    gps, gpb = Sx.next_psum()
                    self.inproj(gps, gpb, wC, wCb, 1024 + c * 128, 128, t0, 512)
                    Sx.op("act", lambda e: e.activation(out=sgt[:, :], in_=gps[:, :], func=AF.Silu), reads=[gpb], writes=[sgb])
                    yt = yb[c % 2]; ytb = ybb[c % 2]
                    Sx.op("pool", lambda e: e.tensor_tensor(out=yt[:, :], in0=t1[:, :], in1=sgt[:, :], op=ALU.mult), reads=[t1b, sgb], writes=[ytb])
                    Sx.dma("sp", ydst[c * 128:(c + 1) * 128, t0:t0 + 512], yt[:, :], reads=[ytb], writes=[self.ybr_buf[2]])
            Sx.barrier()

    def phase_epi(self, s, l):
        Sx = self.S
        d = self.din
        cb = self.cb
        pcb = self.pcol_b
        with contextlib.ExitStack() as es0:
            mT = self.sb(es0, "ep_mT", [128, 8, S], BF16); mTb = Buf()
            with contextlib.ExitStack() as es:
                sb = lambda n, shp, dt=F32: self.sb(es, "e1_" + n, shp, dt)
                yin = sb("yin", [128, 4, 4, S], BF16); yinb = Buf()
                prs = []
                for n in range(4):
                    src = self.ybr[n].rearrange("(c p) t -> p c t", p=128)
                    for c in range(4):
                        prs.append((yin[:, n, c, :], src[:, c, :]))
                Sx.dma_group("sp", prs, reads=self.ybr_buf, writes=[yinb])
                wG = [sb(f"wG{i}", [128, 8, 4, 128], BF16) for i in range(2)]; wGb = [Buf(), Buf()]
                Wo = [sb(f"Wo{i}", [128, 4, 4, 128], BF16) for i in range(2)]; Wob = [Buf(), Buf()]
                gt = sb("gt", [128, 512]); gtb = Buf()
                acc = sb("acc", [128, 512]); accb = Buf()
                tmp = sb("tmp", [128, 512]); tmpb = Buf()
                for db in range(8):
                    i = db % 2
                    pg = []; po = []
                    for n in range(4):
                        c0 = OFF_MERGE + n * 1024 + db * 128
                        vg = d["w_in"][l, :, c0:c0 + 128].rearrange("(kc p) c -> p kc c", p=128)
                        pg.append((wG[i][:, :, n, :], vg))
                        vo = d["w_o_branch"][l, n, :, db * 128:(db + 1) * 128].rearrange("(cc p) c -> p cc c", p=128)
                        po.append((Wo[i][:, n, :, :], vo))
                    Sx.dma_group("pool", pg, writes=[wGb[i]])
                    Sx.dma_group("pool", po, writes=[Wob[i]])
                    for g in range(4):
                        t0 = g * 512
                        for n in range(4):
                            psP, pbP = Sx.next_psum()
                            for cc in range(4):
                                Sx.op("pe", lambda e: e.matmul(psP[:, :], Wo[i][:, n, cc, :], yin[:, n, cc, t0:t0 + 512], start=(cc == 0), stop=(cc == 3)),
                                      reads=[Wob[i], yinb], writes=[pbP])
                            psG, pbG = Sx.next_psum()
                            for kc in range(8):
                                Sx.op("pe", lambda e: e.matmul(psG[:, :], wG[i][:, kc, n, :], self.xT[:, kc, t0:t0 + 512], start=(kc == 0), stop=(kc == 7)),
                                      reads=[wGb[i], self.xT_b], writes=[pbG])
                            Sx.op("act", lambda e: e.activation(out=gt[:, :], in_=psG[:, :], func=AF.Sigmoid, bias=self.pc("b_gate", n * 8 + db)),
                                  reads=[pbG, pcb], writes=[gtb])
                            if n == 0:
                                Sx.op("dve", lambda e: e.tensor_tensor(out=acc[:, :], in0=psP[:, :], in1=gt[:, :], op=ALU.mult), reads=[pbP, gtb], writes=[accb])
                            else:
                                Sx.op("dve", lambda e: e.tensor_tensor(out=tmp[:, :], in0=psP[:, :], in1=gt[:, :], op=ALU.mult), reads=[pbP, gtb], writes=[tmpb])
                                if n < 3:
                                    Sx.op("pool", lambda e: e.tensor_tensor(out=acc[:, :], in0=acc[:, :], in1=tmp[:, :], op=ALU.add), reads=[accb, tmpb], writes=[accb])
                                else:
                                    Sx.op("pool", lambda e: e.tensor_tensor(out=mT[:, db, t0:t0 + 512], in0=acc[:, :], in1=tmp[:, :], op=ALU.add),
                                          reads=[accb, tmpb], writes=[mTb])
                Sx.barrier()
            with contextlib.ExitStack() as es:
                sb = lambda n, shp, dt=F32: self.sb(es, "e2_" + n, shp, dt)
                Wout = sb("Wout", [128, 8, D], BF16); Woutb = Buf()
                self.load_w_bf(Wout, Woutb, d["w_out"][l], 8)
                xt = [sb(f"xt{i}", [128, D]) for i in range(2)]; xtb = [Buf(), Buf()]
                z = [sb(f"z{i}", [128, D]) for i in range(2)]; zb = [Buf(), Buf()]
                st = sb("st", [128, 12]); stb = Buf()
                mv = sb("mv", [128, 2]); mvb = Buf()
                rs = sb("rs", [128, 1]); rsb = Buf()
                for tt in range(S // 128):
                    i = tt % 2
                    if l == 0:
                        Sx.dma("sp", xt[i][:], d["x"][s, tt * 128:(tt + 1) * 128, :], writes=[xtb[i]])
                    else:
                        Sx.dma("sp", xt[i][:], self.x1[s, tt * 128:(tt + 1) * 128, :], reads=[self.x1_buf[s]], writes=[xtb[i]])
                    for half in range(2):
                        ps, pb = Sx.next_psum()
                        for kc in range(8):
                            Sx.op("pe", lambda e: e.matmul(ps[:, :], mT[:, kc, tt * 128:(tt + 1) * 128], Wout[:, kc, half * 512:(half + 1) * 512],
                                                           start=(kc == 0), stop=(kc == 7)), reads=[mTb, Woutb], writes=[pb])
                        Sx.op("dve", lambda e: e.scalar_tensor_tensor(out=z[i][:, half * 512:(half + 1) * 512], in0=xt[i][:, half * 512:(half + 1) * 512],
                                                                      scalar=float(ALPHA), in1=ps[:, :], op0=ALU.mult, op1=ALU.add),
                              reads=[xtb[i], pb], writes=[zb[i]])
                        Sx.op("dve", lambda e: e.bn_stats(st[:, half * 6:(half + 1) * 6], z[i][:, half * 512:(half + 1) * 512]), reads=[zb[i]], writes=[stb])
                    Sx.op("dve", lambda e: e.bn_aggr(mv[:, :], st[:, :]), reads=[stb], writes=[mvb])
                    Sx.op("dve", lambda e: e.tensor_scalar(out=rs[:, :], in0=mv[:, 1:2], scalar1=1e-5, scalar2=None, op0=ALU.add), reads=[mvb], writes=[rsb])
                    Sx.op("act", lambda e: e.activation(out=rs[:, :], in_=rs[:, :], func=AF.Sqrt), reads=[rsb], writes=[rsb])
                    Sx.op("dve", lambda e: e.reciprocal(rs[:, :], rs[:, :]), reads=[rsb], writes=[rsb])
                    Sx.op("dve", lambda e: e.tensor_scalar(out=z[i][:, :], in0=z[i][:, :], scalar1=mv[:, 0:1], scalar2=rs[:, 0:1], op0=ALU.subtract, op1=ALU.mult),
                          reads=[zb[i], mvb, rsb], writes=[zb[i]])
                    Sx.op("pool", lambda e: e.tensor_tensor(out=z[i][:, :], in0=z[i][:, :], in1=self.lng_bc[:, :], op=ALU.mult), reads=[zb[i], self.ln_b_], writes=[zb[i]])
                    Sx.op("pool", lambda e: e.tensor_tensor(out=z[i][:, :], in0=z[i][:, :], in1=self.lnb_bc[:, :], op=ALU.add), reads=[zb[i], self.ln_b_], writes=[zb[i]])
                    if l == DEPTH - 1 or self.single:
                        Sx.dma("sp", self.out[s, tt * 128:(tt + 1) * 128, :], z[i][:, :], reads=[zb[i]], writes=[self.out_b])
                    else:
                        Sx.dma("sp", self.x1[s, tt * 128:(tt + 1) * 128, :], z[i][:, :], reads=[zb[i]], writes=[self.x1_buf[s]])
                        self.transpose_into_xT(z[i], zb[i], tt)
                Sx.barrier()


def _shard_inputs(inputs):
    consts = _consts_host()
    maps = []
    for c in range(NCORES):
        m = {}
        sl = slice(c * SEQ_PER_CORE, (c + 1) * SEQ_PER_CORE)
        for n, shp, dt in PARAM_SPECS:
            a = np.asarray(inputs[n])
            if n in ("x", "mem", "positions"):
                a = a[sl]
            m[n] = np.ascontiguousarray(a)
        m.update(consts)
        maps.append(m)
    return maps


_PROG = {}


FUSED = True


def kernel(**inputs):
    if FUSED:
        if "f" not in _PROG:
            _PROG["f"] = K({}).build()
        res = run_bass_kernel_spmd(_PROG["f"], _shard_inputs(inputs), core_ids=list(range(NCORES)))
        return np.concatenate([np.asarray(r["out"], dtype=np.float32) for r in res.results], axis=0)
    return kernel_unfused(**inputs)


def kernel_unfused(**inputs):
    if "p" not in _PROG:
        kb = K({"nseq": 1, "nlay": 1, "single": True, "seqs": [0], "layers": [0]})
        _PROG["p"] = kb.build()
    nc = _PROG["p"]
    consts = _consts_host()
    xs = np.asarray(inputs["x"], dtype=np.float32)
    names = [n for n, _, _ in PARAM_SPECS if n not in ("x", "mem", "positions")]
    out = np.empty_like(xs)
    for slot in range(SEQ_PER_CORE):
        cur = [np.ascontiguousarray(xs[c * SEQ_PER_CORE + slot][None]) for c in range(NCORES)]
        for l in range(DEPTH):
            wl = {n: np.ascontiguousarray(np.asarray(inputs[n])[l:l + 1]) for n in names}
            maps = []
            for c in range(NCORES):
                b = c * SEQ_PER_CORE + slot
                m = dict(wl)
                m["x"] = cur[c]
                m["mem"] = np.ascontiguousarray(np.asarray(inputs["mem"])[b:b + 1])
                m["positions"] = np.ascontiguousarray(np.asarray(inputs["positions"])[b:b + 1])
                m.update(consts)
                maps.append(m)
            res = run_bass_kernel_spmd(nc, maps, core_ids=list(range(NCORES)))
            cur = [np.ascontiguousarray(np.asarray(r["out"], dtype=np.float32)) for r in res.results]
        for c in range(NCORES):
            out[c * SEQ_PER_CORE + slot] = cur[c][0]
    return out
```

```python
import contextlib
import numpy as np
import concourse.bass as bass
import concourse.mybir as mybir
from concourse.bass_utils import run_bass_kernel_spmd

F32 = mybir.dt.float32
BF16 = mybir.dt.bfloat16
I32 = mybir.dt.int32
AF = mybir.ActivationFunctionType
ALU = mybir.AluOpType

NCORES = 8
SEQ_PER_CORE = 4
S = 2048
D = 1024
DEPTH = 2
IN_COLS = 10016
W = 512
ALPHA = (2.0 * DEPTH) ** 0.25
SEM_LIMIT = 30000
VCLOCK = False
NDQ = 12

OFF_RW = 0
OFF_QLAT = 2176
OFF_KVLAT = 2560
OFF_KPE = 2816
OFF_MGATE = 2848
OFF_CONV = 3360
OFF_XQ = 4896
OFF_MERGE = 5920

PC = {}
_o = 0
for _n, _c in [("mu", 13), ("w0", 4), ("a0", 4), ("k_k", 4), ("k_a", 4), ("r_k", 4), ("lnx_g", 4),
               ("lnx_b", 4), ("q_norm", 3), ("kv_norm", 2), ("conv_b", 4), ("conv_ln_g", 4),
               ("conv_ln_b", 4), ("b_gate", 32), ("conv_w", 124)]:
    PC[_n] = _o
    _o += _c
NPC_RAW = _o
PC["omm"] = NPC_RAW
PC["omka"] = NPC_RAW + 13
NPC = NPC_RAW + 17


class Buf:
    __slots__ = ("w", "r")

    def __init__(self):
        self.w = []
        self.r = {}


class Sched:
    def __init__(self, nc, es):
        self.nc = nc
        self.es = es
        self.eng = {"pe": nc.tensor, "act": nc.scalar, "dve": nc.vector, "pool": nc.gpsimd, "sp": nc.sync}
        self.sem = {}
        self.cnt = {}
        self.sid = {}
        self.nsem = 0
        self.waited = {k: {} for k in self.eng}
        self.latest = {}
        self.ninst = 0
        for k in self.eng:
            self._newsem(k)
        self.dq = {}
        self.dqi = {}
        for q in ("sp", "pool", "act"):
            lst = []
            for i in range(NDQ):
                s = es.enter_context(nc.semaphore(f"dq_{q}_{i}"))
                self.nsem += 1
                lst.append([s, 0, self.nsem])
            self.dq[q] = lst
            self.dqi[q] = 0
        self.psum = []
        self.psi = 0

    def _newsem(self, k):
        s = self.es.enter_context(self.nc.semaphore(f"s_{k}_{self.nsem}"))
        self.nsem += 1
        self.sem[k] = s
        self.cnt[k] = 0
        self.sid[k] = self.nsem

    def _wait(self, k, tok):
        sem, val, src, sid = tok[0], tok[1], tok[2], tok[3]
        if k == "pe" and src == "pe":
            return
        w = self.waited[k]
        if w.get(sid, 0) >= val:
            return
        self.eng[k].wait_ge(sem, val)
        self.ninst += 1
        w[sid] = val
        snap = tok[4] if (VCLOCK and len(tok) > 4) else None
        if snap:
            for a, b in snap.items():
                if w.get(a, 0) < b:
                    w[a] = b

    def _deps(self, reads, writes):
        toks = []
        for b in reads:
            toks.extend(b.w)
        for b in writes:
            toks.extend(b.w)
            toks.extend(b.r.values())
        return toks

    def _commit(self, tok, reads, writes):
        for b in reads:
            b.r[tok[3]] = tok
        for b in writes:
            b.w = [tok]
            b.r = {}
        self.latest[tok[3]] = tok

    def dma_group(self, q, pairs, reads=(), writes=()):
        deps = self._deps(reads, writes)
        toks = []
        lim = 1 if q == "pool" else 4
        for (out, in_) in pairs:
            for t in deps:
                self._wait(q, t)
            if len(toks) >= lim:
                self._wait(q, toks[len(toks) - lim])
            i = self.dqi[q]
            self.dqi[q] = (i + 1) % NDQ
            ent = self.dq[q][i]
            if ent[1] > 0:
                self._wait(q, (ent[0], 16 * ent[1], "dma", ent[2]))
            self.eng[q].dma_start(out=out, in_=in_).then_inc(ent[0], 16)
            self.ninst += 1
            ent[1] += 1
            tok = (ent[0], 16 * ent[1], "dma", ent[2], dict(self.waited[q]))
            toks.append(tok)
            self.latest[tok[3]] = tok
        for b in reads:
            for tok in toks:
                b.r[tok[3]] = tok
        for b in writes:
            b.w = list(toks)
            b.r = {}

    def op(self, k, fn, reads=(), writes=()):
        for t in self._deps(reads, writes):
            self._wait(k, t)
        if self.cnt[k] >= SEM_LIMIT:
            self._newsem(k)
        inst = fn(self.eng[k])
        self.cnt[k] += 1
        self.ninst += 1
        inst.then_inc(self.sem[k], 1)
        snap = dict(self.waited[k])
        if k != "pe":
            snap[self.sid[k]] = self.cnt[k] - 1
        tok = (self.sem[k], self.cnt[k], k, self.sid[k], snap)
        self._commit(tok, reads, writes)
        return tok

    def dma(self, q, out, in_, reads=(), writes=()):
        for t in self._deps(reads, writes):
            self._wait(q, t)
        i = self.dqi[q]
        self.dqi[q] = (i + 1) % NDQ
        ent = self.dq[q][i]
        if ent[1] > 0:
            self._wait(q, (ent[0], 16 * ent[1], "dma", ent[2]))
        self.eng[q].dma_start(out=out, in_=in_).then_inc(ent[0], 16)
        self.ninst += 1
        ent[1] += 1
        tok = (ent[0], 16 * ent[1], "dma", ent[2], dict(self.waited[q]))
        self._commit(tok, reads, writes)
        return tok

    def barrier(self, engines=("pe", "act", "dve", "pool", "sp")):
        toks = list(self.latest.values())
        for k in engines:
            for t in toks:
                self._wait(k, t)

    def next_psum(self, n=8):
        self.psi = (self.psi + 1) % n
        return self.psum[self.psi]


def _consts_host():
    c = {}
    c["c_ident"] = np.eye(128, dtype=np.float32)
    bo = np.zeros((128, 128), np.float32)
    bo[:64, :64] = 1.0
    bo[64:, 64:] = 1.0
    c["c_bo"] = bo
    i = np.arange(64)
    strict = (i[:, None] < i[None, :]).astype(np.float32)
    incl = (i[:, None] <= i[None, :]).astype(np.float32)
    lower = (i[None, :] < i[:, None]).astype(np.float32)
    mA = np.zeros((128, 3, 2, 64), np.float32)
    for h in range(2):
        mA[h * 64:(h + 1) * 64, 0, h, :] = strict
        mA[h * 64:(h + 1) * 64, 1, h, :] = strict
        mA[h * 64:(h + 1) * 64, 2, h, :] = lower
    c["c_maskA"] = mA.reshape(128, 384)
    mB = np.zeros((128, 2, 64), np.float32)
    for h in range(2):
        mB[h * 64:(h + 1) * 64, :, :] = incl[:, None, :]
    c["c_maskB"] = mB.reshape(128, 128)
    mbd = np.zeros((128, 2), np.float32)
    mbd[:64, 0] = 1.0
    mbd[64:, 1] = 1.0
    c["c_mbd"] = mbd
    cm = np.ones((128, 128), np.float32)
    cm[:, 0] = 0.0
    cm[:, 64] = 0.0
    c["c_cmask"] = cm
    k = np.arange(128)[:, None]
    q = np.arange(512)[None, :]
    mm = np.stack([(q >= v * 128 + k) for v in range(4)], axis=1).astype(np.float32)
    c["c_cmla"] = mm.reshape(128, 2048)
    inv = (10000.0 ** (-np.arange(0, 32, 2, dtype=np.float32) / 32.0)).astype(np.float32)
    rp = np.zeros((128, 2), np.float32)
    rp[64:96, 0] = np.concatenate([inv, inv])
    rp[64:96, 1] = np.concatenate([-np.ones(16, np.float32), np.ones(16, np.float32)])
    c["c_rope"] = rp
    c["c_ones"] = np.ones((128, 128), np.float32)
    return c


CONST_SHAPES = {k: v.shape for k, v in _consts_host().items()}

def param_specs(SEQ_PER_CORE, DEPTH):
  return [
    ("x", [SEQ_PER_CORE, S, D], F32), ("mem", [SEQ_PER_CORE, 256, D], F32), ("positions", [SEQ_PER_CORE, S], I32),
    ("w_in", [DEPTH, D, IN_COLS], F32), ("b_gate", [DEPTH, 4, D], F32), ("rwkv_mu", [DEPTH, 1664], F32),
    ("rwkv_w0", [DEPTH, W], F32), ("rwkv_w2", [DEPTH, 64, W], F32), ("rwkv_a0", [DEPTH, W], F32),
    ("rwkv_a2", [DEPTH, 64, W], F32), ("rwkv_k_k", [DEPTH, W], F32), ("rwkv_k_a", [DEPTH, W], F32),
    ("rwkv_r_k", [DEPTH, 8, 64], F32), ("rwkv_lnx_g", [DEPTH, W], F32), ("rwkv_lnx_b", [DEPTH, W], F32),
    ("mla_q_norm", [DEPTH, 384], F32), ("mla_w_uq", [DEPTH, 384, 768], F32), ("mla_kv_norm", [DEPTH, 256], F32),
    ("mla_w_ukv", [DEPTH, 256, 1024], F32), ("conv_w", [DEPTH, 31, W], F32), ("conv_b", [DEPTH, W], F32),
    ("conv_ln_g", [DEPTH, W], F32), ("conv_ln_b", [DEPTH, W], F32), ("xattn_w_mem_kv", [DEPTH, D, 2 * W], F32),
    ("w_o_branch", [DEPTH, 4, W, D], F32), ("w_out", [DEPTH, D, D], F32), ("ln_g", [DEPTH, D], F32),
    ("ln_b", [DEPTH, D], F32),
  ]


PARAM_SPECS = param_specs(SEQ_PER_CORE, DEPTH)


class K:
    def __init__(self, cfg):
        self.cfg = cfg
        self.nc = bass.Bass("TRN2", target_bir_lowering=False)
        nc = self.nc
        self.din = {}
        nseq = cfg.get("nseq", SEQ_PER_CORE)
        nlay = cfg.get("nlay", DEPTH)
        self.single = cfg.get("single", False)
        for n, shp, dt in param_specs(nseq, nlay):
            self.din[n] = nc.dram_tensor(n, shp, dt, kind="ExternalInput").ap()
        for n, shp in CONST_SHAPES.items():
            self.din[n] = nc.dram_tensor(n, list(shp), F32, kind="ExternalInput").ap()
        self.out = nc.dram_tensor("out", [nseq, S, D], F32, kind="ExternalOutput").ap()
        dbg = cfg.get("debug", False)
        kind = "ExternalOutput" if dbg else "Internal"
        self.ybr = nc.dram_tensor("ybr", [4, W, S], BF16, kind=kind).ap()
        self.ybr_buf = [Buf() for _ in range(4)]
        self.x1 = nc.dram_tensor("x1s", [nseq, S, D], F32, kind=kind).ap()
        self.x1_buf = [Buf() for _ in range(SEQ_PER_CORE)]

    def sb(self, es, name, shape, dt):
        self._uid = getattr(self, "_uid", 0) + 1
        return es.enter_context(self.nc.sbuf_tensor(f"{name}_{self._uid}", shape, dt))

    def build(self):
        nc = self.nc
        with contextlib.ExitStack() as es:
            self.S = Sched(nc, es)
            Sx = self.S
            for i in range(8):
                t = es.enter_context(nc.psum_tensor(f"ps{i}", [128, 512], F32))
                Sx.psum.append((t, Buf()))
            self.setup_consts(es)
            self.xT = self.sb(es, "xT", [128, 8, S], BF16)
            self.xT_b = Buf()
            self.ropeC = self.sb(es, "ropeC", [128, S], F32)
            self.ropeS = self.sb(es, "ropeS", [128, S], F32)
            self.rope_b = Buf()
            self.memT = self.sb(es, "memT", [128, 8, 256], BF16)
            self.memT_b = Buf()
            self.out_b = Buf()
            phases = self.cfg.get("phases", "RMCXE")
            seqs = self.cfg.get("seqs", list(range(SEQ_PER_CORE)))
            layers = self.cfg.get("layers", list(range(DEPTH)))
            for s in seqs:
                self.load_xT(s)
                if "M" in phases:
                    self.rope_tables(s)
                if "X" in phases:
                    self.load_memT(s)
                for l in layers:
                    if l == 0 or True:
                        self.load_params(l)
                    if "R" in phases:
                        self.phase_rwkv(s, l)
                    if "M" in phases:
                        self.phase_mla(s, l)
                    if "C" in phases:
                        self.phase_conv(s, l)
                    if "X" in phases:
                        self.phase_xattn(s, l)
                    if "E" in phases:
                        self.phase_epi(s, l)
            Sx.barrier(engines=("sp",))
        return nc

    def setup_consts(self, es):
        Sx = self.S
        self.cb = Buf()
        c = {}
        for n, shp in CONST_SHAPES.items():
            if n == "c_cmla":
                continue
            t = self.sb(es, "k_" + n, list(shp), F32)
            Sx.dma("sp", t[:], self.din[n], writes=[self.cb])
            c[n] = t
        self.c = c
        self.ident = c["c_ident"]
        self.bo = c["c_bo"]
        self.ident_bf = self.sb(es, "ident_bf", [128, 128], BF16)
        self.ones_bf = self.sb(es, "ones_bf", [128, 128], BF16)
        self.cmla_bf = self.sb(es, "cmla_bf", [128, 2048], BF16)
        Sx.op("pool", lambda e: e.tensor_copy(self.ident_bf[:], self.ident[:]), reads=[self.cb], writes=[self.cb])
        Sx.op("pool", lambda e: e.memset(self.ones_bf[:], 1.0), writes=[self.cb])
        with contextlib.ExitStack() as es2:
            tmpc = self.sb(es2, "k_cmla_tmp", [128, 2048], F32)
            Sx.dma("sp", tmpc[:], self.din["c_cmla"], writes=[self.cb])
            Sx.op("pool", lambda e: e.tensor_copy(self.cmla_bf[:], tmpc[:]), reads=[self.cb], writes=[self.cb])
            Sx.barrier()
        self.pcol = self.sb(es, "pcol", [128, NPC], F32)
        self.pcol_b = Buf()
        self.stageA = self.sb(es, "stageA", [128, 128], F32)
        self.stageB = self.sb(es, "stageB", [128, 128], F32)
        self.stage_b = Buf()
        self.lng_bc = self.sb(es, "lng_bc", [128, D], F32)
        self.lnb_bc = self.sb(es, "lnb_bc", [128, D], F32)
        self.ln_b_ = Buf()
        self.ones_row = self.sb(es, "ones_row", [1, 128], F32)
        Sx.op("pool", lambda e: e.memset(self.ones_row[:], 1.0), writes=[self.cb])

    def pc(self, name, j=0):
        i = PC[name] + j
        return self.pcol[:, i:i + 1]

    def load_params(self, l):
        Sx = self.S
        d = self.din
        rows = []

        def vec(name, key):
            ap = d[key][l]
            n = 1
            for s_ in ap.shape:
                n *= s_
            rows.append((PC[name], n // 128, ap))

        vec("mu", "rwkv_mu"); vec("w0", "rwkv_w0"); vec("a0", "rwkv_a0"); vec("k_k", "rwkv_k_k")
        vec("k_a", "rwkv_k_a"); vec("r_k", "rwkv_r_k"); vec("lnx_g", "rwkv_lnx_g"); vec("lnx_b", "rwkv_lnx_b")
        vec("q_norm", "mla_q_norm"); vec("kv_norm", "mla_kv_norm"); vec("conv_b", "conv_b")
        vec("conv_ln_g", "conv_ln_g"); vec("conv_ln_b", "conv_ln_b"); vec("b_gate", "b_gate"); vec("conv_w", "conv_w")
        for (c0, nr, ap) in rows:
            if len(ap.shape) == 2:
                if ap.shape[1] == 64:
                    flat = ap.rearrange("h n -> (h n)")
                    src = flat.rearrange("(r p) -> r p", p=128)
                elif ap.shape[0] == 31:
                    src = ap.rearrange("j (c p) -> (j c) p", p=128)
                else:
                    src = ap.rearrange("n (c p) -> (n c) p", p=128)
            else:
                src = ap.rearrange("(r p) -> r p", p=128)
            r = 0
            while r < nr:
                g = c0 + r
                if g < 128:
                    n = min(nr - r, 128 - g)
                    Sx.dma("sp", self.stageA[g:g + n, :], src[r:r + n, :], writes=[self.stage_b])
                else:
                    n = nr - r
                    Sx.dma("sp", self.stageB[g - 128:g - 128 + n, :], src[r:r + n, :], writes=[self.stage_b])
                r += n
        nb = NPC_RAW - 128
        ps, pb = Sx.next_psum()
        Sx.op("pe", lambda e: e.matmul(ps[:, 0:128], self.stageA[:, :], self.ident[:, :], start=True, stop=True),
              reads=[self.stage_b, self.cb], writes=[pb])
        Sx.op("pe", lambda e: e.matmul(ps[:, 128:128 + nb], self.stageB[0:nb, :], self.ident[0:nb, 0:nb], start=True, stop=True),
              reads=[self.stage_b, self.cb], writes=[pb])
        Sx.op("act", lambda e: e.activation(out=self.pcol[:, 0:NPC_RAW], in_=ps[:, 0:NPC_RAW], func=AF.Copy),
              reads=[pb], writes=[self.pcol_b])
        o = PC["omm"]
        Sx.op("dve", lambda e: e.tensor_scalar(out=self.pcol[:, o:o + 13], in0=self.pcol[:, 0:13], scalar1=-1.0, scalar2=1.0,
                                               op0=ALU.mult, op1=ALU.add), reads=[self.pcol_b], writes=[self.pcol_b])
        o2 = PC["omka"]
        ka = PC["k_a"]
        Sx.op("dve", lambda e: e.tensor_scalar(out=self.pcol[:, o2:o2 + 4], in0=self.pcol[:, ka:ka + 4], scalar1=-1.0, scalar2=1.0,
                                               op0=ALU.mult, op1=ALU.add), reads=[self.pcol_b], writes=[self.pcol_b])
        es3 = contextlib.ExitStack()
        self.lnrow = self.sb(es3, "lnrow", [1, 2 * D], F32)
        Sx.dma("sp", self.lnrow[0:1, 0:D], d["ln_g"][l:l + 1, :], writes=[self.ln_b_])
        Sx.dma("sp", self.lnrow[0:1, D:2 * D], d["ln_b"][l:l + 1, :], writes=[self.ln_b_])
        for j, dst in enumerate((self.lng_bc, self.lnb_bc)):
            for hh in range(2):
                ps, pb = Sx.next_psum()
                Sx.op("pe", lambda e, ps=ps, j=j, hh=hh: e.matmul(ps[:, :], self.ones_row[0:1, :],
                                                                  self.lnrow[0:1, j * D + hh * 512:j * D + hh * 512 + 512],
                                                                  start=True, stop=True),
                      reads=[self.ln_b_, self.cb], writes=[pb])
                Sx.op("act", lambda e, ps=ps, dst=dst, hh=hh: e.activation(out=dst[:, hh * 512:(hh + 1) * 512], in_=ps[:, :], func=AF.Copy),
                      reads=[pb], writes=[self.ln_b_])
        Sx.barrier()
        es3.close()

    def load_xT(self, s):
        Sx = self.S
        with contextlib.ExitStack() as es:
            xt = [self.sb(es, f"xtok{i}", [128, D], F32) for i in range(2)]
            xb = [Buf(), Buf()]
            for tt in range(S // 128):
                i = tt % 2
                Sx.dma("sp", xt[i][:], self.din["x"][s, tt * 128:(tt + 1) * 128, :], writes=[xb[i]])
                self.transpose_into_xT(xt[i], xb[i], tt)
            Sx.barrier()

    def transpose_into_xT(self, xtok, xbuf, tt):
        Sx = self.S
        for half in range(2):
            ps, pb = Sx.next_psum()
            for j in range(4):
                kc = half * 4 + j
                Sx.op("pe", lambda e, ps=ps, j=j, kc=kc: e.matmul(ps[:, j * 128:(j + 1) * 128], xtok[:, kc * 128:(kc + 1) * 128],
                                                                  self.ident[:, :], start=True, stop=True),
                      reads=[xbuf, self.cb], writes=[pb])
            out = self.xT[:, half * 4:(half + 1) * 4, tt * 128:(tt + 1) * 128]
            Sx.op("act" if half == 0 else "dve",
                  (lambda e, ps=ps, out=out: e.activation(out=out, in_=ps[:, :].rearrange("p (j t) -> p j t", j=4), func=AF.Copy)) if half == 0 else
                  (lambda e, ps=ps, out=out: e.tensor_copy(out, ps[:, :].rearrange("p (j t) -> p j t", j=4))),
                  reads=[pb], writes=[self.xT_b])

    def load_w_bf(self, dst, dst_buf, src, nkc):
        Sx = self.S
        v = src.rearrange("(kc p) c -> p kc c", p=128)
        Sx.dma_group("pool", [(dst[:, kc, :], v[:, kc, :]) for kc in range(nkc)], writes=[dst_buf])

    def phase_rwkv(self, s, l):
        Sx = self.S
        nc = self.nc
        d = self.din
        T = 128
        with contextlib.ExitStack() as es:
            sb = lambda n, shp, dt=F32: self.sb(es, "rw_" + n, shp, dt)
            wR = sb("wR", [128, 8, 2176], BF16); wRb = Buf()
            self.load_w_bf(wR, wRb, d["w_in"][l, :, OFF_RW:OFF_RW + 2176], 8)
            W2z = sb("W2z", [128, 512]); A2z = sb("A2z", [128, 512]); lb = Buf()
            Sx.op("pool", lambda e: e.memset(W2z[:], 0.0), writes=[lb])
            Sx.op("pool", lambda e: e.memset(A2z[:], 0.0), writes=[lb])
            Sx.dma("sp", W2z[0:64, :], d["rwkv_w2"][l], writes=[lb])
            Sx.dma("sp", A2z[64:128, :], d["rwkv_a2"][l], writes=[lb])
            p_raw = sb("p_raw", [128, 13, T + 1]); prb = Buf()
            Sx.op("pool", lambda e: e.memset(p_raw[:], 0.0), writes=[prb])
            pm = sb("pm", [128, 13, T]); pmc = [Buf() for _ in range(13)]
            pm_r = pmc[0:4]; pm_k = pmc[4:8]; pm_v = pmc[8:12]
            sgate = sb("sgate", [128, 4, T]); sgb = Buf()
            T12 = sb("T12", [128, T]); t12b = Buf()
            names = ["lw", "logP", "asig", "eP", "eN", "ePm", "kk", "kkn", "kp", "rT", "t1", "t2", "t3"]
            tt_ = {n: sb(n, [128, 4, T]) for n in names}
            tb = {n: Buf() for n in names}
            Z = {n: sb("Z" + n, [128, 4, 2, 128]) for n in "abkv"}
            Zb_ = {n: Buf() for n in "abkv"}
            H = [sb(f"H{p}", [128, 128]) for p in range(4)]
            Hb = [Buf() for _ in range(4)]
            for p in range(4):
                Sx.op("pool", lambda e, p=p: e.memset(H[p][:], 0.0), writes=[Hb[p]])
            NSET = self.cfg.get("nset", 4)
            A_sb = [sb(f"A{i}", [128, 384]) for i in range(NSET)]; Ab = [Buf() for _ in range(NSET)]
            R_sb = [sb(f"R{i}", [128, 128]) for i in range(NSET)]; Rb = [Buf() for _ in range(NSET)]
            BKV = [sb(f"BKV{i}", [128, 384]) for i in range(NSET)]; BKVb = [Buf() for _ in range(NSET)]
            Wt = [[sb(f"W{i}_{j}", [128, 128]) for j in range(2)] for i in range(NSET)]
            Wtb = [[Buf() for j in range(2)] for i in range(NSET)]
            MP = [[sb(f"MP{i}_{j}", [128, 256]) for j in range(2)] for i in range(NSET)]
            MPb = [[Buf() for j in range(2)] for i in range(NSET)]
            X_sb = [sb(f"X{i}", [128, 128]) for i in range(NSET)]; Xb = [Buf() for _ in range(NSET)]
            U_sb = [sb(f"U{i}", [128, 128]) for i in range(NSET)]; Ub = [Buf() for _ in range(NSET)]
            HpC = [sb(f"HpC{i}", [128, 128]) for i in range(NSET)]; HpCb = [Buf() for _ in range(NSET)]
            Ycm = sb("Ycm", [128, 4, T]); Ybp = [Buf() for _ in range(4)]
            ybf = sb("ybf", [128, 4, T], BF16); ybb = Buf()
            cb = self.cb
            ident = self.ident
            maskA = self.c["c_maskA"]; maskB = self.c["c_maskB"]; mbd = self.c["c_mbd"]; cmask = self.c["c_cmask"]
            pcb = self.pcol_b
            ydst = self.ybr[0].rearrange("(c p) t -> p c t", p=128)

            def flat(t):
                return t[:, :, :].rearrange("p c t -> p (c t)")

            for blk in range(self.cfg.get('nblk', S // T)):
                t0 = blk * T
                for cbk in range(17):
                    ps, pb = Sx.next_psum()
                    for kc in range(8):
                        Sx.op("pe", lambda e, ps=ps, kc=kc, cbk=cbk: e.matmul(
                            ps[:, 0:T], wR[:, kc, cbk * 128:(cbk + 1) * 128], self.xT[:, kc, t0:t0 + T],
                            start=(kc == 0), stop=(kc == 7)), reads=[wRb, self.xT_b], writes=[pb])
                    if cbk < 13:
                        Sx.op("act", lambda e, ps=ps, cbk=cbk: e.activation(out=p_raw[:, cbk, 1:T + 1], in_=ps[:, 0:T], func=AF.Copy),
                              reads=[pb], writes=[prb])
                    else:
                        Sx.op("act", lambda e, ps=ps, cbk=cbk: e.activation(out=sgate[:, cbk - 13, :], in_=ps[:, 0:T], func=AF.Silu),
                              reads=[pb], writes=[sgb])
                for cbk in range(13):
                    Sx.op("pool", lambda e, cbk=cbk: e.tensor_scalar(out=pm[:, cbk, :], in0=p_raw[:, cbk, 1:T + 1],
                                                                     scalar1=self.pc("omm", cbk), scalar2=None, op0=ALU.mult),
                          reads=[prb, pcb], writes=[pmc[cbk]])
                    Sx.op("dve", lambda e, cbk=cbk: e.scalar_tensor_tensor(out=pm[:, cbk, :], in0=p_raw[:, cbk, 0:T],
                                                                           scalar=self.pc("mu", cbk), in1=pm[:, cbk, :],
                                                                           op0=ALU.mult, op1=ALU.add),
                          reads=[prb, pcb], writes=[pmc[cbk]])
                Sx.op("pool", lambda e: e.tensor_copy(p_raw[:, :, 0:1], p_raw[:, :, T:T + 1]), reads=[prb], writes=[prb])
                r_ = pm[:, 0:4, :]; k_ = pm[:, 4:8, :]; v_ = pm[:, 8:12, :]
                if self.cfg.get("stop_after", 9) < 1:
                    continue
                Sx.op("act", lambda e: e.activation(out=T12[0:64, :], in_=pm[0:64, 12, :], func=AF.Tanh), reads=[pmc[12]], writes=[t12b])
                Sx.op("dve", lambda e: e.tensor_copy(T12[64:128, :], pm[64:128, 12, :]), reads=[pmc[12]], writes=[t12b])
                for c4 in range(4):
                    ps, pb = Sx.next_psum()
                    Sx.op("pe", lambda e, ps=ps, c4=c4: e.matmul(ps[:, 0:T], W2z[:, c4 * 128:(c4 + 1) * 128], T12[:, :], start=True, stop=True),
                          reads=[lb, t12b], writes=[pb])
                    Sx.op("pe", lambda e, ps=ps, c4=c4: e.matmul(ps[:, T:2 * T], A2z[:, c4 * 128:(c4 + 1) * 128], T12[:, :], start=True, stop=True),
                          reads=[lb, t12b], writes=[pb])
                    Sx.op("act", lambda e, ps=ps, c4=c4: e.activation(out=tt_["lw"][:, c4, :], in_=ps[:, 0:T], func=AF.Sigmoid,
                                                                      bias=self.pc("w0", c4)), reads=[pb, pcb], writes=[tb["lw"]])
                    Sx.op("act", lambda e, ps=ps, c4=c4: e.activation(out=tt_["asig"][:, c4, :], in_=ps[:, T:2 * T], func=AF.Sigmoid,
                                                                      bias=self.pc("a0", c4)), reads=[pb, pcb], writes=[tb["asig"]])
                Sx.op("pool", lambda e: e.tensor_scalar(out=flat(tt_["lw"]), in0=flat(tt_["lw"]), scalar1=-0.6065306597126334,
                                                        scalar2=None, op0=ALU.mult), reads=[tb["lw"]], writes=[tb["lw"]])
                for c4 in range(4):
                    Sx.op("dve", lambda e, c4=c4: e.tensor_tensor_scan(out=tt_["logP"][:, c4, :], data0=cmask[:, 0:T], data1=tt_["lw"][:, c4, :],
                                                                      initial=0.0, op0=ALU.mult, op1=ALU.add),
                          reads=[tb["lw"], cb], writes=[tb["logP"]])
                Sx.op("act", lambda e: e.activation(out=flat(tt_["eP"]), in_=flat(tt_["logP"]), func=AF.Exp), reads=[tb["logP"]], writes=[tb["eP"]])
                Sx.op("act", lambda e: e.activation(out=flat(tt_["eN"]), in_=flat(tt_["logP"]), func=AF.Exp, scale=-1.0),
                      reads=[tb["logP"]], writes=[tb["eN"]])
                Sx.op("pool", lambda e: e.tensor_tensor(out=flat(tt_["t1"]), in0=flat(tt_["logP"]), in1=flat(tt_["lw"]), op=ALU.subtract),
                      reads=[tb["logP"], tb["lw"]], writes=[tb["t1"]])
                Sx.op("act", lambda e: e.activation(out=flat(tt_["ePm"]), in_=flat(tt_["t1"]), func=AF.Exp), reads=[tb["t1"]], writes=[tb["ePm"]])
                for c4 in range(4):
                    Sx.op("pool", lambda e, c4=c4: e.tensor_scalar(out=tt_["kk"][:, c4, :], in0=k_[:, c4, :], scalar1=self.pc("k_k", c4),
                                                                   scalar2=None, op0=ALU.mult), reads=[pm_k[c4], pcb], writes=[tb["kk"]])
                Sx.op("act", lambda e: e.activation(out=flat(tt_["t2"]), in_=flat(tt_["kk"]), func=AF.Square), reads=[tb["kk"]], writes=[tb["t2"]])
                ps, pb = Sx.next_psum()
                Sx.op("pe", lambda e, ps=ps: e.matmul(ps[:, 0:4 * T], self.bo[:, :], flat(tt_["t2"]), start=True, stop=True),
                      reads=[tb["t2"], cb], writes=[pb])
                Sx.op("act", lambda e, ps=ps: e.activation(out=flat(tt_["t3"]), in_=ps[:, 0:4 * T], func=AF.Sqrt), reads=[pb], writes=[tb["t3"]])
                Sx.op("dve", lambda e: e.tensor_scalar(out=flat(tt_["t3"]), in0=flat(tt_["t3"]), scalar1=1e-12, scalar2=None, op0=ALU.max),
                      reads=[tb["t3"]], writes=[tb["t3"]])
                Sx.op("dve", lambda e: e.reciprocal(flat(tt_["t2"]), flat(tt_["t3"])), reads=[tb["t3"]], writes=[tb["t2"]])
                Sx.op("pool", lambda e: e.tensor_tensor(out=flat(tt_["kkn"]), in0=flat(tt_["kk"]), in1=flat(tt_["t2"]), op=ALU.mult),
                      reads=[tb["kk"], tb["t2"]], writes=[tb["kkn"]])
                for c4 in range(4):
                    Sx.op("act", lambda e, c4=c4: e.activation(out=tt_["t3"][:, c4, :], in_=tt_["asig"][:, c4, :], func=AF.Identity,
                                                               scale=self.pc("k_a", c4), bias=self.pc("omka", c4)),
                          reads=[tb["asig"], pcb], writes=[tb["t3"]])
                Sx.op("pool", lambda e: e.tensor_tensor(out=flat(tt_["kp"]), in0=k_.rearrange("p c t -> p (c t)"), in1=flat(tt_["t3"]), op=ALU.mult),
                      reads=pm_k + [tb["t3"]], writes=[tb["kp"]])
                Sx.op("dve", lambda e: e.scalar_tensor_tensor(out=flat(tt_["t1"]), in0=flat(tt_["kkn"]), scalar=-1.0, in1=flat(tt_["ePm"]),
                                                              op0=ALU.mult, op1=ALU.mult), reads=[tb["kkn"], tb["ePm"]], writes=[tb["t1"]])
                Sx.op("pool", lambda e: e.tensor_tensor(out=flat(tt_["t2"]), in0=flat(tt_["kkn"]), in1=flat(tt_["asig"]), op=ALU.mult),
                      reads=[tb["kkn"], tb["asig"]], writes=[tb["t2"]])
                Sx.op("pool", lambda e: e.tensor_tensor(out=flat(tt_["t2"]), in0=flat(tt_["t2"]), in1=flat(tt_["eN"]), op=ALU.mult),
                      reads=[tb["t2"], tb["eN"]], writes=[tb["t2"]])
                Sx.op("dve", lambda e: e.tensor_tensor(out=flat(tt_["t3"]), in0=flat(tt_["kp"]), in1=flat(tt_["eN"]), op=ALU.mult),
                      reads=[tb["kp"], tb["eN"]], writes=[tb["t3"]])
                Sx.op("dve", lambda e: e.tensor_tensor(out=flat(tt_["rT"]), in0=r_.rearrange("p c t -> p (c t)"), in1=flat(tt_["eP"]), op=ALU.mult),
                      reads=pm_r + [tb["eP"]], writes=[tb["rT"]])
                mb4 = mbd[:, 0:2].unsqueeze(1).unsqueeze(3).to_broadcast([128, 8, 2, 64])
                for zi, (zn, srcap, srcb) in enumerate([("a", tt_["t1"], tb["t1"]), ("b", tt_["t2"], tb["t2"]), ("k", tt_["t3"], tb["t3"]),
                                                        ("v", None, None)]):
                    if srcap is None:
                        sview = v_.rearrange("p c (h t) -> p (c h) t", h=2)
                    else:
                        sview = srcap[:, :, :].rearrange("p c (h t) -> p (c h) t", h=2)
                    in0 = sview.unsqueeze(2).to_broadcast([128, 8, 2, 64])
                    outv = Z[zn][:, :, :, :].rearrange("p c h (g t) -> p (c h) g t", g=2)
                    Sx.op("dve" if zi % 2 == 0 else "pool",
                          lambda e, outv=outv, in0=in0: e.tensor_tensor(out=outv, in0=in0, in1=mb4, op=ALU.mult),
                          reads=(pm_v if srcb is None else [srcb]) + [cb], writes=[Zb_[zn]])
                if self.cfg.get("stop_after", 9) < 2:
                    continue
                for ch in range(2):
                    U_ = []
                    for pr in range(4):
                        U_.append(dict(Za=Z["a"][:, pr, ch, :], Zb=Z["b"][:, pr, ch, :], Zk=Z["k"][:, pr, ch, :], Zv=Z["v"][:, pr, ch, :],
                                       rTu=tt_["rT"][:, pr, ch * 64:(ch + 1) * 64], pC=tt_["eP"][:, pr, ch * 64 + 63:ch * 64 + 64]))
                    for PRS in self.cfg.get('pr_groups', [[0, 1, 2, 3]]):
                        for pr in PRS:
                            u = U_[pr]; si = pr % NSET
                            Za, Zb, Zk, Zv, rTu = u["Za"], u["Zb"], u["Zk"], u["Zv"], u["rTu"]
                            psA, pbA = Sx.next_psum()
                            Sx.op("pe", lambda e: e.matmul(psA[:, 0:128], Zb, Za, start=True, stop=True), reads=[Zb_["b"], Zb_["a"]], writes=[pbA])
                            Sx.op("pe", lambda e: e.matmul(psA[:, 128:256], Zk, Za, start=True, stop=True), reads=[Zb_["k"], Zb_["a"]], writes=[pbA])
                            Sx.op("pe", lambda e: e.matmul(psA[:, 256:384], Za, Zb, start=True, stop=True), reads=[Zb_["b"], Zb_["a"]], writes=[pbA])
                            Sx.op("dve", lambda e: e.tensor_tensor(out=A_sb[si][:, :], in0=psA[:, 0:384], in1=maskA[:, :], op=ALU.mult),
                                  reads=[pbA, cb], writes=[Ab[si]])
                            psB, pbB = Sx.next_psum()
                            Sx.op("pe", lambda e: e.matmul(psB[:, 0:64], Zb, rTu, start=True, stop=True), reads=[Zb_["b"], tb["rT"]], writes=[pbB])
                            Sx.op("pe", lambda e: e.matmul(psB[:, 64:128], Zk, rTu, start=True, stop=True), reads=[Zb_["k"], tb["rT"]], writes=[pbB])
                            Sx.op("dve", lambda e: e.tensor_tensor(out=R_sb[si][:, :], in0=psB[:, 0:128], in1=maskB[:, :], op=ALU.mult),
                                  reads=[pbB, cb], writes=[Rb[si]])
                        for pr in PRS:
                            u = U_[pr]; si = pr % NSET
                            Za, Zb, Zk, Zv, rTu = u["Za"], u["Zb"], u["Zk"], u["Zv"], u["rTu"]
                            psC, pbC = Sx.next_psum()
                            Sx.op("pe", lambda e: e.matmul(psC[:, 0:128], Zb, ident[:, :], start=True, stop=True), reads=[Zb_["b"], cb], writes=[pbC])
                            Sx.op("pe", lambda e: e.matmul(psC[:, 128:256], Zk, ident[:, :], start=True, stop=True), reads=[Zb_["k"], cb], writes=[pbC])
                            Sx.op("pe", lambda e: e.matmul(psC[:, 256:384], Zv, ident[:, :], start=True, stop=True), reads=[Zb_["v"], cb], writes=[pbC])
                            Sx.op("act", lambda e: e.activation(out=BKV[si][:, :], in_=psC[:, 0:384], func=AF.Copy), reads=[pbC], writes=[BKVb[si]])
                            Sx.op("pool", lambda e: e.tensor_tensor(out=Wt[si][0][:, :], in0=A_sb[si][:, 0:128], in1=ident[:, :], op=ALU.add),
                                  reads=[Ab[si], cb], writes=[Wtb[si][0]])
                        Mp = {pr: A_sb[pr % NSET][:, 0:128] for pr in PRS}
                        Pp = {pr: A_sb[pr % NSET][:, 256:384] for pr in PRS}
                        mpb = {pr: Ab[pr % NSET] for pr in PRS}
                        for j in range(1, 6):
                            cur = j % 2
                            for pr in PRS:
                                si = pr % NSET
                                psN, pbN = Sx.next_psum()
                                Sx.op("pe", lambda e: e.matmul(psN[:, 0:128], Pp[pr], Mp[pr], start=True, stop=True), reads=[mpb[pr]], writes=[pbN])
                                Sx.op("pe", lambda e: e.matmul(psN[:, 128:256], Mp[pr], Pp[pr], start=True, stop=True), reads=[mpb[pr]], writes=[pbN])
                                Sx.op("act", lambda e: e.activation(out=MP[si][cur][:, :], in_=psN[:, 0:256], func=AF.Copy), reads=[pbN], writes=[MPb[si][cur]])
                                Mp[pr] = MP[si][cur][:, 0:128]; Pp[pr] = MP[si][cur][:, 128:256]; mpb[pr] = MPb[si][cur]
                            for pr in PRS:
                                si = pr % NSET
                                psW, pbW = Sx.next_psum()
                                Sx.op("pe", lambda e: e.matmul(psW[:, 0:128], Pp[pr], Wt[si][(j - 1) % 2][:, :], start=True, stop=True),
                                      reads=[mpb[pr], Wtb[si][(j - 1) % 2]], writes=[pbW])
                                Sx.op("dve", lambda e: e.tensor_tensor(out=Wt[si][j % 2][:, :], in0=psW[:, 0:128], in1=Wt[si][(j - 1) % 2][:, :], op=ALU.add),
                                      reads=[pbW, Wtb[si][(j - 1) % 2]], writes=[Wtb[si][j % 2]])
                        for pr in PRS:
                            u = U_[pr]; si = pr % NSET
                            psX, pbX = Sx.next_psum()
                            Sx.op("pe", lambda e: e.matmul(psX[:, 0:128], u["Za"], H[pr][:, :], start=True, stop=False), reads=[Zb_["a"], Hb[pr]], writes=[pbX])
                            Sx.op("pe", lambda e: e.matmul(psX[:, 0:128], A_sb[si][:, 128:256], BKV[si][:, 256:384], start=False, stop=True),
                                  reads=[Ab[si], BKVb[si]], writes=[pbX])
                            Sx.op("act", lambda e: e.activation(out=X_sb[si][:, :], in_=psX[:, 0:128], func=AF.Copy), reads=[pbX], writes=[Xb[si]])
                        for pr in PRS:
                            si = pr % NSET
                            psU, pbU = Sx.next_psum()
                            Sx.op("pe", lambda e: e.matmul(psU[:, 0:128], Wt[si][1][:, :], X_sb[si][:, :], start=True, stop=True), reads=[Wtb[si][1], Xb[si]], writes=[pbU])
                            Sx.op("dve", lambda e: e.tensor_copy(U_sb[si][:, :], psU[:, 0:128]), reads=[pbU], writes=[Ub[si]])
                        for pr in PRS:
                            u = U_[pr]; si = pr % NSET
                            psY, pbY = Sx.next_psum()
                            Sx.op("pe", lambda e: e.matmul(psY[:, 0:64], H[pr][:, :], u["rTu"], start=True, stop=False), reads=[Hb[pr], tb["rT"]], writes=[pbY])
                            Sx.op("pe", lambda e: e.matmul(psY[:, 0:64], U_sb[si][:, :], R_sb[si][:, 0:64], start=False, stop=False),
                                  reads=[Ub[si], Rb[si]], writes=[pbY])
                            Sx.op("pe", lambda e: e.matmul(psY[:, 0:64], BKV[si][:, 256:384], R_sb[si][:, 64:128], start=False, stop=True),
                                  reads=[BKVb[si], Rb[si]], writes=[pbY])
                            Sx.op("act", lambda e: e.activation(out=Ycm[:, pr, ch * 64:(ch + 1) * 64], in_=psY[:, 0:64], func=AF.Copy), reads=[pbY], writes=[Ybp[pr]])
                            psG, pbG = Sx.next_psum()
                            Sx.op("pe", lambda e: e.matmul(psG[:, 0:128], BKV[si][:, 0:128], U_sb[si][:, :], start=True, stop=False),
                                  reads=[BKVb[si], Ub[si]], writes=[pbG])
                            Sx.op("pe", lambda e: e.matmul(psG[:, 0:128], BKV[si][:, 128:256], BKV[si][:, 256:384], start=False, stop=True),
                                  reads=[BKVb[si]], writes=[pbG])
                            Sx.op("pool", lambda e: e.tensor_scalar(out=HpC[si][:, :], in0=H[pr][:, :], scalar1=u["pC"], scalar2=None, op0=ALU.mult),
                                  reads=[Hb[pr], tb["eP"]], writes=[HpCb[si]])
                            Sx.op("dve", lambda e: e.scalar_tensor_tensor(out=H[pr][:, :], in0=psG[:, 0:128], scalar=u["pC"], in1=HpC[si][:, :],
                                                                          op0=ALU.mult, op1=ALU.add),
                                  reads=[pbG, HpCb[si], tb["eP"]], writes=[Hb[pr]])
                if self.cfg.get("stop_after", 9) < 3:
                    continue
                NT_ = 4 * T
                psM, pbM = Sx.next_psum()
                Sx.op("pe", lambda e: e.matmul(psM[:, 0:NT_], self.bo[:, :], flat(Ycm), start=True, stop=True), reads=Ybp + [cb], writes=[pbM])
                Sx.op("act", lambda e: e.activation(out=flat(tt_["t1"]), in_=flat(Ycm), func=AF.Square), reads=Ybp, writes=[tb["t1"]])
                psQ, pbQ = Sx.next_psum()
                Sx.op("pe", lambda e: e.matmul(psQ[:, 0:NT_], self.bo[:, :], flat(tt_["t1"]), start=True, stop=True), reads=[tb["t1"], cb], writes=[pbQ])
                Sx.op("act", lambda e: e.activation(out=flat(tt_["t2"]), in_=psM[:, 0:NT_], func=AF.Copy, scale=1.0 / 64.0), reads=[pbM], writes=[tb["t2"]])
                Sx.op("pool", lambda e: e.tensor_tensor(out=flat(tt_["t3"]), in0=flat(tt_["t2"]), in1=flat(tt_["t2"]), op=ALU.mult),
                      reads=[tb["t2"]], writes=[tb["t3"]])
                Sx.op("dve", lambda e: e.scalar_tensor_tensor(out=flat(tt_["t3"]), in0=psQ[:, 0:NT_], scalar=1.0 / 64.0, in1=flat(tt_["t3"]),
                                                              op0=ALU.mult, op1=ALU.subtract), reads=[pbQ, tb["t3"]], writes=[tb["t3"]])
                Sx.op("dve", lambda e: e.tensor_scalar(out=flat(tt_["t3"]), in0=flat(tt_["t3"]), scalar1=64e-5, scalar2=None, op0=ALU.add),
                      reads=[tb["t3"]], writes=[tb["t3"]])
                Sx.op("act", lambda e: e.activation(out=flat(tt_["t3"]), in_=flat(tt_["t3"]), func=AF.Sqrt), reads=[tb["t3"]], writes=[tb["t3"]])
                Sx.op("dve", lambda e: e.reciprocal(flat(tt_["t1"]), flat(tt_["t3"])), reads=[tb["t3"]], writes=[tb["t1"]])
                Sx.op("pool", lambda e: e.tensor_tensor(out=flat(tt_["t2"]), in0=flat(Ycm), in1=flat(tt_["t2"]), op=ALU.subtract),
                      reads=Ybp + [tb["t2"]], writes=[tb["t2"]])
                Sx.op("pool", lambda e: e.tensor_tensor(out=flat(tt_["t2"]), in0=flat(tt_["t2"]), in1=flat(tt_["t1"]), op=ALU.mult),
                      reads=[tb["t2"], tb["t1"]], writes=[tb["t2"]])
                for c4 in range(4):
                    Sx.op("act", lambda e, c4=c4: e.activation(out=tt_["t2"][:, c4, :], in_=tt_["t2"][:, c4, :], func=AF.Identity,
                                                               scale=self.pc("lnx_g", c4), bias=self.pc("lnx_b", c4)),
                          reads=[tb["t2"], pcb], writes=[tb["t2"]])
                    Sx.op("dve", lambda e, c4=c4: e.scalar_tensor_tensor(out=tt_["t1"][:, c4, :], in0=r_[:, c4, :], scalar=self.pc("r_k", c4),
                                                                         in1=tt_["kp"][:, c4, :], op0=ALU.mult, op1=ALU.mult),
                          reads=[pm_r[c4], tb["kp"], pcb, tb["t1"]], writes=[tb["t1"]])
                psR, pbR = Sx.next_psum()
                Sx.op("pe", lambda e: e.matmul(psR[:, 0:NT_], self.bo[:, :], flat(tt_["t1"]), start=True, stop=True), reads=[tb["t1"], cb], writes=[pbR])
                Sx.op("dve", lambda e: e.tensor_tensor(out=flat(tt_["t3"]), in0=psR[:, 0:NT_], in1=v_.rearrange("p c t -> p (c t)"), op=ALU.mult),
                      reads=[pbR] + pm_v, writes=[tb["t3"]])
                Sx.op("pool", lambda e: e.tensor_tensor(out=flat(tt_["t3"]), in0=flat(tt_["t3"]), in1=flat(tt_["t2"]), op=ALU.add),
                      reads=[tb["t3"], tb["t2"]], writes=[tb["t3"]])
                Sx.op("pool", lambda e: e.tensor_tensor(out=flat(ybf), in0=flat(tt_["t3"]), in1=flat(sgate), op=ALU.mult),
                      reads=[tb["t3"], sgb], writes=[ybb])
                Sx.dma("sp", ydst[:, :, t0:t0 + T], ybf[:, :, :], reads=[ybb], writes=[self.ybr_buf[0]])
            Sx.barrier()

    def rope_tables(self, s):
        Sx = self.S
        rp = self.c["c_rope"]
        cb = self.cb
        with contextlib.ExitStack() as es:
            posi = self.sb(es, "posi", [128, S], I32)
            ang = self.sb(es, "ang", [128, S], F32)
            t1 = self.sb(es, "rp_t1", [128, S], F32)
            t2 = self.sb(es, "rp_t2", [128, S], F32)
            ki = self.sb(es, "rp_ki", [128, S], I32)
            b = Buf()
            R = slice(64, 96)
            Sx.dma("sp", posi[R, :], self.din["positions"][s:s + 1, :].broadcast_to([32, S]), writes=[b])
            Sx.op("dve", lambda e: e.tensor_copy(ang[R, :], posi[R, :]), reads=[b], writes=[b])
            Sx.op("dve", lambda e: e.tensor_scalar(out=ang[R, :], in0=ang[R, :], scalar1=rp[R, 0:1], scalar2=None, op0=ALU.mult),
                  reads=[b, cb], writes=[b])
            TWO_PI = 6.283185307179586
            for which, dst in ((0, self.ropeS), (1, self.ropeC)):
                shift = 0.0 if which == 0 else 1.5707963267948966
                Sx.op("dve", lambda e: e.tensor_scalar(out=t1[R, :], in0=ang[R, :], scalar1=shift, scalar2=None, op0=ALU.add), reads=[b], writes=[b])
                Sx.op("dve", lambda e: e.tensor_scalar(out=t2[R, :], in0=t1[R, :], scalar1=1.0 / TWO_PI, scalar2=0.5, op0=ALU.mult, op1=ALU.add),
                      reads=[b], writes=[b])
                Sx.op("dve", lambda e: e.tensor_copy(ki[R, :], t2[R, :]), reads=[b], writes=[b])
                Sx.op("dve", lambda e: e.tensor_copy(t2[R, :], ki[R, :]), reads=[b], writes=[b])
                Sx.op("dve", lambda e: e.scalar_tensor_tensor(out=t1[R, :], in0=t2[R, :], scalar=-TWO_PI, in1=t1[R, :], op0=ALU.mult, op1=ALU.add),
                      reads=[b], writes=[b])
                Sx.op("dve", lambda e: e.tensor_scalar(out=t2[R, :], in0=t1[R, :], scalar1=-3.141592653589793, scalar2=TWO_PI, op0=ALU.is_lt, op1=ALU.mult),
                      reads=[b], writes=[b])
                Sx.op("dve", lambda e: e.tensor_tensor(out=t1[R, :], in0=t1[R, :], in1=t2[R, :], op=ALU.add), reads=[b], writes=[b])
                Sx.op("dve", lambda e: e.tensor_scalar(out=t2[R, :], in0=t1[R, :], scalar1=3.141592653589793, scalar2=-TWO_PI, op0=ALU.is_gt, op1=ALU.mult),
                      reads=[b], writes=[b])
                Sx.op("dve", lambda e: e.tensor_tensor(out=t1[R, :], in0=t1[R, :], in1=t2[R, :], op=ALU.add), reads=[b], writes=[b])
                Sx.op("dve", lambda e: e.tensor_scalar(out=t1[R, :], in0=t1[R, :], scalar1=3.1415925, scalar2=-3.1415925, op0=ALU.min, op1=ALU.max),
                      reads=[b], writes=[b])
                Sx.op("act", lambda e, dst=dst: e.activation(out=dst[R, :], in_=t1[R, :], func=AF.Sin), reads=[b], writes=[self.rope_b])
            Sx.op("dve", lambda e: e.tensor_scalar(out=self.ropeS[R, :], in0=self.ropeS[R, :], scalar1=rp[R, 1:2], scalar2=None, op0=ALU.mult),
                  reads=[self.rope_b, cb], writes=[self.rope_b])
            Sx.barrier()

    def load_memT(self, s):
        Sx = self.S
        with contextlib.ExitStack() as es:
            mt = [self.sb(es, f"mtok{i}", [128, D], F32) for i in range(2)]
            mb = [Buf(), Buf()]
            for tt in range(2):
                Sx.dma("sp", mt[tt][:], self.din["mem"][s, tt * 128:(tt + 1) * 128, :], writes=[mb[tt]])
                for half in range(2):
                    ps, pb = Sx.next_psum()
                    for j in range(4):
                        kc = half * 4 + j
                        Sx.op("pe", lambda e: e.matmul(ps[:, j * 128:(j + 1) * 128], mt[tt][:, kc * 128:(kc + 1) * 128], self.ident[:, :], start=True, stop=True),
                              reads=[mb[tt], self.cb], writes=[pb])
                    Sx.op("act", lambda e: e.activation(out=self.memT[:, half * 4:(half + 1) * 4, tt * 128:(tt + 1) * 128],
                                                        in_=ps[:, :].rearrange("p (j t) -> p j t", j=4), func=AF.Copy),
                          reads=[pb], writes=[self.memT_b])
            Sx.barrier()

    def inproj(self, ps, pb, wt, wb, c0, ncol, t0, ntok):
        Sx = self.S
        for kc in range(8):
            Sx.op("pe", lambda e, kc=kc: e.matmul(ps[0:ncol, 0:ntok], wt[:, kc, c0:c0 + ncol], self.xT[:, kc, t0:t0 + ntok],
                                                  start=(kc == 0), stop=(kc == 7)), reads=[wb, self.xT_b], writes=[pb])

    def rms_latent(self, es, tag, lat_f, sq, nch, dim, gname, outn, bufs):
        Sx = self.S
        lb, sqb, ob, rb, rstd = bufs
        ps, pb = Sx.next_psum(4)
        for i in range(nch):
            Sx.op("pe", lambda e, i=i: e.matmul(ps[:, :], self.c["c_ones"][:, :], sq[:, i, :], start=(i == 0), stop=(i == nch - 1)),
                  reads=[sqb, self.cb], writes=[pb])
        Sx.op("dve", lambda e: e.tensor_scalar(out=rstd[:, :], in0=ps[:, :], scalar1=1.0 / dim, scalar2=1e-6, op0=ALU.mult, op1=ALU.add),
              reads=[pb], writes=[rb])
        Sx.op("act", lambda e: e.activation(out=rstd[:, :], in_=rstd[:, :], func=AF.Sqrt), reads=[rb], writes=[rb])
        Sx.op("dve", lambda e: e.reciprocal(rstd[:, :], rstd[:, :]), reads=[rb], writes=[rb])
        for i in range(nch):
            Sx.op("dve", lambda e, i=i: e.scalar_tensor_tensor(out=outn[:, i, :], in0=lat_f[:, i, :], scalar=self.pc(gname, i), in1=rstd[:, :],
                                                               op0=ALU.mult, op1=ALU.mult), reads=[lb, rb, self.pcol_b], writes=[ob])

    def phase_mla(self, s, l):
        Sx = self.S
        d = self.din
        cb = self.cb
        scale = 96.0 ** -0.5
        with contextlib.ExitStack() as es:
            sb = lambda n, shp, dt=F32: self.sb(es, "ml_" + n, shp, dt)
            wM = sb("wM", [128, 8, 1184], BF16); wMb = Buf()
            self.load_w_bf(wM, wMb, d["w_in"][l, :, OFF_QLAT:OFF_QLAT + 1184], 8)
            wks = sb("wks", [128, 8, 96], BF16); wksb = Buf()
            Sx.op("pool", lambda e: e.memset(wks[:], 0.0), writes=[wksb])
            vk = d["w_in"][l, :, OFF_KPE:OFF_KPE + 32].rearrange("(kc p) c -> p kc c", p=128)
            Sx.dma("pool", wks[:, :, 64:80], vk[:, :, 16:32], writes=[wksb])
            Sx.dma("pool", wks[:, :, 80:96], vk[:, :, 0:16], writes=[wksb])
            Wuq = sb("Wuq", [128, 3, 768], BF16); Wuqb = Buf()
            self.load_w_bf(Wuq, Wuqb, d["mla_w_uq"][l], 3)
            Wus = sb("Wus", [128, 3, 768], BF16); Wusb = Buf()
            Wq4 = Wuq[:, :, :].rearrange("p k (h c) -> p k h c", h=8)
            Ws4 = Wus[:, :, :].rearrange("p k (h c) -> p k h c", h=8)
            Sx.op("pool", lambda e: e.tensor_copy(Ws4[:, :, :, 0:64], Wq4[:, :, :, 0:64]), reads=[Wuqb], writes=[Wusb])
            Sx.op("pool", lambda e: e.tensor_copy(Ws4[:, :, :, 64:80], Wq4[:, :, :, 80:96]), reads=[Wuqb], writes=[Wusb])
            Sx.op("pool", lambda e: e.tensor_copy(Ws4[:, :, :, 80:96], Wq4[:, :, :, 64:80]), reads=[Wuqb], writes=[Wusb])
            Wukv = sb("Wukv", [128, 2, 1024], BF16); Wukvb = Buf()
            self.load_w_bf(Wukv, Wukvb, d["mla_w_ukv"][l], 2)
            QT = sb("QT", [128, 8, 512], BF16); QTh = [Buf() for _ in range(8)]
            KT = sb("KT", [128, 8, S], BF16); KTh = [Buf() for _ in range(8)]
            V = sb("V", [128, 16, 512], BF16); Vb = Buf()
            qlf = sb("qlf", [128, 3, 512]); qlb = Buf()
            qsq = sb("qsq", [128, 3, 512]); qsb = Buf()
            qn = sb("qn", [128, 3, 512], BF16); qnb = Buf()
            klf = sb("klf", [128, 2, 512]); klb = Buf()
            ksq = sb("ksq", [128, 2, 512]); ksb = Buf()
            kvn = sb("kvn", [128, 2, 512], BF16); knb = Buf()
            rq = sb("rq", [128, 512]); rqb = Buf()
            rk = sb("rk", [128, 512]); rkb = Buf()
            tas = [sb(f"ta{i}", [128, 512]) for i in range(2)]; tabs = [Buf(), Buf()]
            tbs = [sb(f"tb{i}", [128, 512]) for i in range(2)]; tbbs = [Buf(), Buf()]
            ta = tas[0]; tab = tabs[0]; tbb_ = tbs[0]; tbb = tbbs[0]
            PT = [sb(f"PT{i}", [128, 512], BF16) for i in range(3)]; PTb = [Buf() for _ in range(3)]
            sgt = sb("sgt", [128, 512]); sgb = Buf()
            rl = sb("rl", [128, 512]); rlb = Buf()
            yb = [sb(f"yb{i}", [128, 512], BF16) for i in range(2)]; ybb = [Buf(), Buf()]
            ydst = self.ybr[1]
            R = slice(64, 96)
            pti = 0
            for g in range(4):
                t0 = g * 512
                for i in range(3):
                    ps, pb = Sx.next_psum(4)
                    self.inproj(ps, pb, wM, wMb, i * 128, 128, t0, 512)
                    Sx.op("act", lambda e: e.activation(out=qlf[:, i, :], in_=ps[:, :], func=AF.Copy), reads=[pb], writes=[qlb])
                    Sx.op("act", lambda e: e.activation(out=qsq[:, i, :], in_=ps[:, :], func=AF.Square), reads=[pb], writes=[qsb])
                self.rms_latent(es, "q", qlf, qsq, 3, 384.0, "q_norm", qn, (qlb, qsb, qnb, rqb, rq))
                for i in range(2):
                    ps, pb = Sx.next_psum(4)
                    self.inproj(ps, pb, wM, wMb, 384 + i * 128, 128, t0, 512)
                    Sx.op("act", lambda e: e.activation(out=klf[:, i, :], in_=ps[:, :], func=AF.Copy), reads=[pb], writes=[klb])
                    Sx.op("act", lambda e: e.activation(out=ksq[:, i, :], in_=ps[:, :], func=AF.Square), reads=[pb], writes=[ksb])
                self.rms_latent(es, "k", klf, ksq, 2, 256.0, "kv_norm", kvn, (klb, ksb, knb, rkb, rk))
                ps1, pb1 = Sx.next_psum(4)
                self.inproj(ps1, pb1, wM, wMb, 576, 96, t0, 512)
                ps2, pb2 = Sx.next_psum(4)
                self.inproj(ps2, pb2, wks, wksb, 0, 96, t0, 512)
                Sx.op("dve", lambda e: e.tensor_tensor(out=ta[R, :], in0=ps1[R, :], in1=self.ropeC[R, t0:t0 + 512], op=ALU.mult),
                      reads=[pb1, self.rope_b], writes=[tab])
                Sx.op("dve", lambda e: e.tensor_tensor(out=tbb_[R, :], in0=ps2[R, :], in1=self.ropeS[R, t0:t0 + 512], op=ALU.mult),
                      reads=[pb2, self.rope_b], writes=[tbb])
                Sx.op("pool", lambda e: e.tensor_tensor(out=ta[R, :], in0=ta[R, :], in1=tbb_[R, :], op=ALU.add), reads=[tab, tbb], writes=[tab])
                for h in range(8):
                    Sx.op("pool" if h % 2 else "act",
                          (lambda e: e.tensor_copy(KT[R, h, t0:t0 + 512], ta[R, :])) if h % 2 else
                          (lambda e: e.activation(out=KT[R, h, t0:t0 + 512], in_=ta[R, :], func=AF.Copy)),
                          reads=[tab], writes=[KTh[h]])
                for h in range(8):
                    ta = tas[h % 2]; tab = tabs[h % 2]; tbb_ = tbs[h % 2]; tbb = tbbs[h % 2]
                    ps1, pb1 = Sx.next_psum(4)
                    ps2, pb2 = Sx.next_psum(4)
                    for kc in range(3):
                        Sx.op("pe", lambda e: e.matmul(ps1[0:96, :], Wuq[:, kc, h * 96:(h + 1) * 96], qn[:, kc, :], start=(kc == 0), stop=(kc == 2)),
                              reads=[Wuqb, qnb], writes=[pb1])
                    for kc in range(3):
                        Sx.op("pe", lambda e: e.matmul(ps2[0:96, :], Wus[:, kc, h * 96:(h + 1) * 96], qn[:, kc, :], start=(kc == 0), stop=(kc == 2)),
                              reads=[Wusb, qnb], writes=[pb2])
                    Sx.op("act", lambda e: e.activation(out=QT[0:64, h, :], in_=ps1[0:64, :], func=AF.Copy), reads=[pb1], writes=[QTh[h]])
                    Sx.op("dve", lambda e: e.tensor_tensor(out=ta[R, :], in0=ps1[R, :], in1=self.ropeC[R, t0:t0 + 512], op=ALU.mult),
                          reads=[pb1, self.rope_b], writes=[tab])
                    Sx.op("dve", lambda e: e.tensor_tensor(out=tbb_[R, :], in0=ps2[R, :], in1=self.ropeS[R, t0:t0 + 512], op=ALU.mult),
                          reads=[pb2, self.rope_b], writes=[tbb])
                    Sx.op("pool", lambda e: e.tensor_tensor(out=QT[R, h, :], in0=ta[R, :], in1=tbb_[R, :], op=ALU.add), reads=[tab, tbb], writes=[QTh[h]])
                    ps3, pb3 = Sx.next_psum(4)
                    for kc in range(2):
                        Sx.op("pe", lambda e: e.matmul(ps3[0:64, :], Wukv[:, kc, h * 128:h * 128 + 64], kvn[:, kc, :], start=(kc == 0), stop=(kc == 1)),
                              reads=[Wukvb, knb], writes=[pb3])
                    Sx.op("act", lambda e: e.activation(out=KT[0:64, h, t0:t0 + 512], in_=ps3[0:64, :], func=AF.Copy), reads=[pb3], writes=[KTh[h]])
                Wv = Wukv[:, :, :].rearrange("p k (h c) -> p k h c", h=8)
                for tt in range(4):
                    ps, pb = Sx.next_psum(4)
                    for kc in range(2):
                        Sx.op("pe", lambda e: e.matmul(ps[:, :].rearrange("p (h c) -> p h c", h=8), kvn[:, kc, tt * 128:(tt + 1) * 128], Wv[:, kc, :, 64:128],
                                                       start=(kc == 0), stop=(kc == 1)), reads=[Wukvb, knb], writes=[pb])
                    Sx.op("dve", lambda e: e.tensor_copy(V[:, g * 4 + tt, :], ps[:, :]), reads=[pb], writes=[Vb])
                for pr in range(4):
                    accs = []
                    for hh in range(2):
                        h = pr * 2 + hh
                        o_ps, o_pb = Sx.psum[4 + hh * 2]
                        l_ps, l_pb = Sx.psum[5 + hh * 2]
                        accs.append((o_ps, o_pb, l_ps, l_pb))
                        nj = 4 * g + 4
                        for j in range(nj):
                            v = max(0, j - 4 * g)
                            c0 = v * 128
                            sps, spb = Sx.next_psum(4)
                            Sx.op("pe", lambda e: e.matmul(sps[:, c0:512], KT[0:96, h, j * 128:(j + 1) * 128], QT[0:96, h, c0:512], start=True, stop=True),
                                  reads=[KTh[h], QTh[h]], writes=[spb])
                            pt = PT[pti]; ptb = PTb[pti]; pti = (pti + 1) % 3
                            Sx.op("act", lambda e: e.activation(out=pt[:, c0:512], in_=sps[:, c0:512], func=AF.Exp, scale=scale), reads=[spb], writes=[ptb])
                            if j >= 4 * g:
                                Sx.op("pool", lambda e: e.tensor_tensor(out=pt[:, c0:c0 + 128], in0=pt[:, c0:c0 + 128],
                                                                        in1=self.cmla_bf[:, v * 512 + c0:v * 512 + c0 + 128], op=ALU.mult),
                                      reads=[ptb, cb], writes=[ptb])
                            Sx.op("pe", lambda e: e.matmul(o_ps[:, c0:512], V[:, j, pr * 128:(pr + 1) * 128], pt[:, c0:512], start=(j == 0), stop=(j == nj - 1)),
                                  reads=[Vb, ptb], writes=[o_pb])
                            Sx.op("pe", lambda e: e.matmul(l_ps[:, c0:512], self.ones_bf[:, :], pt[:, c0:512], start=(j == 0), stop=(j == nj - 1)),
                                  reads=[cb, ptb], writes=[l_pb])
                    gps, gpb = Sx.next_psum(4)
                    self.inproj(gps, gpb, wM, wMb, 672 + pr * 128, 128, t0, 512)
                    Sx.op("act", lambda e: e.activation(out=sgt[:, :], in_=gps[:, :], func=AF.Silu), reads=[gpb], writes=[sgb])
                    yt = yb[pr % 2]; ytb = ybb[pr % 2]
                    for hh in range(2):
                        o_ps, o_pb, l_ps, l_pb = accs[hh]
                        HR = slice(hh * 64, hh * 64 + 64)
                        Sx.op("dve", lambda e: e.reciprocal(rl[HR, :], l_ps[HR, :]), reads=[l_pb], writes=[rlb])
                        Sx.op("dve", lambda e: e.tensor_tensor(out=rl[HR, :], in0=o_ps[HR, :], in1=rl[HR, :], op=ALU.mult), reads=[o_pb, rlb], writes=[rlb])
                        Sx.op("pool", lambda e: e.tensor_tensor(out=yt[HR, :], in0=rl[HR, :], in1=sgt[HR, :], op=ALU.mult), reads=[rlb, sgb], writes=[ytb])
                    Sx.dma("sp", ydst[pr * 128:(pr + 1) * 128, t0:t0 + 512], yt[:, :], reads=[ytb], writes=[self.ybr_buf[1]])
            Sx.barrier()

    def phase_xattn(self, s, l):
        Sx = self.S
        d = self.din
        cb = self.cb
        scale = 128.0 ** -0.5
        with contextlib.ExitStack() as es:
            sb = lambda n, shp, dt=F32: self.sb(es, "xa_" + n, shp, dt)
            wkv = sb("wkv", [128, 8, 1024], BF16); wkvb = Buf()
            self.load_w_bf(wkv, wkvb, d["xattn_w_mem_kv"][l], 8)
            wX = sb("wX", [128, 8, 1024], BF16); wXb = Buf()
            self.load_w_bf(wX, wXb, d["w_in"][l, :, OFF_XQ:OFF_XQ + 1024], 8)
            KxT = sb("KxT", [128, 4, 256], BF16); Kb = Buf()
            Vx = sb("Vx", [128, 2, 512], BF16); Vb = Buf()
            qx = sb("qx", [128, 512], BF16); qb = Buf()
            PT = [sb(f"PT{i}", [128, 512], BF16) for i in range(2)]; PTb = [Buf(), Buf()]
            sgt = sb("sgt", [128, 512]); sgb = Buf()
            rl = sb("rl", [128, 512]); rlb = Buf()
            yb = [sb(f"yb{i}", [128, 512], BF16) for i in range(2)]; ybb = [Buf(), Buf()]
            for h in range(4):
                ps, pb = Sx.next_psum(4)
                for kc in range(8):
                    Sx.op("pe", lambda e: e.matmul(ps[:, 0:256], wkv[:, kc, h * 128:(h + 1) * 128], self.memT[:, kc, :], start=(kc == 0), stop=(kc == 7)),
                          reads=[wkvb, self.memT_b], writes=[pb])
                Sx.op("act", lambda e: e.activation(out=KxT[:, h, :], in_=ps[:, 0:256], func=AF.Copy), reads=[pb], writes=[Kb])
            for mt in range(2):
                ps, pb = Sx.next_psum(4)
                for kc in range(8):
                    Sx.op("pe", lambda e: e.matmul(ps[:, :], self.memT[:, kc, mt * 128:(mt + 1) * 128], wkv[:, kc, 512:1024], start=(kc == 0), stop=(kc == 7)),
                          reads=[wkvb, self.memT_b], writes=[pb])
                Sx.op("act", lambda e: e.activation(out=Vx[:, mt, :], in_=ps[:, :], func=AF.Copy), reads=[pb], writes=[Vb])
            ydst = self.ybr[3]
            k = 0
            for g in range(4):
                t0 = g * 512
                for h in range(4):
                    ps, pb = Sx.next_psum(4)
                    self.inproj(ps, pb, wX, wXb, h * 128, 128, t0, 512)
                    Sx.op("act", lambda e: e.activation(out=qx[:, :], in_=ps[:, :], func=AF.Copy), reads=[pb], writes=[qb])
                    o_ps, o_pb = Sx.psum[4]
                    l_ps, l_pb = Sx.psum[5]
                    for mt in range(2):
                        sps, spb = Sx.next_psum(4)
                        Sx.op("pe", lambda e: e.matmul(sps[:, :], KxT[:, h, mt * 128:(mt + 1) * 128], qx[:, :], start=True, stop=True), reads=[Kb, qb], writes=[spb])
                        pt = PT[mt]; ptb = PTb[mt]
                        Sx.op("act", lambda e: e.activation(out=pt[:, :], in_=sps[:, :], func=AF.Exp, scale=scale), reads=[spb], writes=[ptb])
                        Sx.op("pe", lambda e: e.matmul(o_ps[:, :], Vx[:, mt, h * 128:(h + 1) * 128], pt[:, :], start=(mt == 0), stop=(mt == 1)),
                              reads=[Vb, ptb], writes=[o_pb])
                        Sx.op("pe", lambda e: e.matmul(l_ps[:, :], self.ones_bf[:, :], pt[:, :], start=(mt == 0), stop=(mt == 1)),
                              reads=[cb, ptb], writes=[l_pb])
                    gps, gpb = Sx.next_psum(4)
                    self.inproj(gps, gpb, wX, wXb, 512 + h * 128, 128, t0, 512)
                    Sx.op("act", lambda e: e.activation(out=sgt[:, :], in_=gps[:, :], func=AF.Silu), reads=[gpb], writes=[sgb])
                    yt = yb[k % 2]; ytb = ybb[k % 2]; k += 1
                    Sx.op("dve", lambda e: e.reciprocal(rl[:, :], l_ps[:, :]), reads=[l_pb], writes=[rlb])
                    Sx.op("dve", lambda e: e.tensor_tensor(out=rl[:, :], in0=o_ps[:, :], in1=rl[:, :], op=ALU.mult), reads=[o_pb, rlb], writes=[rlb])
                    Sx.op("pool", lambda e: e.tensor_tensor(out=yt[:, :], in0=rl[:, :], in1=sgt[:, :], op=ALU.mult), reads=[rlb, sgb], writes=[ytb])
                    Sx.dma("sp", ydst[h * 128:(h + 1) * 128, t0:t0 + 512], yt[:, :], reads=[ytb], writes=[self.ybr_buf[3]])
            Sx.barrier()

    def phase_conv(self, s, l):
        Sx = self.S
        d = self.din
        cb = self.cb
        pcb = self.pcol_b
        ones = self.c["c_ones"]
        with contextlib.ExitStack() as es:
            sb = lambda n, shp, dt=F32: self.sb(es, "cv_" + n, shp, dt)
            wC = sb("wC", [128, 8, 1536], BF16); wCb = Buf()
            self.load_w_bf(wC, wCb, d["w_in"][l, :, OFF_CONV:OFF_CONV + 1536], 8)
            Dg = sb("Dg", [128, 4, 31, 128], BF16); Dgb = Buf()
            for c in range(4):
                for j in range(31):
                    Sx.op("pool" if (j % 2) else "dve",
                          lambda e: e.tensor_scalar(out=Dg[:, c, j, :], in0=self.ident[:, :], scalar1=self.pc("conv_w", j * 4 + c), scalar2=None, op0=ALU.mult),
                          reads=[cb, pcb], writes=[Dgb])
            hb = sb("hb", [128, 4, 30 + S], BF16); hbb = Buf()
            Sx.op("pool", lambda e: e.memset(hb[:, :, 0:30], 0.0), writes=[hbb])
            sig = sb("sig", [128, 512]); sigb = Buf()
            cv = sb("cvv", [128, 4, 512]); cvb = Buf()
            sq = sb("sq", [128, 4, 512]); sqb = Buf()
            mean = sb("mean", [128, 512]); mb = Buf()
            rstd = sb("rstd", [128, 512]); rb = Buf()
            t1 = sb("t1", [128, 512]); t1b = Buf()
            sgt = sb("sgt", [128, 512]); sgb = Buf()
            yb = [sb(f"yb{i}", [128, 512], BF16) for i in range(2)]; ybb = [Buf(), Buf()]
            ydst = self.ybr[2]
            for g in range(4):
                t0 = g * 512
                for c in range(4):
                    ps1, pb1 = Sx.next_psum()
                    self.inproj(ps1, pb1, wC, wCb, c * 128, 128, t0, 512)
                    ps2, pb2 = Sx.next_psum()
                    self.inproj(ps2, pb2, wC, wCb, 512 + c * 128, 128, t0, 512)
                    Sx.op("act", lambda e: e.activation(out=sig[:, :], in_=ps2[:, :], func=AF.Sigmoid), reads=[pb2], writes=[sigb])
                    Sx.op("dve", lambda e: e.tensor_tensor(out=hb[:, c, 30 + t0:30 + t0 + 512], in0=ps1[:, :], in1=sig[:, :], op=ALU.mult),
                          reads=[pb1, sigb], writes=[hbb])
                for c in range(4):
                    ps, pb = Sx.next_psum()
                    for j in range(31):
                        Sx.op("pe", lambda e: e.matmul(ps[:, :], Dg[:, c, j, :], hb[:, c, t0 + j:t0 + j + 512], start=(j == 0), stop=(j == 30)),
                              reads=[Dgb, hbb], writes=[pb])
                    Sx.op("act", lambda e: e.activation(out=cv[:, c, :], in_=ps[:, :], func=AF.Identity, bias=self.pc("conv_b", c)), reads=[pb, pcb], writes=[cvb])
                Sx.op("act", lambda e: e.activation(out=sq[:, :, :].rearrange("p c t -> p (c t)"), in_=cv[:, :, :].rearrange("p c t -> p (c t)"), func=AF.Square),
                      reads=[cvb], writes=[sqb])
                psM, pbM = Sx.next_psum()
                psQ, pbQ = Sx.next_psum()
                for c in range(4):
                    Sx.op("pe", lambda e: e.matmul(psM[:, :], ones[:, :], cv[:, c, :], start=(c == 0), stop=(c == 3)), reads=[cvb, cb], writes=[pbM])
                for c in range(4):
                    Sx.op("pe", lambda e: e.matmul(psQ[:, :], ones[:, :], sq[:, c, :], start=(c == 0), stop=(c == 3)), reads=[sqb, cb], writes=[pbQ])
                Sx.op("act", lambda e: e.activation(out=mean[:, :], in_=psM[:, :], func=AF.Copy, scale=1.0 / 512.0), reads=[pbM], writes=[mb])
                Sx.op("pool", lambda e: e.tensor_tensor(out=t1[:, :], in0=mean[:, :], in1=mean[:, :], op=ALU.mult), reads=[mb], writes=[t1b])
                Sx.op("dve", lambda e: e.scalar_tensor_tensor(out=rstd[:, :], in0=psQ[:, :], scalar=1.0 / 512.0, in1=t1[:, :], op0=ALU.mult, op1=ALU.subtract),
                      reads=[pbQ, t1b], writes=[rb])
                Sx.op("dve", lambda e: e.tensor_scalar(out=rstd[:, :], in0=rstd[:, :], scalar1=1e-5, scalar2=None, op0=ALU.add), reads=[rb], writes=[rb])
                Sx.op("act", lambda e: e.activation(out=rstd[:, :], in_=rstd[:, :], func=AF.Sqrt), reads=[rb], writes=[rb])
                Sx.op("dve", lambda e: e.reciprocal(rstd[:, :], rstd[:, :]), reads=[rb], writes=[rb])
                for c in range(4):
                    Sx.op("pool", lambda e: e.tensor_tensor(out=t1[:, :], in0=cv[:, c, :], in1=mean[:, :], op=ALU.subtract), reads=[cvb, mb, t1b], writes=[t1b])
                    Sx.op("dve", lambda e: e.tensor_tensor(out=t1[:, :], in0=t1[:, :], in1=rstd[:, :], op=ALU.mult), reads=[t1b, rb], writes=[t1b])
                    Sx.op("act", lambda e: e.activation(out=t1[:, :], in_=t1[:, :], func=AF.Silu, scale=self.pc("conv_ln_g", c), bias=self.pc("conv_ln_b", c)),
                          reads=[t1b, pcb], writes=[t1b])
                    gps, gpb = Sx.next_psum()
                    self.inproj(gps, gpb, wC, wCb, 1024 + c * 128, 128, t0, 512)
                    Sx.op("act", lambda e: e.activation(out=sgt[:, :], in_=gps[:, :], func=AF.Silu), reads=[gpb], writes=[sgb])
                    yt = yb[c % 2]; ytb = ybb[c % 2]
                    Sx.op("pool", lambda e: e.tensor_tensor(out=yt[:, :], in0=t1[:, :], in1=sgt[:, :], op=ALU.mult), reads=[t1b, sgb], writes=[ytb])
                    Sx.dma("sp", ydst[c * 128:(c + 1) * 128, t0:t0 + 512], yt[:, :], reads=[ytb], writes=[self.ybr_buf[2]])
            Sx.barrier()

    def phase_epi(self, s, l):
        Sx = self.S
        d = self.din
        cb = self.cb
        pcb = self.pcol_b
        with contextlib.ExitStack() as es0:
            mT = self.sb(es0, "ep_mT", [128, 8, S], BF16); mTb = Buf()
            with contextlib.ExitStack() as es:
                sb = lambda n, shp, dt=F32: self.sb(es, "e1_" + n, shp, dt)
                yin = sb("yin", [128, 4, 4, S], BF16); yinb = Buf()
                prs = []
                for n in range(4):
                    src = self.ybr[n].rearrange("(c p) t -> p c t", p=128)
                    for c in range(4):
                        prs.append((yin[:, n, c, :], src[:, c, :]))
                Sx.dma_group("sp", prs, reads=self.ybr_buf, writes=[yinb])
                wG = [sb(f"wG{i}", [128, 8, 4, 128], BF16) for i in range(2)]; wGb = [Buf(), Buf()]
                Wo = [sb(f"Wo{i}", [128, 4, 4, 128], BF16) for i in range(2)]; Wob = [Buf(), Buf()]
                gt = sb("gt", [128, 512]); gtb = Buf()
                acc = sb("acc", [128, 512]); accb = Buf()
                tmp = sb("tmp", [128, 512]); tmpb = Buf()
                def load_db(db_):
                    i_ = db_ % 2
                    pg = []; po = []
                    for n in range(4):
                        c0 = OFF_MERGE + n * 1024 + db_ * 128
                        vg = d["w_in"][l, :, c0:c0 + 128].rearrange("(kc p) c -> p kc c", p=128)
                        pg.append((wG[i_][:, :, n, :], vg))
                        vo = d["w_o_branch"][l, n, :, db_ * 128:(db_ + 1) * 128].rearrange("(cc p) c -> p cc c", p=128)
                        po.append((Wo[i_][:, n, :, :], vo))
                    Sx.dma_group("pool", pg, writes=[wGb[i_]])
                    Sx.dma_group("pool", po, writes=[Wob[i_]])

                load_db(0)
                for db in range(8):
                    i = db % 2
                    if db + 1 < 8:
                        load_db(db + 1)
                    for g in range(4):
                        t0 = g * 512
                        for n in range(4):
                            psP, pbP = Sx.next_psum()
                            for cc in range(4):
                                Sx.op("pe", lambda e: e.matmul(psP[:, :], Wo[i][:, n, cc, :], yin[:, n, cc, t0:t0 + 512], start=(cc == 0), stop=(cc == 3)),
                                      reads=[Wob[i], yinb], writes=[pbP])
                            psG, pbG = Sx.next_psum()
                            for kc in range(8):
                                Sx.op("pe", lambda e: e.matmul(psG[:, :], wG[i][:, kc, n, :], self.xT[:, kc, t0:t0 + 512], start=(kc == 0), stop=(kc == 7)),
                                      reads=[wGb[i], self.xT_b], writes=[pbG])
                            Sx.op("act", lambda e: e.activation(out=gt[:, :], in_=psG[:, :], func=AF.Sigmoid, bias=self.pc("b_gate", n * 8 + db)),
                                  reads=[pbG, pcb], writes=[gtb])
                            if n == 0:
                                Sx.op("dve", lambda e: e.tensor_tensor(out=acc[:, :], in0=psP[:, :], in1=gt[:, :], op=ALU.mult), reads=[pbP, gtb], writes=[accb])
                            else:
                                Sx.op("dve", lambda e: e.tensor_tensor(out=tmp[:, :], in0=psP[:, :], in1=gt[:, :], op=ALU.mult), reads=[pbP, gtb], writes=[tmpb])
                                if n < 3:
                                    Sx.op("pool", lambda e: e.tensor_tensor(out=acc[:, :], in0=acc[:, :], in1=tmp[:, :], op=ALU.add), reads=[accb, tmpb], writes=[accb])
                                else:
                                    Sx.op("pool", lambda e: e.tensor_tensor(out=mT[:, db, t0:t0 + 512], in0=acc[:, :], in1=tmp[:, :], op=ALU.add),
                                          reads=[accb, tmpb], writes=[mTb])
                Sx.barrier()
            with contextlib.ExitStack() as es:
                sb = lambda n, shp, dt=F32: self.sb(es, "e2_" + n, shp, dt)
                Wout = sb("Wout", [128, 8, D], BF16); Woutb = Buf()
                self.load_w_bf(Wout, Woutb, d["w_out"][l], 8)
                xt = [sb(f"xt{i}", [128, D]) for i in range(2)]; xtb = [Buf(), Buf()]
                z = [sb(f"z{i}", [128, D]) for i in range(2)]; zb = [Buf(), Buf()]
                st = sb("st", [128, 12]); stb = Buf()
                mv = sb("mv", [128, 2]); mvb = Buf()
                rs = sb("rs", [128, 1]); rsb = Buf()
                for tt in range(S // 128):
                    i = tt % 2
                    if l == 0:
                        Sx.dma("sp", xt[i][:], d["x"][s, tt * 128:(tt + 1) * 128, :], writes=[xtb[i]])
                    else:
                        Sx.dma("sp", xt[i][:], self.x1[s, tt * 128:(tt + 1) * 128, :], reads=[self.x1_buf[s]], writes=[xtb[i]])
                    for half in range(2):
                        ps, pb = Sx.next_psum()
                        for kc in range(8):
                            Sx.op("pe", lambda e: e.matmul(ps[:, :], mT[:, kc, tt * 128:(tt + 1) * 128], Wout[:, kc, half * 512:(half + 1) * 512],
                                                           start=(kc == 0), stop=(kc == 7)), reads=[mTb, Woutb], writes=[pb])
                        Sx.op("dve", lambda e: e.scalar_tensor_tensor(out=z[i][:, half * 512:(half + 1) * 512], in0=xt[i][:, half * 512:(half + 1) * 512],
                                                                      scalar=float(ALPHA), in1=ps[:, :], op0=ALU.mult, op1=ALU.add),
                              reads=[xtb[i], pb], writes=[zb[i]])
                        Sx.op("dve", lambda e: e.bn_stats(st[:, half * 6:(half + 1) * 6], z[i][:, half * 512:(half + 1) * 512]), reads=[zb[i]], writes=[stb])
                    Sx.op("dve", lambda e: e.bn_aggr(mv[:, :], st[:, :]), reads=[stb], writes=[mvb])
                    Sx.op("dve", lambda e: e.tensor_scalar(out=rs[:, :], in0=mv[:, 1:2], scalar1=1e-5, scalar2=None, op0=ALU.add), reads=[mvb], writes=[rsb])
                    Sx.op("act", lambda e: e.activation(out=rs[:, :], in_=rs[:, :], func=AF.Sqrt), reads=[rsb], writes=[rsb])
                    Sx.op("dve", lambda e: e.reciprocal(rs[:, :], rs[:, :]), reads=[rsb], writes=[rsb])
                    Sx.op("dve", lambda e: e.tensor_scalar(out=z[i][:, :], in0=z[i][:, :], scalar1=mv[:, 0:1], scalar2=rs[:, 0:1], op0=ALU.subtract, op1=ALU.mult),
                          reads=[zb[i], mvb, rsb], writes=[zb[i]])
                    Sx.op("pool", lambda e: e.tensor_tensor(out=z[i][:, :], in0=z[i][:, :], in1=self.lng_bc[:, :], op=ALU.mult), reads=[zb[i], self.ln_b_], writes=[zb[i]])
                    Sx.op("pool", lambda e: e.tensor_tensor(out=z[i][:, :], in0=z[i][:, :], in1=self.lnb_bc[:, :], op=ALU.add), reads=[zb[i], self.ln_b_], writes=[zb[i]])
                    if l == DEPTH - 1 or self.single:
                        Sx.dma("sp", self.out[s, tt * 128:(tt + 1) * 128, :], z[i][:, :], reads=[zb[i]], writes=[self.out_b])
                    else:
                        Sx.dma("sp", self.x1[s, tt * 128:(tt + 1) * 128, :], z[i][:, :], reads=[zb[i]], writes=[self.x1_buf[s]])
                        self.transpose_into_xT(z[i], zb[i], tt)
                Sx.barrier()


def _shard_inputs(inputs):
    consts = _consts_host()
    maps = []
    for c in range(NCORES):
        m = {}
        sl = slice(c * SEQ_PER_CORE, (c + 1) * SEQ_PER_CORE)
        for n, shp, dt in PARAM_SPECS:
            a = np.asarray(inputs[n])
            if n in ("x", "mem", "positions"):
                a = a[sl]
            m[n] = np.ascontiguousarray(a)
        m.update(consts)
        maps.append(m)
    return maps


_PROG = {}


FUSED = True


def kernel(**inputs):
    if FUSED:
        if "f" not in _PROG:
            _PROG["f"] = K({}).build()
        res = run_bass_kernel_spmd(_PROG["f"], _shard_inputs(inputs), core_ids=list(range(NCORES)))
        return np.concatenate([np.asarray(r["out"], dtype=np.float32) for r in res.results], axis=0)
    return kernel_unfused(**inputs)


def kernel_unfused(**inputs):
    if "p" not in _PROG:
        kb = K({"nseq": 1, "nlay": 1, "single": True, "seqs": [0], "layers": [0]})
        _PROG["p"] = kb.build()
    nc = _PROG["p"]
    consts = _consts_host()
    xs = np.asarray(inputs["x"], dtype=np.float32)
    names = [n for n, _, _ in PARAM_SPECS if n not in ("x", "mem", "positions")]
    out = np.empty_like(xs)
    for slot in range(SEQ_PER_CORE):
        cur = [np.ascontiguousarray(xs[c * SEQ_PER_CORE + slot][None]) for c in range(NCORES)]
        for l in range(DEPTH):
            wl = {n: np.ascontiguousarray(np.asarray(inputs[n])[l:l + 1]) for n in names}
            maps = []
            for c in range(NCORES):
                b = c * SEQ_PER_CORE + slot
                m = dict(wl)
                m["x"] = cur[c]
                m["mem"] = np.ascontiguousarray(np.asarray(inputs["mem"])[b:b + 1])
                m["positions"] = np.ascontiguousarray(np.asarray(inputs["positions"])[b:b + 1])
                m.update(consts)
                maps.append(m)
            res = run_bass_kernel_spmd(nc, maps, core_ids=list(range(NCORES)))
            cur = [np.ascontiguousarray(np.asarray(r["out"], dtype=np.float32)) for r in res.results]
        for c in range(NCORES):
            out[c * SEQ_PER_CORE + slot] = cur[c][0]
    return out
```

```python
import contextlib
import numpy as np
import concourse.bass as bass
import concourse.mybir as mybir
from concourse.bass_utils import run_bass_kernel_spmd

F32 = mybir.dt.float32
BF16 = mybir.dt.bfloat16
I32 = mybir.dt.int32
AF = mybir.ActivationFunctionType
ALU = mybir.AluOpType

NCORES = 8
SEQ_PER_CORE = 4
S = 2048
D = 1024
DEPTH = 2
IN_COLS = 10016
W = 512
ALPHA = (2.0 * DEPTH) ** 0.25
SEM_LIMIT = 30000
VCLOCK = False
NDQ = 12

OFF_RW = 0
OFF_QLAT = 2176
OFF_KVLAT = 2560
OFF_KPE = 2816
OFF_MGATE = 2848
OFF_CONV = 3360
OFF_XQ = 4896
OFF_MERGE = 5920

PC = {}
_o = 0
for _n, _c in [("mu", 13), ("w0", 4), ("a0", 4), ("k_k", 4), ("k_a", 4), ("r_k", 4), ("lnx_g", 4),
               ("lnx_b", 4), ("q_norm", 3), ("kv_norm", 2), ("conv_b", 4), ("conv_ln_g", 4),
               ("conv_ln_b", 4), ("b_gate", 32), ("conv_w", 124)]:
    PC[_n] = _o
    _o += _c
NPC_RAW = _o
PC["omm"] = NPC_RAW
PC["omka"] = NPC_RAW + 13
NPC = NPC_RAW + 17


class Buf:
    __slots__ = ("w", "r")

    def __init__(self):
        self.w = []
        self.r = {}


class BufG(list):
    pass


def _flat(bufs):
    out = []
    for b in bufs:
        if isinstance(b, BufG):
            out.extend(b)
        else:
            out.append(b)
    return out


class Sched:
    def __init__(self, nc, es):
        self.nc = nc
        self.es = es
        self.eng = {"pe": nc.tensor, "act": nc.scalar, "dve": nc.vector, "pool": nc.gpsimd, "sp": nc.sync}
        self.sem = {}
        self.cnt = {}
        self.sid = {}
        self.nsem = 0
        self.waited = {k: {} for k in self.eng}
        self.latest = {}
        self.ninst = 0
        for k in self.eng:
            self._newsem(k)
        self.dq = {}
        self.dqi = {}
        for q in ("sp", "pool", "act"):
            lst = []
            for i in range(NDQ):
                s = es.enter_context(nc.semaphore(f"dq_{q}_{i}"))
                self.nsem += 1
                lst.append([s, 0, self.nsem])
            self.dq[q] = lst
            self.dqi[q] = 0
        self.psum = []
        self.psi = 0

    def _newsem(self, k):
        s = self.es.enter_context(self.nc.semaphore(f"s_{k}_{self.nsem}"))
        self.nsem += 1
        self.sem[k] = s
        self.cnt[k] = 0
        self.sid[k] = self.nsem

    def _wait(self, k, tok):
        sem, val, src, sid = tok[0], tok[1], tok[2], tok[3]
        if k == "pe" and src == "pe":
            return
        w = self.waited[k]
        if w.get(sid, 0) >= val:
            return
        self.eng[k].wait_ge(sem, val)
        self.ninst += 1
        w[sid] = val
        snap = tok[4] if (VCLOCK and len(tok) > 4) else None
        if snap:
            for a, b in snap.items():
                if w.get(a, 0) < b:
                    w[a] = b

    def _deps(self, reads, writes):
        reads = _flat(reads); writes = _flat(writes)
        toks = []
        for b in reads:
            toks.extend(b.w)
        for b in writes:
            toks.extend(b.w)
            toks.extend(b.r.values())
        return toks

    def _commit(self, tok, reads, writes):
        reads = _flat(reads); writes = _flat(writes)
        for b in reads:
            b.r[tok[3]] = tok
        for b in writes:
            b.w = [tok]
            b.r = {}
        self.latest[tok[3]] = tok

    def dma_group(self, q, pairs, reads=(), writes=()):
        deps = self._deps(reads, writes)
        toks = []
        lim = 1 if q == "pool" else 4
        for (out, in_) in pairs:
            for t in deps:
                self._wait(q, t)
            if len(toks) >= lim:
                self._wait(q, toks[len(toks) - lim])
            i = self.dqi[q]
            self.dqi[q] = (i + 1) % NDQ
            ent = self.dq[q][i]
            if ent[1] > 0:
                self._wait(q, (ent[0], 16 * ent[1], "dma", ent[2]))
            self.eng[q].dma_start(out=out, in_=in_).then_inc(ent[0], 16)
            self.ninst += 1
            ent[1] += 1
            tok = (ent[0], 16 * ent[1], "dma", ent[2], dict(self.waited[q]))
            toks.append(tok)
            self.latest[tok[3]] = tok
        for b in _flat(reads):
            for tok in toks:
                b.r[tok[3]] = tok
        for b in _flat(writes):
            b.w = list(toks)
            b.r = {}

    def op(self, k, fn, reads=(), writes=()):
        for t in self._deps(reads, writes):
            self._wait(k, t)
        if self.cnt[k] >= SEM_LIMIT:
            self._newsem(k)
        inst = fn(self.eng[k])
        self.cnt[k] += 1
        self.ninst += 1
        inst.then_inc(self.sem[k], 1)
        snap = dict(self.waited[k])
        if k != "pe":
            snap[self.sid[k]] = self.cnt[k] - 1
        tok = (self.sem[k], self.cnt[k], k, self.sid[k], snap)
        self._commit(tok, reads, writes)
        return tok

    def dma(self, q, out, in_, reads=(), writes=()):
        for t in self._deps(reads, writes):
            self._wait(q, t)
        i = self.dqi[q]
        self.dqi[q] = (i + 1) % NDQ
        ent = self.dq[q][i]
        if ent[1] > 0:
            self._wait(q, (ent[0], 16 * ent[1], "dma", ent[2]))
        self.eng[q].dma_start(out=out, in_=in_).then_inc(ent[0], 16)
        self.ninst += 1
        ent[1] += 1
        tok = (ent[0], 16 * ent[1], "dma", ent[2], dict(self.waited[q]))
        self._commit(tok, reads, writes)
        return tok

    def barrier(self, engines=("pe", "act", "dve", "pool", "sp")):
        toks = list(self.latest.values())
        for k in engines:
            for t in toks:
                self._wait(k, t)

    def next_psum(self, n=8):
        self.psi = (self.psi + 1) % n
        return self.psum[self.psi]


def _consts_host():
    c = {}
    c["c_ident"] = np.eye(128, dtype=np.float32)
    bo = np.zeros((128, 128), np.float32)
    bo[:64, :64] = 1.0
    bo[64:, 64:] = 1.0
    c["c_bo"] = bo
    i = np.arange(64)
    strict = (i[:, None] < i[None, :]).astype(np.float32)
    incl = (i[:, None] <= i[None, :]).astype(np.float32)
    lower = (i[None, :] < i[:, None]).astype(np.float32)
    mA = np.zeros((128, 3, 2, 64), np.float32)
    for h in range(2):
        mA[h * 64:(h + 1) * 64, 0, h, :] = strict
        mA[h * 64:(h + 1) * 64, 1, h, :] = strict
        mA[h * 64:(h + 1) * 64, 2, h, :] = lower
    c["c_maskA"] = mA.reshape(128, 384)
    mB = np.zeros((128, 2, 64), np.float32)
    for h in range(2):
        mB[h * 64:(h + 1) * 64, :, :] = incl[:, None, :]
    c["c_maskB"] = mB.reshape(128, 128)
    mbd = np.zeros((128, 2), np.float32)
    mbd[:64, 0] = 1.0
    mbd[64:, 1] = 1.0
    c["c_mbd"] = mbd
    cm = np.ones((128, 128), np.float32)
    cm[:, 0] = 0.0
    cm[:, 64] = 0.0
    c["c_cmask"] = cm
    k = np.arange(128)[:, None]
    q = np.arange(512)[None, :]
    mm = np.stack([(q >= v * 128 + k) for v in range(4)], axis=1).astype(np.float32)
    c["c_cmla"] = mm.reshape(128, 2048)
    inv = (10000.0 ** (-np.arange(0, 32, 2, dtype=np.float32) / 32.0)).astype(np.float32)
    rp = np.zeros((128, 2), np.float32)
    rp[64:96, 0] = np.concatenate([inv, inv])
    rp[64:96, 1] = np.concatenate([-np.ones(16, np.float32), np.ones(16, np.float32)])
    c["c_rope"] = rp
    c["c_ones"] = np.ones((128, 128), np.float32)
    return c


CONST_SHAPES = {k: v.shape for k, v in _consts_host().items()}

def param_specs(SEQ_PER_CORE, DEPTH):
  return [
    ("x", [SEQ_PER_CORE, S, D], F32), ("mem", [SEQ_PER_CORE, 256, D], F32), ("positions", [SEQ_PER_CORE, S], I32),
    ("w_in", [DEPTH, D, IN_COLS], F32), ("b_gate", [DEPTH, 4, D], F32), ("rwkv_mu", [DEPTH, 1664], F32),
    ("rwkv_w0", [DEPTH, W], F32), ("rwkv_w2", [DEPTH, 64, W], F32), ("rwkv_a0", [DEPTH, W], F32),
    ("rwkv_a2", [DEPTH, 64, W], F32), ("rwkv_k_k", [DEPTH, W], F32), ("rwkv_k_a", [DEPTH, W], F32),
    ("rwkv_r_k", [DEPTH, 8, 64], F32), ("rwkv_lnx_g", [DEPTH, W], F32), ("rwkv_lnx_b", [DEPTH, W], F32),
    ("mla_q_norm", [DEPTH, 384], F32), ("mla_w_uq", [DEPTH, 384, 768], F32), ("mla_kv_norm", [DEPTH, 256], F32),
    ("mla_w_ukv", [DEPTH, 256, 1024], F32), ("conv_w", [DEPTH, 31, W], F32), ("conv_b", [DEPTH, W], F32),
    ("conv_ln_g", [DEPTH, W], F32), ("conv_ln_b", [DEPTH, W], F32), ("xattn_w_mem_kv", [DEPTH, D, 2 * W], F32),
    ("w_o_branch", [DEPTH, 4, W, D], F32), ("w_out", [DEPTH, D, D], F32), ("ln_g", [DEPTH, D], F32),
    ("ln_b", [DEPTH, D], F32),
  ]


PARAM_SPECS = param_specs(SEQ_PER_CORE, DEPTH)


class K:
    def __init__(self, cfg):
        self.cfg = cfg
        self.nc = bass.Bass("TRN2", target_bir_lowering=False)
        nc = self.nc
        self.din = {}
        nseq = cfg.get("nseq", SEQ_PER_CORE)
        nlay = cfg.get("nlay", DEPTH)
        self.single = cfg.get("single", False)
        for n, shp, dt in param_specs(nseq, nlay):
            self.din[n] = nc.dram_tensor(n, shp, dt, kind="ExternalInput").ap()
        for n, shp in CONST_SHAPES.items():
            self.din[n] = nc.dram_tensor(n, list(shp), F32, kind="ExternalInput").ap()
        self.out = nc.dram_tensor("out", [nseq, S, D], F32, kind="ExternalOutput").ap()
        dbg = cfg.get("debug", False)
        kind = "ExternalOutput" if dbg else "Internal"
        self.ybr = nc.dram_tensor("ybr", [4, W, S], BF16, kind=kind).ap()
        self.ybr_buf = [Buf() for _ in range(4)]
        self.x1 = nc.dram_tensor("x1s", [nseq, S, D], F32, kind=kind).ap()
        self.x1_buf = [Buf() for _ in range(SEQ_PER_CORE)]

    def sb(self, es, name, shape, dt):
        self._uid = getattr(self, "_uid", 0) + 1
        return es.enter_context(self.nc.sbuf_tensor(f"{name}_{self._uid}", shape, dt))

    def build(self):
        nc = self.nc
        with contextlib.ExitStack() as es:
            self.S = Sched(nc, es)
            Sx = self.S
            for i in range(8):
                t = es.enter_context(nc.psum_tensor(f"ps{i}", [128, 512], F32))
                Sx.psum.append((t, Buf()))
            self.setup_consts(es)
            self.xT = self.sb(es, "xT", [128, 8, S], BF16)
            self.xT_b = BufG([Buf(), Buf()])
            self.ropeC = self.sb(es, "ropeC", [128, S], F32)
            self.ropeS = self.sb(es, "ropeS", [128, S], F32)
            self.rope_b = Buf()
            self.memT = self.sb(es, "memT", [128, 8, 256], BF16)
            self.memT_b = Buf()
            self.out_b = Buf()
            phases = self.cfg.get("phases", "RMCXE")
            seqs = self.cfg.get("seqs", list(range(SEQ_PER_CORE)))
            layers = self.cfg.get("layers", list(range(DEPTH)))
            for s in seqs:
                self.load_xT(s)
                if "M" in phases:
                    self.rope_tables(s)
                if "X" in phases:
                    self.load_memT(s)
                for l in layers:
                    if l == 0 or True:
                        self.load_params(l)
                    if "R" in phases:
                        self.phase_rwkv(s, l)
                    if "M" in phases:
                        self.phase_mla(s, l)
                    if "C" in phases:
                        self.phase_conv(s, l)
                    if "X" in phases:
                        self.phase_xattn(s, l)
                    if "E" in phases:
                        self.phase_epi(s, l)
            Sx.barrier(engines=("sp",))
        return nc

    def setup_consts(self, es):
        Sx = self.S
        self.cb = Buf()
        c = {}
        for n, shp in CONST_SHAPES.items():
            if n == "c_cmla":
                continue
            t = self.sb(es, "k_" + n, list(shp), F32)
            Sx.dma("sp", t[:], self.din[n], writes=[self.cb])
            c[n] = t
        self.c = c
        self.ident = c["c_ident"]
        self.bo = c["c_bo"]
        self.ident_bf = self.sb(es, "ident_bf", [128, 128], BF16)
        self.ones_bf = self.sb(es, "ones_bf", [128, 128], BF16)
        self.cmla_bf = self.sb(es, "cmla_bf", [128, 2048], BF16)
        Sx.op("pool", lambda e: e.tensor_copy(self.ident_bf[:], self.ident[:]), reads=[self.cb], writes=[self.cb])
        Sx.op("pool", lambda e: e.memset(self.ones_bf[:], 1.0), writes=[self.cb])
        with contextlib.ExitStack() as es2:
            tmpc = self.sb(es2, "k_cmla_tmp", [128, 2048], F32)
            Sx.dma("sp", tmpc[:], self.din["c_cmla"], writes=[self.cb])
            Sx.op("pool", lambda e: e.tensor_copy(self.cmla_bf[:], tmpc[:]), reads=[self.cb], writes=[self.cb])
            Sx.barrier()
        self.pcol = self.sb(es, "pcol", [128, NPC], F32)
        self.pcol_b = Buf()
        self.stageA = self.sb(es, "stageA", [128, 128], F32)
        self.stageB = self.sb(es, "stageB", [128, 128], F32)
        self.stage_b = Buf()
        self.lng_bc = self.sb(es, "lng_bc", [128, D], F32)
        self.lnb_bc = self.sb(es, "lnb_bc", [128, D], F32)
        self.ln_b_ = Buf()
        self.ones_row = self.sb(es, "ones_row", [1, 128], F32)
        Sx.op("pool", lambda e: e.memset(self.ones_row[:], 1.0), writes=[self.cb])

    def pc(self, name, j=0):
        i = PC[name] + j
        return self.pcol[:, i:i + 1]

    def load_params(self, l):
        Sx = self.S
        d = self.din
        rows = []

        def vec(name, key):
            ap = d[key][l]
            n = 1
            for s_ in ap.shape:
                n *= s_
            rows.append((PC[name], n // 128, ap))

        vec("mu", "rwkv_mu"); vec("w0", "rwkv_w0"); vec("a0", "rwkv_a0"); vec("k_k", "rwkv_k_k")
        vec("k_a", "rwkv_k_a"); vec("r_k", "rwkv_r_k"); vec("lnx_g", "rwkv_lnx_g"); vec("lnx_b", "rwkv_lnx_b")
        vec("q_norm", "mla_q_norm"); vec("kv_norm", "mla_kv_norm"); vec("conv_b", "conv_b")
        vec("conv_ln_g", "conv_ln_g"); vec("conv_ln_b", "conv_ln_b"); vec("b_gate", "b_gate"); vec("conv_w", "conv_w")
        for (c0, nr, ap) in rows:
            if len(ap.shape) == 2:
                if ap.shape[1] == 64:
                    flat = ap.rearrange("h n -> (h n)")
                    src = flat.rearrange("(r p) -> r p", p=128)
                elif ap.shape[0] == 31:
                    src = ap.rearrange("j (c p) -> (j c) p", p=128)
                else:
                    src = ap.rearrange("n (c p) -> (n c) p", p=128)
            else:
                src = ap.rearrange("(r p) -> r p", p=128)
            r = 0
            while r < nr:
                g = c0 + r
                if g < 128:
                    n = min(nr - r, 128 - g)
                    Sx.dma("sp", self.stageA[g:g + n, :], src[r:r + n, :], writes=[self.stage_b])
                else:
                    n = nr - r
                    Sx.dma("sp", self.stageB[g - 128:g - 128 + n, :], src[r:r + n, :], writes=[self.stage_b])
                r += n
        nb = NPC_RAW - 128
        ps, pb = Sx.next_psum()
        Sx.op("pe", lambda e: e.matmul(ps[:, 0:128], self.stageA[:, :], self.ident[:, :], start=True, stop=True),
              reads=[self.stage_b, self.cb], writes=[pb])
        Sx.op("pe", lambda e: e.matmul(ps[:, 128:128 + nb], self.stageB[0:nb, :], self.ident[0:nb, 0:nb], start=True, stop=True),
              reads=[self.stage_b, self.cb], writes=[pb])
        Sx.op("act", lambda e: e.activation(out=self.pcol[:, 0:NPC_RAW], in_=ps[:, 0:NPC_RAW], func=AF.Copy),
              reads=[pb], writes=[self.pcol_b])
        o = PC["omm"]
        Sx.op("dve", lambda e: e.tensor_scalar(out=self.pcol[:, o:o + 13], in0=self.pcol[:, 0:13], scalar1=-1.0, scalar2=1.0,
                                               op0=ALU.mult, op1=ALU.add), reads=[self.pcol_b], writes=[self.pcol_b])
        o2 = PC["omka"]
        ka = PC["k_a"]
        Sx.op("dve", lambda e: e.tensor_scalar(out=self.pcol[:, o2:o2 + 4], in0=self.pcol[:, ka:ka + 4], scalar1=-1.0, scalar2=1.0,
                                               op0=ALU.mult, op1=ALU.add), reads=[self.pcol_b], writes=[self.pcol_b])
        es3 = contextlib.ExitStack()
        self.lnrow = self.sb(es3, "lnrow", [1, 2 * D], F32)
        Sx.dma("sp", self.lnrow[0:1, 0:D], d["ln_g"][l:l + 1, :], writes=[self.ln_b_])
        Sx.dma("sp", self.lnrow[0:1, D:2 * D], d["ln_b"][l:l + 1, :], writes=[self.ln_b_])
        for j, dst in enumerate((self.lng_bc, self.lnb_bc)):
            for hh in range(2):
                ps, pb = Sx.next_psum()
                Sx.op("pe", lambda e, ps=ps, j=j, hh=hh: e.matmul(ps[:, :], self.ones_row[0:1, :],
                                                                  self.lnrow[0:1, j * D + hh * 512:j * D + hh * 512 + 512],
                                                                  start=True, stop=True),
                      reads=[self.ln_b_, self.cb], writes=[pb])
                Sx.op("act", lambda e, ps=ps, dst=dst, hh=hh: e.activation(out=dst[:, hh * 512:(hh + 1) * 512], in_=ps[:, :], func=AF.Copy),
                      reads=[pb], writes=[self.ln_b_])
        Sx.barrier()
        es3.close()

    def load_xT(self, s):
        Sx = self.S
        with contextlib.ExitStack() as es:
            xt = [self.sb(es, f"xtok{i}", [128, D], F32) for i in range(2)]
            xb = [Buf(), Buf()]
            for tt in range(S // 128):
                i = tt % 2
                Sx.dma("sp", xt[i][:], self.din["x"][s, tt * 128:(tt + 1) * 128, :], writes=[xb[i]])
                self.transpose_into_xT(xt[i], xb[i], tt)
            Sx.barrier()

    def transpose_into_xT(self, xtok, xbuf, tt):
        Sx = self.S
        for half in range(2):
            ps, pb = Sx.next_psum()
            for j in range(4):
                kc = half * 4 + j
                Sx.op("pe", lambda e, ps=ps, j=j, kc=kc: e.matmul(ps[:, j * 128:(j + 1) * 128], xtok[:, kc * 128:(kc + 1) * 128],
                                                                  self.ident[:, :], start=True, stop=True),
                      reads=[xbuf, self.cb], writes=[pb])
            out = self.xT[:, half * 4:(half + 1) * 4, tt * 128:(tt + 1) * 128]
            Sx.op("act" if half == 0 else "dve",
                  (lambda e, ps=ps, out=out: e.activation(out=out, in_=ps[:, :].rearrange("p (j t) -> p j t", j=4), func=AF.Copy)) if half == 0 else
                  (lambda e, ps=ps, out=out: e.tensor_copy(out, ps[:, :].rearrange("p (j t) -> p j t", j=4))),
                  reads=[pb], writes=[self.xT_b[half]])

    def load_w_bf(self, dst, dst_buf, src, nkc):
        Sx = self.S
        v = src.rearrange("(kc p) c -> p kc c", p=128)
        Sx.dma_group("pool", [(dst[:, kc, :], v[:, kc, :]) for kc in range(nkc)], writes=[dst_buf])

    def phase_rwkv(self, s, l):
        Sx = self.S
        nc = self.nc
        d = self.din
        T = 128
        with contextlib.ExitStack() as es:
            sb = lambda n, shp, dt=F32: self.sb(es, "rw_" + n, shp, dt)
            wR = sb("wR", [128, 8, 2176], BF16); wRb = Buf()
            self.load_w_bf(wR, wRb, d["w_in"][l, :, OFF_RW:OFF_RW + 2176], 8)
            W2z = sb("W2z", [128, 512]); A2z = sb("A2z", [128, 512]); lb = Buf()
            Sx.op("pool", lambda e: e.memset(W2z[:], 0.0), writes=[lb])
            Sx.op("pool", lambda e: e.memset(A2z[:], 0.0), writes=[lb])
            Sx.dma("sp", W2z[0:64, :], d["rwkv_w2"][l], writes=[lb])
            Sx.dma("sp", A2z[64:128, :], d["rwkv_a2"][l], writes=[lb])
            p_raw = sb("p_raw", [128, 13, T + 1]); prb = Buf()
            Sx.op("pool", lambda e: e.memset(p_raw[:], 0.0), writes=[prb])
            pm = sb("pm", [128, 13, T]); pmc = [Buf() for _ in range(13)]
            pm_r = pmc[0:4]; pm_k = pmc[4:8]; pm_v = pmc[8:12]
            sgate = sb("sgate", [128, 4, T]); sgb = Buf()
            T12 = sb("T12", [128, T]); t12b = Buf()
            names = ["lw", "logP", "asig", "eP", "eN", "ePm", "kk", "kkn", "kp", "rT", "t1", "t2", "t3"]
            tt_ = {n: sb(n, [128, 4, T]) for n in names}
            tb = {n: Buf() for n in names}
            Z = {n: sb("Z" + n, [128, 4, 2, 128]) for n in "abkv"}
            Zb_ = {n: Buf() for n in "abkv"}
            H = [sb(f"H{p}", [128, 128]) for p in range(4)]
            Hb = [Buf() for _ in range(4)]
            for p in range(4):
                Sx.op("pool", lambda e, p=p: e.memset(H[p][:], 0.0), writes=[Hb[p]])
            NSET = self.cfg.get("nset", 4)
            A_sb = [sb(f"A{i}", [128, 384]) for i in range(NSET)]; Ab = [Buf() for _ in range(NSET)]
            R_sb = [sb(f"R{i}", [128, 128]) for i in range(NSET)]; Rb = [Buf() for _ in range(NSET)]
            BKV = [sb(f"BKV{i}", [128, 384]) for i in range(NSET)]; BKVb = [Buf() for _ in range(NSET)]
            Wt = [[sb(f"W{i}_{j}", [128, 128]) for j in range(2)] for i in range(NSET)]
            Wtb = [[Buf() for j in range(2)] for i in range(NSET)]
            MP = [[sb(f"MP{i}_{j}", [128, 256]) for j in range(2)] for i in range(NSET)]
            MPb = [[Buf() for j in range(2)] for i in range(NSET)]
            X_sb = [sb(f"X{i}", [128, 128]) for i in range(NSET)]; Xb = [Buf() for _ in range(NSET)]
            U_sb = [sb(f"U{i}", [128, 128]) for i in range(NSET)]; Ub = [Buf() for _ in range(NSET)]
            HpC = [sb(f"HpC{i}", [128, 128]) for i in range(NSET)]; HpCb = [Buf() for _ in range(NSET)]
            Ycm = sb("Ycm", [128, 4, T]); Ybp = [Buf() for _ in range(4)]
            ybf = sb("ybf", [128, 4, T], BF16); ybb = Buf()
            cb = self.cb
            ident = self.ident
            maskA = self.c["c_maskA"]; maskB = self.c["c_maskB"]; mbd = self.c["c_mbd"]; cmask = self.c["c_cmask"]
            pcb = self.pcol_b
            ydst = self.ybr[0].rearrange("(c p) t -> p c t", p=128)

            def flat(t):
                return t[:, :, :].rearrange("p c t -> p (c t)")

            for blk in range(self.cfg.get('nblk', S // T)):
                t0 = blk * T
                for cbk in range(17):
                    ps, pb = Sx.next_psum()
                    for kc in range(8):
                        Sx.op("pe", lambda e, ps=ps, kc=kc, cbk=cbk: e.matmul(
                            ps[:, 0:T], wR[:, kc, cbk * 128:(cbk + 1) * 128], self.xT[:, kc, t0:t0 + T],
                            start=(kc == 0), stop=(kc == 7)), reads=[wRb, self.xT_b], writes=[pb])
                    if cbk < 13:
                        Sx.op("act", lambda e, ps=ps, cbk=cbk: e.activation(out=p_raw[:, cbk, 1:T + 1], in_=ps[:, 0:T], func=AF.Copy),
                              reads=[pb], writes=[prb])
                    else:
                        Sx.op("act", lambda e, ps=ps, cbk=cbk: e.activation(out=sgate[:, cbk - 13, :], in_=ps[:, 0:T], func=AF.Silu),
                              reads=[pb], writes=[sgb])
                for cbk in range(13):
                    Sx.op("pool", lambda e, cbk=cbk: e.tensor_scalar(out=pm[:, cbk, :], in0=p_raw[:, cbk, 1:T + 1],
                                                                     scalar1=self.pc("omm", cbk), scalar2=None, op0=ALU.mult),
                          reads=[prb, pcb], writes=[pmc[cbk]])
                    Sx.op("dve", lambda e, cbk=cbk: e.scalar_tensor_tensor(out=pm[:, cbk, :], in0=p_raw[:, cbk, 0:T],
                                                                           scalar=self.pc("mu", cbk), in1=pm[:, cbk, :],
                                                                           op0=ALU.mult, op1=ALU.add),
                          reads=[prb, pcb], writes=[pmc[cbk]])
                Sx.op("pool", lambda e: e.tensor_copy(p_raw[:, :, 0:1], p_raw[:, :, T:T + 1]), reads=[prb], writes=[prb])
                r_ = pm[:, 0:4, :]; k_ = pm[:, 4:8, :]; v_ = pm[:, 8:12, :]
                if self.cfg.get("stop_after", 9) < 1:
                    continue
                Sx.op("act", lambda e: e.activation(out=T12[0:64, :], in_=pm[0:64, 12, :], func=AF.Tanh), reads=[pmc[12]], writes=[t12b])
                Sx.op("dve", lambda e: e.tensor_copy(T12[64:128, :], pm[64:128, 12, :]), reads=[pmc[12]], writes=[t12b])
                for c4 in range(4):
                    ps, pb = Sx.next_psum()
                    Sx.op("pe", lambda e, ps=ps, c4=c4: e.matmul(ps[:, 0:T], W2z[:, c4 * 128:(c4 + 1) * 128], T12[:, :], start=True, stop=True),
                          reads=[lb, t12b], writes=[pb])
                    Sx.op("pe", lambda e, ps=ps, c4=c4: e.matmul(ps[:, T:2 * T], A2z[:, c4 * 128:(c4 + 1) * 128], T12[:, :], start=True, stop=True),
                          reads=[lb, t12b], writes=[pb])
                    Sx.op("act", lambda e, ps=ps, c4=c4: e.activation(out=tt_["lw"][:, c4, :], in_=ps[:, 0:T], func=AF.Sigmoid,
                                                                      bias=self.pc("w0", c4)), reads=[pb, pcb], writes=[tb["lw"]])
                    Sx.op("act", lambda e, ps=ps, c4=c4: e.activation(out=tt_["asig"][:, c4, :], in_=ps[:, T:2 * T], func=AF.Sigmoid,
                                                                      bias=self.pc("a0", c4)), reads=[pb, pcb], writes=[tb["asig"]])
                Sx.op("pool", lambda e: e.tensor_scalar(out=flat(tt_["lw"]), in0=flat(tt_["lw"]), scalar1=-0.6065306597126334,
                                                        scalar2=None, op0=ALU.mult), reads=[tb["lw"]], writes=[tb["lw"]])
                for c4 in range(4):
                    Sx.op("dve", lambda e, c4=c4: e.tensor_tensor_scan(out=tt_["logP"][:, c4, :], data0=cmask[:, 0:T], data1=tt_["lw"][:, c4, :],
                                                                      initial=0.0, op0=ALU.mult, op1=ALU.add),
                          reads=[tb["lw"], cb], writes=[tb["logP"]])
                Sx.op("act", lambda e: e.activation(out=flat(tt_["eP"]), in_=flat(tt_["logP"]), func=AF.Exp), reads=[tb["logP"]], writes=[tb["eP"]])
                Sx.op("act", lambda e: e.activation(out=flat(tt_["eN"]), in_=flat(tt_["logP"]), func=AF.Exp, scale=-1.0),
                      reads=[tb["logP"]], writes=[tb["eN"]])
                Sx.op("pool", lambda e: e.tensor_tensor(out=flat(tt_["t1"]), in0=flat(tt_["logP"]), in1=flat(tt_["lw"]), op=ALU.subtract),
                      reads=[tb["logP"], tb["lw"]], writes=[tb["t1"]])
                Sx.op("act", lambda e: e.activation(out=flat(tt_["ePm"]), in_=flat(tt_["t1"]), func=AF.Exp), reads=[tb["t1"]], writes=[tb["ePm"]])
                for c4 in range(4):
                    Sx.op("pool", lambda e, c4=c4: e.tensor_scalar(out=tt_["kk"][:, c4, :], in0=k_[:, c4, :], scalar1=self.pc("k_k", c4),
                                                                   scalar2=None, op0=ALU.mult), reads=[pm_k[c4], pcb], writes=[tb["kk"]])
                Sx.op("act", lambda e: e.activation(out=flat(tt_["t2"]), in_=flat(tt_["kk"]), func=AF.Square), reads=[tb["kk"]], writes=[tb["t2"]])
                ps, pb = Sx.next_psum()
                Sx.op("pe", lambda e, ps=ps: e.matmul(ps[:, 0:4 * T], self.bo[:, :], flat(tt_["t2"]), start=True, stop=True),
                      reads=[tb["t2"], cb], writes=[pb])
                Sx.op("act", lambda e, ps=ps: e.activation(out=flat(tt_["t3"]), in_=ps[:, 0:4 * T], func=AF.Sqrt), reads=[pb], writes=[tb["t3"]])
                Sx.op("dve", lambda e: e.tensor_scalar(out=flat(tt_["t3"]), in0=flat(tt_["t3"]), scalar1=1e-12, scalar2=None, op0=ALU.max),
                      reads=[tb["t3"]], writes=[tb["t3"]])
                Sx.op("dve", lambda e: e.reciprocal(flat(tt_["t2"]), flat(tt_["t3"])), reads=[tb["t3"]], writes=[tb["t2"]])
                Sx.op("pool", lambda e: e.tensor_tensor(out=flat(tt_["kkn"]), in0=flat(tt_["kk"]), in1=flat(tt_["t2"]), op=ALU.mult),
                      reads=[tb["kk"], tb["t2"]], writes=[tb["kkn"]])
                for c4 in range(4):
                    Sx.op("act", lambda e, c4=c4: e.activation(out=tt_["t3"][:, c4, :], in_=tt_["asig"][:, c4, :], func=AF.Identity,
                                                               scale=self.pc("k_a", c4), bias=self.pc("omka", c4)),
                          reads=[tb["asig"], pcb], writes=[tb["t3"]])
                Sx.op("pool", lambda e: e.tensor_tensor(out=flat(tt_["kp"]), in0=k_.rearrange("p c t -> p (c t)"), in1=flat(tt_["t3"]), op=ALU.mult),
                      reads=pm_k + [tb["t3"]], writes=[tb["kp"]])
                Sx.op("dve", lambda e: e.scalar_tensor_tensor(out=flat(tt_["t1"]), in0=flat(tt_["kkn"]), scalar=-1.0, in1=flat(tt_["ePm"]),
                                                              op0=ALU.mult, op1=ALU.mult), reads=[tb["kkn"], tb["ePm"]], writes=[tb["t1"]])
                Sx.op("pool", lambda e: e.tensor_tensor(out=flat(tt_["t2"]), in0=flat(tt_["kkn"]), in1=flat(tt_["asig"]), op=ALU.mult),
                      reads=[tb["kkn"], tb["asig"]], writes=[tb["t2"]])
                Sx.op("pool", lambda e: e.tensor_tensor(out=flat(tt_["t2"]), in0=flat(tt_["t2"]), in1=flat(tt_["eN"]), op=ALU.mult),
                      reads=[tb["t2"], tb["eN"]], writes=[tb["t2"]])
                Sx.op("dve", lambda e: e.tensor_tensor(out=flat(tt_["t3"]), in0=flat(tt_["kp"]), in1=flat(tt_["eN"]), op=ALU.mult),
                      reads=[tb["kp"], tb["eN"]], writes=[tb["t3"]])
                Sx.op("dve", lambda e: e.tensor_tensor(out=flat(tt_["rT"]), in0=r_.rearrange("p c t -> p (c t)"), in1=flat(tt_["eP"]), op=ALU.mult),
                      reads=pm_r + [tb["eP"]], writes=[tb["rT"]])
                mb4 = mbd[:, 0:2].unsqueeze(1).unsqueeze(3).to_broadcast([128, 8, 2, 64])
                for zi, (zn, srcap, srcb) in enumerate([("a", tt_["t1"], tb["t1"]), ("b", tt_["t2"], tb["t2"]), ("k", tt_["t3"], tb["t3"]),
                                                        ("v", None, None)]):
                    if srcap is None:
                        sview = v_.rearrange("p c (h t) -> p (c h) t", h=2)
                    else:
                        sview = srcap[:, :, :].rearrange("p c (h t) -> p (c h) t", h=2)
                    in0 = sview.unsqueeze(2).to_broadcast([128, 8, 2, 64])
                    outv = Z[zn][:, :, :, :].rearrange("p c h (g t) -> p (c h) g t", g=2)
                    Sx.op("dve" if zi % 2 == 0 else "pool",
                          lambda e, outv=outv, in0=in0: e.tensor_tensor(out=outv, in0=in0, in1=mb4, op=ALU.mult),
                          reads=(pm_v if srcb is None else [srcb]) + [cb], writes=[Zb_[zn]])
                if self.cfg.get("stop_after", 9) < 2:
                    continue
                for ch in range(2):
                    U_ = []
                    for pr in range(4):
                        U_.append(dict(Za=Z["a"][:, pr, ch, :], Zb=Z["b"][:, pr, ch, :], Zk=Z["k"][:, pr, ch, :], Zv=Z["v"][:, pr, ch, :],
                                       rTu=tt_["rT"][:, pr, ch * 64:(ch + 1) * 64], pC=tt_["eP"][:, pr, ch * 64 + 63:ch * 64 + 64]))
                    for PRS in self.cfg.get('pr_groups', [[0, 1, 2, 3]]):
                        for pr in PRS:
                            u = U_[pr]; si = pr % NSET
                            Za, Zb, Zk, Zv, rTu = u["Za"], u["Zb"], u["Zk"], u["Zv"], u["rTu"]
                            psA, pbA = Sx.next_psum()
                            Sx.op("pe", lambda e: e.matmul(psA[:, 0:128], Zb, Za, start=True, stop=True), reads=[Zb_["b"], Zb_["a"]], writes=[pbA])
                            Sx.op("pe", lambda e: e.matmul(psA[:, 128:256], Zk, Za, start=True, stop=True), reads=[Zb_["k"], Zb_["a"]], writes=[pbA])
                            Sx.op("pe", lambda e: e.matmul(psA[:, 256:384], Za, Zb, start=True, stop=True), reads=[Zb_["b"], Zb_["a"]], writes=[pbA])
                            Sx.op("dve", lambda e: e.tensor_tensor(out=A_sb[si][:, :], in0=psA[:, 0:384], in1=maskA[:, :], op=ALU.mult),
                                  reads=[pbA, cb], writes=[Ab[si]])
                            psB, pbB = Sx.next_psum()
                            Sx.op("pe", lambda e: e.matmul(psB[:, 0:64], Zb, rTu, start=True, stop=True), reads=[Zb_["b"], tb["rT"]], writes=[pbB])
                            Sx.op("pe", lambda e: e.matmul(psB[:, 64:128], Zk, rTu, start=True, stop=True), reads=[Zb_["k"], tb["rT"]], writes=[pbB])
                            Sx.op("dve", lambda e: e.tensor_tensor(out=R_sb[si][:, :], in0=psB[:, 0:128], in1=maskB[:, :], op=ALU.mult),
                                  reads=[pbB, cb], writes=[Rb[si]])
                        for pr in PRS:
                            u = U_[pr]; si = pr % NSET
                            Za, Zb, Zk, Zv, rTu = u["Za"], u["Zb"], u["Zk"], u["Zv"], u["rTu"]
                            psC, pbC = Sx.next_psum()
                            Sx.op("pe", lambda e: e.matmul(psC[:, 0:128], Zb, ident[:, :], start=True, stop=True), reads=[Zb_["b"], cb], writes=[pbC])
                            Sx.op("pe", lambda e: e.matmul(psC[:, 128:256], Zk, ident[:, :], start=True, stop=True), reads=[Zb_["k"], cb], writes=[pbC])
                            Sx.op("pe", lambda e: e.matmul(psC[:, 256:384], Zv, ident[:, :], start=True, stop=True), reads=[Zb_["v"], cb], writes=[pbC])
                            Sx.op("act", lambda e: e.activation(out=BKV[si][:, :], in_=psC[:, 0:384], func=AF.Copy), reads=[pbC], writes=[BKVb[si]])
                            Sx.op("pool", lambda e: e.tensor_tensor(out=Wt[si][0][:, :], in0=A_sb[si][:, 0:128], in1=ident[:, :], op=ALU.add),
                                  reads=[Ab[si], cb], writes=[Wtb[si][0]])
                        Mp = {pr: A_sb[pr % NSET][:, 0:128] for pr in PRS}
                        Pp = {pr: A_sb[pr % NSET][:, 256:384] for pr in PRS}
                        mpb = {pr: Ab[pr % NSET] for pr in PRS}
                        for j in range(1, 6):
                            cur = j % 2
                            for pr in PRS:
                                si = pr % NSET
                                psN, pbN = Sx.next_psum()
                                Sx.op("pe", lambda e: e.matmul(psN[:, 0:128], Pp[pr], Mp[pr], start=True, stop=True), reads=[mpb[pr]], writes=[pbN])
                                Sx.op("pe", lambda e: e.matmul(psN[:, 128:256], Mp[pr], Pp[pr], start=True, stop=True), reads=[mpb[pr]], writes=[pbN])
                                Sx.op("act", lambda e: e.activation(out=MP[si][cur][:, :], in_=psN[:, 0:256], func=AF.Copy), reads=[pbN], writes=[MPb[si][cur]])
                                Mp[pr] = MP[si][cur][:, 0:128]; Pp[pr] = MP[si][cur][:, 128:256]; mpb[pr] = MPb[si][cur]
                            for pr in PRS:
                                si = pr % NSET
                                psW, pbW = Sx.next_psum()
                                Sx.op("pe", lambda e: e.matmul(psW[:, 0:128], Pp[pr], Wt[si][(j - 1) % 2][:, :], start=True, stop=True),
                                      reads=[mpb[pr], Wtb[si][(j - 1) % 2]], writes=[pbW])
                                Sx.op("dve", lambda e: e.tensor_tensor(out=Wt[si][j % 2][:, :], in0=psW[:, 0:128], in1=Wt[si][(j - 1) % 2][:, :], op=ALU.add),
                                      reads=[pbW, Wtb[si][(j - 1) % 2]], writes=[Wtb[si][j % 2]])
                        for pr in PRS:
                            u = U_[pr]; si = pr % NSET
                            psX, pbX = Sx.next_psum()
                            Sx.op("pe", lambda e: e.matmul(psX[:, 0:128], u["Za"], H[pr][:, :], start=True, stop=False), reads=[Zb_["a"], Hb[pr]], writes=[pbX])
                            Sx.op("pe", lambda e: e.matmul(psX[:, 0:128], A_sb[si][:, 128:256], BKV[si][:, 256:384], start=False, stop=True),
                                  reads=[Ab[si], BKVb[si]], writes=[pbX])
                            Sx.op("act", lambda e: e.activation(out=X_sb[si][:, :], in_=psX[:, 0:128], func=AF.Copy), reads=[pbX], writes=[Xb[si]])
                        for pr in PRS:
                            si = pr % NSET
                            psU, pbU = Sx.next_psum()
                            Sx.op("pe", lambda e: e.matmul(psU[:, 0:128], Wt[si][1][:, :], X_sb[si][:, :], start=True, stop=True), reads=[Wtb[si][1], Xb[si]], writes=[pbU])
                            Sx.op("dve", lambda e: e.tensor_copy(U_sb[si][:, :], psU[:, 0:128]), reads=[pbU], writes=[Ub[si]])
                        for pr in PRS:
                            u = U_[pr]; si = pr % NSET
                            psY, pbY = Sx.next_psum()
                            Sx.op("pe", lambda e: e.matmul(psY[:, 0:64], H[pr][:, :], u["rTu"], start=True, stop=False), reads=[Hb[pr], tb["rT"]], writes=[pbY])
                            Sx.op("pe", lambda e: e.matmul(psY[:, 0:64], U_sb[si][:, :], R_sb[si][:, 0:64], start=False, stop=False),
                                  reads=[Ub[si], Rb[si]], writes=[pbY])
                            Sx.op("pe", lambda e: e.matmul(psY[:, 0:64], BKV[si][:, 256:384], R_sb[si][:, 64:128], start=False, stop=True),
                                  reads=[BKVb[si], Rb[si]], writes=[pbY])
                            Sx.op("act", lambda e: e.activation(out=Ycm[:, pr, ch * 64:(ch + 1) * 64], in_=psY[:, 0:64], func=AF.Copy), reads=[pbY], writes=[Ybp[pr]])
                            psG, pbG = Sx.next_psum()
                            Sx.op("pe", lambda e: e.matmul(psG[:, 0:128], BKV[si][:, 0:128], U_sb[si][:, :], start=True, stop=False),
                                  reads=[BKVb[si], Ub[si]], writes=[pbG])
                            Sx.op("pe", lambda e: e.matmul(psG[:, 0:128], BKV[si][:, 128:256], BKV[si][:, 256:384], start=False, stop=True),
                                  reads=[BKVb[si]], writes=[pbG])
                            Sx.op("pool", lambda e: e.tensor_scalar(out=HpC[si][:, :], in0=H[pr][:, :], scalar1=u["pC"], scalar2=None, op0=ALU.mult),
                                  reads=[Hb[pr], tb["eP"]], writes=[HpCb[si]])
                            Sx.op("dve", lambda e: e.scalar_tensor_tensor(out=H[pr][:, :], in0=psG[:, 0:128], scalar=u["pC"], in1=HpC[si][:, :],
                                                                          op0=ALU.mult, op1=ALU.add),
                                  reads=[pbG, HpCb[si], tb["eP"]], writes=[Hb[pr]])
                if self.cfg.get("stop_after", 9) < 3:
                    continue
                NT_ = 4 * T
                psM, pbM = Sx.next_psum()
                Sx.op("pe", lambda e: e.matmul(psM[:, 0:NT_], self.bo[:, :], flat(Ycm), start=True, stop=True), reads=Ybp + [cb], writes=[pbM])
                Sx.op("act", lambda e: e.activation(out=flat(tt_["t1"]), in_=flat(Ycm), func=AF.Square), reads=Ybp, writes=[tb["t1"]])
                psQ, pbQ = Sx.next_psum()
                Sx.op("pe", lambda e: e.matmul(psQ[:, 0:NT_], self.bo[:, :], flat(tt_["t1"]), start=True, stop=True), reads=[tb["t1"], cb], writes=[pbQ])
                Sx.op("act", lambda e: e.activation(out=flat(tt_["t2"]), in_=psM[:, 0:NT_], func=AF.Copy, scale=1.0 / 64.0), reads=[pbM], writes=[tb["t2"]])
                Sx.op("pool", lambda e: e.tensor_tensor(out=flat(tt_["t3"]), in0=flat(tt_["t2"]), in1=flat(tt_["t2"]), op=ALU.mult),
                      reads=[tb["t2"]], writes=[tb["t3"]])
                Sx.op("dve", lambda e: e.scalar_tensor_tensor(out=flat(tt_["t3"]), in0=psQ[:, 0:NT_], scalar=1.0 / 64.0, in1=flat(tt_["t3"]),
                                                              op0=ALU.mult, op1=ALU.subtract), reads=[pbQ, tb["t3"]], writes=[tb["t3"]])
                Sx.op("dve", lambda e: e.tensor_scalar(out=flat(tt_["t3"]), in0=flat(tt_["t3"]), scalar1=64e-5, scalar2=None, op0=ALU.add),
                      reads=[tb["t3"]], writes=[tb["t3"]])
                Sx.op("act", lambda e: e.activation(out=flat(tt_["t3"]), in_=flat(tt_["t3"]), func=AF.Sqrt), reads=[tb["t3"]], writes=[tb["t3"]])
                Sx.op("dve", lambda e: e.reciprocal(flat(tt_["t1"]), flat(tt_["t3"])), reads=[tb["t3"]], writes=[tb["t1"]])
                Sx.op("pool", lambda e: e.tensor_tensor(out=flat(tt_["t2"]), in0=flat(Ycm), in1=flat(tt_["t2"]), op=ALU.subtract),
                      reads=Ybp + [tb["t2"]], writes=[tb["t2"]])
                Sx.op("pool", lambda e: e.tensor_tensor(out=flat(tt_["t2"]), in0=flat(tt_["t2"]), in1=flat(tt_["t1"]), op=ALU.mult),
                      reads=[tb["t2"], tb["t1"]], writes=[tb["t2"]])
                for c4 in range(4):
                    Sx.op("act", lambda e, c4=c4: e.activation(out=tt_["t2"][:, c4, :], in_=tt_["t2"][:, c4, :], func=AF.Identity,
                                                               scale=self.pc("lnx_g", c4), bias=self.pc("lnx_b", c4)),
                          reads=[tb["t2"], pcb], writes=[tb["t2"]])
                    Sx.op("dve", lambda e, c4=c4: e.scalar_tensor_tensor(out=tt_["t1"][:, c4, :], in0=r_[:, c4, :], scalar=self.pc("r_k", c4),
                                                                         in1=tt_["kp"][:, c4, :], op0=ALU.mult, op1=ALU.mult),
                          reads=[pm_r[c4], tb["kp"], pcb, tb["t1"]], writes=[tb["t1"]])
                psR, pbR = Sx.next_psum()
                Sx.op("pe", lambda e: e.matmul(psR[:, 0:NT_], self.bo[:, :], flat(tt_["t1"]), start=True, stop=True), reads=[tb["t1"], cb], writes=[pbR])
                Sx.op("dve", lambda e: e.tensor_tensor(out=flat(tt_["t3"]), in0=psR[:, 0:NT_], in1=v_.rearrange("p c t -> p (c t)"), op=ALU.mult),
                      reads=[pbR] + pm_v, writes=[tb["t3"]])
                Sx.op("pool", lambda e: e.tensor_tensor(out=flat(tt_["t3"]), in0=flat(tt_["t3"]), in1=flat(tt_["t2"]), op=ALU.add),
                      reads=[tb["t3"], tb["t2"]], writes=[tb["t3"]])
                Sx.op("pool", lambda e: e.tensor_tensor(out=flat(ybf), in0=flat(tt_["t3"]), in1=flat(sgate), op=ALU.mult),
                      reads=[tb["t3"], sgb], writes=[ybb])
                Sx.dma("sp", ydst[:, :, t0:t0 + T], ybf[:, :, :], reads=[ybb], writes=[self.ybr_buf[0]])
            Sx.barrier()

    def rope_tables(self, s):
        Sx = self.S
        rp = self.c["c_rope"]
        cb = self.cb
        with contextlib.ExitStack() as es:
            posi = self.sb(es, "posi", [128, S], I32)
            ang = self.sb(es, "ang", [128, S], F32)
            t1 = self.sb(es, "rp_t1", [128, S], F32)
            t2 = self.sb(es, "rp_t2", [128, S], F32)
            ki = self.sb(es, "rp_ki", [128, S], I32)
            b = Buf()
            R = slice(64, 96)
            Sx.dma("sp", posi[R, :], self.din["positions"][s:s + 1, :].broadcast_to([32, S]), writes=[b])
            Sx.op("dve", lambda e: e.tensor_copy(ang[R, :], posi[R, :]), reads=[b], writes=[b])
            Sx.op("dve", lambda e: e.tensor_scalar(out=ang[R, :], in0=ang[R, :], scalar1=rp[R, 0:1], scalar2=None, op0=ALU.mult),
                  reads=[b, cb], writes=[b])
            TWO_PI = 6.283185307179586
            for which, dst in ((0, self.ropeS), (1, self.ropeC)):
                shift = 0.0 if which == 0 else 1.5707963267948966
                Sx.op("dve", lambda e: e.tensor_scalar(out=t1[R, :], in0=ang[R, :], scalar1=shift, scalar2=None, op0=ALU.add), reads=[b], writes=[b])
                Sx.op("dve", lambda e: e.tensor_scalar(out=t2[R, :], in0=t1[R, :], scalar1=1.0 / TWO_PI, scalar2=0.5, op0=ALU.mult, op1=ALU.add),
                      reads=[b], writes=[b])
                Sx.op("dve", lambda e: e.tensor_copy(ki[R, :], t2[R, :]), reads=[b], writes=[b])
                Sx.op("dve", lambda e: e.tensor_copy(t2[R, :], ki[R, :]), reads=[b], writes=[b])
                Sx.op("dve", lambda e: e.scalar_tensor_tensor(out=t1[R, :], in0=t2[R, :], scalar=-TWO_PI, in1=t1[R, :], op0=ALU.mult, op1=ALU.add),
                      reads=[b], writes=[b])
                Sx.op("dve", lambda e: e.tensor_scalar(out=t2[R, :], in0=t1[R, :], scalar1=-3.141592653589793, scalar2=TWO_PI, op0=ALU.is_lt, op1=ALU.mult),
                      reads=[b], writes=[b])
                Sx.op("dve", lambda e: e.tensor_tensor(out=t1[R, :], in0=t1[R, :], in1=t2[R, :], op=ALU.add), reads=[b], writes=[b])
                Sx.op("dve", lambda e: e.tensor_scalar(out=t2[R, :], in0=t1[R, :], scalar1=3.141592653589793, scalar2=-TWO_PI, op0=ALU.is_gt, op1=ALU.mult),
                      reads=[b], writes=[b])
                Sx.op("dve", lambda e: e.tensor_tensor(out=t1[R, :], in0=t1[R, :], in1=t2[R, :], op=ALU.add), reads=[b], writes=[b])
                Sx.op("dve", lambda e: e.tensor_scalar(out=t1[R, :], in0=t1[R, :], scalar1=3.1415925, scalar2=-3.1415925, op0=ALU.min, op1=ALU.max),
                      reads=[b], writes=[b])
                Sx.op("act", lambda e, dst=dst: e.activation(out=dst[R, :], in_=t1[R, :], func=AF.Sin), reads=[b], writes=[self.rope_b])
            Sx.op("dve", lambda e: e.tensor_scalar(out=self.ropeS[R, :], in0=self.ropeS[R, :], scalar1=rp[R, 1:2], scalar2=None, op0=ALU.mult),
                  reads=[self.rope_b, cb], writes=[self.rope_b])
            Sx.barrier()

    def load_memT(self, s):
        Sx = self.S
        with contextlib.ExitStack() as es:
            mt = [self.sb(es, f"mtok{i}", [128, D], F32) for i in range(2)]
            mb = [Buf(), Buf()]
            for tt in range(2):
                Sx.dma("sp", mt[tt][:], self.din["mem"][s, tt * 128:(tt + 1) * 128, :], writes=[mb[tt]])
                for half in range(2):
                    ps, pb = Sx.next_psum()
                    for j in range(4):
                        kc = half * 4 + j
                        Sx.op("pe", lambda e: e.matmul(ps[:, j * 128:(j + 1) * 128], mt[tt][:, kc * 128:(kc + 1) * 128], self.ident[:, :], start=True, stop=True),
                              reads=[mb[tt], self.cb], writes=[pb])
                    Sx.op("act", lambda e: e.activation(out=self.memT[:, half * 4:(half + 1) * 4, tt * 128:(tt + 1) * 128],
                                                        in_=ps[:, :].rearrange("p (j t) -> p j t", j=4), func=AF.Copy),
                          reads=[pb], writes=[self.memT_b])
            Sx.barrier()

    def inproj(self, ps, pb, wt, wb, c0, ncol, t0, ntok):
        Sx = self.S
        for kc in range(8):
            Sx.op("pe", lambda e, kc=kc: e.matmul(ps[0:ncol, 0:ntok], wt[:, kc, c0:c0 + ncol], self.xT[:, kc, t0:t0 + ntok],
                                                  start=(kc == 0), stop=(kc == 7)), reads=[wb, self.xT_b], writes=[pb])

    def rms_latent(self, es, tag, lat_f, sq, nch, dim, gname, outn, bufs):
        Sx = self.S
        lb, sqb, ob, rb, rstd = bufs
        ps, pb = Sx.next_psum(4)
        for i in range(nch):
            Sx.op("pe", lambda e, i=i: e.matmul(ps[:, :], self.c["c_ones"][:, :], sq[:, i, :], start=(i == 0), stop=(i == nch - 1)),
                  reads=[sqb, self.cb], writes=[pb])
        Sx.op("dve", lambda e: e.tensor_scalar(out=rstd[:, :], in0=ps[:, :], scalar1=1.0 / dim, scalar2=1e-6, op0=ALU.mult, op1=ALU.add),
              reads=[pb], writes=[rb])
        Sx.op("act", lambda e: e.activation(out=rstd[:, :], in_=rstd[:, :], func=AF.Sqrt), reads=[rb], writes=[rb])
        Sx.op("dve", lambda e: e.reciprocal(rstd[:, :], rstd[:, :]), reads=[rb], writes=[rb])
        for i in range(nch):
            Sx.op("dve", lambda e, i=i: e.scalar_tensor_tensor(out=outn[:, i, :], in0=lat_f[:, i, :], scalar=self.pc(gname, i), in1=rstd[:, :],
                                                               op0=ALU.mult, op1=ALU.mult), reads=[lb, rb, self.pcol_b], writes=[ob])

    def phase_mla(self, s, l):
        Sx = self.S
        d = self.din
        cb = self.cb
        scale = 96.0 ** -0.5
        with contextlib.ExitStack() as es:
            sb = lambda n, shp, dt=F32: self.sb(es, "ml_" + n, shp, dt)
            wM = sb("wM", [128, 8, 1184], BF16); wMb = Buf()
            self.load_w_bf(wM, wMb, d["w_in"][l, :, OFF_QLAT:OFF_QLAT + 1184], 8)
            wks = sb("wks", [128, 8, 96], BF16); wksb = Buf()
            Sx.op("pool", lambda e: e.memset(wks[:], 0.0), writes=[wksb])
            vk = d["w_in"][l, :, OFF_KPE:OFF_KPE + 32].rearrange("(kc p) c -> p kc c", p=128)
            Sx.dma("pool", wks[:, :, 64:80], vk[:, :, 16:32], writes=[wksb])
            Sx.dma("pool", wks[:, :, 80:96], vk[:, :, 0:16], writes=[wksb])
            Wuq = sb("Wuq", [128, 3, 768], BF16); Wuqb = Buf()
            self.load_w_bf(Wuq, Wuqb, d["mla_w_uq"][l], 3)
            Wus = sb("Wus", [128, 3, 768], BF16); Wusb = Buf()
            Wq4 = Wuq[:, :, :].rearrange("p k (h c) -> p k h c", h=8)
            Ws4 = Wus[:, :, :].rearrange("p k (h c) -> p k h c", h=8)
            Sx.op("pool", lambda e: e.tensor_copy(Ws4[:, :, :, 0:64], Wq4[:, :, :, 0:64]), reads=[Wuqb], writes=[Wusb])
            Sx.op("pool", lambda e: e.tensor_copy(Ws4[:, :, :, 64:80], Wq4[:, :, :, 80:96]), reads=[Wuqb], writes=[Wusb])
            Sx.op("pool", lambda e: e.tensor_copy(Ws4[:, :, :, 80:96], Wq4[:, :, :, 64:80]), reads=[Wuqb], writes=[Wusb])
            Wukv = sb("Wukv", [128, 2, 1024], BF16); Wukvb = Buf()
            self.load_w_bf(Wukv, Wukvb, d["mla_w_ukv"][l], 2)
            QT = sb("QT", [128, 8, 512], BF16); QTh = [Buf() for _ in range(8)]
            KT = sb("KT", [128, 8, S], BF16); KTh = [Buf() for _ in range(8)]
            V = sb("V", [128, 16, 512], BF16); Vb = Buf()
            qlf = sb("qlf", [128, 3, 512]); qlb = Buf()
            qsq = sb("qsq", [128, 3, 512]); qsb = Buf()
            qn = sb("qn", [128, 3, 512], BF16); qnb = Buf()
            klf = sb("klf", [128, 2, 512]); klb = Buf()
            ksq = sb("ksq", [128, 2, 512]); ksb = Buf()
            kvn = sb("kvn", [128, 2, 512], BF16); knb = Buf()
            rq = sb("rq", [128, 512]); rqb = Buf()
            rk = sb("rk", [128, 512]); rkb = Buf()
            tas = [sb(f"ta{i}", [128, 512]) for i in range(2)]; tabs = [Buf(), Buf()]
            tbs = [sb(f"tb{i}", [128, 512]) for i in range(2)]; tbbs = [Buf(), Buf()]
            ta = tas[0]; tab = tabs[0]; tbb_ = tbs[0]; tbb = tbbs[0]
            PT = [sb(f"PT{i}", [128, 512], BF16) for i in range(3)]; PTb = [Buf() for _ in range(3)]
            sgt = sb("sgt", [128, 512]); sgb = Buf()
            rl = sb("rl", [128, 512]); rlb = Buf()
            yb = [sb(f"yb{i}", [128, 512], BF16) for i in range(2)]; ybb = [Buf(), Buf()]
            ydst = self.ybr[1]
            R = slice(64, 96)
            pti = 0
            for g in range(4):
                t0 = g * 512
                for i in range(3):
                    ps, pb = Sx.next_psum(4)
                    self.inproj(ps, pb, wM, wMb, i * 128, 128, t0, 512)
                    Sx.op("act", lambda e: e.activation(out=qlf[:, i, :], in_=ps[:, :], func=AF.Copy), reads=[pb], writes=[qlb])
                    Sx.op("act", lambda e: e.activation(out=qsq[:, i, :], in_=ps[:, :], func=AF.Square), reads=[pb], writes=[qsb])
                self.rms_latent(es, "q", qlf, qsq, 3, 384.0, "q_norm", qn, (qlb, qsb, qnb, rqb, rq))
                for i in range(2):
                    ps, pb = Sx.next_psum(4)
                    self.inproj(ps, pb, wM, wMb, 384 + i * 128, 128, t0, 512)
                    Sx.op("act", lambda e: e.activation(out=klf[:, i, :], in_=ps[:, :], func=AF.Copy), reads=[pb], writes=[klb])
                    Sx.op("act", lambda e: e.activation(out=ksq[:, i, :], in_=ps[:, :], func=AF.Square), reads=[pb], writes=[ksb])
                self.rms_latent(es, "k", klf, ksq, 2, 256.0, "kv_norm", kvn, (klb, ksb, knb, rkb, rk))
                ps1, pb1 = Sx.next_psum(4)
                self.inproj(ps1, pb1, wM, wMb, 576, 96, t0, 512)
                ps2, pb2 = Sx.next_psum(4)
                self.inproj(ps2, pb2, wks, wksb, 0, 96, t0, 512)
                Sx.op("dve", lambda e: e.tensor_tensor(out=ta[R, :], in0=ps1[R, :], in1=self.ropeC[R, t0:t0 + 512], op=ALU.mult),
                      reads=[pb1, self.rope_b], writes=[tab])
                Sx.op("dve", lambda e: e.tensor_tensor(out=tbb_[R, :], in0=ps2[R, :], in1=self.ropeS[R, t0:t0 + 512], op=ALU.mult),
                      reads=[pb2, self.rope_b], writes=[tbb])
                Sx.op("pool", lambda e: e.tensor_tensor(out=ta[R, :], in0=ta[R, :], in1=tbb_[R, :], op=ALU.add), reads=[tab, tbb], writes=[tab])
                for h in range(8):
                    Sx.op("pool" if h % 2 else "act",
                          (lambda e: e.tensor_copy(KT[R, h, t0:t0 + 512], ta[R, :])) if h % 2 else
                          (lambda e: e.activation(out=KT[R, h, t0:t0 + 512], in_=ta[R, :], func=AF.Copy)),
                          reads=[tab], writes=[KTh[h]])
                for h in range(8):
                    ta = tas[h % 2]; tab = tabs[h % 2]; tbb_ = tbs[h % 2]; tbb = tbbs[h % 2]
                    ps1, pb1 = Sx.next_psum(4)
                    ps2, pb2 = Sx.next_psum(4)
                    for kc in range(3):
                        Sx.op("pe", lambda e: e.matmul(ps1[0:96, :], Wuq[:, kc, h * 96:(h + 1) * 96], qn[:, kc, :], start=(kc == 0), stop=(kc == 2)),
                              reads=[Wuqb, qnb], writes=[pb1])
                    for kc in range(3):
                        Sx.op("pe", lambda e: e.matmul(ps2[0:96, :], Wus[:, kc, h * 96:(h + 1) * 96], qn[:, kc, :], start=(kc == 0), stop=(kc == 2)),
                              reads=[Wusb, qnb], writes=[pb2])
                    Sx.op("act", lambda e: e.activation(out=QT[0:64, h, :], in_=ps1[0:64, :], func=AF.Copy), reads=[pb1], writes=[QTh[h]])
                    Sx.op("dve", lambda e: e.tensor_tensor(out=ta[R, :], in0=ps1[R, :], in1=self.ropeC[R, t0:t0 + 512], op=ALU.mult),
                          reads=[pb1, self.rope_b], writes=[tab])
                    Sx.op("dve", lambda e: e.tensor_tensor(out=tbb_[R, :], in0=ps2[R, :], in1=self.ropeS[R, t0:t0 + 512], op=ALU.mult),
                          reads=[pb2, self.rope_b], writes=[tbb])
                    Sx.op("pool", lambda e: e.tensor_tensor(out=QT[R, h, :], in0=ta[R, :], in1=tbb_[R, :], op=ALU.add), reads=[tab, tbb], writes=[QTh[h]])
                    ps3, pb3 = Sx.next_psum(4)
                    for kc in range(2):
                        Sx.op("pe", lambda e: e.matmul(ps3[0:64, :], Wukv[:, kc, h * 128:h * 128 + 64], kvn[:, kc, :], start=(kc == 0), stop=(kc == 1)),
                              reads=[Wukvb, knb], writes=[pb3])
                    Sx.op("act", lambda e: e.activation(out=KT[0:64, h, t0:t0 + 512], in_=ps3[0:64, :], func=AF.Copy), reads=[pb3], writes=[KTh[h]])
                Wv = Wukv[:, :, :].rearrange("p k (h c) -> p k h c", h=8)
                for tt in range(4):
                    ps, pb = Sx.next_psum(4)
                    for kc in range(2):
                        Sx.op("pe", lambda e: e.matmul(ps[:, :].rearrange("p (h c) -> p h c", h=8), kvn[:, kc, tt * 128:(tt + 1) * 128], Wv[:, kc, :, 64:128],
                                                       start=(kc == 0), stop=(kc == 1)), reads=[Wukvb, knb], writes=[pb])
                    Sx.op("dve", lambda e: e.tensor_copy(V[:, g * 4 + tt, :], ps[:, :]), reads=[pb], writes=[Vb])
                for pr in range(4):
                    accs = []
                    for hh in range(2):
                        h = pr * 2 + hh
                        o_ps, o_pb = Sx.psum[4 + hh * 2]
                        l_ps, l_pb = Sx.psum[5 + hh * 2]
                        accs.append((o_ps, o_pb, l_ps, l_pb))
                        nj = 4 * g + 4
                        for j in range(nj):
                            v = max(0, j - 4 * g)
                            c0 = v * 128
                            sps, spb = Sx.next_psum(4)
                            Sx.op("pe", lambda e: e.matmul(sps[:, c0:512], KT[0:96, h, j * 128:(j + 1) * 128], QT[0:96, h, c0:512], start=True, stop=True),
                                  reads=[KTh[h], QTh[h]], writes=[spb])
                            pt = PT[pti]; ptb = PTb[pti]; pti = (pti + 1) % 3
                            Sx.op("act", lambda e: e.activation(out=pt[:, c0:512], in_=sps[:, c0:512], func=AF.Exp, scale=scale), reads=[spb], writes=[ptb])
                            if j >= 4 * g:
                                Sx.op("pool", lambda e: e.tensor_tensor(out=pt[:, c0:c0 + 128], in0=pt[:, c0:c0 + 128],
                                                                        in1=self.cmla_bf[:, v * 512 + c0:v * 512 + c0 + 128], op=ALU.mult),
                                      reads=[ptb, cb], writes=[ptb])
                            Sx.op("pe", lambda e: e.matmul(o_ps[:, c0:512], V[:, j, pr * 128:(pr + 1) * 128], pt[:, c0:512], start=(j == 0), stop=(j == nj - 1)),
                                  reads=[Vb, ptb], writes=[o_pb])
                            Sx.op("pe", lambda e: e.matmul(l_ps[:, c0:512], self.ones_bf[:, :], pt[:, c0:512], start=(j == 0), stop=(j == nj - 1)),
                                  reads=[cb, ptb], writes=[l_pb])
                    gps, gpb = Sx.next_psum(4)
                    self.inproj(gps, gpb, wM, wMb, 672 + pr * 128, 128, t0, 512)
                    Sx.op("act", lambda e: e.activation(out=sgt[:, :], in_=gps[:, :], func=AF.Silu), reads=[gpb], writes=[sgb])
                    yt = yb[pr % 2]; ytb = ybb[pr % 2]
                    for hh in range(2):
                        o_ps, o_pb, l_ps, l_pb = accs[hh]
                        HR = slice(hh * 64, hh * 64 + 64)
                        Sx.op("dve", lambda e: e.reciprocal(rl[HR, :], l_ps[HR, :]), reads=[l_pb], writes=[rlb])
                        Sx.op("dve", lambda e: e.tensor_tensor(out=rl[HR, :], in0=o_ps[HR, :], in1=rl[HR, :], op=ALU.mult), reads=[o_pb, rlb], writes=[rlb])
                        Sx.op("pool", lambda e: e.tensor_tensor(out=yt[HR, :], in0=rl[HR, :], in1=sgt[HR, :], op=ALU.mult), reads=[rlb, sgb], writes=[ytb])
                    Sx.dma("sp", ydst[pr * 128:(pr + 1) * 128, t0:t0 + 512], yt[:, :], reads=[ytb], writes=[self.ybr_buf[1]])
            Sx.barrier()

    def phase_xattn(self, s, l):
        Sx = self.S
        d = self.din
        cb = self.cb
        scale = 128.0 ** -0.5
        with contextlib.ExitStack() as es:
            sb = lambda n, shp, dt=F32: self.sb(es, "xa_" + n, shp, dt)
            wkv = sb("wkv", [128, 8, 1024], BF16); wkvb = Buf()
            self.load_w_bf(wkv, wkvb, d["xattn_w_mem_kv"][l], 8)
            wX = sb("wX", [128, 8, 1024], BF16); wXb = Buf()
            self.load_w_bf(wX, wXb, d["w_in"][l, :, OFF_XQ:OFF_XQ + 1024], 8)
            KxT = sb("KxT", [128, 4, 256], BF16); Kb = Buf()
            Vx = sb("Vx", [128, 2, 512], BF16); Vb = Buf()
            qx = sb("qx", [128, 512], BF16); qb = Buf()
            PT = [sb(f"PT{i}", [128, 512], BF16) for i in range(2)]; PTb = [Buf(), Buf()]
            sgt = sb("sgt", [128, 512]); sgb = Buf()
            rl = sb("rl", [128, 512]); rlb = Buf()
            yb = [sb(f"yb{i}", [128, 512], BF16) for i in range(2)]; ybb = [Buf(), Buf()]
            for h in range(4):
                ps, pb = Sx.next_psum(4)
                for kc in range(8):
                    Sx.op("pe", lambda e: e.matmul(ps[:, 0:256], wkv[:, kc, h * 128:(h + 1) * 128], self.memT[:, kc, :], start=(kc == 0), stop=(kc == 7)),
                          reads=[wkvb, self.memT_b], writes=[pb])
                Sx.op("act", lambda e: e.activation(out=KxT[:, h, :], in_=ps[:, 0:256], func=AF.Copy), reads=[pb], writes=[Kb])
            for mt in range(2):
                ps, pb = Sx.next_psum(4)
                for kc in range(8):
                    Sx.op("pe", lambda e: e.matmul(ps[:, :], self.memT[:, kc, mt * 128:(mt + 1) * 128], wkv[:, kc, 512:1024], start=(kc == 0), stop=(kc == 7)),
                          reads=[wkvb, self.memT_b], writes=[pb])
                Sx.op("act", lambda e: e.activation(out=Vx[:, mt, :], in_=ps[:, :], func=AF.Copy), reads=[pb], writes=[Vb])
            ydst = self.ybr[3]
            k = 0
            for g in range(4):
                t0 = g * 512
                for h in range(4):
                    ps, pb = Sx.next_psum(4)
                    self.inproj(ps, pb, wX, wXb, h * 128, 128, t0, 512)
                    Sx.op("act", lambda e: e.activation(out=qx[:, :], in_=ps[:, :], func=AF.Copy), reads=[pb], writes=[qb])
                    o_ps, o_pb = Sx.psum[4]
                    l_ps, l_pb = Sx.psum[5]
                    for mt in range(2):
                        sps, spb = Sx.next_psum(4)
                        Sx.op("pe", lambda e: e.matmul(sps[:, :], KxT[:, h, mt * 128:(mt + 1) * 128], qx[:, :], start=True, stop=True), reads=[Kb, qb], writes=[spb])
                        pt = PT[mt]; ptb = PTb[mt]
                        Sx.op("act", lambda e: e.activation(out=pt[:, :], in_=sps[:, :], func=AF.Exp, scale=scale), reads=[spb], writes=[ptb])
                        Sx.op("pe", lambda e: e.matmul(o_ps[:, :], Vx[:, mt, h * 128:(h + 1) * 128], pt[:, :], start=(mt == 0), stop=(mt == 1)),
                              reads=[Vb, ptb], writes=[o_pb])
                        Sx.op("pe", lambda e: e.matmul(l_ps[:, :], self.ones_bf[:, :], pt[:, :], start=(mt == 0), stop=(mt == 1)),
                              reads=[cb, ptb], writes=[l_pb])
                    gps, gpb = Sx.next_psum(4)
                    self.inproj(gps, gpb, wX, wXb, 512 + h * 128, 128, t0, 512)
                    Sx.op("act", lambda e: e.activation(out=sgt[:, :], in_=gps[:, :], func=AF.Silu), reads=[gpb], writes=[sgb])
                    yt = yb[k % 2]; ytb = ybb[k % 2]; k += 1
                    Sx.op("dve", lambda e: e.reciprocal(rl[:, :], l_ps[:, :]), reads=[l_pb], writes=[rlb])
                    Sx.op("dve", lambda e: e.tensor_tensor(out=rl[:, :], in0=o_ps[:, :], in1=rl[:, :], op=ALU.mult), reads=[o_pb, rlb], writes=[rlb])
                    Sx.op("pool", lambda e: e.tensor_tensor(out=yt[:, :], in0=rl[:, :], in1=sgt[:, :], op=ALU.mult), reads=[rlb, sgb], writes=[ytb])
                    Sx.dma("sp", ydst[h * 128:(h + 1) * 128, t0:t0 + 512], yt[:, :], reads=[ytb], writes=[self.ybr_buf[3]])
            Sx.barrier()

    def phase_conv(self, s, l):
        Sx = self.S
        d = self.din
        cb = self.cb
        pcb = self.pcol_b
        ones = self.c["c_ones"]
        with contextlib.ExitStack() as es:
            sb = lambda n, shp, dt=F32: self.sb(es, "cv_" + n, shp, dt)
            wC = sb("wC", [128, 8, 1536], BF16); wCb = Buf()
            self.load_w_bf(wC, wCb, d["w_in"][l, :, OFF_CONV:OFF_CONV + 1536], 8)
            Dg = sb("Dg", [128, 4, 31, 128], BF16); Dgb = BufG([Buf(), Buf()])
            for c in range(4):
                for j in range(31):
                    Sx.op("pool" if (j % 2) else "dve",
                          lambda e: e.tensor_scalar(out=Dg[:, c, j, :], in0=self.ident[:, :], scalar1=self.pc("conv_w", j * 4 + c), scalar2=None, op0=ALU.mult),
                          reads=[cb, pcb], writes=[Dgb[j % 2]])
            hb = sb("hb", [128, 4, 30 + S], BF16); hbb = Buf()
            Sx.op("pool", lambda e: e.memset(hb[:, :, 0:30], 0.0), writes=[hbb])
            sig = sb("sig", [128, 512]); sigb = Buf()
            cv = sb("cvv", [128, 4, 512]); cvb = Buf()
            sq = sb("sq", [128, 4, 512]); sqb = Buf()
            mean = sb("mean", [128, 512]); mb = Buf()
            rstd = sb("rstd", [128, 512]); rb = Buf()
            t1 = sb("t1", [128, 512]); t1b = Buf()
            sgt = sb("sgt", [128, 512]); sgb = Buf()
            yb = [sb(f"yb{i}", [128, 512], BF16) for i in range(2)]; ybb = [Buf(), Buf()]
            ydst = self.ybr[2]
            for g in range(4):
                t0 = g * 512
                for c in range(4):
                    ps1, pb1 = Sx.next_psum()
                    self.inproj(ps1, pb1, wC, wCb, c * 128, 128, t0, 512)
                    ps2, pb2 = Sx.next_psum()
                    self.inproj(ps2, pb2, wC, wCb, 512 + c * 128, 128, t0, 512)
                    Sx.op("act", lambda e: e.activation(out=sig[:, :], in_=ps2[:, :], func=AF.Sigmoid), reads=[pb2], writes=[sigb])
                    Sx.op("dve", lambda e: e.tensor_tensor(out=hb[:, c, 30 + t0:30 + t0 + 512], in0=ps1[:, :], in1=sig[:, :], op=ALU.mult),
                          reads=[pb1, sigb], writes=[hbb])
                for c in range(4):
                    ps, pb = Sx.next_psum()
                    for j in range(31):
                        Sx.op("pe", lambda e: e.matmul(ps[:, :], Dg[:, c, j, :], hb[:, c, t0 + j:t0 + j + 512], start=(j == 0), stop=(j == 30)),
                              reads=[Dgb, hbb], writes=[pb])
                    Sx.op("act", lambda e: e.activation(out=cv[:, c, :], in_=ps[:, :], func=AF.Identity, bias=self.pc("conv_b", c)), reads=[pb, pcb], writes=[cvb])
                Sx.op("act", lambda e: e.activation(out=sq[:, :, :].rearrange("p c t -> p (c t)"), in_=cv[:, :, :].rearrange("p c t -> p (c t)"), func=AF.Square),
                      reads=[cvb], writes=[sqb])
                psM, pbM = Sx.next_psum()
                psQ, pbQ = Sx.next_psum()
                for c in range(4):
                    Sx.op("pe", lambda e: e.matmul(psM[:, :], ones[:, :], cv[:, c, :], start=(c == 0), stop=(c == 3)), reads=[cvb, cb], writes=[pbM])
                for c in range(4):
                    Sx.op("pe", lambda e: e.matmul(psQ[:, :], ones[:, :], sq[:, c, :], start=(c == 0), stop=(c == 3)), reads=[sqb, cb], writes=[pbQ])
                Sx.op("act", lambda e: e.activation(out=mean[:, :], in_=psM[:, :], func=AF.Copy, scale=1.0 / 512.0), reads=[pbM], writes=[mb])
                Sx.op("pool", lambda e: e.tensor_tensor(out=t1[:, :], in0=mean[:, :], in1=mean[:, :], op=ALU.mult), reads=[mb], writes=[t1b])
                Sx.op("dve", lambda e: e.scalar_tensor_tensor(out=rstd[:, :], in0=psQ[:, :], scalar=1.0 / 512.0, in1=t1[:, :], op0=ALU.mult, op1=ALU.subtract),
                      reads=[pbQ, t1b], writes=[rb])
                Sx.op("dve", lambda e: e.tensor_scalar(out=rstd[:, :], in0=rstd[:, :], scalar1=1e-5, scalar2=None, op0=ALU.add), reads=[rb], writes=[rb])
                Sx.op("act", lambda e: e.activation(out=rstd[:, :], in_=rstd[:, :], func=AF.Sqrt), reads=[rb], writes=[rb])
                Sx.op("dve", lambda e: e.reciprocal(rstd[:, :], rstd[:, :]), reads=[rb], writes=[rb])
                for c in range(4):
                    Sx.op("pool", lambda e: e.tensor_tensor(out=t1[:, :], in0=cv[:, c, :], in1=mean[:, :], op=ALU.subtract), reads=[cvb, mb, t1b], writes=[t1b])
                    Sx.op("dve", lambda e: e.tensor_tensor(out=t1[:, :], in0=t1[:, :], in1=rstd[:, :], op=ALU.mult), reads=[t1b, rb], writes=[t1b])
                    Sx.op("act", lambda e: e.activation(out=t1[:, :], in_=t1[:, :], func=AF.Silu, scale=self.pc("conv_ln_g", c), bias=self.pc("conv_ln_b", c)),
                          reads=[t1b, pcb], writes=[t1b])
                    gps, gpb = Sx.next_psum()
                    self.inproj(gps, gpb, wC, wCb, 1024 + c * 128, 128, t0, 512)
                    Sx.op("act", lambda e: e.activation(out=sgt[:, :], in_=gps[:, :], func=AF.Silu), reads=[gpb], writes=[sgb])
                    yt = yb[c % 2]; ytb = ybb[c % 2]
                    Sx.op("pool", lambda e: e.tensor_tensor(out=yt[:, :], in0=t1[:, :], in1=sgt[:, :], op=ALU.mult), reads=[t1b, sgb], writes=[ytb])
                    Sx.dma("sp", ydst[c * 128:(c + 1) * 128, t0:t0 + 512], yt[:, :], reads=[ytb], writes=[self.ybr_buf[2]])
            Sx.barrier()

    def phase_epi(self, s, l):
        Sx = self.S
        d = self.din
        cb = self.cb
        pcb = self.pcol_b
        with contextlib.ExitStack() as es0:
            mT = self.sb(es0, "ep_mT", [128, 8, S], BF16); mTb = Buf()
            with contextlib.ExitStack() as es:
                sb = lambda n, shp, dt=F32: self.sb(es, "e1_" + n, shp, dt)
                yin = sb("yin", [128, 4, 4, S], BF16); yinb = Buf()
                prs = []
                for n in range(4):
                    src = self.ybr[n].rearrange("(c p) t -> p c t", p=128)
                    for c in range(4):
                        prs.append((yin[:, n, c, :], src[:, c, :]))
                Sx.dma_group("sp", prs, reads=self.ybr_buf, writes=[yinb])
                wG = [sb(f"wG{i}", [128, 8, 4, 128], BF16) for i in range(2)]; wGb = [Buf(), Buf()]
                Wo = [sb(f"Wo{i}", [128, 4, 4, 128], BF16) for i in range(2)]; Wob = [Buf(), Buf()]
                gt = sb("gt", [128, 512]); gtb = Buf()
                acc = sb("acc", [128, 512]); accb = Buf()
                tmp = sb("tmp", [128, 512]); tmpb = Buf()
                def load_db(db_):
                    i_ = db_ % 2
                    pg = []; po = []
                    for n in range(4):
                        c0 = OFF_MERGE + n * 1024 + db_ * 128
                        vg = d["w_in"][l, :, c0:c0 + 128].rearrange("(kc p) c -> p kc c", p=128)
                        pg.append((wG[i_][:, :, n, :], vg))
                        vo = d["w_o_branch"][l, n, :, db_ * 128:(db_ + 1) * 128].rearrange("(cc p) c -> p cc c", p=128)
                        po.append((Wo[i_][:, n, :, :], vo))
                    Sx.dma_group("pool", pg, writes=[wGb[i_]])
                    Sx.dma_group("pool", po, writes=[Wob[i_]])

                load_db(0)
                for db in range(8):
                    i = db % 2
                    if db + 1 < 8:
                        load_db(db + 1)
                    for g in range(4):
                        t0 = g * 512
                        for n in range(4):
                            psP, pbP = Sx.next_psum()
                            for cc in range(4):
                                Sx.op("pe", lambda e: e.matmul(psP[:, :], Wo[i][:, n, cc, :], yin[:, n, cc, t0:t0 + 512], start=(cc == 0), stop=(cc == 3)),
                                      reads=[Wob[i], yinb], writes=[pbP])
                            psG, pbG = Sx.next_psum()
                            for kc in range(8):
                                Sx.op("pe", lambda e: e.matmul(psG[:, :], wG[i][:, kc, n, :], self.xT[:, kc, t0:t0 + 512], start=(kc == 0), stop=(kc == 7)),
                                      reads=[wGb[i], self.xT_b], writes=[pbG])
                            Sx.op("act", lambda e: e.activation(out=gt[:, :], in_=psG[:, :], func=AF.Sigmoid, bias=self.pc("b_gate", n * 8 + db)),
                                  reads=[pbG, pcb], writes=[gtb])
                            if n == 0:
                                Sx.op("dve", lambda e: e.tensor_tensor(out=acc[:, :], in0=psP[:, :], in1=gt[:, :], op=ALU.mult), reads=[pbP, gtb], writes=[accb])
                            else:
                                Sx.op("dve", lambda e: e.tensor_tensor(out=tmp[:, :], in0=psP[:, :], in1=gt[:, :], op=ALU.mult), reads=[pbP, gtb], writes=[tmpb])
                                if n < 3:
                                    Sx.op("pool", lambda e: e.tensor_tensor(out=acc[:, :], in0=acc[:, :], in1=tmp[:, :], op=ALU.add), reads=[accb, tmpb], writes=[accb])
                                else:
                                    Sx.op("pool", lambda e: e.tensor_tensor(out=mT[:, db, t0:t0 + 512], in0=acc[:, :], in1=tmp[:, :], op=ALU.add),
                                          reads=[accb, tmpb], writes=[mTb])
                Sx.barrier()
            with contextlib.ExitStack() as es:
                sb = lambda n, shp, dt=F32: self.sb(es, "e2_" + n, shp, dt)
                Wout = sb("Wout", [128, 8, D], BF16); Woutb = Buf()
                self.load_w_bf(Wout, Woutb, d["w_out"][l], 8)
                xt = [sb(f"xt{i}", [128, D]) for i in range(2)]; xtb = [Buf(), Buf()]
                z = [sb(f"z{i}", [128, D]) for i in range(2)]; zb = [Buf(), Buf()]
                st = sb("st", [128, 12]); stb = Buf()
                mv = sb("mv", [128, 2]); mvb = Buf()
                rs = sb("rs", [128, 1]); rsb = Buf()
                for tt in range(S // 128):
                    i = tt % 2
                    if l == 0:
                        Sx.dma("sp", xt[i][:], d["x"][s, tt * 128:(tt + 1) * 128, :], writes=[xtb[i]])
                    else:
                        Sx.dma("sp", xt[i][:], self.x1[s, tt * 128:(tt + 1) * 128, :], reads=[self.x1_buf[s]], writes=[xtb[i]])
                    for half in range(2):
                        ps, pb = Sx.next_psum()
                        for kc in range(8):
                            Sx.op("pe", lambda e: e.matmul(ps[:, :], mT[:, kc, tt * 128:(tt + 1) * 128], Wout[:, kc, half * 512:(half + 1) * 512],
                                                           start=(kc == 0), stop=(kc == 7)), reads=[mTb, Woutb], writes=[pb])
                        Sx.op("dve", lambda e: e.scalar_tensor_tensor(out=z[i][:, half * 512:(half + 1) * 512], in0=xt[i][:, half * 512:(half + 1) * 512],
                                                                      scalar=float(ALPHA), in1=ps[:, :], op0=ALU.mult, op1=ALU.add),
                              reads=[xtb[i], pb], writes=[zb[i]])
                        Sx.op("dve", lambda e: e.bn_stats(st[:, half * 6:(half + 1) * 6], z[i][:, half * 512:(half + 1) * 512]), reads=[zb[i]], writes=[stb])
                    Sx.op("dve", lambda e: e.bn_aggr(mv[:, :], st[:, :]), reads=[stb], writes=[mvb])
                    Sx.op("dve", lambda e: e.tensor_scalar(out=rs[:, :], in0=mv[:, 1:2], scalar1=1e-5, scalar2=None, op0=ALU.add), reads=[mvb], writes=[rsb])
                    Sx.op("act", lambda e: e.activation(out=rs[:, :], in_=rs[:, :], func=AF.Sqrt), reads=[rsb], writes=[rsb])
                    Sx.op("dve", lambda e: e.reciprocal(rs[:, :], rs[:, :]), reads=[rsb], writes=[rsb])
                    Sx.op("dve", lambda e: e.tensor_scalar(out=z[i][:, :], in0=z[i][:, :], scalar1=mv[:, 0:1], scalar2=rs[:, 0:1], op0=ALU.subtract, op1=ALU.mult),
                          reads=[zb[i], mvb, rsb], writes=[zb[i]])
                    Sx.op("pool", lambda e: e.tensor_tensor(out=z[i][:, :], in0=z[i][:, :], in1=self.lng_bc[:, :], op=ALU.mult), reads=[zb[i], self.ln_b_], writes=[zb[i]])
                    Sx.op("pool", lambda e: e.tensor_tensor(out=z[i][:, :], in0=z[i][:, :], in1=self.lnb_bc[:, :], op=ALU.add), reads=[zb[i], self.ln_b_], writes=[zb[i]])
                    if l == DEPTH - 1 or self.single:
                        Sx.dma("sp", self.out[s, tt * 128:(tt + 1) * 128, :], z[i][:, :], reads=[zb[i]], writes=[self.out_b])
                    else:
                        Sx.dma("sp", self.x1[s, tt * 128:(tt + 1) * 128, :], z[i][:, :], reads=[zb[i]], writes=[self.x1_buf[s]])
                        self.transpose_into_xT(z[i], zb[i], tt)
                Sx.barrier()


def _shard_inputs(inputs):
    consts = _consts_host()
    maps = []
    for c in range(NCORES):
        m = {}
        sl = slice(c * SEQ_PER_CORE, (c + 1) * SEQ_PER_CORE)
        for n, shp, dt in PARAM_SPECS:
            a = np.asarray(inputs[n])
            if n in ("x", "mem", "positions"):
                a = a[sl]
            m[n] = np.ascontiguousarray(a)
        m.update(consts)
        maps.append(m)
    return maps


_PROG = {}


FUSED = True


def kernel(**inputs):
    if FUSED:
        if "f" not in _PROG:
            _PROG["f"] = K({}).build()
        res = run_bass_kernel_spmd(_PROG["f"], _shard_inputs(inputs), core_ids=list(range(NCORES)))
        return np.concatenate([np.asarray(r["out"], dtype=np.float32) for r in res.results], axis=0)
    return kernel_unfused(**inputs)


def kernel_unfused(**inputs):
    if "p" not in _PROG:
        kb = K({"nseq": 1, "nlay": 1, "single": True, "seqs": [0], "layers": [0]})
        _PROG["p"] = kb.build()
    nc = _PROG["p"]
    consts = _consts_host()
    xs = np.asarray(inputs["x"], dtype=np.float32)
    names = [n for n, _, _ in PARAM_SPECS if n not in ("x", "mem", "positions")]
    out = np.empty_like(xs)
    for slot in range(SEQ_PER_CORE):
        cur = [np.ascontiguousarray(xs[c * SEQ_PER_CORE + slot][None]) for c in range(NCORES)]
        for l in range(DEPTH):
            wl = {n: np.ascontiguousarray(np.asarray(inputs[n])[l:l + 1]) for n in names}
            maps = []
            for c in range(NCORES):
                b = c * SEQ_PER_CORE + slot
                m = dict(wl)
                m["x"] = cur[c]
                m["mem"] = np.ascontiguousarray(np.asarray(inputs["mem"])[b:b + 1])
                m["positions"] = np.ascontiguousarray(np.asarray(inputs["positions"])[b:b + 1])
                m.update(consts)
                maps.append(m)
            res = run_bass_kernel_spmd(nc, maps, core_ids=list(range(NCORES)))
            cur = [np.ascontiguousarray(np.asarray(r["out"], dtype=np.float32)) for r in res.results]
        for c in range(NCORES):
            out[c * SEQ_PER_CORE + slot] = cur[c][0]
    return out
```

```python
import contextlib
import numpy as np
import concourse.bass as bass
import concourse.mybir as mybir
from concourse.bass_utils import run_bass_kernel_spmd

F32 = mybir.dt.float32
BF16 = mybir.dt.bfloat16
I32 = mybir.dt.int32
AF = mybir.ActivationFunctionType
ALU = mybir.AluOpType

NCORES = 8
SEQ_PER_CORE = 4
S = 2048
D = 1024
DEPTH = 2
IN_COLS = 10016
W = 512
ALPHA = (2.0 * DEPTH) ** 0.25
SEM_LIMIT = 30000
VCLOCK = True
NDQ = 12

OFF_RW = 0
OFF_QLAT = 2176
OFF_KVLAT = 2560
OFF_KPE = 2816
OFF_MGATE = 2848
OFF_CONV = 3360
OFF_XQ = 4896
OFF_MERGE = 5920

PC = {}
_o = 0
for _n, _c in [("mu", 13), ("w0", 4), ("a0", 4), ("k_k", 4), ("k_a", 4), ("r_k", 4), ("lnx_g", 4),
               ("lnx_b", 4), ("q_norm", 3), ("kv_norm", 2), ("conv_b", 4), ("conv_ln_g", 4),
               ("conv_ln_b", 4), ("b_gate", 32), ("conv_w", 124)]:
    PC[_n] = _o
    _o += _c
NPC_RAW = _o
PC["omm"] = NPC_RAW
PC["omka"] = NPC_RAW + 13
NPC = NPC_RAW + 17


class Buf:
    __slots__ = ("w", "r")

    def __init__(self):
        self.w = []
        self.r = {}


class BufG(list):
    pass


def _flat(bufs):
    out = []
    for b in bufs:
        if isinstance(b, BufG):
            out.extend(b)
        else:
            out.append(b)
    return out


class Sched:
    def __init__(self, nc, es):
        self.nc = nc
        self.es = es
        self.eng = {"pe": nc.tensor, "act": nc.scalar, "dve": nc.vector, "pool": nc.gpsimd, "sp": nc.sync}
        self.sem = {}
        self.cnt = {}
        self.sid = {}
        self.nsem = 0
        self.waited = {k: {} for k in self.eng}
        self.latest = {}
        self.ninst = 0
        for k in self.eng:
            self._newsem(k)
        self.dq = {}
        self.dqi = {}
        for q in ("sp", "pool", "act"):
            lst = []
            for i in range(NDQ):
                s = es.enter_context(nc.semaphore(f"dq_{q}_{i}"))
                self.nsem += 1
                lst.append([s, 0, self.nsem])
            self.dq[q] = lst
            self.dqi[q] = 0
        self.psum = []
        self.psi = 0

    def _newsem(self, k):
        s = self.es.enter_context(self.nc.semaphore(f"s_{k}_{self.nsem}"))
        self.nsem += 1
        self.sem[k] = s
        self.cnt[k] = 0
        self.sid[k] = self.nsem

    def _wait(self, k, tok):
        sem, val, src, sid = tok[0], tok[1], tok[2], tok[3]
        if k == "pe" and src == "pe":
            return
        w = self.waited[k]
        if w.get(sid, 0) >= val:
            return
        self.eng[k].wait_ge(sem, val)
        self.ninst += 1
        w[sid] = val
        snap = tok[4] if (VCLOCK and len(tok) > 4) else None
        if snap:
            for a, b in snap.items():
                if w.get(a, 0) < b:
                    w[a] = b

    def _deps(self, reads, writes):
        reads = _flat(reads); writes = _flat(writes)
        toks = []
        for b in reads:
            toks.extend(b.w)
        for b in writes:
            toks.extend(b.w)
            toks.extend(b.r.values())
        return toks

    def _commit(self, tok, reads, writes):
        reads = _flat(reads); writes = _flat(writes)
        for b in reads:
            b.r[tok[3]] = tok
        for b in writes:
            b.w = [tok]
            b.r = {}
        self.latest[tok[3]] = tok

    def dma_group(self, q, pairs, reads=(), writes=()):
        deps = self._deps(reads, writes)
        toks = []
        lim = 1 if q == "pool" else 4
        for (out, in_) in pairs:
            for t in deps:
                self._wait(q, t)
            if len(toks) >= lim:
                self._wait(q, toks[len(toks) - lim])
            i = self.dqi[q]
            self.dqi[q] = (i + 1) % NDQ
            ent = self.dq[q][i]
            if ent[1] > 0:
                self._wait(q, (ent[0], 16 * ent[1], "dma", ent[2]))
            self.eng[q].dma_start(out=out, in_=in_).then_inc(ent[0], 16)
            self.ninst += 1
            ent[1] += 1
            tok = (ent[0], 16 * ent[1], "dma", ent[2], dict(self.waited[q]))
            toks.append(tok)
            self.latest[tok[3]] = tok
        for b in _flat(reads):
            for tok in toks:
                b.r[tok[3]] = tok
        for b in _flat(writes):
            b.w = list(toks)
            b.r = {}

    def op(self, k, fn, reads=(), writes=()):
        for t in self._deps(reads, writes):
            self._wait(k, t)
        if self.cnt[k] >= SEM_LIMIT:
            self._newsem(k)
        inst = fn(self.eng[k])
        self.cnt[k] += 1
        self.ninst += 1
        inst.then_inc(self.sem[k], 1)
        snap = dict(self.waited[k])
        if k != "pe":
            snap[self.sid[k]] = self.cnt[k] - 1
        tok = (self.sem[k], self.cnt[k], k, self.sid[k], snap)
        self._commit(tok, reads, writes)
        return tok

    def dma(self, q, out, in_, reads=(), writes=()):
        for t in self._deps(reads, writes):
            self._wait(q, t)
        i = self.dqi[q]
        self.dqi[q] = (i + 1) % NDQ
        ent = self.dq[q][i]
        if ent[1] > 0:
            self._wait(q, (ent[0], 16 * ent[1], "dma", ent[2]))
        self.eng[q].dma_start(out=out, in_=in_).then_inc(ent[0], 16)
        self.ninst += 1
        ent[1] += 1
        tok = (ent[0], 16 * ent[1], "dma", ent[2], dict(self.waited[q]))
        self._commit(tok, reads, writes)
        return tok

    def barrier(self, engines=("pe", "act", "dve", "pool", "sp")):
        toks = list(self.latest.values())
        for k in engines:
            for t in toks:
                self._wait(k, t)

    def next_psum(self, n=8):
        self.psi = (self.psi + 1) % n
        return self.psum[self.psi]


def _consts_host():
    c = {}
    c["c_ident"] = np.eye(128, dtype=np.float32)
    bo = np.zeros((128, 128), np.float32)
    bo[:64, :64] = 1.0
    bo[64:, 64:] = 1.0
    c["c_bo"] = bo
    i = np.arange(64)
    strict = (i[:, None] < i[None, :]).astype(np.float32)
    incl = (i[:, None] <= i[None, :]).astype(np.float32)
    lower = (i[None, :] < i[:, None]).astype(np.float32)
    mA = np.zeros((128, 3, 2, 64), np.float32)
    for h in range(2):
        mA[h * 64:(h + 1) * 64, 0, h, :] = strict
        mA[h * 64:(h + 1) * 64, 1, h, :] = strict
        mA[h * 64:(h + 1) * 64, 2, h, :] = lower
    c["c_maskA"] = mA.reshape(128, 384)
    mB = np.zeros((128, 2, 64), np.float32)
    for h in range(2):
        mB[h * 64:(h + 1) * 64, :, :] = incl[:, None, :]
    c["c_maskB"] = mB.reshape(128, 128)
    mbd = np.zeros((128, 2), np.float32)
    mbd[:64, 0] = 1.0
    mbd[64:, 1] = 1.0
    c["c_mbd"] = mbd
    cm = np.ones((128, 128), np.float32)
    cm[:, 0] = 0.0
    cm[:, 64] = 0.0
    c["c_cmask"] = cm
    k = np.arange(128)[:, None]
    q = np.arange(512)[None, :]
    mm = np.stack([(q >= v * 128 + k) for v in range(4)], axis=1).astype(np.float32)
    c["c_cmla"] = mm.reshape(128, 2048)
    inv = (10000.0 ** (-np.arange(0, 32, 2, dtype=np.float32) / 32.0)).astype(np.float32)
    rp = np.zeros((128, 2), np.float32)
    rp[64:96, 0] = np.concatenate([inv, inv])
    rp[64:96, 1] = np.concatenate([-np.ones(16, np.float32), np.ones(16, np.float32)])
    c["c_rope"] = rp
    c["c_ones"] = np.ones((128, 128), np.float32)
    return c


CONST_SHAPES = {k: v.shape for k, v in _consts_host().items()}

def param_specs(SEQ_PER_CORE, DEPTH):
  return [
    ("x", [SEQ_PER_CORE, S, D], F32), ("mem", [SEQ_PER_CORE, 256, D], F32), ("positions", [SEQ_PER_CORE, S], I32),
    ("w_in", [DEPTH, D, IN_COLS], F32), ("b_gate", [DEPTH, 4, D], F32), ("rwkv_mu", [DEPTH, 1664], F32),
    ("rwkv_w0", [DEPTH, W], F32), ("rwkv_w2", [DEPTH, 64, W], F32), ("rwkv_a0", [DEPTH, W], F32),
    ("rwkv_a2", [DEPTH, 64, W], F32), ("rwkv_k_k", [DEPTH, W], F32), ("rwkv_k_a", [DEPTH, W], F32),
    ("rwkv_r_k", [DEPTH, 8, 64], F32), ("rwkv_lnx_g", [DEPTH, W], F32), ("rwkv_lnx_b", [DEPTH, W], F32),
    ("mla_q_norm", [DEPTH, 384], F32), ("mla_w_uq", [DEPTH, 384, 768], F32), ("mla_kv_norm", [DEPTH, 256], F32),
    ("mla_w_ukv", [DEPTH, 256, 1024], F32), ("conv_w", [DEPTH, 31, W], F32), ("conv_b", [DEPTH, W], F32),
    ("conv_ln_g", [DEPTH, W], F32), ("conv_ln_b", [DEPTH, W], F32), ("xattn_w_mem_kv", [DEPTH, D, 2 * W], F32),
    ("w_o_branch", [DEPTH, 4, W, D], F32), ("w_out", [DEPTH, D, D], F32), ("ln_g", [DEPTH, D], F32),
    ("ln_b", [DEPTH, D], F32),
  ]


PARAM_SPECS = param_specs(SEQ_PER_CORE, DEPTH)


class K:
    def __init__(self, cfg):
        self.cfg = cfg
        self.nc = bass.Bass("TRN2", target_bir_lowering=False)
        nc = self.nc
        self.din = {}
        nseq = cfg.get("nseq", SEQ_PER_CORE)
        nlay = cfg.get("nlay", DEPTH)
        self.single = cfg.get("single", False)
        for n, shp, dt in param_specs(nseq, nlay):
            self.din[n] = nc.dram_tensor(n, shp, dt, kind="ExternalInput").ap()
        for n, shp in CONST_SHAPES.items():
            self.din[n] = nc.dram_tensor(n, list(shp), F32, kind="ExternalInput").ap()
        self.out = nc.dram_tensor("out", [nseq, S, D], F32, kind="ExternalOutput").ap()
        dbg = cfg.get("debug", False)
        kind = "ExternalOutput" if dbg else "Internal"
        self.ybr = nc.dram_tensor("ybr", [4, W, S], BF16, kind=kind).ap()
        self.ybr_buf = [Buf() for _ in range(4)]
        self.x1 = nc.dram_tensor("x1s", [nseq, S, D], F32, kind=kind).ap()
        self.x1_buf = [Buf() for _ in range(SEQ_PER_CORE)]

    def sb(self, es, name, shape, dt):
        self._uid = getattr(self, "_uid", 0) + 1
        return es.enter_context(self.nc.sbuf_tensor(f"{name}_{self._uid}", shape, dt))

    def build(self):
        nc = self.nc
        with contextlib.ExitStack() as es:
            self.S = Sched(nc, es)
            Sx = self.S
            for i in range(8):
                t = es.enter_context(nc.psum_tensor(f"ps{i}", [128, 512], F32))
                Sx.psum.append((t, Buf()))
            self.setup_consts(es)
            self.xT = self.sb(es, "xT", [128, 8, S], BF16)
            self.xT_b = BufG([Buf(), Buf()])
            self.ropeC = self.sb(es, "ropeC", [128, S], F32)
            self.ropeS = self.sb(es, "ropeS", [128, S], F32)
            self.rope_b = Buf()
            self.memT = self.sb(es, "memT", [128, 8, 256], BF16)
            self.memT_b = Buf()
            self.out_b = Buf()
            phases = self.cfg.get("phases", "RMCXE")
            seqs = self.cfg.get("seqs", list(range(SEQ_PER_CORE)))
            layers = self.cfg.get("layers", list(range(DEPTH)))
            for s in seqs:
                self.load_xT(s)
                if "M" in phases:
                    self.rope_tables(s)
                if "X" in phases:
                    self.load_memT(s)
                for l in layers:
                    if l == 0 or True:
                        self.load_params(l)
                    if "R" in phases:
                        self.phase_rwkv(s, l)
                    if "M" in phases:
                        self.phase_mla(s, l)
                    if "C" in phases:
                        self.phase_conv(s, l)
                    if "X" in phases:
                        self.phase_xattn(s, l)
                    if "E" in phases:
                        self.phase_epi(s, l)
            Sx.barrier(engines=("sp",))
        return nc

    def setup_consts(self, es):
        Sx = self.S
        self.cb = Buf()
        c = {}
        for n, shp in CONST_SHAPES.items():
            if n == "c_cmla":
                continue
            t = self.sb(es, "k_" + n, list(shp), F32)
            Sx.dma("sp", t[:], self.din[n], writes=[self.cb])
            c[n] = t
        self.c = c
        self.ident = c["c_ident"]
        self.bo = c["c_bo"]
        self.ident_bf = self.sb(es, "ident_bf", [128, 128], BF16)
        self.ones_bf = self.sb(es, "ones_bf", [128, 128], BF16)
        self.cmla_bf = self.sb(es, "cmla_bf", [128, 2048], BF16)
        Sx.op("pool", lambda e: e.tensor_copy(self.ident_bf[:], self.ident[:]), reads=[self.cb], writes=[self.cb])
        Sx.op("pool", lambda e: e.memset(self.ones_bf[:], 1.0), writes=[self.cb])
        with contextlib.ExitStack() as es2:
            tmpc = self.sb(es2, "k_cmla_tmp", [128, 2048], F32)
            Sx.dma("sp", tmpc[:], self.din["c_cmla"], writes=[self.cb])
            Sx.op("pool", lambda e: e.tensor_copy(self.cmla_bf[:], tmpc[:]), reads=[self.cb], writes=[self.cb])
            Sx.barrier()
        self.pcol = self.sb(es, "pcol", [128, NPC], F32)
        self.pcol_b = Buf()
        self.stageA = self.sb(es, "stageA", [128, 128], F32)
        self.stageB = self.sb(es, "stageB", [128, 128], F32)
        self.stage_b = Buf()
        self.lng_bc = self.sb(es, "lng_bc", [128, D], F32)
        self.lnb_bc = self.sb(es, "lnb_bc", [128, D], F32)
        self.ln_b_ = Buf()
        self.ones_row = self.sb(es, "ones_row", [1, 128], F32)
        Sx.op("pool", lambda e: e.memset(self.ones_row[:], 1.0), writes=[self.cb])

    def pc(self, name, j=0):
        i = PC[name] + j
        return self.pcol[:, i:i + 1]

    def load_params(self, l):
        Sx = self.S
        d = self.din
        rows = []

        def vec(name, key):
            ap = d[key][l]
            n = 1
            for s_ in ap.shape:
                n *= s_
            rows.append((PC[name], n // 128, ap))

        vec("mu", "rwkv_mu"); vec("w0", "rwkv_w0"); vec("a0", "rwkv_a0"); vec("k_k", "rwkv_k_k")
        vec("k_a", "rwkv_k_a"); vec("r_k", "rwkv_r_k"); vec("lnx_g", "rwkv_lnx_g"); vec("lnx_b", "rwkv_lnx_b")
        vec("q_norm", "mla_q_norm"); vec("kv_norm", "mla_kv_norm"); vec("conv_b", "conv_b")
        vec("conv_ln_g", "conv_ln_g"); vec("conv_ln_b", "conv_ln_b"); vec("b_gate", "b_gate"); vec("conv_w", "conv_w")
        for (c0, nr, ap) in rows:
            if len(ap.shape) == 2:
                if ap.shape[1] == 64:
                    flat = ap.rearrange("h n -> (h n)")
                    src = flat.rearrange("(r p) -> r p", p=128)
                elif ap.shape[0] == 31:
                    src = ap.rearrange("j (c p) -> (j c) p", p=128)
                else:
                    src = ap.rearrange("n (c p) -> (n c) p", p=128)
            else:
                src = ap.rearrange("(r p) -> r p", p=128)
            r = 0
            while r < nr:
                g = c0 + r
                if g < 128:
                    n = min(nr - r, 128 - g)
                    Sx.dma("sp", self.stageA[g:g + n, :], src[r:r + n, :], writes=[self.stage_b])
                else:
                    n = nr - r
                    Sx.dma("sp", self.stageB[g - 128:g - 128 + n, :], src[r:r + n, :], writes=[self.stage_b])
                r += n
        nb = NPC_RAW - 128
        ps, pb = Sx.next_psum()
        Sx.op("pe", lambda e: e.matmul(ps[:, 0:128], self.stageA[:, :], self.ident[:, :], start=True, stop=True),
              reads=[self.stage_b, self.cb], writes=[pb])
        Sx.op("pe", lambda e: e.matmul(ps[:, 128:128 + nb], self.stageB[0:nb, :], self.ident[0:nb, 0:nb], start=True, stop=True),
              reads=[self.stage_b, self.cb], writes=[pb])
        Sx.op("act", lambda e: e.activation(out=self.pcol[:, 0:NPC_RAW], in_=ps[:, 0:NPC_RAW], func=AF.Copy),
              reads=[pb], writes=[self.pcol_b])
        o = PC["omm"]
        Sx.op("dve", lambda e: e.tensor_scalar(out=self.pcol[:, o:o + 13], in0=self.pcol[:, 0:13], scalar1=-1.0, scalar2=1.0,
                                               op0=ALU.mult, op1=ALU.add), reads=[self.pcol_b], writes=[self.pcol_b])
        o2 = PC["omka"]
        ka = PC["k_a"]
        Sx.op("dve", lambda e: e.tensor_scalar(out=self.pcol[:, o2:o2 + 4], in0=self.pcol[:, ka:ka + 4], scalar1=-1.0, scalar2=1.0,
                                               op0=ALU.mult, op1=ALU.add), reads=[self.pcol_b], writes=[self.pcol_b])
        es3 = contextlib.ExitStack()
        self.lnrow = self.sb(es3, "lnrow", [1, 2 * D], F32)
        Sx.dma("sp", self.lnrow[0:1, 0:D], d["ln_g"][l:l + 1, :], writes=[self.ln_b_])
        Sx.dma("sp", self.lnrow[0:1, D:2 * D], d["ln_b"][l:l + 1, :], writes=[self.ln_b_])
        for j, dst in enumerate((self.lng_bc, self.lnb_bc)):
            for hh in range(2):
                ps, pb = Sx.next_psum()
                Sx.op("pe", lambda e, ps=ps, j=j, hh=hh: e.matmul(ps[:, :], self.ones_row[0:1, :],
                                                                  self.lnrow[0:1, j * D + hh * 512:j * D + hh * 512 + 512],
                                                                  start=True, stop=True),
                      reads=[self.ln_b_, self.cb], writes=[pb])
                Sx.op("act", lambda e, ps=ps, dst=dst, hh=hh: e.activation(out=dst[:, hh * 512:(hh + 1) * 512], in_=ps[:, :], func=AF.Copy),
                      reads=[pb], writes=[self.ln_b_])
        Sx.barrier()
        es3.close()

    def load_xT(self, s):
        Sx = self.S
        with contextlib.ExitStack() as es:
            xt = [self.sb(es, f"xtok{i}", [128, D], F32) for i in range(2)]
            xb = [Buf(), Buf()]
            for tt in range(S // 128):
                i = tt % 2
                Sx.dma("sp", xt[i][:], self.din["x"][s, tt * 128:(tt + 1) * 128, :], writes=[xb[i]])
                self.transpose_into_xT(xt[i], xb[i], tt)
            Sx.barrier()

    def transpose_into_xT(self, xtok, xbuf, tt):
        Sx = self.S
        for half in range(2):
            ps, pb = Sx.next_psum()
            for j in range(4):
                kc = half * 4 + j
                Sx.op("pe", lambda e, ps=ps, j=j, kc=kc: e.matmul(ps[:, j * 128:(j + 1) * 128], xtok[:, kc * 128:(kc + 1) * 128],
                                                                  self.ident[:, :], start=True, stop=True),
                      reads=[xbuf, self.cb], writes=[pb])
            out = self.xT[:, half * 4:(half + 1) * 4, tt * 128:(tt + 1) * 128]
            Sx.op("act" if half == 0 else "dve",
                  (lambda e, ps=ps, out=out: e.activation(out=out, in_=ps[:, :].rearrange("p (j t) -> p j t", j=4), func=AF.Copy)) if half == 0 else
                  (lambda e, ps=ps, out=out: e.tensor_copy(out, ps[:, :].rearrange("p (j t) -> p j t", j=4))),
                  reads=[pb], writes=[self.xT_b[half]])

    def load_w_bf(self, dst, dst_buf, src, nkc):
        Sx = self.S
        v = src.rearrange("(kc p) c -> p kc c", p=128)
        Sx.dma_group("pool", [(dst[:, kc, :], v[:, kc, :]) for kc in range(nkc)], writes=[dst_buf])

    def phase_rwkv(self, s, l):
        Sx = self.S
        nc = self.nc
        d = self.din
        T = 128
        with contextlib.ExitStack() as es:
            sb = lambda n, shp, dt=F32: self.sb(es, "rw_" + n, shp, dt)
            wR = sb("wR", [128, 8, 2176], BF16); wRb = Buf()
            self.load_w_bf(wR, wRb, d["w_in"][l, :, OFF_RW:OFF_RW + 2176], 8)
            W2z = sb("W2z", [128, 512]); A2z = sb("A2z", [128, 512]); lb = Buf()
            Sx.op("pool", lambda e: e.memset(W2z[:], 0.0), writes=[lb])
            Sx.op("pool", lambda e: e.memset(A2z[:], 0.0), writes=[lb])
            Sx.dma("sp", W2z[0:64, :], d["rwkv_w2"][l], writes=[lb])
            Sx.dma("sp", A2z[64:128, :], d["rwkv_a2"][l], writes=[lb])
            p_raw = sb("p_raw", [128, 13, T + 1]); prb = Buf()
            Sx.op("pool", lambda e: e.memset(p_raw[:], 0.0), writes=[prb])
            pm = sb("pm", [128, 13, T]); pmc = [Buf() for _ in range(13)]
            pm_r = pmc[0:4]; pm_k = pmc[4:8]; pm_v = pmc[8:12]
            sgate = sb("sgate", [128, 4, T]); sgb = Buf()
            T12 = sb("T12", [128, T]); t12b = Buf()
            names = ["lw", "logP", "asig", "eP", "eN", "ePm", "kk", "kkn", "kp", "rT", "t1", "t2", "t3"]
            tt_ = {n: sb(n, [128, 4, T]) for n in names}
            tb = {n: Buf() for n in names}
            Z = {n: sb("Z" + n, [128, 4, 2, 128]) for n in "abkv"}
            Zb_ = {n: Buf() for n in "abkv"}
            H = [sb(f"H{p}", [128, 128]) for p in range(4)]
            Hb = [Buf() for _ in range(4)]
            for p in range(4):
                Sx.op("pool", lambda e, p=p: e.memset(H[p][:], 0.0), writes=[Hb[p]])
            NSET = self.cfg.get("nset", 4)
            A_sb = [sb(f"A{i}", [128, 384]) for i in range(NSET)]; Ab = [Buf() for _ in range(NSET)]
            R_sb = [sb(f"R{i}", [128, 128]) for i in range(NSET)]; Rb = [Buf() for _ in range(NSET)]
            BKV = [sb(f"BKV{i}", [128, 384]) for i in range(NSET)]; BKVb = [Buf() for _ in range(NSET)]
            Wt = [[sb(f"W{i}_{j}", [128, 128]) for j in range(2)] for i in range(NSET)]
            Wtb = [[Buf() for j in range(2)] for i in range(NSET)]
            MP = [[sb(f"MP{i}_{j}", [128, 256]) for j in range(2)] for i in range(NSET)]
            MPb = [[Buf() for j in range(2)] for i in range(NSET)]
            X_sb = [sb(f"X{i}", [128, 128]) for i in range(NSET)]; Xb = [Buf() for _ in range(NSET)]
            U_sb = [sb(f"U{i}", [128, 128]) for i in range(NSET)]; Ub = [Buf() for _ in range(NSET)]
            HpC = [sb(f"HpC{i}", [128, 128]) for i in range(NSET)]; HpCb = [Buf() for _ in range(NSET)]
            Ycm = sb("Ycm", [128, 4, T]); Ybp = [Buf() for _ in range(4)]
            ybf = sb("ybf", [128, 4, T], BF16); ybb = Buf()
            cb = self.cb
            ident = self.ident
            maskA = self.c["c_maskA"]; maskB = self.c["c_maskB"]; mbd = self.c["c_mbd"]; cmask = self.c["c_cmask"]
            pcb = self.pcol_b
            ydst = self.ybr[0].rearrange("(c p) t -> p c t", p=128)

            def flat(t):
                return t[:, :, :].rearrange("p c t -> p (c t)")

            for blk in range(self.cfg.get('nblk', S // T)):
                t0 = blk * T
                for cbk in range(17):
                    ps, pb = Sx.next_psum()
                    for kc in range(8):
                        Sx.op("pe", lambda e, ps=ps, kc=kc, cbk=cbk: e.matmul(
                            ps[:, 0:T], wR[:, kc, cbk * 128:(cbk + 1) * 128], self.xT[:, kc, t0:t0 + T],
                            start=(kc == 0), stop=(kc == 7)), reads=[wRb, self.xT_b], writes=[pb])
                    if cbk < 13:
                        Sx.op("act", lambda e, ps=ps, cbk=cbk: e.activation(out=p_raw[:, cbk, 1:T + 1], in_=ps[:, 0:T], func=AF.Copy),
                              reads=[pb], writes=[prb])
                    else:
                        Sx.op("act", lambda e, ps=ps, cbk=cbk: e.activation(out=sgate[:, cbk - 13, :], in_=ps[:, 0:T], func=AF.Silu),
                              reads=[pb], writes=[sgb])
                for cbk in range(13):
                    Sx.op("pool", lambda e, cbk=cbk: e.tensor_scalar(out=pm[:, cbk, :], in0=p_raw[:, cbk, 1:T + 1],
                                                                     scalar1=self.pc("omm", cbk), scalar2=None, op0=ALU.mult),
                          reads=[prb, pcb], writes=[pmc[cbk]])
                    Sx.op("dve", lambda e, cbk=cbk: e.scalar_tensor_tensor(out=pm[:, cbk, :], in0=p_raw[:, cbk, 0:T],
                                                                           scalar=self.pc("mu", cbk), in1=pm[:, cbk, :],
                                                                           op0=ALU.mult, op1=ALU.add),
                          reads=[prb, pcb], writes=[pmc[cbk]])
                Sx.op("pool", lambda e: e.tensor_copy(p_raw[:, :, 0:1], p_raw[:, :, T:T + 1]), reads=[prb], writes=[prb])
                r_ = pm[:, 0:4, :]; k_ = pm[:, 4:8, :]; v_ = pm[:, 8:12, :]
                if self.cfg.get("stop_after", 9) < 1:
                    continue
                Sx.op("act", lambda e: e.activation(out=T12[0:64, :], in_=pm[0:64, 12, :], func=AF.Tanh), reads=[pmc[12]], writes=[t12b])
                Sx.op("dve", lambda e: e.tensor_copy(T12[64:128, :], pm[64:128, 12, :]), reads=[pmc[12]], writes=[t12b])
                for c4 in range(4):
                    ps, pb = Sx.next_psum()
                    Sx.op("pe", lambda e, ps=ps, c4=c4: e.matmul(ps[:, 0:T], W2z[:, c4 * 128:(c4 + 1) * 128], T12[:, :], start=True, stop=True),
                          reads=[lb, t12b], writes=[pb])
                    Sx.op("pe", lambda e, ps=ps, c4=c4: e.matmul(ps[:, T:2 * T], A2z[:, c4 * 128:(c4 + 1) * 128], T12[:, :], start=True, stop=True),
                          reads=[lb, t12b], writes=[pb])
                    Sx.op("act", lambda e, ps=ps, c4=c4: e.activation(out=tt_["lw"][:, c4, :], in_=ps[:, 0:T], func=AF.Sigmoid,
                                                                      bias=self.pc("w0", c4)), reads=[pb, pcb], writes=[tb["lw"]])
                    Sx.op("act", lambda e, ps=ps, c4=c4: e.activation(out=tt_["asig"][:, c4, :], in_=ps[:, T:2 * T], func=AF.Sigmoid,
                                                                      bias=self.pc("a0", c4)), reads=[pb, pcb], writes=[tb["asig"]])
                Sx.op("pool", lambda e: e.tensor_scalar(out=flat(tt_["lw"]), in0=flat(tt_["lw"]), scalar1=-0.6065306597126334,
                                                        scalar2=None, op0=ALU.mult), reads=[tb["lw"]], writes=[tb["lw"]])
                for c4 in range(4):
                    Sx.op("dve", lambda e, c4=c4: e.tensor_tensor_scan(out=tt_["logP"][:, c4, :], data0=cmask[:, 0:T], data1=tt_["lw"][:, c4, :],
                                                                      initial=0.0, op0=ALU.mult, op1=ALU.add),
                          reads=[tb["lw"], cb], writes=[tb["logP"]])
                Sx.op("act", lambda e: e.activation(out=flat(tt_["eP"]), in_=flat(tt_["logP"]), func=AF.Exp), reads=[tb["logP"]], writes=[tb["eP"]])
                Sx.op("act", lambda e: e.activation(out=flat(tt_["eN"]), in_=flat(tt_["logP"]), func=AF.Exp, scale=-1.0),
                      reads=[tb["logP"]], writes=[tb["eN"]])
                Sx.op("pool", lambda e: e.tensor_tensor(out=flat(tt_["t1"]), in0=flat(tt_["logP"]), in1=flat(tt_["lw"]), op=ALU.subtract),
                      reads=[tb["logP"], tb["lw"]], writes=[tb["t1"]])
                Sx.op("act", lambda e: e.activation(out=flat(tt_["ePm"]), in_=flat(tt_["t1"]), func=AF.Exp), reads=[tb["t1"]], writes=[tb["ePm"]])
                for c4 in range(4):
                    Sx.op("pool", lambda e, c4=c4: e.tensor_scalar(out=tt_["kk"][:, c4, :], in0=k_[:, c4, :], scalar1=self.pc("k_k", c4),
                                                                   scalar2=None, op0=ALU.mult), reads=[pm_k[c4], pcb], writes=[tb["kk"]])
                Sx.op("act", lambda e: e.activation(out=flat(tt_["t2"]), in_=flat(tt_["kk"]), func=AF.Square), reads=[tb["kk"]], writes=[tb["t2"]])
                ps, pb = Sx.next_psum()
                Sx.op("pe", lambda e, ps=ps: e.matmul(ps[:, 0:4 * T], self.bo[:, :], flat(tt_["t2"]), start=True, stop=True),
                      reads=[tb["t2"], cb], writes=[pb])
                Sx.op("act", lambda e, ps=ps: e.activation(out=flat(tt_["t3"]), in_=ps[:, 0:4 * T], func=AF.Sqrt), reads=[pb], writes=[tb["t3"]])
                Sx.op("dve", lambda e: e.tensor_scalar(out=flat(tt_["t3"]), in0=flat(tt_["t3"]), scalar1=1e-12, scalar2=None, op0=ALU.max),
                      reads=[tb["t3"]], writes=[tb["t3"]])
                Sx.op("dve", lambda e: e.reciprocal(flat(tt_["t2"]), flat(tt_["t3"])), reads=[tb["t3"]], writes=[tb["t2"]])
                Sx.op("pool", lambda e: e.tensor_tensor(out=flat(tt_["kkn"]), in0=flat(tt_["kk"]), in1=flat(tt_["t2"]), op=ALU.mult),
                      reads=[tb["kk"], tb["t2"]], writes=[tb["kkn"]])
                for c4 in range(4):
                    Sx.op("act", lambda e, c4=c4: e.activation(out=tt_["t3"][:, c4, :], in_=tt_["asig"][:, c4, :], func=AF.Identity,
                                                               scale=self.pc("k_a", c4), bias=self.pc("omka", c4)),
                          reads=[tb["asig"], pcb], writes=[tb["t3"]])
                Sx.op("pool", lambda e: e.tensor_tensor(out=flat(tt_["kp"]), in0=k_.rearrange("p c t -> p (c t)"), in1=flat(tt_["t3"]), op=ALU.mult),
                      reads=pm_k + [tb["t3"]], writes=[tb["kp"]])
                Sx.op("dve", lambda e: e.scalar_tensor_tensor(out=flat(tt_["t1"]), in0=flat(tt_["kkn"]), scalar=-1.0, in1=flat(tt_["ePm"]),
                                                              op0=ALU.mult, op1=ALU.mult), reads=[tb["kkn"], tb["ePm"]], writes=[tb["t1"]])
                Sx.op("pool", lambda e: e.tensor_tensor(out=flat(tt_["t2"]), in0=flat(tt_["kkn"]), in1=flat(tt_["asig"]), op=ALU.mult),
                      reads=[tb["kkn"], tb["asig"]], writes=[tb["t2"]])
                Sx.op("pool", lambda e: e.tensor_tensor(out=flat(tt_["t2"]), in0=flat(tt_["t2"]), in1=flat(tt_["eN"]), op=ALU.mult),
                      reads=[tb["t2"], tb["eN"]], writes=[tb["t2"]])
                Sx.op("dve", lambda e: e.tensor_tensor(out=flat(tt_["t3"]), in0=flat(tt_["kp"]), in1=flat(tt_["eN"]), op=ALU.mult),
                      reads=[tb["kp"], tb["eN"]], writes=[tb["t3"]])
                Sx.op("dve", lambda e: e.tensor_tensor(out=flat(tt_["rT"]), in0=r_.rearrange("p c t -> p (c t)"), in1=flat(tt_["eP"]), op=ALU.mult),
                      reads=pm_r + [tb["eP"]], writes=[tb["rT"]])
                mb4 = mbd[:, 0:2].unsqueeze(1).unsqueeze(3).to_broadcast([128, 8, 2, 64])
                for zi, (zn, srcap, srcb) in enumerate([("a", tt_["t1"], tb["t1"]), ("b", tt_["t2"], tb["t2"]), ("k", tt_["t3"], tb["t3"]),
                                                        ("v", None, None)]):
                    if srcap is None:
                        sview = v_.rearrange("p c (h t) -> p (c h) t", h=2)
                    else:
                        sview = srcap[:, :, :].rearrange("p c (h t) -> p (c h) t", h=2)
                    in0 = sview.unsqueeze(2).to_broadcast([128, 8, 2, 64])
                    outv = Z[zn][:, :, :, :].rearrange("p c h (g t) -> p (c h) g t", g=2)
                    Sx.op("dve" if zi % 2 == 0 else "pool",
                          lambda e, outv=outv, in0=in0: e.tensor_tensor(out=outv, in0=in0, in1=mb4, op=ALU.mult),
                          reads=(pm_v if srcb is None else [srcb]) + [cb], writes=[Zb_[zn]])
                if self.cfg.get("stop_after", 9) < 2:
                    continue
                for ch in range(2):
                    U_ = []
                    for pr in range(4):
                        U_.append(dict(Za=Z["a"][:, pr, ch, :], Zb=Z["b"][:, pr, ch, :], Zk=Z["k"][:, pr, ch, :], Zv=Z["v"][:, pr, ch, :],
                                       rTu=tt_["rT"][:, pr, ch * 64:(ch + 1) * 64], pC=tt_["eP"][:, pr, ch * 64 + 63:ch * 64 + 64]))
                    for PRS in self.cfg.get('pr_groups', [[0, 1, 2, 3]]):
                        for pr in PRS:
                            u = U_[pr]; si = pr % NSET
                            Za, Zb, Zk, Zv, rTu = u["Za"], u["Zb"], u["Zk"], u["Zv"], u["rTu"]
                            psA, pbA = Sx.next_psum()
                            Sx.op("pe", lambda e: e.matmul(psA[:, 0:128], Zb, Za, start=True, stop=True), reads=[Zb_["b"], Zb_["a"]], writes=[pbA])
                            Sx.op("pe", lambda e: e.matmul(psA[:, 128:256], Zk, Za, start=True, stop=True), reads=[Zb_["k"], Zb_["a"]], writes=[pbA])
                            Sx.op("pe", lambda e: e.matmul(psA[:, 256:384], Za, Zb, start=True, stop=True), reads=[Zb_["b"], Zb_["a"]], writes=[pbA])
                            Sx.op("dve", lambda e: e.tensor_tensor(out=A_sb[si][:, :], in0=psA[:, 0:384], in1=maskA[:, :], op=ALU.mult),
                                  reads=[pbA, cb], writes=[Ab[si]])
                            psB, pbB = Sx.next_psum()
                            Sx.op("pe", lambda e: e.matmul(psB[:, 0:64], Zb, rTu, start=True, stop=True), reads=[Zb_["b"], tb["rT"]], writes=[pbB])
                            Sx.op("pe", lambda e: e.matmul(psB[:, 64:128], Zk, rTu, start=True, stop=True), reads=[Zb_["k"], tb["rT"]], writes=[pbB])
                            Sx.op("dve", lambda e: e.tensor_tensor(out=R_sb[si][:, :], in0=psB[:, 0:128], in1=maskB[:, :], op=ALU.mult),
                                  reads=[pbB, cb], writes=[Rb[si]])
                        for pr in PRS:
                            u = U_[pr]; si = pr % NSET
                            Za, Zb, Zk, Zv, rTu = u["Za"], u["Zb"], u["Zk"], u["Zv"], u["rTu"]
                            psC, pbC = Sx.next_psum()
                            Sx.op("pe", lambda e: e.matmul(psC[:, 0:128], Zb, ident[:, :], start=True, stop=True), reads=[Zb_["b"], cb], writes=[pbC])
                            Sx.op("pe", lambda e: e.matmul(psC[:, 128:256], Zk, ident[:, :], start=True, stop=True), reads=[Zb_["k"], cb], writes=[pbC])
                            Sx.op("pe", lambda e: e.matmul(psC[:, 256:384], Zv, ident[:, :], start=True, stop=True), reads=[Zb_["v"], cb], writes=[pbC])
                            Sx.op("act", lambda e: e.activation(out=BKV[si][:, :], in_=psC[:, 0:384], func=AF.Copy), reads=[pbC], writes=[BKVb[si]])
                            Sx.op("pool", lambda e: e.tensor_tensor(out=Wt[si][0][:, :], in0=A_sb[si][:, 0:128], in1=ident[:, :], op=ALU.add),
                                  reads=[Ab[si], cb], writes=[Wtb[si][0]])
                        Mp = {pr: A_sb[pr % NSET][:, 0:128] for pr in PRS}
                        Pp = {pr: A_sb[pr % NSET][:, 256:384] for pr in PRS}
                        mpb = {pr: Ab[pr % NSET] for pr in PRS}
                        for j in range(1, 6):
                            cur = j % 2
                            for pr in PRS:
                                si = pr % NSET
                                psN, pbN = Sx.next_psum()
                                Sx.op("pe", lambda e: e.matmul(psN[:, 0:128], Pp[pr], Mp[pr], start=True, stop=True), reads=[mpb[pr]], writes=[pbN])
                                Sx.op("pe", lambda e: e.matmul(psN[:, 128:256], Mp[pr], Pp[pr], start=True, stop=True), reads=[mpb[pr]], writes=[pbN])
                                Sx.op("act", lambda e: e.activation(out=MP[si][cur][:, :], in_=psN[:, 0:256], func=AF.Copy), reads=[pbN], writes=[MPb[si][cur]])
                                Mp[pr] = MP[si][cur][:, 0:128]; Pp[pr] = MP[si][cur][:, 128:256]; mpb[pr] = MPb[si][cur]
                            for pr in PRS:
                                si = pr % NSET
                                psW, pbW = Sx.next_psum()
                                Sx.op("pe", lambda e: e.matmul(psW[:, 0:128], Pp[pr], Wt[si][(j - 1) % 2][:, :], start=True, stop=True),
                                      reads=[mpb[pr], Wtb[si][(j - 1) % 2]], writes=[pbW])
                                Sx.op("dve", lambda e: e.tensor_tensor(out=Wt[si][j % 2][:, :], in0=psW[:, 0:128], in1=Wt[si][(j - 1) % 2][:, :], op=ALU.add),
                                      reads=[pbW, Wtb[si][(j - 1) % 2]], writes=[Wtb[si][j % 2]])
                        for pr in PRS:
                            u = U_[pr]; si = pr % NSET
                            psX, pbX = Sx.next_psum()
                            Sx.op("pe", lambda e: e.matmul(psX[:, 0:128], u["Za"], H[pr][:, :], start=True, stop=False), reads=[Zb_["a"], Hb[pr]], writes=[pbX])
                            Sx.op("pe", lambda e: e.matmul(psX[:, 0:128], A_sb[si][:, 128:256], BKV[si][:, 256:384], start=False, stop=True),
                                  reads=[Ab[si], BKVb[si]], writes=[pbX])
                            Sx.op("act", lambda e: e.activation(out=X_sb[si][:, :], in_=psX[:, 0:128], func=AF.Copy), reads=[pbX], writes=[Xb[si]])
                        for pr in PRS:
                            si = pr % NSET
                            psU, pbU = Sx.next_psum()
                            Sx.op("pe", lambda e: e.matmul(psU[:, 0:128], Wt[si][1][:, :], X_sb[si][:, :], start=True, stop=True), reads=[Wtb[si][1], Xb[si]], writes=[pbU])
                            Sx.op("dve", lambda e: e.tensor_copy(U_sb[si][:, :], psU[:, 0:128]), reads=[pbU], writes=[Ub[si]])
                        for pr in PRS:
                            u = U_[pr]; si = pr % NSET
                            psY, pbY = Sx.next_psum()
                            Sx.op("pe", lambda e: e.matmul(psY[:, 0:64], H[pr][:, :], u["rTu"], start=True, stop=False), reads=[Hb[pr], tb["rT"]], writes=[pbY])
                            Sx.op("pe", lambda e: e.matmul(psY[:, 0:64], U_sb[si][:, :], R_sb[si][:, 0:64], start=False, stop=False),
                                  reads=[Ub[si], Rb[si]], writes=[pbY])
                            Sx.op("pe", lambda e: e.matmul(psY[:, 0:64], BKV[si][:, 256:384], R_sb[si][:, 64:128], start=False, stop=True),
                                  reads=[BKVb[si], Rb[si]], writes=[pbY])
                            Sx.op("act", lambda e: e.activation(out=Ycm[:, pr, ch * 64:(ch + 1) * 64], in_=psY[:, 0:64], func=AF.Copy), reads=[pbY], writes=[Ybp[pr]])
                            psG, pbG = Sx.next_psum()
                            Sx.op("pe", lambda e: e.matmul(psG[:, 0:128], BKV[si][:, 0:128], U_sb[si][:, :], start=True, stop=False),
                                  reads=[BKVb[si], Ub[si]], writes=[pbG])
                            Sx.op("pe", lambda e: e.matmul(psG[:, 0:128], BKV[si][:, 128:256], BKV[si][:, 256:384], start=False, stop=True),
                                  reads=[BKVb[si]], writes=[pbG])
                            Sx.op("pool", lambda e: e.tensor_scalar(out=HpC[si][:, :], in0=H[pr][:, :], scalar1=u["pC"], scalar2=None, op0=ALU.mult),
                                  reads=[Hb[pr], tb["eP"]], writes=[HpCb[si]])
                            Sx.op("dve", lambda e: e.scalar_tensor_tensor(out=H[pr][:, :], in0=psG[:, 0:128], scalar=u["pC"], in1=HpC[si][:, :],
                                                                          op0=ALU.mult, op1=ALU.add),
                                  reads=[pbG, HpCb[si], tb["eP"]], writes=[Hb[pr]])
                if self.cfg.get("stop_after", 9) < 3:
                    continue
                NT_ = 4 * T
                psM, pbM = Sx.next_psum()
                Sx.op("pe", lambda e: e.matmul(psM[:, 0:NT_], self.bo[:, :], flat(Ycm), start=True, stop=True), reads=Ybp + [cb], writes=[pbM])
                Sx.op("act", lambda e: e.activation(out=flat(tt_["t1"]), in_=flat(Ycm), func=AF.Square), reads=Ybp, writes=[tb["t1"]])
                psQ, pbQ = Sx.next_psum()
                Sx.op("pe", lambda e: e.matmul(psQ[:, 0:NT_], self.bo[:, :], flat(tt_["t1"]), start=True, stop=True), reads=[tb["t1"], cb], writes=[pbQ])
                Sx.op("act", lambda e: e.activation(out=flat(tt_["t2"]), in_=psM[:, 0:NT_], func=AF.Copy, scale=1.0 / 64.0), reads=[pbM], writes=[tb["t2"]])
                Sx.op("pool", lambda e: e.tensor_tensor(out=flat(tt_["t3"]), in0=flat(tt_["t2"]), in1=flat(tt_["t2"]), op=ALU.mult),
                      reads=[tb["t2"]], writes=[tb["t3"]])
                Sx.op("dve", lambda e: e.scalar_tensor_tensor(out=flat(tt_["t3"]), in0=psQ[:, 0:NT_], scalar=1.0 / 64.0, in1=flat(tt_["t3"]),
                                                              op0=ALU.mult, op1=ALU.subtract), reads=[pbQ, tb["t3"]], writes=[tb["t3"]])
                Sx.op("dve", lambda e: e.tensor_scalar(out=flat(tt_["t3"]), in0=flat(tt_["t3"]), scalar1=64e-5, scalar2=None, op0=ALU.add),
                      reads=[tb["t3"]], writes=[tb["t3"]])
                Sx.op("act", lambda e: e.activation(out=flat(tt_["t3"]), in_=flat(tt_["t3"]), func=AF.Sqrt), reads=[tb["t3"]], writes=[tb["t3"]])
                Sx.op("dve", lambda e: e.reciprocal(flat(tt_["t1"]), flat(tt_["t3"])), reads=[tb["t3"]], writes=[tb["t1"]])
                Sx.op("pool", lambda e: e.tensor_tensor(out=flat(tt_["t2"]), in0=flat(Ycm), in1=flat(tt_["t2"]), op=ALU.subtract),
                      reads=Ybp + [tb["t2"]], writes=[tb["t2"]])
                Sx.op("pool", lambda e: e.tensor_tensor(out=flat(tt_["t2"]), in0=flat(tt_["t2"]), in1=flat(tt_["t1"]), op=ALU.mult),
                      reads=[tb["t2"], tb["t1"]], writes=[tb["t2"]])
                for c4 in range(4):
                    Sx.op("act", lambda e, c4=c4: e.activation(out=tt_["t2"][:, c4, :], in_=tt_["t2"][:, c4, :], func=AF.Identity,
                                                               scale=self.pc("lnx_g", c4), bias=self.pc("lnx_b", c4)),
                          reads=[tb["t2"], pcb], writes=[tb["t2"]])
                    Sx.op("dve", lambda e, c4=c4: e.scalar_tensor_tensor(out=tt_["t1"][:, c4, :], in0=r_[:, c4, :], scalar=self.pc("r_k", c4),
                                                                         in1=tt_["kp"][:, c4, :], op0=ALU.mult, op1=ALU.mult),
                          reads=[pm_r[c4], tb["kp"], pcb, tb["t1"]], writes=[tb["t1"]])
                psR, pbR = Sx.next_psum()
                Sx.op("pe", lambda e: e.matmul(psR[:, 0:NT_], self.bo[:, :], flat(tt_["t1"]), start=True, stop=True), reads=[tb["t1"], cb], writes=[pbR])
                Sx.op("dve", lambda e: e.tensor_tensor(out=flat(tt_["t3"]), in0=psR[:, 0:NT_], in1=v_.rearrange("p c t -> p (c t)"), op=ALU.mult),
                      reads=[pbR] + pm_v, writes=[tb["t3"]])
                Sx.op("pool", lambda e: e.tensor_tensor(out=flat(tt_["t3"]), in0=flat(tt_["t3"]), in1=flat(tt_["t2"]), op=ALU.add),
                      reads=[tb["t3"], tb["t2"]], writes=[tb["t3"]])
                Sx.op("pool", lambda e: e.tensor_tensor(out=flat(ybf), in0=flat(tt_["t3"]), in1=flat(sgate), op=ALU.mult),
                      reads=[tb["t3"], sgb], writes=[ybb])
                Sx.dma("sp", ydst[:, :, t0:t0 + T], ybf[:, :, :], reads=[ybb], writes=[self.ybr_buf[0]])
            Sx.barrier()

    def rope_tables(self, s):
        Sx = self.S
        rp = self.c["c_rope"]
        cb = self.cb
        with contextlib.ExitStack() as es:
            posi = self.sb(es, "posi", [128, S], I32)
            ang = self.sb(es, "ang", [128, S], F32)
            t1 = self.sb(es, "rp_t1", [128, S], F32)
            t2 = self.sb(es, "rp_t2", [128, S], F32)
            ki = self.sb(es, "rp_ki", [128, S], I32)
            b = Buf()
            R = slice(64, 96)
            Sx.dma("sp", posi[R, :], self.din["positions"][s:s + 1, :].broadcast_to([32, S]), writes=[b])
            Sx.op("dve", lambda e: e.tensor_copy(ang[R, :], posi[R, :]), reads=[b], writes=[b])
            Sx.op("dve", lambda e: e.tensor_scalar(out=ang[R, :], in0=ang[R, :], scalar1=rp[R, 0:1], scalar2=None, op0=ALU.mult),
                  reads=[b, cb], writes=[b])
            TWO_PI = 6.283185307179586
            for which, dst in ((0, self.ropeS), (1, self.ropeC)):
                shift = 0.0 if which == 0 else 1.5707963267948966
                Sx.op("dve", lambda e: e.tensor_scalar(out=t1[R, :], in0=ang[R, :], scalar1=shift, scalar2=None, op0=ALU.add), reads=[b], writes=[b])
                Sx.op("dve", lambda e: e.tensor_scalar(out=t2[R, :], in0=t1[R, :], scalar1=1.0 / TWO_PI, scalar2=0.5, op0=ALU.mult, op1=ALU.add),
                      reads=[b], writes=[b])
                Sx.op("dve", lambda e: e.tensor_copy(ki[R, :], t2[R, :]), reads=[b], writes=[b])
                Sx.op("dve", lambda e: e.tensor_copy(t2[R, :], ki[R, :]), reads=[b], writes=[b])
                Sx.op("dve", lambda e: e.scalar_tensor_tensor(out=t1[R, :], in0=t2[R, :], scalar=-TWO_PI, in1=t1[R, :], op0=ALU.mult, op1=ALU.add),
                      reads=[b], writes=[b])
                Sx.op("dve", lambda e: e.tensor_scalar(out=t2[R, :], in0=t1[R, :], scalar1=-3.141592653589793, scalar2=TWO_PI, op0=ALU.is_lt, op1=ALU.mult),
                      reads=[b], writes=[b])
                Sx.op("dve", lambda e: e.tensor_tensor(out=t1[R, :], in0=t1[R, :], in1=t2[R, :], op=ALU.add), reads=[b], writes=[b])
                Sx.op("dve", lambda e: e.tensor_scalar(out=t2[R, :], in0=t1[R, :], scalar1=3.141592653589793, scalar2=-TWO_PI, op0=ALU.is_gt, op1=ALU.mult),
                      reads=[b], writes=[b])
                Sx.op("dve", lambda e: e.tensor_tensor(out=t1[R, :], in0=t1[R, :], in1=t2[R, :], op=ALU.add), reads=[b], writes=[b])
                Sx.op("dve", lambda e: e.tensor_scalar(out=t1[R, :], in0=t1[R, :], scalar1=3.1415925, scalar2=-3.1415925, op0=ALU.min, op1=ALU.max),
                      reads=[b], writes=[b])
                Sx.op("act", lambda e, dst=dst: e.activation(out=dst[R, :], in_=t1[R, :], func=AF.Sin), reads=[b], writes=[self.rope_b])
            Sx.op("dve", lambda e: e.tensor_scalar(out=self.ropeS[R, :], in0=self.ropeS[R, :], scalar1=rp[R, 1:2], scalar2=None, op0=ALU.mult),
                  reads=[self.rope_b, cb], writes=[self.rope_b])
            Sx.barrier()

    def load_memT(self, s):
        Sx = self.S
        with contextlib.ExitStack() as es:
            mt = [self.sb(es, f"mtok{i}", [128, D], F32) for i in range(2)]
            mb = [Buf(), Buf()]
            for tt in range(2):
                Sx.dma("sp", mt[tt][:], self.din["mem"][s, tt * 128:(tt + 1) * 128, :], writes=[mb[tt]])
                for half in range(2):
                    ps, pb = Sx.next_psum()
                    for j in range(4):
                        kc = half * 4 + j
                        Sx.op("pe", lambda e: e.matmul(ps[:, j * 128:(j + 1) * 128], mt[tt][:, kc * 128:(kc + 1) * 128], self.ident[:, :], start=True, stop=True),
                              reads=[mb[tt], self.cb], writes=[pb])
                    Sx.op("act", lambda e: e.activation(out=self.memT[:, half * 4:(half + 1) * 4, tt * 128:(tt + 1) * 128],
                                                        in_=ps[:, :].rearrange("p (j t) -> p j t", j=4), func=AF.Copy),
                          reads=[pb], writes=[self.memT_b])
            Sx.barrier()

    def inproj(self, ps, pb, wt, wb, c0, ncol, t0, ntok):
        Sx = self.S
        for kc in range(8):
            Sx.op("pe", lambda e, kc=kc: e.matmul(ps[0:ncol, 0:ntok], wt[:, kc, c0:c0 + ncol], self.xT[:, kc, t0:t0 + ntok],
                                                  start=(kc == 0), stop=(kc == 7)), reads=[wb, self.xT_b], writes=[pb])

    def rms_latent(self, es, tag, lat_f, sq, nch, dim, gname, outn, bufs):
        Sx = self.S
        lb, sqb, ob, rb, rstd = bufs
        ps, pb = Sx.next_psum(4)
        for i in range(nch):
            Sx.op("pe", lambda e, i=i: e.matmul(ps[:, :], self.c["c_ones"][:, :], sq[:, i, :], start=(i == 0), stop=(i == nch - 1)),
                  reads=[sqb, self.cb], writes=[pb])
        Sx.op("dve", lambda e: e.tensor_scalar(out=rstd[:, :], in0=ps[:, :], scalar1=1.0 / dim, scalar2=1e-6, op0=ALU.mult, op1=ALU.add),
              reads=[pb], writes=[rb])
        Sx.op("act", lambda e: e.activation(out=rstd[:, :], in_=rstd[:, :], func=AF.Sqrt), reads=[rb], writes=[rb])
        Sx.op("dve", lambda e: e.reciprocal(rstd[:, :], rstd[:, :]), reads=[rb], writes=[rb])
        for i in range(nch):
            Sx.op("dve", lambda e, i=i: e.scalar_tensor_tensor(out=outn[:, i, :], in0=lat_f[:, i, :], scalar=self.pc(gname, i), in1=rstd[:, :],
                                                               op0=ALU.mult, op1=ALU.mult), reads=[lb, rb, self.pcol_b], writes=[ob])

    def phase_mla(self, s, l):
        Sx = self.S
        d = self.din
        cb = self.cb
        scale = 96.0 ** -0.5
        with contextlib.ExitStack() as es:
            sb = lambda n, shp, dt=F32: self.sb(es, "ml_" + n, shp, dt)
            wM = sb("wM", [128, 8, 1184], BF16); wMb = Buf()
            self.load_w_bf(wM, wMb, d["w_in"][l, :, OFF_QLAT:OFF_QLAT + 1184], 8)
            wks = sb("wks", [128, 8, 96], BF16); wksb = Buf()
            Sx.op("pool", lambda e: e.memset(wks[:], 0.0), writes=[wksb])
            vk = d["w_in"][l, :, OFF_KPE:OFF_KPE + 32].rearrange("(kc p) c -> p kc c", p=128)
            Sx.dma("pool", wks[:, :, 64:80], vk[:, :, 16:32], writes=[wksb])
            Sx.dma("pool", wks[:, :, 80:96], vk[:, :, 0:16], writes=[wksb])
            Wuq = sb("Wuq", [128, 3, 768], BF16); Wuqb = Buf()
            self.load_w_bf(Wuq, Wuqb, d["mla_w_uq"][l], 3)
            Wus = sb("Wus", [128, 3, 768], BF16); Wusb = Buf()
            Wq4 = Wuq[:, :, :].rearrange("p k (h c) -> p k h c", h=8)
            Ws4 = Wus[:, :, :].rearrange("p k (h c) -> p k h c", h=8)
            Sx.op("pool", lambda e: e.tensor_copy(Ws4[:, :, :, 0:64], Wq4[:, :, :, 0:64]), reads=[Wuqb], writes=[Wusb])
            Sx.op("pool", lambda e: e.tensor_copy(Ws4[:, :, :, 64:80], Wq4[:, :, :, 80:96]), reads=[Wuqb], writes=[Wusb])
            Sx.op("pool", lambda e: e.tensor_copy(Ws4[:, :, :, 80:96], Wq4[:, :, :, 64:80]), reads=[Wuqb], writes=[Wusb])
            Wukv = sb("Wukv", [128, 2, 1024], BF16); Wukvb = Buf()
            self.load_w_bf(Wukv, Wukvb, d["mla_w_ukv"][l], 2)
            QT = sb("QT", [128, 8, 512], BF16); QTh = [Buf() for _ in range(8)]
            KT = sb("KT", [128, 8, S], BF16); KTh = [Buf() for _ in range(8)]
            V = sb("V", [128, 16, 512], BF16); Vb = Buf()
            qlf = sb("qlf", [128, 3, 512]); qlb = Buf()
            qsq = sb("qsq", [128, 3, 512]); qsb = Buf()
            qn = sb("qn", [128, 3, 512], BF16); qnb = Buf()
            klf = sb("klf", [128, 2, 512]); klb = Buf()
            ksq = sb("ksq", [128, 2, 512]); ksb = Buf()
            kvn = sb("kvn", [128, 2, 512], BF16); knb = Buf()
            rq = sb("rq", [128, 512]); rqb = Buf()
            rk = sb("rk", [128, 512]); rkb = Buf()
            tas = [sb(f"ta{i}", [128, 512]) for i in range(2)]; tabs = [Buf(), Buf()]
            tbs = [sb(f"tb{i}", [128, 512]) for i in range(2)]; tbbs = [Buf(), Buf()]
            ta = tas[0]; tab = tabs[0]; tbb_ = tbs[0]; tbb = tbbs[0]
            PT = [sb(f"PT{i}", [128, 512], BF16) for i in range(3)]; PTb = [Buf() for _ in range(3)]
            sgt = sb("sgt", [128, 512]); sgb = Buf()
            rl = sb("rl", [128, 512]); rlb = Buf()
            yb = [sb(f"yb{i}", [128, 512], BF16) for i in range(2)]; ybb = [Buf(), Buf()]
            ydst = self.ybr[1]
            R = slice(64, 96)
            pti = 0
            for g in range(4):
                t0 = g * 512
                for i in range(3):
                    ps, pb = Sx.next_psum(4)
                    self.inproj(ps, pb, wM, wMb, i * 128, 128, t0, 512)
                    Sx.op("act", lambda e: e.activation(out=qlf[:, i, :], in_=ps[:, :], func=AF.Copy), reads=[pb], writes=[qlb])
                    Sx.op("act", lambda e: e.activation(out=qsq[:, i, :], in_=ps[:, :], func=AF.Square), reads=[pb], writes=[qsb])
                self.rms_latent(es, "q", qlf, qsq, 3, 384.0, "q_norm", qn, (qlb, qsb, qnb, rqb, rq))
                for i in range(2):
                    ps, pb = Sx.next_psum(4)
                    self.inproj(ps, pb, wM, wMb, 384 + i * 128, 128, t0, 512)
                    Sx.op("act", lambda e: e.activation(out=klf[:, i, :], in_=ps[:, :], func=AF.Copy), reads=[pb], writes=[klb])
                    Sx.op("act", lambda e: e.activation(out=ksq[:, i, :], in_=ps[:, :], func=AF.Square), reads=[pb], writes=[ksb])
                self.rms_latent(es, "k", klf, ksq, 2, 256.0, "kv_norm", kvn, (klb, ksb, knb, rkb, rk))
                ps1, pb1 = Sx.next_psum(4)
                self.inproj(ps1, pb1, wM, wMb, 576, 96, t0, 512)
                ps2, pb2 = Sx.next_psum(4)
                self.inproj(ps2, pb2, wks, wksb, 0, 96, t0, 512)
                Sx.op("dve", lambda e: e.tensor_tensor(out=ta[R, :], in0=ps1[R, :], in1=self.ropeC[R, t0:t0 + 512], op=ALU.mult),
                      reads=[pb1, self.rope_b], writes=[tab])
                Sx.op("dve", lambda e: e.tensor_tensor(out=tbb_[R, :], in0=ps2[R, :], in1=self.ropeS[R, t0:t0 + 512], op=ALU.mult),
                      reads=[pb2, self.rope_b], writes=[tbb])
                Sx.op("pool", lambda e: e.tensor_tensor(out=ta[R, :], in0=ta[R, :], in1=tbb_[R, :], op=ALU.add), reads=[tab, tbb], writes=[tab])
                for h in range(8):
                    Sx.op("pool" if h % 2 else "act",
                          (lambda e: e.tensor_copy(KT[R, h, t0:t0 + 512], ta[R, :])) if h % 2 else
                          (lambda e: e.activation(out=KT[R, h, t0:t0 + 512], in_=ta[R, :], func=AF.Copy)),
                          reads=[tab], writes=[KTh[h]])
                for h in range(8):
                    ta = tas[h % 2]; tab = tabs[h % 2]; tbb_ = tbs[h % 2]; tbb = tbbs[h % 2]
                    ps1, pb1 = Sx.next_psum(4)
                    ps2, pb2 = Sx.next_psum(4)
                    for kc in range(3):
                        Sx.op("pe", lambda e: e.matmul(ps1[0:96, :], Wuq[:, kc, h * 96:(h + 1) * 96], qn[:, kc, :], start=(kc == 0), stop=(kc == 2)),
                              reads=[Wuqb, qnb], writes=[pb1])
                    for kc in range(3):
                        Sx.op("pe", lambda e: e.matmul(ps2[0:96, :], Wus[:, kc, h * 96:(h + 1) * 96], qn[:, kc, :], start=(kc == 0), stop=(kc == 2)),
                              reads=[Wusb, qnb], writes=[pb2])
                    Sx.op("act", lambda e: e.activation(out=QT[0:64, h, :], in_=ps1[0:64, :], func=AF.Copy), reads=[pb1], writes=[QTh[h]])
                    Sx.op("dve", lambda e: e.tensor_tensor(out=ta[R, :], in0=ps1[R, :], in1=self.ropeC[R, t0:t0 + 512], op=ALU.mult),
                          reads=[pb1, self.rope_b], writes=[tab])
                    Sx.op("dve", lambda e: e.tensor_tensor(out=tbb_[R, :], in0=ps2[R, :], in1=self.ropeS[R, t0:t0 + 512], op=ALU.mult),
                          reads=[pb2, self.rope_b], writes=[tbb])
                    Sx.op("pool", lambda e: e.tensor_tensor(out=QT[R, h, :], in0=ta[R, :], in1=tbb_[R, :], op=ALU.add), reads=[tab, tbb], writes=[QTh[h]])
                    ps3, pb3 = Sx.next_psum(4)
                    for kc in range(2):
                        Sx.op("pe", lambda e: e.matmul(ps3[0:64, :], Wukv[:, kc, h * 128:h * 128 + 64], kvn[:, kc, :], start=(kc == 0), stop=(kc == 1)),
                              reads=[Wukvb, knb], writes=[pb3])
                    Sx.op("act", lambda e: e.activation(out=KT[0:64, h, t0:t0 + 512], in_=ps3[0:64, :], func=AF.Copy), reads=[pb3], writes=[KTh[h]])
                Wv = Wukv[:, :, :].rearrange("p k (h c) -> p k h c", h=8)
                for tt in range(4):
                    ps, pb = Sx.next_psum(4)
                    for kc in range(2):
                        Sx.op("pe", lambda e: e.matmul(ps[:, :].rearrange("p (h c) -> p h c", h=8), kvn[:, kc, tt * 128:(tt + 1) * 128], Wv[:, kc, :, 64:128],
                                                       start=(kc == 0), stop=(kc == 1)), reads=[Wukvb, knb], writes=[pb])
                    Sx.op("dve", lambda e: e.tensor_copy(V[:, g * 4 + tt, :], ps[:, :]), reads=[pb], writes=[Vb])
                for pr in range(4):
                    accs = []
                    for hh in range(2):
                        h = pr * 2 + hh
                        o_ps, o_pb = Sx.psum[4 + hh * 2]
                        l_ps, l_pb = Sx.psum[5 + hh * 2]
                        accs.append((o_ps, o_pb, l_ps, l_pb))
                        nj = 4 * g + 4
                        for j in range(nj):
                            v = max(0, j - 4 * g)
                            c0 = v * 128
                            sps, spb = Sx.next_psum(4)
                            Sx.op("pe", lambda e: e.matmul(sps[:, c0:512], KT[0:96, h, j * 128:(j + 1) * 128], QT[0:96, h, c0:512], start=True, stop=True),
                                  reads=[KTh[h], QTh[h]], writes=[spb])
                            pt = PT[pti]; ptb = PTb[pti]; pti = (pti + 1) % 3
                            Sx.op("act", lambda e: e.activation(out=pt[:, c0:512], in_=sps[:, c0:512], func=AF.Exp, scale=scale), reads=[spb], writes=[ptb])
                            if j >= 4 * g:
                                Sx.op("pool", lambda e: e.tensor_tensor(out=pt[:, c0:c0 + 128], in0=pt[:, c0:c0 + 128],
                                                                        in1=self.cmla_bf[:, v * 512 + c0:v * 512 + c0 + 128], op=ALU.mult),
                                      reads=[ptb, cb], writes=[ptb])
                            Sx.op("pe", lambda e: e.matmul(o_ps[:, c0:512], V[:, j, pr * 128:(pr + 1) * 128], pt[:, c0:512], start=(j == 0), stop=(j == nj - 1)),
                                  reads=[Vb, ptb], writes=[o_pb])
                            Sx.op("pe", lambda e: e.matmul(l_ps[:, c0:512], self.ones_bf[:, :], pt[:, c0:512], start=(j == 0), stop=(j == nj - 1)),
                                  reads=[cb, ptb], writes=[l_pb])
                    gps, gpb = Sx.next_psum(4)
                    self.inproj(gps, gpb, wM, wMb, 672 + pr * 128, 128, t0, 512)
                    Sx.op("act", lambda e: e.activation(out=sgt[:, :], in_=gps[:, :], func=AF.Silu), reads=[gpb], writes=[sgb])
                    yt = yb[pr % 2]; ytb = ybb[pr % 2]
                    for hh in range(2):
                        o_ps, o_pb, l_ps, l_pb = accs[hh]
                        HR = slice(hh * 64, hh * 64 + 64)
                        Sx.op("dve", lambda e: e.reciprocal(rl[HR, :], l_ps[HR, :]), reads=[l_pb], writes=[rlb])
                        Sx.op("dve", lambda e: e.tensor_tensor(out=rl[HR, :], in0=o_ps[HR, :], in1=rl[HR, :], op=ALU.mult), reads=[o_pb, rlb], writes=[rlb])
                        Sx.op("pool", lambda e: e.tensor_tensor(out=yt[HR, :], in0=rl[HR, :], in1=sgt[HR, :], op=ALU.mult), reads=[rlb, sgb], writes=[ytb])
                    Sx.dma("sp", ydst[pr * 128:(pr + 1) * 128, t0:t0 + 512], yt[:, :], reads=[ytb], writes=[self.ybr_buf[1]])
            Sx.barrier()

    def phase_xattn(self, s, l):
        Sx = self.S
        d = self.din
        cb = self.cb
        scale = 128.0 ** -0.5
        with contextlib.ExitStack() as es:
            sb = lambda n, shp, dt=F32: self.sb(es, "xa_" + n, shp, dt)
            wkv = sb("wkv", [128, 8, 1024], BF16); wkvb = Buf()
            self.load_w_bf(wkv, wkvb, d["xattn_w_mem_kv"][l], 8)
            wX = sb("wX", [128, 8, 1024], BF16); wXb = Buf()
            self.load_w_bf(wX, wXb, d["w_in"][l, :, OFF_XQ:OFF_XQ + 1024], 8)
            KxT = sb("KxT", [128, 4, 256], BF16); Kb = Buf()
            Vx = sb("Vx", [128, 2, 512], BF16); Vb = Buf()
            qx = sb("qx", [128, 512], BF16); qb = Buf()
            PT = [sb(f"PT{i}", [128, 512], BF16) for i in range(2)]; PTb = [Buf(), Buf()]
            sgt = sb("sgt", [128, 512]); sgb = Buf()
            rl = sb("rl", [128, 512]); rlb = Buf()
            yb = [sb(f"yb{i}", [128, 512], BF16) for i in range(2)]; ybb = [Buf(), Buf()]
            for h in range(4):
                ps, pb = Sx.next_psum(4)
                for kc in range(8):
                    Sx.op("pe", lambda e: e.matmul(ps[:, 0:256], wkv[:, kc, h * 128:(h + 1) * 128], self.memT[:, kc, :], start=(kc == 0), stop=(kc == 7)),
                          reads=[wkvb, self.memT_b], writes=[pb])
                Sx.op("act", lambda e: e.activation(out=KxT[:, h, :], in_=ps[:, 0:256], func=AF.Copy), reads=[pb], writes=[Kb])
            for mt in range(2):
                ps, pb = Sx.next_psum(4)
                for kc in range(8):
                    Sx.op("pe", lambda e: e.matmul(ps[:, :], self.memT[:, kc, mt * 128:(mt + 1) * 128], wkv[:, kc, 512:1024], start=(kc == 0), stop=(kc == 7)),
                          reads=[wkvb, self.memT_b], writes=[pb])
                Sx.op("act", lambda e: e.activation(out=Vx[:, mt, :], in_=ps[:, :], func=AF.Copy), reads=[pb], writes=[Vb])
            ydst = self.ybr[3]
            k = 0
            for g in range(4):
                t0 = g * 512
                for h in range(4):
                    ps, pb = Sx.next_psum(4)
                    self.inproj(ps, pb, wX, wXb, h * 128, 128, t0, 512)
                    Sx.op("act", lambda e: e.activation(out=qx[:, :], in_=ps[:, :], func=AF.Copy), reads=[pb], writes=[qb])
                    o_ps, o_pb = Sx.psum[4]
                    l_ps, l_pb = Sx.psum[5]
                    for mt in range(2):
                        sps, spb = Sx.next_psum(4)
                        Sx.op("pe", lambda e: e.matmul(sps[:, :], KxT[:, h, mt * 128:(mt + 1) * 128], qx[:, :], start=True, stop=True), reads=[Kb, qb], writes=[spb])
                        pt = PT[mt]; ptb = PTb[mt]
                        Sx.op("act", lambda e: e.activation(out=pt[:, :], in_=sps[:, :], func=AF.Exp, scale=scale), reads=[spb], writes=[ptb])
                        Sx.op("pe", lambda e: e.matmul(o_ps[:, :], Vx[:, mt, h * 128:(h + 1) * 128], pt[:, :], start=(mt == 0), stop=(mt == 1)),
                              reads=[Vb, ptb], writes=[o_pb])
                        Sx.op("pe", lambda e: e.matmul(l_ps[:, :], self.ones_bf[:, :], pt[:, :], start=(mt == 0), stop=(mt == 1)),
                              reads=[cb, ptb], writes=[l_pb])
                    gps, gpb = Sx.next_psum(4)
                    self.inproj(gps, gpb, wX, wXb, 512 + h * 128, 128, t0, 512)
                    Sx.op("act", lambda e: e.activation(out=sgt[:, :], in_=gps[:, :], func=AF.Silu), reads=[gpb], writes=[sgb])
                    yt = yb[k % 2]; ytb = ybb[k % 2]; k += 1
                    Sx.op("dve", lambda e: e.reciprocal(rl[:, :], l_ps[:, :]), reads=[l_pb], writes=[rlb])
                    Sx.op("dve", lambda e: e.tensor_tensor(out=rl[:, :], in0=o_ps[:, :], in1=rl[:, :], op=ALU.mult), reads=[o_pb, rlb], writes=[rlb])
                    Sx.op("pool", lambda e: e.tensor_tensor(out=yt[:, :], in0=rl[:, :], in1=sgt[:, :], op=ALU.mult), reads=[rlb, sgb], writes=[ytb])
                    Sx.dma("sp", ydst[h * 128:(h + 1) * 128, t0:t0 + 512], yt[:, :], reads=[ytb], writes=[self.ybr_buf[3]])
            Sx.barrier()

    def phase_conv(self, s, l):
        Sx = self.S
        d = self.din
        cb = self.cb
        pcb = self.pcol_b
        ones = self.c["c_ones"]
        with contextlib.ExitStack() as es:
            sb = lambda n, shp, dt=F32: self.sb(es, "cv_" + n, shp, dt)
            wC = sb("wC", [128, 8, 1536], BF16); wCb = Buf()
            self.load_w_bf(wC, wCb, d["w_in"][l, :, OFF_CONV:OFF_CONV + 1536], 8)
            Dg = sb("Dg", [128, 4, 31, 128], BF16); Dgb = BufG([Buf(), Buf()])
            for c in range(4):
                for j in range(31):
                    Sx.op("pool" if (j % 2) else "dve",
                          lambda e: e.tensor_scalar(out=Dg[:, c, j, :], in0=self.ident[:, :], scalar1=self.pc("conv_w", j * 4 + c), scalar2=None, op0=ALU.mult),
                          reads=[cb, pcb], writes=[Dgb[j % 2]])
            hb = sb("hb", [128, 4, 30 + S], BF16); hbb = Buf()
            Sx.op("pool", lambda e: e.memset(hb[:, :, 0:30], 0.0), writes=[hbb])
            sig = sb("sig", [128, 512]); sigb = Buf()
            cv = sb("cvv", [128, 4, 512]); cvb = Buf()
            sq = sb("sq", [128, 4, 512]); sqb = Buf()
            mean = sb("mean", [128, 512]); mb = Buf()
            rstd = sb("rstd", [128, 512]); rb = Buf()
            t1 = sb("t1", [128, 512]); t1b = Buf()
            sgt = sb("sgt", [128, 512]); sgb = Buf()
            yb = [sb(f"yb{i}", [128, 512], BF16) for i in range(2)]; ybb = [Buf(), Buf()]
            ydst = self.ybr[2]
            for g in range(4):
                t0 = g * 512
                for c in range(4):
                    ps1, pb1 = Sx.next_psum()
                    self.inproj(ps1, pb1, wC, wCb, c * 128, 128, t0, 512)
                    ps2, pb2 = Sx.next_psum()
                    self.inproj(ps2, pb2, wC, wCb, 512 + c * 128, 128, t0, 512)
                    Sx.op("act", lambda e: e.activation(out=sig[:, :], in_=ps2[:, :], func=AF.Sigmoid), reads=[pb2], writes=[sigb])
                    Sx.op("dve", lambda e: e.tensor_tensor(out=hb[:, c, 30 + t0:30 + t0 + 512], in0=ps1[:, :], in1=sig[:, :], op=ALU.mult),
                          reads=[pb1, sigb], writes=[hbb])
                for c in range(4):
                    ps, pb = Sx.next_psum()
                    for j in range(31):
                        Sx.op("pe", lambda e: e.matmul(ps[:, :], Dg[:, c, j, :], hb[:, c, t0 + j:t0 + j + 512], start=(j == 0), stop=(j == 30)),
                              reads=[Dgb, hbb], writes=[pb])
                    Sx.op("act", lambda e: e.activation(out=cv[:, c, :], in_=ps[:, :], func=AF.Identity, bias=self.pc("conv_b", c)), reads=[pb, pcb], writes=[cvb])
                Sx.op("act", lambda e: e.activation(out=sq[:, :, :].rearrange("p c t -> p (c t)"), in_=cv[:, :, :].rearrange("p c t -> p (c t)"), func=AF.Square),
                      reads=[cvb], writes=[sqb])
                psM, pbM = Sx.next_psum()
                psQ, pbQ = Sx.next_psum()
                for c in range(4):
                    Sx.op("pe", lambda e: e.matmul(psM[:, :], ones[:, :], cv[:, c, :], start=(c == 0), stop=(c == 3)), reads=[cvb, cb], writes=[pbM])
                for c in range(4):
                    Sx.op("pe", lambda e: e.matmul(psQ[:, :], ones[:, :], sq[:, c, :], start=(c == 0), stop=(c == 3)), reads=[sqb, cb], writes=[pbQ])
                Sx.op("act", lambda e: e.activation(out=mean[:, :], in_=psM[:, :], func=AF.Copy, scale=1.0 / 512.0), reads=[pbM], writes=[mb])
                Sx.op("pool", lambda e: e.tensor_tensor(out=t1[:, :], in0=mean[:, :], in1=mean[:, :], op=ALU.mult), reads=[mb], writes=[t1b])
                Sx.op("dve", lambda e: e.scalar_tensor_tensor(out=rstd[:, :], in0=psQ[:, :], scalar=1.0 / 512.0, in1=t1[:, :], op0=ALU.mult, op1=ALU.subtract),
                      reads=[pbQ, t1b], writes=[rb])
                Sx.op("dve", lambda e: e.tensor_scalar(out=rstd[:, :], in0=rstd[:, :], scalar1=1e-5, scalar2=None, op0=ALU.add), reads=[rb], writes=[rb])
                Sx.op("act", lambda e: e.activation(out=rstd[:, :], in_=rstd[:, :], func=AF.Sqrt), reads=[rb], writes=[rb])
                Sx.op("dve", lambda e: e.reciprocal(rstd[:, :], rstd[:, :]), reads=[rb], writes=[rb])
                for c in range(4):
                    Sx.op("pool", lambda e: e.tensor_tensor(out=t1[:, :], in0=cv[:, c, :], in1=mean[:, :], op=ALU.subtract), reads=[cvb, mb, t1b], writes=[t1b])
                    Sx.op("dve", lambda e: e.tensor_tensor(out=t1[:, :], in0=t1[:, :], in1=rstd[:, :], op=ALU.mult), reads=[t1b, rb], writes=[t1b])
                    Sx.op("act", lambda e: e.activation(out=t1[:, :], in_=t1[:, :], func=AF.Silu, scale=self.pc("conv_ln_g", c), bias=self.pc("conv_ln_b", c)),
                          reads=[t1b, pcb], writes=[t1b])
                    gps, gpb = Sx.next_psum()
                    self.inproj(gps, gpb, wC, wCb, 1024 + c * 128, 128, t0, 512)
                    Sx.op("act", lambda e: e.activation(out=sgt[:, :], in_=gps[:, :], func=AF.Silu), reads=[gpb], writes=[sgb])
                    yt = yb[c % 2]; ytb = ybb[c % 2]
                    Sx.op("pool", lambda e: e.tensor_tensor(out=yt[:, :], in0=t1[:, :], in1=sgt[:, :], op=ALU.mult), reads=[t1b, sgb], writes=[ytb])
                    Sx.dma("sp", ydst[c * 128:(c + 1) * 128, t0:t0 + 512], yt[:, :], reads=[ytb], writes=[self.ybr_buf[2]])
            Sx.barrier()

    def phase_epi(self, s, l):
        Sx = self.S
        d = self.din
        cb = self.cb
        pcb = self.pcol_b
        with contextlib.ExitStack() as es0:
            mT = self.sb(es0, "ep_mT", [128, 8, S], BF16); mTb = Buf()
            with contextlib.ExitStack() as es:
                sb = lambda n, shp, dt=F32: self.sb(es, "e1_" + n, shp, dt)
                yin = sb("yin", [128, 4, 4, S], BF16); yinb = Buf()
                prs = []
                for n in range(4):
                    src = self.ybr[n].rearrange("(c p) t -> p c t", p=128)
                    for c in range(4):
                        prs.append((yin[:, n, c, :], src[:, c, :]))
                Sx.dma_group("sp", prs, reads=self.ybr_buf, writes=[yinb])
                wG = [sb(f"wG{i}", [128, 8, 4, 128], BF16) for i in range(2)]; wGb = [Buf(), Buf()]
                Wo = [sb(f"Wo{i}", [128, 4, 4, 128], BF16) for i in range(2)]; Wob = [Buf(), Buf()]
                gt = sb("gt", [128, 512]); gtb = Buf()
                acc = sb("acc", [128, 512]); accb = Buf()
                tmp = sb("tmp", [128, 512]); tmpb = Buf()
                def load_db(db_):
                    i_ = db_ % 2
                    pg = []; po = []
                    for n in range(4):
                        c0 = OFF_MERGE + n * 1024 + db_ * 128
                        vg = d["w_in"][l, :, c0:c0 + 128].rearrange("(kc p) c -> p kc c", p=128)
                        pg.append((wG[i_][:, :, n, :], vg))
                        vo = d["w_o_branch"][l, n, :, db_ * 128:(db_ + 1) * 128].rearrange("(cc p) c -> p cc c", p=128)
                        po.append((Wo[i_][:, n, :, :], vo))
                    Sx.dma_group("pool", pg, writes=[wGb[i_]])
                    Sx.dma_group("pool", po, writes=[Wob[i_]])

                load_db(0)
                for db in range(8):
                    i = db % 2
                    if db + 1 < 8:
                        load_db(db + 1)
                    for g in range(4):
                        t0 = g * 512
                        for n in range(4):
                            psP, pbP = Sx.next_psum()
                            for cc in range(4):
                                Sx.op("pe", lambda e: e.matmul(psP[:, :], Wo[i][:, n, cc, :], yin[:, n, cc, t0:t0 + 512], start=(cc == 0), stop=(cc == 3)),
                                      reads=[Wob[i], yinb], writes=[pbP])
                            psG, pbG = Sx.next_psum()
                            for kc in range(8):
                                Sx.op("pe", lambda e: e.matmul(psG[:, :], wG[i][:, kc, n, :], self.xT[:, kc, t0:t0 + 512], start=(kc == 0), stop=(kc == 7)),
                                      reads=[wGb[i], self.xT_b], writes=[pbG])
                            Sx.op("act", lambda e: e.activation(out=gt[:, :], in_=psG[:, :], func=AF.Sigmoid, bias=self.pc("b_gate", n * 8 + db)),
                                  reads=[pbG, pcb], writes=[gtb])
                            if n == 0:
                                Sx.op("dve", lambda e: e.tensor_tensor(out=acc[:, :], in0=psP[:, :], in1=gt[:, :], op=ALU.mult), reads=[pbP, gtb], writes=[accb])
                            else:
                                Sx.op("dve", lambda e: e.tensor_tensor(out=tmp[:, :], in0=psP[:, :], in1=gt[:, :], op=ALU.mult), reads=[pbP, gtb], writes=[tmpb])
                                if n < 3:
                                    Sx.op("pool", lambda e: e.tensor_tensor(out=acc[:, :], in0=acc[:, :], in1=tmp[:, :], op=ALU.add), reads=[accb, tmpb], writes=[accb])
                                else:
                                    Sx.op("pool", lambda e: e.tensor_tensor(out=mT[:, db, t0:t0 + 512], in0=acc[:, :], in1=tmp[:, :], op=ALU.add),
                                          reads=[accb, tmpb], writes=[mTb])
                Sx.barrier()
            with contextlib.ExitStack() as es:
                sb = lambda n, shp, dt=F32: self.sb(es, "e2_" + n, shp, dt)
                Wout = sb("Wout", [128, 8, D], BF16); Woutb = Buf()
                self.load_w_bf(Wout, Woutb, d["w_out"][l], 8)
                xt = [sb(f"xt{i}", [128, D]) for i in range(2)]; xtb = [Buf(), Buf()]
                z = [sb(f"z{i}", [128, D]) for i in range(2)]; zb = [Buf(), Buf()]
                st = sb("st", [128, 12]); stb = Buf()
                mv = sb("mv", [128, 2]); mvb = Buf()
                rs = sb("rs", [128, 1]); rsb = Buf()
                for tt in range(S // 128):
                    i = tt % 2
                    if l == 0:
                        Sx.dma("sp", xt[i][:], d["x"][s, tt * 128:(tt + 1) * 128, :], writes=[xtb[i]])
                    else:
                        Sx.dma("sp", xt[i][:], self.x1[s, tt * 128:(tt + 1) * 128, :], reads=[self.x1_buf[s]], writes=[xtb[i]])
                    for half in range(2):
                        ps, pb = Sx.next_psum()
                        for kc in range(8):
                            Sx.op("pe", lambda e: e.matmul(ps[:, :], mT[:, kc, tt * 128:(tt + 1) * 128], Wout[:, kc, half * 512:(half + 1) * 512],
                                                           start=(kc == 0), stop=(kc == 7)), reads=[mTb, Woutb], writes=[pb])
                        Sx.op("dve", lambda e: e.scalar_tensor_tensor(out=z[i][:, half * 512:(half + 1) * 512], in0=xt[i][:, half * 512:(half + 1) * 512],
                                                                      scalar=float(ALPHA), in1=ps[:, :], op0=ALU.mult, op1=ALU.add),
                              reads=[xtb[i], pb], writes=[zb[i]])
                        Sx.op("dve", lambda e: e.bn_stats(st[:, half * 6:(half + 1) * 6], z[i][:, half * 512:(half + 1) * 512]), reads=[zb[i]], writes=[stb])
                    Sx.op("dve", lambda e: e.bn_aggr(mv[:, :], st[:, :]), reads=[stb], writes=[mvb])
                    Sx.op("dve", lambda e: e.tensor_scalar(out=rs[:, :], in0=mv[:, 1:2], scalar1=1e-5, scalar2=None, op0=ALU.add), reads=[mvb], writes=[rsb])
                    Sx.op("act", lambda e: e.activation(out=rs[:, :], in_=rs[:, :], func=AF.Sqrt), reads=[rsb], writes=[rsb])
                    Sx.op("dve", lambda e: e.reciprocal(rs[:, :], rs[:, :]), reads=[rsb], writes=[rsb])
                    Sx.op("dve", lambda e: e.tensor_scalar(out=z[i][:, :], in0=z[i][:, :], scalar1=mv[:, 0:1], scalar2=rs[:, 0:1], op0=ALU.subtract, op1=ALU.mult),
                          reads=[zb[i], mvb, rsb], writes=[zb[i]])
                    Sx.op("pool", lambda e: e.tensor_tensor(out=z[i][:, :], in0=z[i][:, :], in1=self.lng_bc[:, :], op=ALU.mult), reads=[zb[i], self.ln_b_], writes=[zb[i]])
                    Sx.op("pool", lambda e: e.tensor_tensor(out=z[i][:, :], in0=z[i][:, :], in1=self.lnb_bc[:, :], op=ALU.add), reads=[zb[i], self.ln_b_], writes=[zb[i]])
                    if l == DEPTH - 1 or self.single:
                        Sx.dma("sp", self.out[s, tt * 128:(tt + 1) * 128, :], z[i][:, :], reads=[zb[i]], writes=[self.out_b])
                    else:
                        Sx.dma("sp", self.x1[s, tt * 128:(tt + 1) * 128, :], z[i][:, :], reads=[zb[i]], writes=[self.x1_buf[s]])
                        self.transpose_into_xT(z[i], zb[i], tt)
                Sx.barrier()


def _shard_inputs(inputs):
    consts = _consts_host()
    maps = []
    for c in range(NCORES):
        m = {}
        sl = slice(c * SEQ_PER_CORE, (c + 1) * SEQ_PER_CORE)
        for n, shp, dt in PARAM_SPECS:
            a = np.asarray(inputs[n])
            if n in ("x", "mem", "positions"):
                a = a[sl]
            m[n] = np.ascontiguousarray(a)
        m.update(consts)
        maps.append(m)
    return maps


_PROG = {}


FUSED = True


def kernel(**inputs):
    if FUSED:
        if "f" not in _PROG:
            _PROG["f"] = K({}).build()
        res = run_bass_kernel_spmd(_PROG["f"], _shard_inputs(inputs), core_ids=list(range(NCORES)))
        return np.concatenate([np.asarray(r["out"], dtype=np.float32) for r in res.results], axis=0)
    return kernel_unfused(**inputs)


def kernel_unfused(**inputs):
    if "p" not in _PROG:
        kb = K({"nseq": 1, "nlay": 1, "single": True, "seqs": [0], "layers": [0]})
        _PROG["p"] = kb.build()
    nc = _PROG["p"]
    consts = _consts_host()
    xs = np.asarray(inputs["x"], dtype=np.float32)
    names = [n for n, _, _ in PARAM_SPECS if n not in ("x", "mem", "positions")]
    out = np.empty_like(xs)
    for slot in range(SEQ_PER_CORE):
        cur = [np.ascontiguousarray(xs[c * SEQ_PER_CORE + slot][None]) for c in range(NCORES)]
        for l in range(DEPTH):
            wl = {n: np.ascontiguousarray(np.asarray(inputs[n])[l:l + 1]) for n in names}
            maps = []
            for c in range(NCORES):
                b = c * SEQ_PER_CORE + slot
                m = dict(wl)
                m["x"] = cur[c]
                m["mem"] = np.ascontiguousarray(np.asarray(inputs["mem"])[b:b + 1])
                m["positions"] = np.ascontiguousarray(np.asarray(inputs["positions"])[b:b + 1])
                m.update(consts)
                maps.append(m)
            res = run_bass_kernel_spmd(nc, maps, core_ids=list(range(NCORES)))
            cur = [np.ascontiguousarray(np.asarray(r["out"], dtype=np.float32)) for r in res.results]
        for c in range(NCORES):
            out[c * SEQ_PER_CORE + slot] = cur[c][0]
    return out
```
